# Optimizing a Trainium2 kernel written in Bass

```python
import jax
import jax.numpy as jnp
from jax import lax
import numpy as np


D_MODEL = 4096
BATCH = 2
SEQ = 8192
DEPTH = 2

HEAD_DIM = 128
MIX_WIDTH = D_MODEL
A_HEADS = MIX_WIDTH // (2 * HEAD_DIM)
B_HEADS = MIX_WIDTH // (2 * HEAD_DIM)
A_WIDTH = A_HEADS * HEAD_DIM
B_WIDTH = B_HEADS * HEAD_DIM
HGRN_CHUNK = 64
SB_Q_BLOCK = 128
NSA_HEADS = MIX_WIDTH // HEAD_DIM
NSA_KV_HEADS = 4
NSA_GROUP = NSA_HEADS // NSA_KV_HEADS
KV_WIDTH = NSA_KV_HEADS * HEAD_DIM
CMP_LEN = 32
CMP_STRIDE = 16
SLC_LEN = 64
SLC_TOP = 16
WINDOW = 512
NSA_Q_BLOCK = 64
N_MEM = 256
XA_HEADS = 4
XA_WIDTH = XA_HEADS * HEAD_DIM
D_FF = ((8 * D_MODEL // 3 + 255) // 256) * 256
CONV_W = 3
ROPE_THETA = 10000.0
DN_ALPHA = (2 * DEPTH) ** 0.25
DN_BETA = (8 * DEPTH) ** -0.25
N_EVEN = (DEPTH + 1) // 2
N_ODD = DEPTH // 2
LN_EPS = 1e-5
RMS_EPS = 1e-6
NEG_INF = -1e30
FORCE_SCORE = 1e9

kernel_name = 'hybrid_hgrn2_stickbreak_nsa_deepnorm'


def layer_norm(x, g, b):
    xf = x.astype(jnp.float32)
    mu = jnp.mean(xf, axis=-1, keepdims=True)
    var = jnp.mean(jnp.square(xf - mu), axis=-1, keepdims=True)
    y = (xf - mu) * lax.rsqrt(var + LN_EPS) * g.astype(jnp.float32) + b.astype(jnp.float32)
    return y.astype(x.dtype)


def rope(t, pos):
    half = t.shape[-1] // 2
    inv_freq = ROPE_THETA ** (-jnp.arange(half, dtype=jnp.float32) / half)
    ang = pos.astype(jnp.float32)[:, None] * inv_freq[None, :]
    cos = jnp.cos(ang)[None, :, None, :]
    sin = jnp.sin(ang)[None, :, None, :]
    tf = t.astype(jnp.float32)
    t1, t2 = tf[..., :half], tf[..., half:]
    return jnp.concatenate([t1 * cos - t2 * sin, t2 * cos + t1 * sin], axis=-1).astype(t.dtype)


def masked_softmax(s, mask):
    s = jnp.where(mask, s, NEG_INF)
    return jnp.where(mask, jax.nn.softmax(s, axis=-1), 0.0)


def hgrn2_chunked(q, f_logit, i, lb):
    B, S, H, Dk = q.shape
    Dv = i.shape[-1]
    C = HGRN_CHUNK
    NC = S // C
    zf = f_logit.astype(jnp.float32)
    log_f = jnp.log(lb + (1.0 - lb) * jax.nn.sigmoid(zf))
    k = (1.0 - lb) * jax.nn.sigmoid(-zf)

    def chunks(t):
        return t.reshape(B, NC, C, H, t.shape[-1]).transpose(1, 0, 3, 2, 4)

    qc = chunks(q.astype(jnp.float32))
    kc = chunks(k)
    vc = chunks(i.astype(jnp.float32))
    bc = jnp.cumsum(chunks(log_f), axis=3)
    causal = jnp.tril(jnp.ones((C, C), dtype=bool))[:, :, None]

    def step(state, inp):
        q_, k_, v_, b_ = inp
        diff = b_[:, :, :, None, :] - b_[:, :, None, :, :]
        decay = jnp.exp(jnp.where(causal, diff, -jnp.inf))
        scores = jnp.einsum('bhtd,bhtsd->bhts', q_, decay * k_[:, :, None, :, :])
        o = jnp.einsum('bhts,bhsv->bhtv', scores, v_)
        o = o + jnp.einsum('bhtd,bhdv->bhtv', q_ * jnp.exp(b_), state)
        b_last = b_[:, :, -1, :]
        k_dec = k_ * jnp.exp(b_last[:, :, None, :] - b_)
        state = state * jnp.exp(b_last)[..., None] + jnp.einsum('bhsd,bhsv->bhdv', k_dec, v_)
        return state, o

    state0 = jnp.zeros((B, H, Dk, Dv), jnp.float32)
    _, o = lax.scan(step, state0, (qc, kc, vc, bc))
    return o.transpose(1, 0, 3, 2, 4).reshape(B, S, H, Dv)


def stick_breaking_attention(q, k, v):
    B, S, H, Dh = q.shape
    nb = S // SB_Q_BLOCK
    scale = Dh ** -0.5
    key_pos = jnp.arange(S)
    qb = q.reshape(B, nb, SB_Q_BLOCK, H, Dh).transpose(1, 0, 3, 2, 4)

    def block(args):
        q_blk, bi = args
        z = jnp.einsum('bhqd,bshd->bhqs', q_blk, k, preferred_element_type=jnp.float32) * scale
        t = bi * SB_Q_BLOCK + jnp.arange(SB_Q_BLOCK)
        mask = key_pos[None, :] < t[:, None]
        log_1m = jnp.where(mask, jax.nn.log_sigmoid(-z), 0.0)
        rev = lax.cumsum(log_1m, axis=3, reverse=True) - log_1m
        w = jnp.where(mask, jnp.exp(jax.nn.log_sigmoid(z) + rev), 0.0)
        return jnp.einsum('bhqs,bshd->bqhd', w.astype(v.dtype), v)

    out = lax.map(block, (qb, jnp.arange(nb)))
    return out.transpose(1, 0, 2, 3, 4).reshape(B, S, H, Dh)


def hgrn_stickbreak_mixer(h, w_in, lb, norm_w, w_out):
    B, S, _ = h.shape
    proj = h @ w_in
    cuts = [A_WIDTH, 2 * A_WIDTH, 3 * A_WIDTH, 4 * A_WIDTH, 4 * A_WIDTH + B_WIDTH, 4 * A_WIDTH + 2 * B_WIDTH]
    a_q, a_f, a_i, a_g, b_q, b_k, b_v = jnp.split(proj, cuts, axis=-1)

    def heads(t, n):
        return t.reshape(B, S, n, HEAD_DIM)

    o_a = hgrn2_chunked(heads(a_q, A_HEADS), heads(a_f, A_HEADS), heads(a_i, A_HEADS),
                        lb.reshape(A_HEADS, HEAD_DIM))
    gate = jax.nn.silu(heads(a_g, A_HEADS).astype(jnp.float32))
    o_a = o_a * lax.rsqrt(jnp.mean(jnp.square(o_a), axis=-1, keepdims=True) + RMS_EPS) \
        * norm_w.astype(jnp.float32) * gate
    o_b = stick_breaking_attention(heads(b_q, B_HEADS), heads(b_k, B_HEADS), heads(b_v, B_HEADS))
    o = jnp.concatenate([o_a.astype(h.dtype).reshape(B, S, A_WIDTH), o_b.reshape(B, S, B_WIDTH)], axis=-1)
    return o @ w_out


def compress_blocks(t, pos_emb, w1, w2, n_cmp):
    idx = jnp.arange(n_cmp)[:, None] * CMP_STRIDE + jnp.arange(CMP_LEN)[None, :]
    blocks = t[:, idx] + pos_emb[None, None, :, None, :]
    hid = jax.nn.gelu(jnp.einsum('bnlgd,lde->bnge', blocks, w1), approximate=False)
    return jnp.einsum('bnge,ef->bngf', hid, w2)


def nsa_mixer(h, w_in, cmp_pos, cmp_w1, cmp_w2, w_out, pos):
    B, S, _ = h.shape
    G, R, Dh = NSA_KV_HEADS, NSA_GROUP, HEAD_DIM
    scale = Dh ** -0.5
    proj = h @ w_in
    q_w = NSA_HEADS * Dh
    cuts = [q_w + j * KV_WIDTH for j in range(7)]
    q, kc, vc, ks, vs, kw, vw, gl = jnp.split(proj, cuts, axis=-1)
    q = q.reshape(B, S, NSA_HEADS, Dh)
    q_nope = q.reshape(B, S, G, R, Dh)
    q_rot = rope(q, pos).reshape(B, S, G, R, Dh)
    kc, vc, ks, vs, kw, vw = [t.reshape(B, S, G, Dh) for t in (kc, vc, ks, vs, kw, vw)]
    ks = rope(ks, pos)
    kw = rope(kw, pos)
    gates = jax.nn.sigmoid(gl.astype(jnp.float32)).reshape(B, S, G, R, 3)

    n_cmp = (S - CMP_LEN) // CMP_STRIDE + 1
    k_cmp = compress_blocks(kc, cmp_pos[0], cmp_w1[0], cmp_w2[0], n_cmp)
    v_cmp = compress_blocks(vc, cmp_pos[1], cmp_w1[1], cmp_w2[1], n_cmp)
    cmp_start = jnp.arange(n_cmp) * CMP_STRIDE
    cmp_end = cmp_start + CMP_LEN - 1
    n_slc = S // SLC_LEN
    n_top = min(SLC_TOP, n_slc)
    slc_start = jnp.arange(n_slc) * SLC_LEN
    overlap = ((cmp_start[:, None] < slc_start[None, :] + SLC_LEN)
               & (cmp_start[:, None] + CMP_LEN > slc_start[None, :])).astype(jnp.float32)
    k_blocks = ks.reshape(B, n_slc, SLC_LEN, G, Dh).transpose(0, 3, 1, 2, 4)
    v_blocks = vs.reshape(B, n_slc, SLC_LEN, G, Dh).transpose(0, 3, 1, 2, 4)
    b_ix = jnp.arange(B)[:, None, None, None]
    g_ix = jnp.arange(G)[None, :, None, None]
    kw_pad = jnp.pad(kw, ((0, 0), (WINDOW, 0), (0, 0), (0, 0)))
    vw_pad = jnp.pad(vw, ((0, 0), (WINDOW, 0), (0, 0), (0, 0)))
    QB = NSA_Q_BLOCK
    nb = S // QB

    def block(bi):
        q0 = bi * QB
        t = q0 + jnp.arange(QB)
        qr = lax.dynamic_slice_in_dim(q_rot, q0, QB, axis=1)
        qn = lax.dynamic_slice_in_dim(q_nope, q0, QB, axis=1)
        gt = lax.dynamic_slice_in_dim(gates, q0, QB, axis=1)
        s_c = jnp.einsum('bqgrd,bngd->bgrqn', qn, k_cmp, preferred_element_type=jnp.float32) * scale
        p_c = masked_softmax(s_c, cmp_end[None, :] <= t[:, None])
        o_c = jnp.einsum('bgrqn,bngd->bqgrd', p_c.astype(v_cmp.dtype), v_cmp)
        imp = jnp.einsum('bgrqn,nj->bgqj', p_c, overlap)
        cur = t // SLC_LEN
        j = jnp.arange(n_slc)
        forced = (j[None, :] == 0) | (j[None, :] == cur[:, None]) | (j[None, :] == cur[:, None] - 1)
        allowed = j[None, :] * SLC_LEN <= t[:, None]
        score = jnp.where(forced, FORCE_SCORE, jnp.where(allowed, imp, -1.0))
        _, idx = lax.top_k(score, n_top)
        k_sel = k_blocks[b_ix, g_ix, idx]
        v_sel = v_blocks[b_ix, g_ix, idx]
        tok = idx[..., None] * SLC_LEN + jnp.arange(SLC_LEN)
        smask = (tok <= t[None, None, :, None, None]).reshape(B, G, 1, QB, n_top * SLC_LEN)
        s_s = jnp.einsum('bqgrd,bgqkld->bgrqkl', qr, k_sel, preferred_element_type=jnp.float32) * scale
        p_s = masked_softmax(s_s.reshape(B, G, R, QB, n_top * SLC_LEN), smask)
        o_s = jnp.einsum('bgrqkl,bgqkld->bqgrd',
                         p_s.reshape(B, G, R, QB, n_top, SLC_LEN).astype(v_sel.dtype), v_sel)
        kwb = lax.dynamic_slice_in_dim(kw_pad, q0, QB + WINDOW, axis=1)
        vwb = lax.dynamic_slice_in_dim(vw_pad, q0, QB + WINDOW, axis=1)
        kp = q0 - WINDOW + jnp.arange(QB + WINDOW)
        wmask = (kp[None, :] >= 0) & (kp[None, :] <= t[:, None]) & (kp[None, :] > t[:, None] - WINDOW)
        s_w = jnp.einsum('bqgrd,bkgd->bgrqk', qr, kwb, preferred_element_type=jnp.float32) * scale
        p_w = masked_softmax(s_w, wmask)
        o_w = jnp.einsum('bgrqk,bkgd->bqgrd', p_w.astype(vwb.dtype), vwb)
        out = gt[..., 0:1] * o_c + gt[..., 1:2] * o_s + gt[..., 2:3] * o_w
        return out.astype(h.dtype)

    o = lax.map(block, jnp.arange(nb))
    o = o.transpose(1, 0, 2, 3, 4, 5).reshape(B, S, NSA_HEADS * Dh)
    return o @ w_out


def memory_cross_attention(h, mem, w_q, w_kv, w_o):
    B, S, _ = h.shape
    q = (h @ w_q).reshape(B, S, XA_HEADS, HEAD_DIM)
    kv = (mem @ w_kv).reshape(mem.shape[0], mem.shape[1], 2, XA_HEADS, HEAD_DIM)
    k, v = kv[:, :, 0], kv[:, :, 1]
    s = jnp.einsum('bshd,bmhd->bhsm', q, k, preferred_element_type=jnp.float32) * HEAD_DIM ** -0.5
    p = jax.nn.softmax(s, axis=-1)
    o = jnp.einsum('bhsm,bmhd->bshd', p.astype(v.dtype), v).reshape(B, S, XA_WIDTH)
    return o @ w_o


def conv_glu_ffn(h, w_up, conv_w, w_down):
    S = h.shape[1]
    a, u = jnp.split(h @ w_up, 2, axis=-1)
    a_pad = jnp.pad(a, ((0, 0), (CONV_W - 1, 0), (0, 0)))
    c = conv_w[CONV_W - 1] * a
    for tap in range(CONV_W - 1):
        c = c + conv_w[tap] * a_pad[:, tap:tap + S]
    return (jax.nn.gelu(c, approximate=False) * u) @ w_down


def setup_inputs(seed: int = 0) -> dict:
    key = jax.random.key(seed)
    ks = jax.random.split(key, 20)
    D = D_MODEL
    ab_in = 4 * A_WIDTH + 3 * B_WIDTH
    nsa_in = NSA_HEADS * HEAD_DIM + 6 * KV_WIDTH + 3 * NSA_HEADS

    def nrm(k, shape, s):
        return jax.random.normal(k, shape, jnp.float32) * s

    return {
        'x': nrm(ks[0], (BATCH, SEQ, D), 1.0),
        'mem': nrm(ks[1], (BATCH, N_MEM, D), 1.0),
        'ab_w_in': nrm(ks[2], (N_EVEN, D, ab_in), D ** -0.5),
        'hgrn_lb': nrm(ks[3], (N_EVEN + 1, A_WIDTH), 0.1),
        'hgrn_norm_w': 1.0 + nrm(ks[4], (N_EVEN, HEAD_DIM), 0.02),
        'ab_w_out': nrm(ks[5], (N_EVEN, A_WIDTH + B_WIDTH, D), DN_BETA * (A_WIDTH + B_WIDTH) ** -0.5),
        'nsa_w_in': nrm(ks[6], (N_ODD, D, nsa_in), D ** -0.5),
        'nsa_cmp_pos': nrm(ks[7], (N_ODD, 2, CMP_LEN, HEAD_DIM), 0.1),
        'nsa_cmp_w1': nrm(ks[8], (N_ODD, 2, CMP_LEN, HEAD_DIM, HEAD_DIM), (CMP_LEN * HEAD_DIM) ** -0.5),
        'nsa_cmp_w2': nrm(ks[9], (N_ODD, 2, HEAD_DIM, HEAD_DIM), HEAD_DIM ** -0.5),
        'nsa_w_out': nrm(ks[10], (N_ODD, NSA_HEADS * HEAD_DIM, D), DN_BETA * (NSA_HEADS * HEAD_DIM) ** -0.5),
        'xa_w_q': nrm(ks[11], (DEPTH, D, XA_WIDTH), D ** -0.5),
        'xa_w_kv': nrm(ks[12], (DEPTH, D, 2 * XA_WIDTH), D ** -0.5),
        'xa_w_o': nrm(ks[13], (DEPTH, XA_WIDTH, D), DN_BETA * XA_WIDTH ** -0.5),
        'ffn_w_up': nrm(ks[14], (DEPTH, D, 2 * D_FF), D ** -0.5),
        'ffn_conv': nrm(ks[15], (DEPTH, CONV_W, D_FF), CONV_W ** -0.5),
        'ffn_w_down': nrm(ks[16], (DEPTH, D_FF, D), DN_BETA * D_FF ** -0.5),
        'ln_g': 1.0 + nrm(ks[17], (DEPTH, 3, D), 0.02),
        'ln_b': nrm(ks[18], (DEPTH, 3, D), 0.02),
    }


def reference(x, mem, ab_w_in, hgrn_lb, hgrn_norm_w, ab_w_out, nsa_w_in, nsa_cmp_pos, nsa_cmp_w1,
              nsa_cmp_w2, nsa_w_out, xa_w_q, xa_w_kv, xa_w_o, ffn_w_up, ffn_conv, ffn_w_down, ln_g, ln_b):
    S = x.shape[1]
    pos = jnp.arange(S)
    lb_all = jnp.cumsum(jax.nn.softmax(hgrn_lb.astype(jnp.float32), axis=0), axis=0)
    h = x
    for layer in range(DEPTH):
        if layer % 2 == 0:
            e = layer // 2
            mix = hgrn_stickbreak_mixer(h, ab_w_in[e], lb_all[e], hgrn_norm_w[e], ab_w_out[e])
        else:
            o = layer // 2
            mix = nsa_mixer(h, nsa_w_in[o], nsa_cmp_pos[o], nsa_cmp_w1[o], nsa_cmp_w2[o], nsa_w_out[o], pos)
        h = layer_norm(DN_ALPHA * h + mix, ln_g[layer, 0], ln_b[layer, 0])
        h = layer_norm(DN_ALPHA * h + memory_cross_attention(h, mem, xa_w_q[layer], xa_w_kv[layer], xa_w_o[layer]),
                       ln_g[layer, 1], ln_b[layer, 1])
        h = layer_norm(DN_ALPHA * h + conv_glu_ffn(h, ffn_w_up[layer], ffn_conv[layer], ffn_w_down[layer]),
                       ln_g[layer, 2], ln_b[layer, 2])
    return h
```

```python
import numpy as np
import ml_dtypes
from contextlib import ExitStack
import concourse.bass as bass
import concourse.mybir as mybir


F32 = mybir.dt.float32
BF16 = mybir.dt.bfloat16
AF = mybir.ActivationFunctionType
ALU = mybir.AluOpType


class Res:
    __slots__ = ("name", "w", "r", "t", "multi", "mw")

    def __init__(self, name, t=None, multi=False):
        self.name = name
        self.w = None
        self.r = {}
        self.t = t
        self.multi = multi
        self.mw = {}

    def __getitem__(self, idx):
        return self.t[idx]


class Sched:
    def __init__(self, nc, es: ExitStack, n_dma_slots=8):
        self.nc = nc
        self.es = es
        self.es0 = es
        self.E = {"pe": nc.tensor, "act": nc.scalar, "dve": nc.vector, "pool": nc.gpsimd, "sp": nc.sync}
        self.sem = {k: es.enter_context(nc.semaphore("prog_" + k)) for k in self.E}
        self.cnt = {k: 0 for k in self.E}
        self.known = {k: {} for k in self.E}
        self.slots = {}
        for q in ("sp", "pool", "act"):
            self.slots[q] = [[es.enter_context(nc.semaphore(f"dq_{q}_{i}")), 0] for i in range(n_dma_slots)]
        self.slot_i = {q: 0 for q in self.slots}
        self.n_ins = 0
        self.n_wait = 0
        self.prefix = ""
        self.nphase = 0

    def new_phase(self):
        self.prefix = f"p{self.nphase}_"
        self.nphase += 1

    def sbuf(self, name, shape, dt):
        t = self.es.enter_context(self.nc.sbuf_tensor("sb_" + self.prefix + name, list(shape), dt))
        return Res(name, t)

    def psum(self, name, shape, dt=F32):
        t = self.es.enter_context(self.nc.psum_tensor("ps_" + self.prefix + name, list(shape), dt))
        return Res(name, t)

    def dram(self, name, shape, dt, kind="Internal", multi=True):
        t = self.nc.dram_tensor(name, list(shape), dt, kind=kind)
        return Res(name, t.ap(), multi=multi)

    def dram_cc(self, name, shape, dt):
        t = self.nc.dram_tensor(name, list(shape), dt)
        return Res(name, t.ap(), multi=True)

    def collective(self, kind, groups, src, dst):
        if not hasattr(self, "cc_sem"):
            self.cc_sem = self.es0.enter_context(self.nc.semaphore("cc_sem"))
            self.cc_cnt = 0
        self._deps("pool", [src], [])
        for ev in list(dst.mw.values()) + list(dst.r.values()):
            self._wait("pool", ev)
        ins = self.nc.gpsimd.collective_compute(kind, mybir.AluOpType.bypass, replica_groups=groups, ins=[src.t.opt()], outs=[dst.t.opt()])
        self.cc_cnt += 1
        ins.then_inc(self.cc_sem, 1)
        ev = (self.cc_sem, self.cc_cnt, "cc")
        src.r[id(self.cc_sem)] = ev
        dst.mw[id(self.cc_sem)] = ev
        self.n_ins += 1

    def _wait(self, eng, ev):
        sem, val, src = ev
        if src == eng and eng == "pe":
            return
        k = id(sem)
        if self.known[eng].get(k, 0) >= val:
            return
        self.E[eng].wait_ge(sem, val)
        self.known[eng][k] = val
        self.n_wait += 1

    def _deps(self, eng, reads, writes):
        for r in reads:
            if r.w is not None:
                self._wait(eng, r.w)
            for ev in r.mw.values():
                self._wait(eng, ev)
        for w in writes:
            if w.multi:
                continue
            if w.w is not None:
                self._wait(eng, w.w)
            for ev in w.r.values():
                self._wait(eng, ev)

    def _record(self, ev, reads, writes):
        sem = ev[0]
        for r in reads:
            r.r[id(sem)] = ev
        for w in writes:
            if w.multi:
                w.mw[id(sem)] = ev
                continue
            w.w = ev
            w.r = {}

    def op(self, eng, reads, writes, fn):
        self._deps(eng, reads, writes)
        ins = fn(self.E[eng])
        self.cnt[eng] += 1
        ins.then_inc(self.sem[eng], 1)
        ev = (self.sem[eng], self.cnt[eng], eng)
        self.known[eng][id(self.sem[eng])] = max(self.known[eng].get(id(self.sem[eng]), 0), 0)
        self._record(ev, reads, writes)
        self.n_ins += 1
        return ins

    def dma(self, q, reads, writes, out, in_, **kw):
        slots = self.slots[q]
        i = self.slot_i[q]
        self.slot_i[q] = (i + 1) % len(slots)
        s = slots[i]
        if s[1] > 0:
            self._wait(q, (s[0], s[1], "dma"))
        self._deps(q, reads, writes)
        ins = self.E[q].dma_start(out=out, in_=in_, **kw)
        s[1] += 16
        ins.then_inc(s[0], 16)
        ev = (s[0], s[1], "dma")
        self._record(ev, reads, writes)
        self.n_ins += 1
        return ins

    def wait_all(self, eng, resources):
        for r in resources:
            if r.w is not None:
                self._wait(eng, r.w)
            for ev in r.mw.values():
                self._wait(eng, ev)

    def barrier(self):
        evs = [(self.sem[k], self.cnt[k], k) for k in self.E if self.cnt[k] > 0]
        for q, sl in self.slots.items():
            for s in sl:
                if s[1] > 0:
                    evs.append((s[0], s[1], "dma"))
        for eng in self.E:
            for ev in evs:
                sem, val, src = ev
                if self.known[eng].get(id(sem), 0) >= val:
                    continue
                self.E[eng].wait_ge(sem, val)
                self.known[eng][id(sem)] = val


class Ring:
    def __init__(self, items):
        self.items = items
        self.i = 0

    def next(self):
        r = self.items[self.i]
        self.i = (self.i + 1) % len(self.items)
        return r


def views(S, name, shape, dt, n, idx_fn):
    t = S.es.enter_context(S.nc.sbuf_tensor("sb_" + S.prefix + name, list(shape), dt))
    return t, [Res(f"{name}{i}", idx_fn(t, i)) for i in range(n)]


class Ring:
    def __init__(self, items):
        self.items = items
        self.i = 0

    def next(self):
        r = self.items[self.i]
        self.i = (self.i + 1) % len(self.items)
        return r


def views(S, name, shape, dt, n, idx_fn):
    t = S.es.enter_context(S.nc.sbuf_tensor("sb_" + S.prefix + name, list(shape), dt))
    return t, [Res(f"{name}{i}", idx_fn(t, i)) for i in range(n)]


def emit_post(S, c, dr, blocks):
    nc = S.nc
    D, DFF, XH, NM = c["D"], c["DFF"], c["XH"], c["NM"]
    KC, FC = D // 128, DFF // 128
    XW = XH * 128
    TM = c["T"]
    alpha, eps = c["alpha"], c["eps"]
    GF = c.get("GF", 8)
    NW = c.get("NW", 4)
    KCW = max(KC, GF, XH)
    oes = S.es
    S.new_phase()
    with ExitStack() as pes:
        S.es = pes
        _, wsl = views(S, "wsl", [128, NW, KCW, 128], BF16, NW, lambda t, i: t[:, i, :, :])
        wring = Ring(wsl)
        ones_f = S.sbuf("ones_f", [128, 128], F32)
        ones_b = S.sbuf("ones_b", [128, 128], BF16)
        epsT = S.sbuf("epsT", [128, 1], F32)
        lng = S.sbuf("lng", [128, 3, KC], F32)
        lnb = S.sbuf("lnb", [128, 3, KC], F32)
        cw = S.sbuf("cw", [128, 3, FC], F32)
        flag = S.sbuf("flag", [128, 1], F32)
        if "oAG_get" in dr:
            cand = Ring([S.sbuf(f"cand{i}", [128, TM], BF16) for i in range(6)])
            selacc = Ring([S.sbuf(f"selacc{i}", [128, TM], BF16) for i in range(4)])
            fsel = S.sbuf("fsel", [128, 4], F32)
            S.dma("sp", [dr["fsel"]], [fsel], out=fsel.t[:], in_=dr["fsel"].t)
        if "hlAG_get" in dr:
            candf = Ring([S.sbuf(f"candf{i}", [128, 128], F32) for i in range(4)])
            fhal = S.sbuf("fhal", [128, 4], F32)
            S.dma("sp", [dr["fhal"]], [fhal], out=fhal.t[:], in_=dr["fhal"].t)
        S.dma("sp", [dr["flag"]], [flag], out=flag.t[:], in_=dr["flag"].t)
        carry = [S.sbuf(f"carry{f}", [128, 2], F32) for f in range(FC)]
        KT = S.sbuf("KT", [128, XH, NM], BF16)
        Vt = [S.sbuf(f"Vt{m}", [128, XW], BF16) for m in range(NM // 128)]
        qT = [S.sbuf(f"qT{h}", [128, TM], BF16) for h in range(XH)]
        xoT = [S.sbuf(f"xoT{h}", [128, TM], BF16) for h in range(XH)]
        pT = Ring([S.sbuf(f"pT{i}", [128, TM], BF16) for i in range(4)])
        tmpf = Ring([S.sbuf(f"tmpf{i}", [128, TM + 2], F32) for i in range(6)])
        st_mean = S.sbuf("st_mean", [128, TM], F32)
        st_rstd = S.sbuf("st_rstd", [128, TM], F32)
        st_nmr = S.sbuf("st_nmr", [128, TM], F32)
        pb = [S.psum(f"pb{i}", [128, 512], F32) for i in range(8)]
        pring = Ring(pb[0:4])
        pA, pB2 = pb[4], pb[5]
        pring2 = Ring(pb[6:8])

        S.op("dve", [], [ones_f], lambda e: e.memset(ones_f.t[:], 1.0))
        S.op("dve", [], [ones_b], lambda e: e.memset(ones_b.t[:], 1.0))
        S.op("dve", [], [epsT], lambda e: e.memset(epsT.t[:], eps))
        S.dma("sp", [dr["ln_g"]], [lng], out=lng.t[:], in_=dr["ln_g"].t)
        S.dma("sp", [dr["ln_b"]], [lnb], out=lnb.t[:], in_=dr["ln_b"].t)
        S.dma("sp", [dr["conv"]], [cw], out=cw.t[:], in_=dr["conv"].t)
        for f in range(FC):
            S.op("dve", [], [carry[f]], lambda e, f=f: e.memset(carry[f].t[:], 0.0))

        def load_w(W, r0, nk, c0, ncol=128):
            slot = wring.next()
            S.dma("pool", [W], [slot], out=slot.t[:, 0:nk, 0:ncol],
                  in_=W.t[r0:r0 + nk * 128, c0:c0 + ncol].rearrange("(kc p) c -> p kc c", p=128))
            return slot

        def linear(W, r0, acts, T, cols, evac, ps_ring):
            nk = len(acts)
            pend = []
            npre = NW - 1
            for i in range(min(npre, len(cols))):
                pend.append(load_w(W, r0, nk, cols[i]))
            for i, c0 in enumerate(cols):
                if i + npre < len(cols):
                    pend.append(load_w(W, r0, nk, cols[i + npre]))
                slot = pend.pop(0)
                ps = ps_ring.next()
                for k in range(nk):
                    S.op("pe", [slot, acts[k]], [ps],
                         lambda e, k=k, slot=slot, ps=ps: e.matmul(ps.t[:, 0:T], lhsT=slot.t[:, k, :], rhs=acts[k].t[:, 0:T],
                                                                    start=(k == 0), stop=(k == nk - 1)))
                evac(i, ps)

        def layer_norm(li, T):
            for k in range(KC):
                S.op("pe", [ones_f, rT[k]], [pA],
                     lambda e, k=k: e.matmul(pA.t[:, 0:T], lhsT=ones_f.t[:], rhs=rT[k].t[:, 0:T], start=(k == 0), stop=(k == KC - 1)))
            for k in range(KC):
                sq = tmpf.next()
                S.op("act", [rT[k]], [sq], lambda e, k=k, sq=sq: e.activation(out=sq.t[:, 0:T], in_=rT[k].t[:, 0:T], func=AF.Square))
                S.op("pe", [ones_f, sq], [pB2],
                     lambda e, k=k, sq=sq: e.matmul(pB2.t[:, 0:T], lhsT=ones_f.t[:], rhs=sq.t[:, 0:T], start=(k == 0), stop=(k == KC - 1)))
            inv = 1.0 / D
            S.op("act", [pA], [st_mean], lambda e: e.activation(out=st_mean.t[:, 0:T], in_=pA.t[:, 0:T], func=AF.Copy, scale=inv))
            m2 = tmpf.next()
            S.op("dve", [st_mean], [m2], lambda e: e.tensor_tensor(out=m2.t[:, 0:T], in0=st_mean.t[:, 0:T], in1=st_mean.t[:, 0:T], op=ALU.mult))
            var = tmpf.next()
            S.op("dve", [pB2, m2], [var], lambda e: e.scalar_tensor_tensor(out=var.t[:, 0:T], in0=pB2.t[:, 0:T], scalar=inv, in1=m2.t[:, 0:T],
                                                                            op0=ALU.mult, op1=ALU.subtract))
            sd = tmpf.next()
            S.op("act", [var, epsT], [sd], lambda e: e.activation(out=sd.t[:, 0:T], in_=var.t[:, 0:T], func=AF.Sqrt, bias=epsT.t[:, 0:1]))
            S.op("dve", [sd], [st_rstd], lambda e: e.reciprocal(out=st_rstd.t[:, 0:T], in_=sd.t[:, 0:T]))
            S.op("dve", [st_mean, st_rstd], [st_nmr],
                 lambda e: e.scalar_tensor_tensor(out=st_nmr.t[:, 0:T], in0=st_mean.t[:, 0:T], scalar=-1.0, in1=st_rstd.t[:, 0:T], op0=ALU.mult, op1=ALU.mult))
            for k in range(KC):
                t1 = tmpf.next()
                S.op("dve", [rT[k], st_rstd], [t1], lambda e, k=k, t1=t1: e.tensor_tensor(out=t1.t[:, 0:T], in0=rT[k].t[:, 0:T], in1=st_rstd.t[:, 0:T], op=ALU.mult))
                t2 = tmpf.next()
                S.op("pool", [t1, st_nmr], [t2], lambda e, t1=t1, t2=t2: e.tensor_tensor(out=t2.t[:, 0:T], in0=t1.t[:, 0:T], in1=st_nmr.t[:, 0:T], op=ALU.add))
                S.op("act", [t2, lng, lnb], [rT[k]],
                     lambda e, k=k, t2=t2: e.activation(out=rT[k].t[:, 0:T], in_=t2.t[:, 0:T], func=AF.Identity,
                                                        scale=lng.t[:, li, k:k + 1], bias=lnb.t[:, li, k:k + 1]))
                S.op("act", [t2, lng, lnb], [aT[k]],
                     lambda e, k=k, t2=t2: e.activation(out=aT[k].t[:, 0:T], in_=t2.t[:, 0:T], func=AF.Identity,
                                                        scale=lng.t[:, li, k:k + 1], bias=lnb.t[:, li, k:k + 1]))

        def resid_evac(T):
            def ev(i, ps):
                S.op("dve", [rT[i], ps], [rT[i]],
                     lambda e, i=i, ps=ps: e.scalar_tensor_tensor(out=rT[i].t[:, 0:T], in0=rT[i].t[:, 0:T], scalar=alpha, in1=ps.t[:, 0:T],
                                                                  op0=ALU.mult, op1=ALU.add))
            return ev

        mes = ExitStack()
        S.es = mes
        memT = S.sbuf("memT", [128, KC, NM], BF16)
        S.es = pes
        S.dma("pool", [dr["memT"]], [memT], out=memT.t[:], in_=dr["memT"].t.rearrange("(kc p) m -> p kc m", p=128))
        memk = [Res(f"memk{k}", memT.t[:, k, :]) for k in range(KC)]
        for mk in memk:
            mk.w = memT.w

        def kt_evac(i, ps):
            S.op("act", [ps], [KT], lambda e, i=i, ps=ps: e.activation(out=KT.t[:, i, :], in_=ps.t[:, 0:NM], func=AF.Copy))
        linear(dr["w_kv"], 0, memk, NM, [h * 128 for h in range(XH)], kt_evac, pring)
        for h in range(XH):
            slot = load_w(dr["w_kv"], 0, KC, XW + h * 128)
            for m in range(NM // 128):
                ps = pring.next()
                for k in range(KC):
                    S.op("pe", [slot, memT], [ps],
                         lambda e, k=k, m=m, slot=slot, ps=ps: e.matmul(ps.t[:, 0:128], lhsT=memT.t[:, k, m * 128:(m + 1) * 128], rhs=slot.t[:, k, :],
                                                                        start=(k == 0), stop=(k == KC - 1)))
                S.op("act", [ps], [Vt[m]], lambda e, m=m, h=h, ps=ps: e.activation(out=Vt[m].t[:, h * 128:(h + 1) * 128], in_=ps.t[:, 0:128], func=AF.Copy))

        S.barrier()
        mes.close()
        _, rT = views(S, "rT", [128, KC, TM], F32, KC, lambda t, i: t[:, i, :])
        _, aT = views(S, "aT", [128, KC, TM], BF16, KC, lambda t, i: t[:, i, :])
        _, gT = views(S, "gT", [128, 2 * GF, TM], BF16, 2 * GF, lambda t, i: t[:, i, :])
        for (tok0, T, halo) in blocks:
            HL = c.get("HALO", 128)
            TOKC_ = c.get("TOKC", 2048)
            for k in range(KC):
                if "oAG_get" in dr:
                    for cq in range(4):
                        if halo:
                            cb = max(cq * TOKC_ - HL, 0)
                        else:
                            cb = cq * TOKC_ + (tok0 - HL)
                        cd = cand.next()
                        _ores, _oap = dr["oAG_get"](k, cb, T)
                        S.dma("sp", [_ores], [cd], out=cd.t[:, 0:T], in_=_oap)
                        if cq == 0:
                            acc = selacc.next()
                            S.op("dve", [cd, fsel], [acc], lambda e: e.tensor_scalar(out=acc.t[:, 0:T], in0=cd.t[:, 0:T], scalar1=fsel.t[:, 0:1], scalar2=None, op0=ALU.mult))
                        else:
                            dst = aT[k] if cq == 3 else selacc.next()
                            S.op("dve", [cd, fsel, acc], [dst], lambda e: e.scalar_tensor_tensor(out=dst.t[:, 0:T], in0=cd.t[:, 0:T], scalar=fsel.t[:, cq:cq + 1], in1=acc.t[:, 0:T],
                                                                                              op0=ALU.mult, op1=ALU.add))
                            acc = dst
                else:
                    S.dma("sp", [dr["oT"]], [aT[k]], out=aT[k].t[:, 0:T], in_=dr["oT"].t[k * 128:(k + 1) * 128, tok0:tok0 + T])
                if halo and "hlAG_get" in dr:
                    for cq in range(4):
                        cd = candf.next()
                        _lres, _lap = dr["hlAG_get"](k, cq)
                        S.dma("sp", [_lres], [cd], out=cd.t[:, 0:T], in_=_lap)
                        if cq == 0:
                            S.op("dve", [cd, fhal], [rT[k]], lambda e: e.tensor_scalar(out=rT[k].t[:, 0:T], in0=cd.t[:, 0:T], scalar1=fhal.t[:, 0:1], scalar2=None, op0=ALU.mult))
                        else:
                            S.op("dve", [cd, fhal, rT[k]], [rT[k]], lambda e: e.scalar_tensor_tensor(out=rT[k].t[:, 0:T], in0=cd.t[:, 0:T], scalar=fhal.t[:, cq:cq + 1], in1=rT[k].t[:, 0:T],
                                                                                                    op0=ALU.mult, op1=ALU.add))
                else:
                    S.dma("sp", [dr["hT_in"]], [rT[k]], out=rT[k].t[:, 0:T], in_=dr["hT_in"].t[k * 128:(k + 1) * 128, tok0:tok0 + T])
            linear(dr["w_out"], 0, aT, T, [i * 128 for i in range(KC)], resid_evac(T), pring)
            layer_norm(0, T)
            qscale = 128.0 ** -0.5

            def q_evac(i, ps):
                S.op("act", [ps], [qT[i]], lambda e, i=i, ps=ps: e.activation(out=qT[i].t[:, 0:T], in_=ps.t[:, 0:T], func=AF.Copy, scale=qscale))
            linear(dr["w_q"], 0, aT, T, [h * 128 for h in range(XH)], q_evac, pring)
            for h in range(XH):
                pts = []
                for m in range(NM // 128):
                    ps = pring.next()
                    S.op("pe", [KT, qT[h]], [ps], lambda e, h=h, m=m, ps=ps: e.matmul(ps.t[:, 0:T], lhsT=KT.t[:, h, m * 128:(m + 1) * 128], rhs=qT[h].t[:, 0:T],
                                                                                      start=True, stop=True))
                    pt = pT.next()
                    S.op("act", [ps], [pt], lambda e, ps=ps, pt=pt: e.activation(out=pt.t[:, 0:T], in_=ps.t[:, 0:T], func=AF.Exp))
                    pts.append(pt)
                nm = len(pts)
                for m, pt in enumerate(pts):
                    S.op("pe", [ones_b, pt], [pA], lambda e, m=m, pt=pt: e.matmul(pA.t[:, 0:T], lhsT=ones_b.t[:], rhs=pt.t[:, 0:T], start=(m == 0), stop=(m == nm - 1)))
                for m, pt in enumerate(pts):
                    S.op("pe", [Vt[m], pt], [pB2], lambda e, m=m, h=h, pt=pt: e.matmul(pB2.t[:, 0:T], lhsT=Vt[m].t[:, h * 128:(h + 1) * 128], rhs=pt.t[:, 0:T],
                                                                                     start=(m == 0), stop=(m == nm - 1)))
                rd = tmpf.next()
                S.op("dve", [pA], [rd], lambda e, rd=rd: e.reciprocal(out=rd.t[:, 0:T], in_=pA.t[:, 0:T]))
                S.op("dve", [pB2, rd], [xoT[h]], lambda e, h=h, rd=rd: e.tensor_tensor(out=xoT[h].t[:, 0:T], in0=pB2.t[:, 0:T], in1=rd.t[:, 0:T], op=ALU.mult))
            linear(dr["w_o"], 0, xoT, T, [i * 128 for i in range(KC)], resid_evac(T), pring)
            layer_norm(1, T)
            groups = [list(range(g0, min(g0 + GF, FC))) for g0 in range(0, FC, GF)]

            def up_group(gi, fl):
                gb = (gi % 2) * GF
                pend = []
                specs = []
                for f in fl:
                    specs.append((f, 0))
                    if not halo:
                        specs.append((f, 1))
                npre = NW - 1
                for i in range(min(npre, len(specs))):
                    f, w = specs[i]
                    pend.append(load_w(dr["w_up"], 0, KC, w * DFF + f * 128))
                abufs = {}
                for i, (f, w) in enumerate(specs):
                    if i + npre < len(specs):
                        f2, w2 = specs[i + npre]
                        pend.append(load_w(dr["w_up"], 0, KC, w2 * DFF + f2 * 128))
                    slot = pend.pop(0)
                    ps = (pring if w == 0 else pring2).next()
                    for k in range(KC):
                        S.op("pe", [slot, aT[k]], [ps], lambda e, k=k, slot=slot, ps=ps: e.matmul(ps.t[:, 0:T], lhsT=slot.t[:, k, :], rhs=aT[k].t[:, 0:T],
                                                                                                   start=(k == 0), stop=(k == KC - 1)))
                    if w == 0:
                        ab = tmpf.next()
                        S.op("act", [carry[f]], [ab], lambda e, ab=ab, f=f: e.activation(out=ab.t[:, 0:2], in_=carry[f].t[:, 0:2], func=AF.Copy))
                        S.op("act", [ps], [ab], lambda e, ab=ab, ps=ps: e.activation(out=ab.t[:, 2:2 + T], in_=ps.t[:, 0:T], func=AF.Copy))
                        if halo:
                            S.op("dve", [ab, flag], [carry[f]], lambda e, ab=ab, f=f: e.tensor_scalar(out=carry[f].t[:, 0:2], in0=ab.t[:, T:T + 2], scalar1=flag.t[:, 0:1], scalar2=None, op0=ALU.mult))
                        else:
                            S.op("dve", [ab], [carry[f]], lambda e, ab=ab, f=f: e.tensor_copy(out=carry[f].t[:, 0:2], in_=ab.t[:, T:T + 2]))
                        abufs[f] = ab
                        if not halo:
                            c1 = tmpf.next()
                            S.op("act", [ab, cw], [c1], lambda e, ab=ab, c1=c1, f=f: e.activation(out=c1.t[:, 0:T], in_=ab.t[:, 2:2 + T], func=AF.Copy,
                                                                                                  scale=cw.t[:, 2, f:f + 1]))
                            c2 = tmpf.next()
                            S.op("dve", [ab, cw, c1], [c2], lambda e, ab=ab, c1=c1, c2=c2, f=f: e.scalar_tensor_tensor(
                                out=c2.t[:, 0:T], in0=ab.t[:, 1:1 + T], scalar=cw.t[:, 1, f:f + 1], in1=c1.t[:, 0:T], op0=ALU.mult, op1=ALU.add))
                            c3 = tmpf.next()
                            S.op("dve", [ab, cw, c2], [c3], lambda e, ab=ab, c2=c2, c3=c3, f=f: e.scalar_tensor_tensor(
                                out=c3.t[:, 0:T], in0=ab.t[:, 0:T], scalar=cw.t[:, 0, f:f + 1], in1=c2.t[:, 0:T], op0=ALU.mult, op1=ALU.add))
                            gl = tmpf.next()
                            S.op("act", [c3], [gl], lambda e, c3=c3, gl=gl: e.activation(out=gl.t[:, 0:T], in_=c3.t[:, 0:T], func=AF.Gelu))
                            abufs[f] = gl
                    else:
                        gl = abufs[f]
                        gdst = gT[gb + (f - fl[0])]
                        S.op("dve", [gl, ps], [gdst], lambda e, gl=gl, ps=ps, gdst=gdst: e.tensor_tensor(out=gdst.t[:, 0:T], in0=gl.t[:, 0:T], in1=ps.t[:, 0:T], op=ALU.mult))

            def down_group(gi, fl):
                gb = (gi % 2) * GF
                nk = len(fl)
                acts = [gT[gb + j] for j in range(nk)]

                def ev(i, ps):
                    if gi == 0:
                        resid_evac(T)(i, ps)
                    else:
                        S.op("dve", [rT[i], ps], [rT[i]], lambda e, i=i, ps=ps: e.tensor_tensor(out=rT[i].t[:, 0:T], in0=rT[i].t[:, 0:T], in1=ps.t[:, 0:T], op=ALU.add))
                linear(dr["w_down"], fl[0] * 128, acts, T, [i * 128 for i in range(KC)], ev, pring)

            for gi, fl in enumerate(groups):
                up_group(gi, fl)
                if halo:
                    continue
                if gi > 0:
                    down_group(gi - 1, groups[gi - 1])
            if halo:
                continue
            down_group(len(groups) - 1, groups[-1])
            layer_norm(2, T)
            for k in range(KC):
                oo = c.get("out_off", 0)
                S.dma("sp", [rT[k]], [dr["hT_out"]], out=dr["hT_out"].t[k * 128:(k + 1) * 128, tok0 - oo:tok0 - oo + T], in_=rT[k].t[:, 0:T])
                if "hTb_put" in dr:
                    _bres, _bap = dr["hTb_put"](k, tok0 - c.get("outb_off", 0), T)
                    S.dma("sp", [aT[k]], [_bres], out=_bap, in_=aT[k].t[:, 0:T])
                elif "hTb_out" in dr:
                    bo = c.get("outb_off", 0)
                    S.dma("sp", [aT[k]], [dr["hTb_out"]], out=dr["hTb_out"].t[k * 128:(k + 1) * 128, tok0 - bo:tok0 - bo + T], in_=aT[k].t[:, 0:T])
                if "hl_put" in dr and (tok0, T, halo) == blocks[-1]:
                    _lres, _lap = dr["hl_put"](k)
                    S.dma("sp", [rT[k]], [_lres], out=_lap, in_=rT[k].t[:, T - 128:T])
        S.barrier()
    S.es = oes


NEG = -30000.0


def _oput(dr, r0, t0, T):
    if "oT_put" in dr:
        return dr["oT_put"](r0, t0, T)
    return dr["oT"], dr["oT"].t[r0:r0 + 128, t0:t0 + T]


def _linear(S, wring, NW, W, acts, T, cols, evac, ps_ring):
    nk = len(acts)

    def load_w(c0):
        slot = wring.next()
        S.dma("pool", [W], [slot], out=slot.t[:, 0:nk, :], in_=W.t[0:nk * 128, c0:c0 + 128].rearrange("(kc p) c -> p kc c", p=128))
        return slot
    pend = []
    npre = NW - 1
    for i in range(min(npre, len(cols))):
        pend.append(load_w(cols[i]))
    for i, c0 in enumerate(cols):
        if i + npre < len(cols):
            pend.append(load_w(cols[i + npre]))
        slot = pend.pop(0)
        ps = ps_ring.next()
        for k in range(nk):
            S.op("pe", [slot, acts[k]], [ps],
                 lambda e, k=k, slot=slot, ps=ps: e.matmul(ps.t[:, 0:T], lhsT=slot.t[:, k, :], rhs=acts[k].t[:, 0:T], start=(k == 0), stop=(k == nk - 1)))
        evac(i, ps)


def emit_sb(S, c, dr, heads, col_q, col_k, col_v, orow0):
    D, SEQ = c["D"], c["SEQ"]
    KC = D // 128
    T = 512
    NH = len(heads)
    NW = c.get("NW", 4)
    NT = SEQ // 128
    oes = S.es
    S.new_phase()
    with ExitStack() as pes:
        S.es = pes
        _, aT = views(S, "aT", [128, KC, T], BF16, KC, lambda t, i: t[:, i, :])
        _, wsl = views(S, "wsl", [128, NW, KC, 128], BF16, NW, lambda t, i: t[:, i, :, :])
        wring = Ring(wsl)
        kT = [S.sbuf(f"kT{h}", [128, SEQ], BF16) for h in range(NH)]
        _, Vall = views(S, "Vall", [128, NT, NH * 128], BF16, NT, lambda t, i: t[:, i, :])
        qT = [S.sbuf(f"qT{h}", [128, T], BF16) for h in range(NH)]
        vf = Ring([S.sbuf(f"vf{i}", [128, T], F32) for i in range(2)])
        ident = S.sbuf("ident", [128, 128], F32)
        identb = S.sbuf("identb", [128, 128], BF16)
        negU = S.sbuf("negU", [128, 128], BF16)
        negO = S.sbuf("negO", [128, 128], BF16)
        mask = S.sbuf("mask", [128, 4, T], BF16)
        eT = [Ring([S.sbuf(f"eT{h}_{i}", [128, T], F32) for i in range(2)]) for h in range(NH)]
        spT = [Ring([S.sbuf(f"spT{h}_{i}", [128, T], BF16) for i in range(2)]) for h in range(NH)]
        tmpT = [Ring([S.sbuf(f"tmpT{h}_{i}", [128, T], F32) for i in range(2)]) for h in range(NH)]
        xT_ = [Ring([S.sbuf(f"xT{h}_{i}", [128, T], F32) for i in range(2)]) for h in range(NH)]
        wT = [Ring([S.sbuf(f"wT{h}_{i}", [128, T], BF16) for i in range(2)]) for h in range(NH)]
        carry = [S.sbuf(f"carry{h}", [128, T], F32) for h in range(NH)]
        oS = Ring([S.sbuf(f"oS{i}", [128, T], BF16) for i in range(2)])
        pb = [S.psum(f"pb{i}", [128, 512], F32) for i in range(8)]
        pz = [pb[0], pb[1]]
        pR = [pb[2], pb[3]]
        pC = [pb[4], pb[5]]
        pO = [pb[6], pb[7]]
        plin = Ring([pb[2], pb[3], pb[4], pb[5]])
        ptr = Ring([pb[6], pb[7]])

        S.dma("sp", [dr["ident"]], [ident], out=ident.t[:], in_=dr["ident"].t)
        S.dma("sp", [dr["identb"]], [identb], out=identb.t[:], in_=dr["identb"].t)
        S.dma("sp", [dr["negU"]], [negU], out=negU.t[:], in_=dr["negU"].t)
        S.dma("sp", [dr["sbmask"]], [mask], out=mask.t[:], in_=dr["sbmask"].t)
        S.op("dve", [], [negO], lambda e: e.memset(negO.t[:], -1.0))
        scale = 128.0 ** -0.5

        for tb in range(SEQ // T):
            t0 = tb * T
            for k in range(KC):
                S.dma("pool", [dr["xT"]], [aT[k]], out=aT[k].t[:, :], in_=dr["xT"].t[k * 128:(k + 1) * 128, t0:t0 + T])
            cols = [col_q + h * 128 for h in heads] + [col_k + h * 128 for h in heads] + [col_v + h * 128 for h in heads]

            def evac(i, ps):
                kind, hi = divmod(i, NH)
                if kind == 0:
                    S.op("act", [ps], [qT[hi]], lambda e: e.activation(out=qT[hi].t[:, :], in_=ps.t[:, :], func=AF.Copy, scale=scale))
                elif kind == 1:
                    S.op("act", [ps], [kT[hi]], lambda e: e.activation(out=kT[hi].t[:, t0:t0 + T], in_=ps.t[:, :], func=AF.Copy))
                else:
                    v = vf.next()
                    S.op("act", [ps], [v], lambda e: e.activation(out=v.t[:, :], in_=ps.t[:, :], func=AF.Copy))
                    for j in range(T // 128):
                        pt = ptr.next()
                        S.op("pe", [v, ident], [pt], lambda e, j=j, pt=pt: e.transpose(out=pt.t[:, 0:128], in_=v.t[:, j * 128:(j + 1) * 128], identity=ident.t[:]))
                        vt = Vall[t0 // 128 + j]
                        S.op("dve", [pt], [vt], lambda e, pt=pt, vt=vt: e.tensor_copy(out=vt.t[:, hi * 128:(hi + 1) * 128], in_=pt.t[:, 0:128]))
            _linear(S, wring, NW, dr["wA"], aT, T, cols, evac, plin)

            kts = list(range(t0 // 128 + 3, -1, -1))

            def emit_z(kt):
                for h in range(NH):
                    kd = kt - t0 // 128
                    S.op("pe", [kT[h], qT[h]], [pz[h]], lambda e, h=h: e.matmul(pz[h].t[:, :], lhsT=kT[h].t[:, kt * 128:(kt + 1) * 128], rhs=qT[h].t[:, :],
                                                                              start=True, stop=(kd < 0)))
                    if kd >= 0:
                        S.op("pe", [identb, mask], [pz[h]], lambda e, h=h, kd=kd: e.matmul(pz[h].t[:, :], lhsT=identb.t[:], rhs=mask.t[:, kd, :], start=False, stop=True))
            emit_z(kts[0])
            for n, kt in enumerate(kts):
                first = (n == 0)
                last = (n == len(kts) - 1)
                cur = {}
                for h in range(NH):
                    e_ = eT[h].next()
                    sp = spT[h].next()
                    S.op("act", [pz[h]], [e_], lambda e, h=h, e_=e_: e.activation(out=e_.t[:, :], in_=pz[h].t[:, :], func=AF.Exp))
                    S.op("act", [e_], [sp], lambda e, e_=e_, sp=sp: e.activation(out=sp.t[:, :], in_=e_.t[:, :], func=AF.Ln, bias=1.0))
                    cur[h] = (e_, sp)
                for h in range(NH):
                    e_, sp = cur[h]
                    S.op("pe", [negU, sp], [pR[h]], lambda e, h=h, sp=sp: e.matmul(pR[h].t[:, :], lhsT=negU.t[:], rhs=sp.t[:, :], start=True, stop=True))
                    if not last:
                        S.op("pe", [negO, sp], [pC[h]], lambda e, h=h, sp=sp: e.matmul(pC[h].t[:, :], lhsT=negO.t[:], rhs=sp.t[:, :], start=True, stop=True))
                if not last:
                    emit_z(kts[n + 1])
                for h in range(NH):
                    e_, sp = cur[h]
                    x_ = xT_[h].next()
                    if first:
                        S.op("act", [pR[h]], [x_], lambda e, h=h, x_=x_: e.activation(out=x_.t[:, :], in_=pR[h].t[:, :], func=AF.Exp))
                        if not last:
                            S.op("dve", [pC[h]], [carry[h]], lambda e, h=h: e.tensor_copy(out=carry[h].t[:, :], in_=pC[h].t[:, :]))
                    else:
                        tm = tmpT[h].next()
                        S.op("dve", [pR[h], carry[h]], [tm], lambda e, h=h, tm=tm: e.tensor_tensor(out=tm.t[:, :], in0=pR[h].t[:, :], in1=carry[h].t[:, :], op=ALU.add))
                        S.op("act", [tm], [x_], lambda e, tm=tm, x_=x_: e.activation(out=x_.t[:, :], in_=tm.t[:, :], func=AF.Exp))
                        if not last:
                            S.op("dve", [pC[h], carry[h]], [carry[h]], lambda e, h=h: e.tensor_tensor(out=carry[h].t[:, :], in0=pC[h].t[:, :], in1=carry[h].t[:, :], op=ALU.add))
                    w_ = wT[h].next()
                    S.op("pool", [e_, x_], [w_], lambda e, e_=e_, x_=x_, w_=w_: e.tensor_tensor(out=w_.t[:, :], in0=e_.t[:, :], in1=x_.t[:, :], op=ALU.mult))
                    vt = Vall[kt]
                    S.op("pe", [vt, w_], [pO[h]], lambda e, h=h, vt=vt, w_=w_: e.matmul(pO[h].t[:, :], lhsT=vt.t[:, h * 128:(h + 1) * 128], rhs=w_.t[:, :],
                                                                                       start=first, stop=last))
            for h in range(NH):
                o_ = oS.next()
                S.op("act", [pO[h]], [o_], lambda e, h=h, o_=o_: e.activation(out=o_.t[:, :], in_=pO[h].t[:, :], func=AF.Copy))
                r0 = orow0 + heads[h] * 128
                _ores, _oap = _oput(dr, r0, t0, T)
                S.dma("sp", [o_], [_ores], out=_oap, in_=o_.t[:, :])
        S.barrier()
    S.es = oes


def emit_hgrn(S, c, dr, heads, col_q, col_f, col_i, col_g, orow0):
    D, SEQ = c["D"], c["SEQ"]
    KC = D // 128
    T = 512
    C = 64
    NH = len(heads)
    NW = c.get("NW", 4)
    oes = S.es
    S.new_phase()
    with ExitStack() as pes:
        S.es = pes
        _, aT = views(S, "aT", [128, KC, T], BF16, KC, lambda t, i: t[:, i, :])
        _, wsl = views(S, "wsl", [128, NW, KC, 128], BF16, NW, lambda t, i: t[:, i, :, :])
        wring = Ring(wsl)
        ident = S.sbuf("ident", [128, 128], F32)
        identb = S.sbuf("identb", [128, 128], BF16)
        maskU = S.sbuf("maskU", [64, 64], F32)
        rmask = S.sbuf("rmask", [128, T], F32)
        NWt = S.sbuf("NWt", [128, 128], F32)
        epsR = S.sbuf("epsR", [128, 1], F32)
        lbr = S.sbuf("lbr", [128, 2, NH], F32)
        lb = S.sbuf("lb", [128, NH], F32)
        oml = S.sbuf("oml", [128, NH], F32)
        noml = S.sbuf("noml", [128, NH], F32)
        qd = [S.sbuf(f"qd{h}", [128, T], BF16) for h in range(NH)]
        kd = [S.sbuf(f"kd{h}", [128, T], BF16) for h in range(NH)]
        iTb = [S.sbuf(f"iTb{h}", [128, T], BF16) for h in range(NH)]
        gf = [S.sbuf(f"gf{h}", [128, T], F32) for h in range(NH)]
        eb = [S.sbuf(f"eb{h}", [128, T], F32) for h in range(NH)]
        qf = [S.sbuf(f"qf{h}", [128, T], F32) for h in range(NH)]
        sg = [S.sbuf(f"sg{h}", [128, T], F32) for h in range(NH)]
        oS = [S.sbuf(f"oS{h}", [128, T], BF16) for h in range(NH)]
        state = [S.sbuf(f"state{h}", [128, 128], F32) for h in range(NH)]
        stateb = [S.sbuf(f"stateb{h}", [128, 128], BF16) for h in range(NH)]
        tmp = Ring([S.sbuf(f"tmp{i}", [128, T], F32) for i in range(6)])
        kvt = Ring([S.sbuf(f"kvt{i}", [64, 256], BF16) for i in range(3)])
        gate = Ring([S.sbuf(f"gate{i}", [64, 128], F32) for i in range(3)])
        scb = Ring([S.sbuf(f"scb{i}", [64, 64], BF16) for i in range(3)])
        junk = Ring([S.sbuf(f"junk{i}", [64, 128], F32) for i in range(2)])
        ss = Ring([S.sbuf(f"ss{i}", [64, 4], F32) for i in range(4)])
        t1r = Ring([S.sbuf(f"t1r{i}", [64, 128], F32) for i in range(3)])
        t2r = Ring([S.sbuf(f"t2r{i}", [64, 128], BF16) for i in range(3)])
        st1 = Ring([S.sbuf(f"st1{i}", [128, 128], F32) for i in range(2)])
        pb = [S.psum(f"pb{i}", [128, 512], F32) for i in range(8)]
        plin = Ring(pb[0:2])
        ptr = Ring(pb[2:4])
        pwk = Ring(pb[4:8])

        S.dma("sp", [dr["ident"]], [ident], out=ident.t[:], in_=dr["ident"].t)
        S.dma("sp", [dr["identb"]], [identb], out=identb.t[:], in_=dr["identb"].t)
        S.dma("sp", [dr["maskU"]], [maskU], out=maskU.t[:], in_=dr["maskU"].t)
        S.dma("sp", [dr["rmask"]], [rmask], out=rmask.t[:], in_=dr["rmask"].t)
        S.dma("sp", [dr["hg_nw"]], [NWt], out=NWt.t[:], in_=dr["hg_nw"].t)
        S.dma("sp", [dr["hg_lb"]], [lbr], out=lbr.t[:], in_=dr["hg_lb"].t)
        S.op("dve", [], [epsR], lambda e: e.memset(epsR.t[:], 1e-6))
        S.op("dve", [lbr], [lb], lambda e: e.tensor_tensor(out=lb.t[:], in0=lbr.t[:, 0, :], in1=lbr.t[:, 1, :], op=ALU.subtract))
        S.op("act", [lb], [lb], lambda e: e.activation(out=lb.t[:], in_=lb.t[:], func=AF.Sigmoid))
        S.op("dve", [lb], [oml], lambda e: e.tensor_scalar(out=oml.t[:], in0=lb.t[:], scalar1=-1.0, scalar2=1.0, op0=ALU.mult, op1=ALU.add))
        S.op("dve", [oml], [noml], lambda e: e.tensor_scalar(out=noml.t[:], in0=oml.t[:], scalar1=-1.0, scalar2=None, op0=ALU.mult))
        for h in range(NH):
            S.op("dve", [], [state[h]], lambda e, h=h: e.memset(state[h].t[:], 0.0))
            S.op("dve", [], [stateb[h]], lambda e, h=h: e.memset(stateb[h].t[:], 0.0))

        for tb in range(SEQ // T):
            t0 = tb * T
            for k in range(KC):
                S.dma("pool", [dr["xT"]], [aT[k]], out=aT[k].t[:, :], in_=dr["xT"].t[k * 128:(k + 1) * 128, t0:t0 + T])
            cols = []
            for h in heads:
                cols += [col_q + h * 128, col_f + h * 128, col_i + h * 128, col_g + h * 128]

            def evac(i, ps):
                hi, kind = divmod(i, 4)
                if kind == 0:
                    S.op("act", [ps], [qf[hi]], lambda e: e.activation(out=qf[hi].t[:, :], in_=ps.t[:, :], func=AF.Copy))
                elif kind == 1:
                    S.op("act", [ps], [sg[hi]], lambda e: e.activation(out=sg[hi].t[:, :], in_=ps.t[:, :], func=AF.Sigmoid))
                elif kind == 2:
                    S.op("act", [ps], [iTb[hi]], lambda e: e.activation(out=iTb[hi].t[:, :], in_=ps.t[:, :], func=AF.Copy))
                else:
                    S.op("dve", [ps], [gf[hi]], lambda e: e.tensor_copy(out=gf[hi].t[:, :], in_=ps.t[:, :]))
            _linear(S, wring, NW, dr["wA"], aT, T, cols, evac, plin)

            for h in range(NH):
                f_ = tmp.next()
                S.op("dve", [sg[h], oml, lb], [f_], lambda e: e.tensor_scalar(out=f_.t[:, :], in0=sg[h].t[:, :], scalar1=oml.t[:, h:h + 1], scalar2=lb.t[:, h:h + 1],
                                                                               op0=ALU.mult, op1=ALU.add))
                lf = tmp.next()
                S.op("act", [f_], [lf], lambda e: e.activation(out=lf.t[:, :], in_=f_.t[:, :], func=AF.Ln))
                k_ = tmp.next()
                S.op("dve", [sg[h], noml, oml], [k_], lambda e: e.tensor_scalar(out=k_.t[:, :], in0=sg[h].t[:, :], scalar1=noml.t[:, h:h + 1], scalar2=oml.t[:, h:h + 1],
                                                                                 op0=ALU.mult, op1=ALU.add))
                b_ = tmp.next()
                S.op("dve", [rmask, lf], [b_], lambda e: e.tensor_tensor_scan(out=b_.t[:, :], data0=rmask.t[:, :], data1=lf.t[:, :], initial=0.0, op0=ALU.mult, op1=ALU.add))
                S.op("act", [b_], [eb[h]], lambda e: e.activation(out=eb[h].t[:, :], in_=b_.t[:, :], func=AF.Exp))
                enb = tmp.next()
                S.op("act", [b_], [enb], lambda e: e.activation(out=enb.t[:, :], in_=b_.t[:, :], func=AF.Exp, scale=-1.0))
                S.op("dve", [qf[h], eb[h]], [qd[h]], lambda e: e.tensor_tensor(out=qd[h].t[:, :], in0=qf[h].t[:, :], in1=eb[h].t[:, :], op=ALU.mult))
                S.op("pool", [k_, enb], [kd[h]], lambda e: e.tensor_tensor(out=kd[h].t[:, :], in0=k_.t[:, :], in1=enb.t[:, :], op=ALU.mult))

            for cc in range(T // C):
                cs = slice(cc * C, (cc + 1) * C)
                for h in range(NH):
                    TR = ptr.next()
                    P = pwk.next()
                    S.op("pe", [kd[h], identb], [TR], lambda e: e.matmul(TR.t[0:64, 0:128], lhsT=kd[h].t[:, cs], rhs=identb.t[:], start=True, stop=True))
                    S.op("pe", [iTb[h], identb], [TR], lambda e: e.matmul(TR.t[0:64, 128:256], lhsT=iTb[h].t[:, cs], rhs=identb.t[:], start=True, stop=True))
                    S.op("pe", [gf[h], ident], [TR], lambda e: e.transpose(out=TR.t[0:64, 256:384], in_=gf[h].t[:, cs], identity=ident.t[:]))
                    kv_ = kvt.next()
                    S.op("dve", [TR], [kv_], lambda e: e.tensor_copy(out=kv_.t[:, :], in_=TR.t[0:64, 0:256]))
                    g_ = gate.next()
                    S.op("act", [TR], [g_], lambda e: e.activation(out=g_.t[:, :], in_=TR.t[0:64, 256:384], func=AF.Silu))
                    S.op("pe", [kd[h], qd[h]], [P], lambda e: e.matmul(P.t[0:64, 0:64], lhsT=kd[h].t[:, cs], rhs=qd[h].t[:, cs], start=True, stop=True))
                    sc_ = scb.next()
                    S.op("dve", [P, maskU], [sc_], lambda e: e.tensor_tensor(out=sc_.t[:, :], in0=P.t[0:64, 0:64], in1=maskU.t[:, :], op=ALU.mult))
                    S.op("pe", [sc_, kv_], [P], lambda e: e.matmul(P.t[0:64, 64:192], lhsT=sc_.t[:, :], rhs=kv_.t[:, 128:256], start=True, stop=False))
                    S.op("pe", [qd[h], stateb[h]], [P], lambda e: e.matmul(P.t[0:64, 64:192], lhsT=qd[h].t[:, cs], rhs=stateb[h].t[:, :], start=False, stop=True))
                    S.op("pe", [kv_], [P], lambda e: e.matmul(P.t[:, 256:384], lhsT=kv_.t[:, 0:128], rhs=kv_.t[:, 128:256], start=True, stop=True))
                    s1 = st1.next()
                    S.op("dve", [state[h], P], [s1], lambda e: e.tensor_tensor(out=s1.t[:, :], in0=state[h].t[:, :], in1=P.t[:, 256:384], op=ALU.add))
                    S.op("dve", [s1, eb[h]], [state[h]], lambda e: e.tensor_scalar(out=state[h].t[:, :], in0=s1.t[:, :], scalar1=eb[h].t[:, cc * C + C - 1:cc * C + C], scalar2=None,
                                                                                  op0=ALU.mult))
                    S.op("act", [state[h]], [stateb[h]], lambda e: e.activation(out=stateb[h].t[:, :], in_=state[h].t[:, :], func=AF.Copy))
                    jk = junk.next()
                    s_ = ss.next()
                    S.op("act", [P], [jk, s_], lambda e: e.activation(out=jk.t[:, :], in_=P.t[0:64, 64:192], func=AF.Square, accum_out=s_.t[:, 0:1]))
                    S.op("act", [s_, epsR], [s_], lambda e: e.activation(out=s_.t[:, 1:2], in_=s_.t[:, 0:1], func=AF.Sqrt, scale=1.0 / 128.0, bias=epsR.t[0:64, 0:1]))
                    S.op("dve", [s_], [s_], lambda e: e.reciprocal(out=s_.t[:, 2:3], in_=s_.t[:, 1:2]))
                    t1 = t1r.next()
                    S.op("dve", [P, s_, NWt], [t1], lambda e: e.scalar_tensor_tensor(out=t1.t[:, :], in0=P.t[0:64, 64:192], scalar=s_.t[:, 2:3], in1=NWt.t[0:64, :],
                                                                                   op0=ALU.mult, op1=ALU.mult))
                    t2 = t2r.next()
                    S.op("pool", [t1, g_], [t2], lambda e: e.tensor_tensor(out=t2.t[:, :], in0=t1.t[:, :], in1=g_.t[:, :], op=ALU.mult))
                    S.op("pe", [t2, identb], [P], lambda e: e.matmul(P.t[:, 384:448], lhsT=t2.t[:, :], rhs=identb.t[0:64, 0:64], start=True, stop=True))
                    S.op("act", [P], [oS[h]], lambda e: e.activation(out=oS[h].t[:, cs], in_=P.t[:, 384:448], func=AF.Copy))
            for h in range(NH):
                r0 = orow0 + h * 128
                _ores, _oap = _oput(dr, r0, t0, T)
                S.dma("sp", [oS[h]], [_ores], out=_oap, in_=oS[h].t[:, :])
        S.barrier()
    S.es = oes


NEGB = 30000.0


def emit_nsa(S, c, dr):
    D, SEQ = c["D"], c["SEQ"]
    KC = D // 128
    T = 512
    R = 8
    NW = c.get("NW", 3)
    NT = SEQ // 128
    NB = SEQ // T
    NMT = (SEQ // 16 + 127) // 128
    NSLOT = NMT * 128
    scale = 128.0 ** -0.5
    oes = S.es
    S.new_phase()
    with ExitStack() as pes:
        S.es = pes
        _, aT = views(S, "aT", [128, KC, T], BF16, KC, lambda t, i: t[:, i, :])
        _, wsl = views(S, "wsl", [128, NW, max(KC, 32), 128], BF16, NW, lambda t, i: t[:, i, :, :])
        wring = Ring(wsl)
        ident = S.sbuf("ident", [128, 128], F32)
        identb = S.sbuf("identb", [128, 128], BF16)
        ones_b = S.sbuf("ones_b", [128, 128], BF16)
        rotm = S.sbuf("rotm", [128, 128], BF16)
        ksT = S.sbuf("ksT", [128, SEQ], BF16)
        _, vsT = views(S, "vsT", [128, NT, 128], BF16, NT, lambda t, i: t[:, i, :])
        kwT = S.sbuf("kwT", [128, 1024], BF16)
        _, vwT = views(S, "vwT", [128, 8, 128], BF16, 8, lambda t, i: t[:, i, :])
        kcmpT = S.sbuf("kcmpT", [128, NSLOT], BF16)
        vcmpT = S.sbuf("vcmpT", [128, NSLOT], F32)
        _, vcmp = views(S, "vcmp", [128, NMT, 128], BF16, NMT, lambda t, i: t[:, i, :])
        kcb = S.sbuf("kcb", [128, 16 + T], BF16)
        vcb = S.sbuf("vcb", [128, 16 + T], BF16)
        w2 = S.sbuf("w2", [128, 2, 128], BF16)
        posT = S.sbuf("posT", [128, 2, 32], BF16)
        posb = S.sbuf("posb", [128, 2], F32)
        ov = S.sbuf("ov", [128, NMT, 128], BF16)
        ex32 = S.sbuf("ex32", [32, 16, 128], BF16)
        wmask = S.sbuf("wmask", [128, 8, T], BF16)
        cmask = S.sbuf("cmask", [128, 5, T], BF16)
        ABt = S.sbuf("ABt", [128, 256], F32)
        FBt = S.sbuf("FBt", [128, 256], F32)
        qn = [S.sbuf(f"qn{r}", [128, T], BF16) for r in range(R)]
        qr = [S.sbuf(f"qr{r}", [128, T], BF16) for r in range(R)]
        cosb = S.sbuf("cosb", [128, T], F32)
        sinb = S.sbuf("sinb", [128, T], F32)
        gT = S.sbuf("gT", [32, T], F32)
        gtok = S.sbuf("gtok", [128, 4, 24], F32)
        _, otok = views(S, "otok", [128, 4, R * 128], F32, 4, lambda t, i: t[:, i, :])
        selT = [[S.sbuf(f"selT{g}_{i}", [32, T], BF16) for g in range(4)] for i in range(1)][0]
        tf = Ring([S.sbuf(f"tf{i}", [128, T], F32) for i in range(4)])
        tb16 = Ring([S.sbuf(f"tb16{i}", [128, T], BF16) for i in range(3)])
        Er = Ring([S.sbuf(f"Er{i}", [128, T], BF16) for i in range(6)])
        Pr = Ring([S.sbuf(f"Pr{i}", [128, T], BF16) for i in range(3)])
        hid = Ring([S.sbuf(f"hid{i}", [128, 32], BF16) for i in range(2)])
        sm = Ring([S.sbuf(f"sm{i}", [128, 128], F32) for i in range(4)])
        smb = Ring([S.sbuf(f"smb{i}", [128, 128], BF16) for i in range(2)])
        m8 = Ring([S.sbuf(f"m8{i}", [128, 16], F32) for i in range(2)])
        fac = Ring([S.sbuf(f"fac{i}", [128, 8], F32) for i in range(4)])
        oS = Ring([S.sbuf(f"oS{i}", [128, T], BF16) for i in range(2)])
        pb = [S.psum(f"pb{i}", [128, 512], F32) for i in range(8)]
        plin = Ring(pb[0:2])
        pS = Ring(pb[2:4])
        pO = Ring([pb[4], pb[7]])
        pD = pb[5]
        pI = pb[6]
        pmisc = Ring([pb[0], pb[1]])

        def ld(q, name, dst, src=None):
            S.dma(q, [dr[name]], [dst], out=dst.t[:], in_=(dr[name].t if src is None else src))
        ld("sp", "ident", ident); ld("sp", "identb", identb); ld("sp", "rotm", rotm)
        ld("pool", "cmp_w2", w2); ld("pool", "cmp_posT", posT)
        ld("sp", "ov", ov); ld("sp", "ex32", ex32); ld("sp", "wmask", wmask); ld("sp", "cmask", cmask)
        ld("sp", "ABt", ABt); ld("sp", "FBt", FBt)
        S.op("dve", [], [ones_b], lambda e: e.memset(ones_b.t[:], 1.0))
        S.op("dve", [], [kcb], lambda e: e.memset(kcb.t[:], 0.0))
        S.op("dve", [], [vcb], lambda e: e.memset(vcb.t[:], 0.0))
        S.op("dve", [], [vcmpT], lambda e: e.memset(vcmpT.t[:], 0.0))
        S.op("dve", [], [kcmpT], lambda e: e.memset(kcmpT.t[:], 0.0))

        def load_w1(j):
            slot = wring.next()
            S.dma("pool", [dr["cmp_w1"]], [slot], out=slot.t[:, 0:32, :], in_=dr["cmp_w1"].t[j].rearrange("l d e -> d l e"))
            return slot
        for j in range(2):
            slot = load_w1(j)
            ps = pmisc.next()
            for l in range(32):
                S.op("pe", [slot, posT], [ps], lambda e, l=l: e.matmul(ps.t[:, 0:1], lhsT=slot.t[:, l, :], rhs=posT.t[:, j, l:l + 1], start=(l == 0), stop=(l == 31)))
            S.op("act", [ps], [posb], lambda e: e.activation(out=posb.t[:, j:j + 1], in_=ps.t[:, 0:1], func=AF.Copy))

        CQ, CKC, CVC, CKS, CVS, CKW, CVW, CG = 0, 1024, 1152, 1280, 1408, 1536, 1664, 1792

        def rope(src_f32, dst_bf, sc):
            sb = tb16.next()
            S.op("act", [src_f32], [sb], lambda e: e.activation(out=sb.t[:, :], in_=src_f32.t[:, :], func=AF.Copy))
            ps = pmisc.next()
            S.op("pe", [rotm, sb], [ps], lambda e: e.matmul(ps.t[:, :], lhsT=rotm.t[:], rhs=sb.t[:, :], start=True, stop=True))
            t1 = tf.next()
            S.op("dve", [src_f32, cosb], [t1], lambda e: e.tensor_tensor(out=t1.t[:, :], in0=src_f32.t[:, :], in1=cosb.t[:, :], op=ALU.mult))
            t2 = tf.next()
            S.op("dve", [ps, sinb], [t2], lambda e: e.tensor_tensor(out=t2.t[:, :], in0=ps.t[:, :], in1=sinb.t[:, :], op=ALU.mult))
            S.op("pool", [t1, t2], [dst_bf[0]], lambda e: e.tensor_tensor(out=dst_bf[1], in0=t1.t[:, :], in1=t2.t[:, :], op=ALU.add))

        def to_tok(src_f32, dst_views, base):
            for j in range(T // 128):
                pt = pmisc.next()
                S.op("pe", [src_f32, ident], [pt], lambda e, j=j, pt=pt: e.transpose(out=pt.t[:, 0:128], in_=src_f32.t[:, j * 128:(j + 1) * 128], identity=ident.t[:]))
                dv = dst_views[base + j]
                S.op("dve", [pt], [dv], lambda e, pt=pt, dv=dv: e.tensor_copy(out=dv.t[:, :], in_=pt.t[:, 0:128]))

        for tb in range(NB):
            t0 = tb * T
            gt0 = t0 // 128
            for k in range(KC):
                if "hAG_get" in dr:
                    _hres, _hap = dr["hAG_get"](k, t0, T)
                    S.dma("sp", [_hres], [aT[k]], out=aT[k].t[:, :], in_=_hap)
                else:
                    S.dma("sp", [dr["hT"]], [aT[k]], out=aT[k].t[:, :], in_=dr["hT"].t[k * 128:(k + 1) * 128, t0:t0 + T])
            S.dma("sp", [dr["cosT"]], [cosb], out=cosb.t[:, :], in_=dr["cosT"].t[:, t0:t0 + T])
            S.dma("sp", [dr["sinT"]], [sinb], out=sinb.t[:, :], in_=dr["sinT"].t[:, t0:t0 + T])
            if tb > 0:
                S.op("dve", [kcb], [kcb], lambda e: e.tensor_copy(out=kcb.t[:, 0:16], in_=kcb.t[:, T:T + 16]))
                S.op("dve", [vcb], [vcb], lambda e: e.tensor_copy(out=vcb.t[:, 0:16], in_=vcb.t[:, T:T + 16]))
            cols = [CQ + r * 128 for r in range(R)] + [CKC, CVC, CKS, CVS, CKW, CVW]

            def evac(i, ps):
                if i < R:
                    qf = tf.next()
                    S.op("act", [ps], [qf], lambda e: e.activation(out=qf.t[:, :], in_=ps.t[:, :], func=AF.Copy, scale=scale))
                    S.op("dve", [qf], [qn[i]], lambda e: e.tensor_copy(out=qn[i].t[:, :], in_=qf.t[:, :]))
                    rope(qf, (qr[i], qr[i].t[:, :]), 1.0)
                elif i == R:
                    S.op("act", [ps], [kcb], lambda e: e.activation(out=kcb.t[:, 16:16 + T], in_=ps.t[:, :], func=AF.Copy))
                elif i == R + 1:
                    S.op("act", [ps], [vcb], lambda e: e.activation(out=vcb.t[:, 16:16 + T], in_=ps.t[:, :], func=AF.Copy))
                elif i == R + 2:
                    kf = tf.next()
                    S.op("act", [ps], [kf], lambda e: e.activation(out=kf.t[:, :], in_=ps.t[:, :], func=AF.Copy))
                    rope(kf, (ksT, ksT.t[:, t0:t0 + T]), 1.0)
                elif i == R + 3:
                    vf = tf.next()
                    S.op("act", [ps], [vf], lambda e: e.activation(out=vf.t[:, :], in_=ps.t[:, :], func=AF.Copy))
                    to_tok(vf, vsT, gt0)
                elif i == R + 4:
                    kf = tf.next()
                    S.op("act", [ps], [kf], lambda e: e.activation(out=kf.t[:, :], in_=ps.t[:, :], func=AF.Copy))
                    c0 = (tb % 2) * T
                    rope(kf, (kwT, kwT.t[:, c0:c0 + T]), 1.0)
                else:
                    vf = tf.next()
                    S.op("act", [ps], [vf], lambda e: e.activation(out=vf.t[:, :], in_=ps.t[:, :], func=AF.Copy))
                    to_tok(vf, vwT, (gt0 % 8))
            _linear(S, wring, NW, dr["wC"], aT, T, cols, evac, plin)
            slot = wring.next()
            S.dma("pool", [dr["wC"]], [slot], out=slot.t[:, 0:KC, 0:24], in_=dr["wC"].t[0:KC * 128, CG:CG + 24].rearrange("(kc p) c -> p kc c", p=128))
            ps = plin.next()
            for k in range(KC):
                S.op("pe", [slot, aT[k]], [ps], lambda e, k=k: e.matmul(ps.t[0:24, :], lhsT=slot.t[:, k, 0:24], rhs=aT[k].t[:, :], start=(k == 0), stop=(k == KC - 1)))
            S.op("act", [ps], [gT], lambda e: e.activation(out=gT.t[0:24, :], in_=ps.t[0:24, :], func=AF.Sigmoid))
            for j in range(4):
                pt = pmisc.next()
                S.op("pe", [gT, ident], [pt], lambda e, j=j, pt=pt: e.transpose(out=pt.t[:, 0:24], in_=gT.t[0:24, j * 128:(j + 1) * 128], identity=ident.t[0:24, 0:24]))
                S.op("dve", [pt], [gtok], lambda e, j=j, pt=pt: e.tensor_copy(out=gtok.t[:, j, :], in_=pt.t[:, 0:24]))

            for j, (buf, dstT) in enumerate(((kcb, kcmpT), (vcb, vcmpT))):
                slot = load_w1(j)
                ps = pmisc.next()
                for l in range(32):
                    S.op("pe", [slot, buf], [ps], lambda e, l=l: e.matmul(ps.t[:, 0:32], lhsT=slot.t[:, l, :], rhs=buf.t[:, l:l + 16 * 31 + 1:16], start=(l == 0), stop=(l == 31)))
                hd = hid.next()
                S.op("act", [ps, posb], [hd], lambda e: e.activation(out=hd.t[:, :], in_=ps.t[:, 0:32], func=AF.Gelu, bias=posb.t[:, j:j + 1]))
                ps2 = pmisc.next()
                S.op("pe", [w2, hd], [ps2], lambda e: e.matmul(ps2.t[:, 0:32], lhsT=w2.t[:, j, :], rhs=hd.t[:, :], start=True, stop=True))
                S.op("act", [ps2], [dstT], lambda e: e.activation(out=dstT.t[:, 32 * tb:32 * tb + 32], in_=ps2.t[:, 0:32], func=AF.Copy))
            mtc = (32 * tb) // 128
            pt = pmisc.next()
            S.op("pe", [vcmpT, ident], [pt], lambda e: e.transpose(out=pt.t[:, 0:128], in_=vcmpT.t[:, mtc * 128:(mtc + 1) * 128], identity=ident.t[:]))
            S.op("dve", [pt], [vcmp[mtc]], lambda e: e.tensor_copy(out=vcmp[mtc].t[:, :], in_=pt.t[:, 0:128]))

            nmt = mtc + 1
            for r in range(R):
                Es = []
                for mt in range(nmt):
                    v = tb - 4 * mt
                    ps = pS.next()
                    mks = ([v] if v < 4 else []) + ([4] if mt == 0 else [])
                    S.op("pe", [kcmpT, qn[r]], [ps], lambda e: e.matmul(ps.t[:, :], lhsT=kcmpT.t[:, mt * 128:(mt + 1) * 128], rhs=qn[r].t[:, :], start=True, stop=not mks))
                    for mi, mv in enumerate(mks):
                        S.op("pe", [identb, cmask], [ps], lambda e: e.matmul(ps.t[:, :], lhsT=identb.t[:], rhs=cmask.t[:, mv, :], start=False, stop=(mi == len(mks) - 1)))
                    E = Er.next()
                    S.op("act", [ps], [E], lambda e: e.activation(out=E.t[:, :], in_=ps.t[:, :], func=AF.Exp))
                    Es.append(E)
                pden = pS.next()
                for mt, E in enumerate(Es):
                    S.op("pe", [ones_b, E], [pden], lambda e, mt=mt, E=E: e.matmul(pden.t[:, :], lhsT=ones_b.t[:], rhs=E.t[:, :], start=(mt == 0), stop=(mt == nmt - 1)))
                rd = tf.next()
                S.op("dve", [pden], [rd], lambda e: e.tensor_scalar(out=rd.t[:, :], in0=pden.t[:, :], scalar1=1e-30, scalar2=None, op0=ALU.max))
                S.op("dve", [rd], [rd], lambda e: e.reciprocal(out=rd.t[:, :], in_=rd.t[:, :]))
                po = pO.next()
                Ps = []
                for mt, E in enumerate(Es):
                    P = Pr.next()
                    S.op("dve", [E, rd], [P], lambda e, E=E, P=P: e.tensor_tensor(out=P.t[:, :], in0=E.t[:, :], in1=rd.t[:, :], op=ALU.mult))
                    for tt in range(4):
                        S.op("pe", [P, vcmp[mt]], [po], lambda e, tt=tt, mt=mt, P=P: e.matmul(po.t[:, tt * 128:(tt + 1) * 128], lhsT=P.t[:, tt * 128:(tt + 1) * 128], rhs=vcmp[mt].t[:, :],
                                                                                           start=(mt == 0 and tt == 0), stop=(mt == nmt - 1), skip_group_check=True))
                        S.op("pe", [P, ov], [pI], lambda e, tt=tt, mt=mt, P=P: e.matmul(pI.t[:, tt * 128:(tt + 1) * 128], lhsT=P.t[:, tt * 128:(tt + 1) * 128], rhs=ov.t[:, mt, :],
                                                                                     start=(r == 0 and mt == 0 and tt == 0), stop=(r == R - 1 and mt == nmt - 1), skip_group_check=True))
                for tt in range(4):
                    S.op("dve", [po, gtok], [otok[tt]], lambda e, tt=tt: e.tensor_scalar(out=otok[tt].t[:, r * 128:(r + 1) * 128], in0=po.t[:, tt * 128:(tt + 1) * 128],
                                                                                      scalar1=gtok.t[:, tt, 3 * r:3 * r + 1], scalar2=None, op0=ALU.mult))

            for tt in range(4):
                gt = gt0 + tt
                x0 = 128 - 2 * gt
                s1 = sm.next()
                S.op("dve", [pI, ABt], [s1], lambda e: e.scalar_tensor_tensor(out=s1.t[:, :], in0=pI.t[:, tt * 128:(tt + 1) * 128], scalar=1.0, in1=ABt.t[:, x0:x0 + 128],
                                                                             op0=ALU.add, op1=ALU.mult))
                s2 = sm.next()
                S.op("dve", [s1, FBt], [s2], lambda e: e.scalar_tensor_tensor(out=s2.t[:, :], in0=s1.t[:, :], scalar=-1.0, in1=FBt.t[:, x0:x0 + 128], op0=ALU.add, op1=ALU.max))
                S.op("dve", [s2], [s2], lambda e: e.memset(s2.t[:, 0:1], 1e9))
                mm = m8.next()
                S.op("dve", [s2], [mm], lambda e: e.max(out=mm.t[:, 0:8], in_=s2.t[:, :]))
                s3 = sm.next()
                S.op("dve", [s2, mm], [s3], lambda e: e.match_replace(out=s3.t[:, :], in_to_replace=mm.t[:, 0:8], in_values=s2.t[:, :], imm_value=-1e30))
                S.op("dve", [s3], [mm], lambda e: e.max(out=mm.t[:, 8:16], in_=s3.t[:, :]))
                sb_ = smb.next()
                S.op("dve", [s2, mm], [sb_], lambda e: e.tensor_scalar(out=sb_.t[:, :], in0=s2.t[:, :], scalar1=mm.t[:, 15:16], scalar2=1.0, op0=ALU.is_ge, op1=ALU.subtract))
                for g in range(4):
                    pt = pmisc.next()
                    S.op("pe", [sb_, identb], [pt], lambda e, g=g, pt=pt: e.matmul(pt.t[0:32, 0:128], lhsT=sb_.t[:, 32 * g:32 * g + 32], rhs=identb.t[:], start=True, stop=True))
                    S.op("act", [pt], [selT[g]], lambda e, g=g, pt=pt: e.activation(out=selT[g].t[:, tt * 128:(tt + 1) * 128], in_=pt.t[0:32, 0:128], func=AF.Copy))

            def attend(r, kTsrc, kcol, vviews, vidx, kts, use_sel, gidx):
                po = pO.next()
                n = len(kts)
                for ni, kt in enumerate(kts):
                    rel = kt - gt0
                    ps = pS.next()
                    has_w = (not use_sel and True) or (rel >= 0)
                    S.op("pe", [kTsrc, qr[r]], [ps], lambda e: e.matmul(ps.t[:, :], lhsT=kTsrc.t[:, kcol(kt):kcol(kt) + 128], rhs=qr[r].t[:, :], start=True,
                                                                       stop=not (use_sel or has_w)))
                    if use_sel:
                        g = kt // 16
                        S.op("pe", [ex32, selT[g]], [ps], lambda e: e.matmul(ps.t[:, :], lhsT=ex32.t[:, kt % 16, :], rhs=selT[g].t[:, :], start=False, stop=not has_w))
                    if has_w:
                        S.op("pe", [identb, wmask], [ps], lambda e: e.matmul(ps.t[:, :], lhsT=identb.t[:], rhs=wmask.t[:, rel + 4, :], start=False, stop=True))
                    E = Er.next()
                    S.op("act", [ps], [E], lambda e: e.activation(out=E.t[:, :], in_=ps.t[:, :], func=AF.Exp))
                    vv = vviews[vidx(kt)]
                    for tt in range(4):
                        S.op("pe", [E, vv], [po], lambda e, tt=tt: e.matmul(po.t[:, tt * 128:(tt + 1) * 128], lhsT=E.t[:, tt * 128:(tt + 1) * 128], rhs=vv.t[:, :],
                                                                          start=(ni == 0 and tt == 0), stop=(ni == n - 1), skip_group_check=True))
                        S.op("pe", [E, ones_b], [pD], lambda e, tt=tt: e.matmul(pD.t[:, (r * 4 + tt) * 2:(r * 4 + tt) * 2 + 1], lhsT=E.t[:, tt * 128:(tt + 1) * 128], rhs=ones_b.t[:, 0:1],
                                                                              start=(ni == 0 and tt == 0), stop=(ni == n - 1), skip_group_check=True))
                fc = fac.next()
                S.op("dve", [pD], [fc], lambda e: e.reciprocal(out=fc.t[:, 0:4], in_=pD.t[:, r * 8:r * 8 + 8:2]))
                for tt in range(4):
                    S.op("dve", [fc, gtok], [fc], lambda e, tt=tt: e.tensor_tensor(out=fc.t[:, 4 + tt:5 + tt], in0=fc.t[:, tt:tt + 1], in1=gtok.t[:, tt, 3 * r + gidx:3 * r + gidx + 1], op=ALU.mult))
                    S.op("dve", [po, fc, otok[tt]], [otok[tt]], lambda e, tt=tt: e.scalar_tensor_tensor(
                        out=otok[tt].t[:, r * 128:(r + 1) * 128], in0=po.t[:, tt * 128:(tt + 1) * 128], scalar=fc.t[:, 4 + tt:5 + tt],
                        in1=otok[tt].t[:, r * 128:(r + 1) * 128], op0=ALU.mult, op1=ALU.add))

            for r in range(R):
                attend(r, ksT, lambda kt: kt * 128, vsT, lambda kt: kt, list(range(0, gt0 + 4)), True, 1)
                attend(r, kwT, lambda kt: (kt * 128) % 1024, vwT, lambda kt: kt % 8, list(range(max(0, gt0 - 4), gt0 + 4)), False, 2)

            for r in range(R):
                pt = pmisc.next()
                for tt in range(4):
                    S.op("pe", [otok[tt], ident], [pt], lambda e, tt=tt: e.transpose(out=pt.t[:, tt * 128:(tt + 1) * 128], in_=otok[tt].t[:, r * 128:(r + 1) * 128], identity=ident.t[:]))
                o_ = oS.next()
                S.op("act", [pt], [o_], lambda e: e.activation(out=o_.t[:, :], in_=pt.t[:, :], func=AF.Copy))
                _ores, _oap = _oput(dr, r * 128, t0, T)
                S.dma("sp", [o_], [_ores], out=_oap, in_=o_.t[:, :])
        S.barrier()
    S.es = oes


BF = ml_dtypes.bfloat16
def make_consts():
    c = {}
    c['ident'] = np.eye(128, dtype=np.float32)
    c['identb'] = np.eye(128).astype(BF)
    j = np.arange(128)[:, None]; s = np.arange(128)[None, :]
    c['negU'] = np.where(j >= s, -1.0, 0.0).astype(BF)
    sl = np.arange(128)[:, None, None]; kd = np.arange(4)[None, :, None]; t = np.arange(512)[None, None, :]
    c['sbmask'] = np.where(128 * kd + sl < t, 0.0, -30000.0).astype(BF)
    return c
def make_consts_hg():
    c = {}
    s = np.arange(64)[:, None]; t = np.arange(64)[None, :]
    c['maskU'] = (s <= t).astype(np.float32)
    c['rmask'] = np.tile((np.arange(512) % 64 != 0).astype(np.float32)[None, :], (128, 1))
    return c
def make_consts_nsa(S):
    c = {}
    rot = np.zeros((128, 128), np.float32)
    for dp in range(64):
        rot[dp + 64, dp] = -1.0
    for dp in range(64, 128):
        rot[dp - 64, dp] = 1.0
    c['rotm'] = rot.astype(BF)
    NMT = (S // 16 + 127) // 128
    m = np.arange(NMT * 128); n = m - 1
    j = np.arange(128)
    ovm = (n[:, None] >= 0) & (16 * n[:, None] < 64 * j[None, :] + 64) & (16 * n[:, None] + 32 > 64 * j[None, :]) & (j[None, :] < S // 64) & (n[:, None] <= (S - 32) // 16)
    c['ov'] = np.ascontiguousarray(ovm.reshape(NMT, 128, 128).transpose(1, 0, 2)).astype(BF)
    i = np.arange(32)[:, None, None]; ktl = np.arange(16)[None, :, None]; s = np.arange(128)[None, None, :]
    c['ex32'] = np.where(i == 2 * ktl + s // 64, 30000.0, 0.0).astype(BF)
    sl = np.arange(128)[:, None, None]; rel = (np.arange(8) - 4)[None, :, None]; tl = np.arange(512)[None, None, :]
    c['wmask'] = np.where((128 * rel + sl <= tl) & (128 * rel + sl > tl - 512), 0.0, -30000.0).astype(BF)
    v = np.arange(4)[None, :, None]
    cm = np.where(16 * sl + 15 <= 512 * v + tl, 0.0, -30000.0)
    cm4 = np.where(sl == 0, -30000.0, 0.0) + 0 * tl
    c['cmask'] = np.concatenate([cm, cm4], axis=1).astype(BF)
    p = np.arange(128)[:, None]; x = np.arange(256)[None, :]
    cc = (p >= 64).astype(np.int64)
    c['ABt'] = ((x - 128) <= cc).astype(np.float32)
    c['FBt'] = np.where(((x - 128) == cc) | ((x - 128) == cc - 1), 1e9, -2.0).astype(np.float32)
    half = 64
    inv = 10000.0 ** (-np.arange(half, dtype=np.float32) / half)
    ang = np.arange(S, dtype=np.float32)[None, :] * np.concatenate([inv, inv])[:, None]
    c['cosT'] = np.cos(ang).astype(np.float32)
    c['sinT'] = np.sin(ang).astype(np.float32)
    return c


from concourse.bass_utils import run_bass_kernel_spmd

D_MODEL = 4096
SEQ = 8192
BATCH = 2
DFF = 11008
NMEM = 256
ALPHA = 4.0 ** 0.25
NCORE = 8
TOKC = 2048
HALO = 128
GROUPS = [[0, 1, 2, 3], [4, 5, 6, 7]]


def _consts_all():
    c = make_consts()
    c.update(make_consts_hg())
    c.update(make_consts_nsa(SEQ))
    return c


def _build_fused():
    nc = bass.Bass("TRN2", target_bir_lowering=False)
    NT = HALO + TOKC
    KC, FC = D_MODEL // 128, DFF // 128
    with ExitStack() as es:
        S = Sched(nc, es)
        dr = {}

        def din(name, shape, dt=F32):
            dr[name] = S.dram(name, shape, dt, kind="ExternalInput")
        din("xT", [D_MODEL, SEQ]); din("xTc", [D_MODEL, NT]); din("memT", [D_MODEL, NMEM])
        din("wA", [D_MODEL, 3584]); din("wC", [D_MODEL, 1816])
        din("ident", [128, 128]); din("identb", [128, 128], BF16); din("negU", [128, 128], BF16); din("sbmask", [128, 4, 512], BF16)
        din("maskU", [64, 64]); din("rmask", [128, 512]); din("hg_nw", [128, 128]); din("hg_lb", [128, 2, 4])
        din("cmp_w1", [2, 32, 128, 128]); din("cmp_w2", [128, 2, 128]); din("cmp_posT", [128, 2, 32])
        din("rotm", [128, 128], BF16)
        din("ov", [128, 4, 128], BF16); din("ex32", [32, 16, 128], BF16); din("wmask", [128, 8, 512], BF16); din("cmask", [128, 5, 512], BF16)
        din("ABt", [128, 256]); din("FBt", [128, 256]); din("cosT", [128, SEQ]); din("sinT", [128, SEQ])
        din("flag", [128, 1]); din("fsel", [128, 4]); din("fhal", [128, 4])
        for l in range(2):
            din(f"w_out{l}", [D_MODEL, D_MODEL]); din(f"w_q{l}", [D_MODEL, 512]); din(f"w_kv{l}", [D_MODEL, 1024]); din(f"w_o{l}", [512, D_MODEL])
            din(f"w_up{l}", [D_MODEL, 2 * DFF]); din(f"w_down{l}", [DFF, D_MODEL])
            din(f"ln_g{l}", [128, 3, KC]); din(f"ln_b{l}", [128, 3, KC]); din(f"conv{l}", [128, 3, FC])
        outT = S.dram("outT", [D_MODEL, TOKC], F32, kind="ExternalOutput")
        NB = SEQ // 512
        oA = [S.dram_cc(f"oA{i}", [1024, 512], BF16) for i in range(NB)]
        oAG0 = [S.dram_cc(f"oAG0_{i}", [4096, 512], BF16) for i in range(NB)]
        oC = [S.dram_cc(f"oC{i}", [1024, 512], BF16) for i in range(NB)]
        oAG1 = [S.dram_cc(f"oAG1_{i}", [4096, 512], BF16) for i in range(NB)]
        h1loc = S.dram_cc("h1loc", [D_MODEL, NT], F32)
        hb = [S.dram_cc(f"hb{i}", [256, TOKC], BF16) for i in range(16)]
        hAG = [S.dram_cc(f"hAG{i}", [4 * 256, TOKC], BF16) for i in range(16)]
        hl = [S.dram_cc(f"hl{i}", [2048, 128], F32) for i in range(2)]
        hlAG = [S.dram_cc(f"hlAG{i}", [4 * 2048, 128], F32) for i in range(2)]

        def put_fn(lst):
            def f(r0, t0, T):
                return lst[t0 // 512], lst[t0 // 512].t[r0:r0 + 128, 0:T]
            return f

        def oget_fn(lst):
            def f(k, cb, T):
                i, off = divmod(cb, 512)
                return lst[i], lst[i].t[k * 128:(k + 1) * 128, off:off + T]
            return f

        def hb_put(k, c0, T):
            return hb[k // 2], hb[k // 2].t[(k % 2) * 128:(k % 2) * 128 + 128, c0:c0 + T]

        def hAG_get(k, t0, T):
            rk, tl = divmod(t0, TOKC)
            r0 = rk * 256 + (k % 2) * 128
            return hAG[k // 2], hAG[k // 2].t[r0:r0 + 128, tl:tl + T]

        def hl_put(k):
            return hl[k // 16], hl[k // 16].t[(k % 16) * 128:(k % 16) * 128 + 128, 0:128]

        def hlAG_get(k, cq):
            r0 = cq * 2048 + (k % 16) * 128
            return hlAG[k // 16], hlAG[k // 16].t[r0:r0 + 128, 0:128]

        cfgA = dict(D=D_MODEL, SEQ=SEQ, NW=4)
        drA = dict(dr); drA["oT_put"] = put_fn(oA)
        emit_sb(S, cfgA, drA, [0, 1], 0, 512, 1024, 512)
        emit_sb(S, cfgA, drA, [2, 3], 0, 512, 1024, 512)
        emit_hgrn(S, cfgA, drA, [0, 1, 2, 3], 1536, 2048, 2560, 3072, 0)
        for i in range(NB):
            S.collective("AllGather", GROUPS, oA[i], oAG0[i])
        blocks = [(0, HALO, True)] + [(HALO + i * 512, 512, False) for i in range(TOKC // 512)]
        cfgP = dict(D=D_MODEL, DFF=DFF, XH=4, NM=NMEM, T=512, alpha=ALPHA, eps=1e-5, GF=8, NW=3, HALO=HALO, TOKC=TOKC)

        def post_dr(l):
            d = dict(flag=dr["flag"], fsel=dr["fsel"], fhal=dr["fhal"], memT=dr["memT"])
            for k in ("w_out", "w_q", "w_kv", "w_o", "w_up", "w_down", "ln_g", "ln_b", "conv"):
                d[k] = dr[f"{k}{l}"]
            return d
        drB = post_dr(0)
        drB.update(oAG_get=oget_fn(oAG0), hT_in=dr["xTc"], hT_out=h1loc, hTb_put=hb_put, hl_put=hl_put)
        emit_post(S, dict(cfgP, out_off=0, outb_off=HALO), drB, blocks)
        for i in range(16):
            S.collective("AllGather", GROUPS, hb[i], hAG[i])
        for i in range(2):
            S.collective("AllGather", GROUPS, hl[i], hlAG[i])
        drC = dict(dr); drC["hAG_get"] = hAG_get; drC["oT_put"] = put_fn(oC)
        emit_nsa(S, dict(D=D_MODEL, SEQ=SEQ, NW=3, TOKC=TOKC), drC)
        for i in range(NB):
            S.collective("AllGather", GROUPS, oC[i], oAG1[i])
        drD = post_dr(1)
        drD.update(oAG_get=oget_fn(oAG1), hT_in=h1loc, hlAG_get=hlAG_get, hT_out=outT)
        emit_post(S, dict(cfgP, out_off=HALO), drD, blocks)
        S.wait_all("sp", [outT])
        S.wait_all("pool", [outT])
    return nc


def _arr3(v, n):
    return np.ascontiguousarray(np.asarray(v, np.float32).reshape(3, n, 128).transpose(2, 0, 1))


def kernel(**inp):
    inp = {k: np.asarray(v) for k, v in inp.items()}
    cs = _consts_all()
    cores = list(range(NCORE))
    KC, FC = D_MODEL // 128, DFF // 128
    NT = HALO + TOKC
    xT = [np.ascontiguousarray(inp["x"][b].T) for b in range(BATCH)]
    memT = [np.ascontiguousarray(inp["mem"][b].T) for b in range(BATCH)]
    w_in = inp["ab_w_in"][0]
    AW = 2048
    wo = inp["ab_w_out"][0]
    wo_perm = np.ascontiguousarray(np.concatenate([np.concatenate([wo[512 * j:512 * j + 512], wo[2048 + 512 * j:2048 + 512 * j + 512]], axis=0) for j in range(4)], axis=0))
    wn = inp["nsa_w_in"][0]
    shared = dict(ident=cs["ident"], identb=cs["identb"], negU=cs["negU"], sbmask=cs["sbmask"], maskU=cs["maskU"], rmask=cs["rmask"],
                  hg_nw=np.ascontiguousarray(np.tile(inp["hgrn_norm_w"][0][None, :], (128, 1))),
                  cmp_w1=np.ascontiguousarray(inp["nsa_cmp_w1"][0]), cmp_w2=np.ascontiguousarray(inp["nsa_cmp_w2"][0].transpose(1, 0, 2)),
                  cmp_posT=np.ascontiguousarray(inp["nsa_cmp_pos"][0].transpose(2, 0, 1)),
                  rotm=cs["rotm"], ov=cs["ov"], ex32=cs["ex32"], wmask=cs["wmask"], cmask=cs["cmask"], ABt=cs["ABt"], FBt=cs["FBt"],
                  cosT=cs["cosT"], sinT=cs["sinT"])
    wouts = [wo_perm, np.ascontiguousarray(inp["nsa_w_out"][0])]
    for l in range(2):
        shared[f"w_out{l}"] = wouts[l]
        shared[f"w_q{l}"] = np.ascontiguousarray(inp["xa_w_q"][l]); shared[f"w_kv{l}"] = np.ascontiguousarray(inp["xa_w_kv"][l])
        shared[f"w_o{l}"] = np.ascontiguousarray(inp["xa_w_o"][l]); shared[f"w_up{l}"] = np.ascontiguousarray(inp["ffn_w_up"][l])
        shared[f"w_down{l}"] = np.ascontiguousarray(inp["ffn_w_down"][l])
        shared[f"ln_g{l}"] = _arr3(inp["ln_g"][l], KC); shared[f"ln_b{l}"] = _arr3(inp["ln_b"][l], KC); shared[f"conv{l}"] = _arr3(inp["ffn_conv"][l], FC)
    maps = []
    for c in cores:
        b, j = divmod(c, 4)
        hs = slice(512 * j, 512 * j + 512)
        segs = [w_in[:, 4 * AW + 0 * AW:][:, hs], w_in[:, 4 * AW + 1 * AW:][:, hs], w_in[:, 4 * AW + 2 * AW:][:, hs],
                w_in[:, 0 * AW:][:, hs], w_in[:, 1 * AW:][:, hs], w_in[:, 2 * AW:][:, hs], w_in[:, 3 * AW:][:, hs]]
        wA = np.ascontiguousarray(np.concatenate(segs, axis=1))
        lb = np.ascontiguousarray(inp["hgrn_lb"][:, hs].reshape(2, 4, 128).transpose(2, 0, 1))
        g = j
        segc = [wn[:, 1024 * g:1024 * g + 1024]] + [wn[:, 4096 + 512 * i + 128 * g:4096 + 512 * i + 128 * g + 128] for i in range(6)] + \
               [wn[:, 4096 + 3072 + 24 * g:4096 + 3072 + 24 * g + 24]]
        wC = np.ascontiguousarray(np.concatenate(segc, axis=1))
        t0 = j * TOKC
        xTc = np.zeros((D_MODEL, NT), np.float32)
        xTc[:, HALO:] = xT[b][:, t0:t0 + TOKC]
        if j > 0:
            xTc[:, :HALO] = xT[b][:, t0 - HALO:t0]
        fsel = np.zeros((128, 4), np.float32); fsel[:, j] = 1.0
        fhal = np.zeros((128, 4), np.float32)
        if j > 0:
            fhal[:, j - 1] = 1.0
        m = dict(shared)
        m.update(xT=xT[b], xTc=xTc, memT=memT[b], wA=wA, wC=wC, hg_lb=lb, flag=np.full((128, 1), 1.0 if j > 0 else 0.0, np.float32), fsel=fsel, fhal=fhal)
        maps.append(m)
    res = run_bass_kernel_spmd(_build_fused(), maps, core_ids=cores)
    out = np.empty((BATCH, SEQ, D_MODEL), np.float32)
    for c in cores:
        b, j = divmod(c, 4)
        out[b, j * TOKC:(j + 1) * TOKC, :] = res.results[c]["outT"].T
    return out
```

```python
import numpy as np
import ml_dtypes
from contextlib import ExitStack
import concourse.bass as bass
import concourse.mybir as mybir


F32 = mybir.dt.float32
BF16 = mybir.dt.bfloat16
AF = mybir.ActivationFunctionType
ALU = mybir.AluOpType


class Res:
    __slots__ = ("name", "w", "r", "t", "multi", "mw")

    def __init__(self, name, t=None, multi=False):
        self.name = name
        self.w = None
        self.r = {}
        self.t = t
        self.multi = multi
        self.mw = {}

    def __getitem__(self, idx):
        return self.t[idx]


class Sched:
    def __init__(self, nc, es: ExitStack, n_dma_slots=8):
        self.nc = nc
        self.es = es
        self.es0 = es
        self.E = {"pe": nc.tensor, "act": nc.scalar, "dve": nc.vector, "pool": nc.gpsimd, "sp": nc.sync}
        self.sem = {k: es.enter_context(nc.semaphore("prog_" + k)) for k in self.E}
        self.cnt = {k: 0 for k in self.E}
        self.known = {k: {} for k in self.E}
        self.slots = {}
        for q in ("sp", "pool", "act"):
            self.slots[q] = [[es.enter_context(nc.semaphore(f"dq_{q}_{i}")), 0] for i in range(n_dma_slots)]
        self.slot_i = {q: 0 for q in self.slots}
        self.n_ins = 0
        self.n_wait = 0
        self.prefix = ""
        self.nphase = 0

    def new_phase(self):
        self.prefix = f"p{self.nphase}_"
        self.nphase += 1

    def sbuf(self, name, shape, dt):
        t = self.es.enter_context(self.nc.sbuf_tensor("sb_" + self.prefix + name, list(shape), dt))
        return Res(name, t)

    def psum(self, name, shape, dt=F32):
        t = self.es.enter_context(self.nc.psum_tensor("ps_" + self.prefix + name, list(shape), dt))
        return Res(name, t)

    def dram(self, name, shape, dt, kind="Internal", multi=True):
        t = self.nc.dram_tensor(name, list(shape), dt, kind=kind)
        return Res(name, t.ap(), multi=multi)

    def dram_cc(self, name, shape, dt):
        t = self.nc.dram_tensor(name, list(shape), dt)
        return Res(name, t.ap(), multi=True)

    def collective(self, kind, groups, src, dst):
        if not hasattr(self, "cc_sem"):
            self.cc_sem = self.es0.enter_context(self.nc.semaphore("cc_sem"))
            self.cc_cnt = 0
        self._deps("pool", [src], [])
        for ev in list(dst.mw.values()) + list(dst.r.values()):
            self._wait("pool", ev)
        ins = self.nc.gpsimd.collective_compute(kind, mybir.AluOpType.bypass, replica_groups=groups, ins=[src.t.opt()], outs=[dst.t.opt()])
        self.cc_cnt += 1
        ins.then_inc(self.cc_sem, 1)
        ev = (self.cc_sem, self.cc_cnt, "cc")
        src.r[id(self.cc_sem)] = ev
        dst.mw[id(self.cc_sem)] = ev
        self.n_ins += 1

    def _wait(self, eng, ev):
        sem, val, src = ev
        if src == eng and eng == "pe":
            return
        k = id(sem)
        if self.known[eng].get(k, 0) >= val:
            return
        self.E[eng].wait_ge(sem, val)
        self.known[eng][k] = val
        self.n_wait += 1

    def _deps(self, eng, reads, writes):
        for r in reads:
            if r.w is not None:
                self._wait(eng, r.w)
            for ev in r.mw.values():
                self._wait(eng, ev)
        for w in writes:
            if w.multi:
                continue
            if w.w is not None:
                self._wait(eng, w.w)
            for ev in w.r.values():
                self._wait(eng, ev)

    def _record(self, ev, reads, writes):
        sem = ev[0]
        for r in reads:
            r.r[id(sem)] = ev
        for w in writes:
            if w.multi:
                w.mw[id(sem)] = ev
                continue
            w.w = ev
            w.r = {}

    def op(self, eng, reads, writes, fn):
        self._deps(eng, reads, writes)
        ins = fn(self.E[eng])
        self.cnt[eng] += 1
        ins.then_inc(self.sem[eng], 1)
        ev = (self.sem[eng], self.cnt[eng], eng)
        self.known[eng][id(self.sem[eng])] = max(self.known[eng].get(id(self.sem[eng]), 0), 0)
        self._record(ev, reads, writes)
        self.n_ins += 1
        return ins

    def dma(self, q, reads, writes, out, in_, **kw):
        slots = self.slots[q]
        i = self.slot_i[q]
        self.slot_i[q] = (i + 1) % len(slots)
        s = slots[i]
        if s[1] > 0:
            self._wait(q, (s[0], s[1], "dma"))
        self._deps(q, reads, writes)
        ins = self.E[q].dma_start(out=out, in_=in_, **kw)
        s[1] += 16
        ins.then_inc(s[0], 16)
        ev = (s[0], s[1], "dma")
        self._record(ev, reads, writes)
        self.n_ins += 1
        return ins

    def wait_all(self, eng, resources):
        for r in resources:
            if r.w is not None:
                self._wait(eng, r.w)
            for ev in r.mw.values():
                self._wait(eng, ev)

    def barrier(self):
        evs = [(self.sem[k], self.cnt[k], k) for k in self.E if self.cnt[k] > 0]
        for q, sl in self.slots.items():
            for s in sl:
                if s[1] > 0:
                    evs.append((s[0], s[1], "dma"))
        for eng in self.E:
            for ev in evs:
                sem, val, src = ev
                if self.known[eng].get(id(sem), 0) >= val:
                    continue
                self.E[eng].wait_ge(sem, val)
                self.known[eng][id(sem)] = val


class Ring:
    def __init__(self, items):
        self.items = items
        self.i = 0

    def next(self):
        r = self.items[self.i]
        self.i = (self.i + 1) % len(self.items)
        return r


def views(S, name, shape, dt, n, idx_fn):
    t = S.es.enter_context(S.nc.sbuf_tensor("sb_" + S.prefix + name, list(shape), dt))
    return t, [Res(f"{name}{i}", idx_fn(t, i)) for i in range(n)]


def make_wring(S, name, nslots):
    _, sl = views(S, name, [128, nslots, 4096], BF16, nslots, lambda t, i: t[:, i, :])
    return Ring(sl)


def linear_g(S, wring, NW, W, r0, acts, T, groups, evac, banks, halo_skip=None):
    nk = len(acts)
    loads = []
    for gi, (c0, n) in enumerate(groups):
        per = 4096 // (n * 128)
        for k0 in range(0, nk, per):
            loads.append((gi, c0, n, k0, min(per, nk - k0)))
    issued = []

    def issue(j):
        gi, c0, n, k0, kn = loads[j]
        slot = wring.next()
        v = slot.t[:, 0:kn * n * 128].rearrange("p (k c) -> p k c", c=n * 128)
        S.dma("pool", [W], [slot], out=v, in_=W.t[r0 + k0 * 128:r0 + (k0 + kn) * 128, c0:c0 + n * 128].rearrange("(kc p) c -> p kc c", p=128))
        issued.append((slot, v))
    npre = NW - 1
    nxt = 0
    while nxt < min(npre, len(loads)):
        issue(nxt); nxt += 1
    ci = 0
    j = 0
    for gi, (c0, n) in enumerate(groups):
        ps = [banks.next() for _ in range(n)]
        while j < len(loads) and loads[j][0] == gi:
            if nxt < len(loads):
                issue(nxt); nxt += 1
            _, _, _, k0, kn = loads[j]
            slot, v = issued[j]
            for oc in range(n):
                for kk in range(kn):
                    k = k0 + kk
                    S.op("pe", [slot, acts[k]], [ps[oc]],
                         lambda e, oc=oc, kk=kk, k=k: e.matmul(ps[oc].t[:, 0:T], lhsT=v[:, kk, oc * 128:(oc + 1) * 128], rhs=acts[k].t[:, 0:T],
                                                              start=(k == 0), stop=(k == nk - 1)))
            j += 1
        for oc in range(n):
            evac(ci, ps[oc])
            ci += 1


def col_groups(c0, nchunks):
    g = []
    i = 0
    while i < nchunks:
        n = min(4, nchunks - i)
        g.append((c0 + i * 128, n))
        i += n
    return g


class Ring:
    def __init__(self, items):
        self.items = items
        self.i = 0

    def next(self):
        r = self.items[self.i]
        self.i = (self.i + 1) % len(self.items)
        return r


def views(S, name, shape, dt, n, idx_fn):
    t = S.es.enter_context(S.nc.sbuf_tensor("sb_" + S.prefix + name, list(shape), dt))
    return t, [Res(f"{name}{i}", idx_fn(t, i)) for i in range(n)]


def emit_post(S, c, dr, blocks):
    nc = S.nc
    D, DFF, XH, NM = c["D"], c["DFF"], c["XH"], c["NM"]
    KC, FC = D // 128, DFF // 128
    XW = XH * 128
    TM = c["T"]
    alpha, eps = c["alpha"], c["eps"]
    GF = c.get("GF", 8)
    NW = c.get("NW", 4)
    KCW = max(KC, GF, XH)
    oes = S.es
    S.new_phase()
    with ExitStack() as pes:
        S.es = pes
        wring = make_wring(S, "wsl", NW)
        ones_f = S.sbuf("ones_f", [128, 128], F32)
        ones_b = S.sbuf("ones_b", [128, 128], BF16)
        epsT = S.sbuf("epsT", [128, 1], F32)
        lng = S.sbuf("lng", [128, 3, KC], F32)
        lnb = S.sbuf("lnb", [128, 3, KC], F32)
        cw = S.sbuf("cw", [128, 3, FC], F32)
        flag = S.sbuf("flag", [128, 1], F32)
        if "oAG_get" in dr:
            cand = Ring([S.sbuf(f"cand{i}", [128, TM], BF16) for i in range(6)])
            selacc = Ring([S.sbuf(f"selacc{i}", [128, TM], BF16) for i in range(4)])
            fsel = S.sbuf("fsel", [128, 4], F32)
            S.dma("sp", [dr["fsel"]], [fsel], out=fsel.t[:], in_=dr["fsel"].t)
        if "hlAG_get" in dr:
            candf = Ring([S.sbuf(f"candf{i}", [128, 128], F32) for i in range(4)])
            fhal = S.sbuf("fhal", [128, 4], F32)
            S.dma("sp", [dr["fhal"]], [fhal], out=fhal.t[:], in_=dr["fhal"].t)
        S.dma("sp", [dr["flag"]], [flag], out=flag.t[:], in_=dr["flag"].t)
        carry = [S.sbuf(f"carry{f}", [128, 2], F32) for f in range(FC)]
        KT = S.sbuf("KT", [128, XH, NM], BF16)
        Vt = [S.sbuf(f"Vt{m}", [128, XW], BF16) for m in range(NM // 128)]
        qT = [S.sbuf(f"qT{h}", [128, TM], BF16) for h in range(XH)]
        xoT = [S.sbuf(f"xoT{h}", [128, TM], BF16) for h in range(XH)]
        pT = Ring([S.sbuf(f"pT{i}", [128, TM], BF16) for i in range(4)])
        tmpf = Ring([S.sbuf(f"tmpf{i}", [128, TM + 2], F32) for i in range(6)])
        st_mean = S.sbuf("st_mean", [128, TM], F32)
        st_rstd = S.sbuf("st_rstd", [128, TM], F32)
        st_nmr = S.sbuf("st_nmr", [128, TM], F32)
        pb = [S.psum(f"pb{i}", [128, 512], F32) for i in range(8)]
        pring = Ring(pb[0:4])
        pA, pB2 = pb[4], pb[5]
        pring2 = Ring(pb[6:8])
        pringu = Ring(pb[4:8])
        glr = Ring([S.sbuf(f"glr{i}", [128, TM], F32) for i in range(5)])

        S.op("dve", [], [ones_f], lambda e: e.memset(ones_f.t[:], 1.0))
        S.op("dve", [], [ones_b], lambda e: e.memset(ones_b.t[:], 1.0))
        S.op("dve", [], [epsT], lambda e: e.memset(epsT.t[:], eps))
        S.dma("sp", [dr["ln_g"]], [lng], out=lng.t[:], in_=dr["ln_g"].t)
        S.dma("sp", [dr["ln_b"]], [lnb], out=lnb.t[:], in_=dr["ln_b"].t)
        S.dma("sp", [dr["conv"]], [cw], out=cw.t[:], in_=dr["conv"].t)
        for f in range(FC):
            S.op("dve", [], [carry[f]], lambda e, f=f: e.memset(carry[f].t[:], 0.0))

        def load_w(W, r0, nk, c0, ncol=128):
            slot = wring.next()
            v = slot.t[:, 0:nk * ncol].rearrange("p (k c) -> p k c", c=ncol)
            S.dma("pool", [W], [slot], out=v, in_=W.t[r0:r0 + nk * 128, c0:c0 + ncol].rearrange("(kc p) c -> p kc c", p=128))
            return slot, v

        def linear(W, r0, acts, T, cols, evac, ps_ring):
            assert all(cols[i + 1] == cols[i] + 128 for i in range(len(cols) - 1))
            linear_g(S, wring, NW, W, r0, acts, T, col_groups(cols[0], len(cols)), evac, ps_ring)

        def layer_norm(li, T):
            for k in range(KC):
                S.op("pe", [ones_f, rT[k]], [pA],
                     lambda e, k=k: e.matmul(pA.t[:, 0:T], lhsT=ones_f.t[:], rhs=rT[k].t[:, 0:T], start=(k == 0), stop=(k == KC - 1)))
            for k in range(KC):
                sq = tmpf.next()
                S.op("act", [rT[k]], [sq], lambda e, k=k, sq=sq: e.activation(out=sq.t[:, 0:T], in_=rT[k].t[:, 0:T], func=AF.Square))
                S.op("pe", [ones_f, sq], [pB2],
                     lambda e, k=k, sq=sq: e.matmul(pB2.t[:, 0:T], lhsT=ones_f.t[:], rhs=sq.t[:, 0:T], start=(k == 0), stop=(k == KC - 1)))
            inv = 1.0 / D
            S.op("act", [pA], [st_mean], lambda e: e.activation(out=st_mean.t[:, 0:T], in_=pA.t[:, 0:T], func=AF.Copy, scale=inv))
            m2 = tmpf.next()
            S.op("dve", [st_mean], [m2], lambda e: e.tensor_tensor(out=m2.t[:, 0:T], in0=st_mean.t[:, 0:T], in1=st_mean.t[:, 0:T], op=ALU.mult))
            var = tmpf.next()
            S.op("dve", [pB2, m2], [var], lambda e: e.scalar_tensor_tensor(out=var.t[:, 0:T], in0=pB2.t[:, 0:T], scalar=inv, in1=m2.t[:, 0:T],
                                                                            op0=ALU.mult, op1=ALU.subtract))
            sd = tmpf.next()
            S.op("act", [var, epsT], [sd], lambda e: e.activation(out=sd.t[:, 0:T], in_=var.t[:, 0:T], func=AF.Sqrt, bias=epsT.t[:, 0:1]))
            S.op("dve", [sd], [st_rstd], lambda e: e.reciprocal(out=st_rstd.t[:, 0:T], in_=sd.t[:, 0:T]))
            S.op("dve", [st_mean, st_rstd], [st_nmr],
                 lambda e: e.scalar_tensor_tensor(out=st_nmr.t[:, 0:T], in0=st_mean.t[:, 0:T], scalar=-1.0, in1=st_rstd.t[:, 0:T], op0=ALU.mult, op1=ALU.mult))
            for k in range(KC):
                t1 = tmpf.next()
                S.op("dve", [rT[k], st_rstd], [t1], lambda e, k=k, t1=t1: e.tensor_tensor(out=t1.t[:, 0:T], in0=rT[k].t[:, 0:T], in1=st_rstd.t[:, 0:T], op=ALU.mult))
                t2 = tmpf.next()
                S.op("pool", [t1, st_nmr], [t2], lambda e, t1=t1, t2=t2: e.tensor_tensor(out=t2.t[:, 0:T], in0=t1.t[:, 0:T], in1=st_nmr.t[:, 0:T], op=ALU.add))
                S.op("act", [t2, lng, lnb], [rT[k]],
                     lambda e, k=k, t2=t2: e.activation(out=rT[k].t[:, 0:T], in_=t2.t[:, 0:T], func=AF.Identity,
                                                        scale=lng.t[:, li, k:k + 1], bias=lnb.t[:, li, k:k + 1]))
                S.op("act", [t2, lng, lnb], [aT[k]],
                     lambda e, k=k, t2=t2: e.activation(out=aT[k].t[:, 0:T], in_=t2.t[:, 0:T], func=AF.Identity,
                                                        scale=lng.t[:, li, k:k + 1], bias=lnb.t[:, li, k:k + 1]))

        def resid_evac(T):
            def ev(i, ps):
                S.op("dve", [rT[i], ps], [rT[i]],
                     lambda e, i=i, ps=ps: e.scalar_tensor_tensor(out=rT[i].t[:, 0:T], in0=rT[i].t[:, 0:T], scalar=alpha, in1=ps.t[:, 0:T],
                                                                  op0=ALU.mult, op1=ALU.add))
            return ev

        mes = ExitStack()
        S.es = mes
        memT = S.sbuf("memT", [128, KC, NM], BF16)
        S.es = pes
        S.dma("pool", [dr["memT"]], [memT], out=memT.t[:], in_=dr["memT"].t.rearrange("(kc p) m -> p kc m", p=128))
        memk = [Res(f"memk{k}", memT.t[:, k, :]) for k in range(KC)]
        for mk in memk:
            mk.w = memT.w

        def kt_evac(i, ps):
            S.op("act", [ps], [KT], lambda e, i=i, ps=ps: e.activation(out=KT.t[:, i, :], in_=ps.t[:, 0:NM], func=AF.Copy))
        linear(dr["w_kv"], 0, memk, NM, [h * 128 for h in range(XH)], kt_evac, pring)
        for h in range(XH):
            slot, sv = load_w(dr["w_kv"], 0, KC, XW + h * 128)
            for m in range(NM // 128):
                ps = pring.next()
                for k in range(KC):
                    S.op("pe", [slot, memT], [ps],
                         lambda e, k=k, m=m, slot=slot, ps=ps: e.matmul(ps.t[:, 0:128], lhsT=memT.t[:, k, m * 128:(m + 1) * 128], rhs=sv[:, k, :],
                                                                        start=(k == 0), stop=(k == KC - 1)))
                S.op("act", [ps], [Vt[m]], lambda e, m=m, h=h, ps=ps: e.activation(out=Vt[m].t[:, h * 128:(h + 1) * 128], in_=ps.t[:, 0:128], func=AF.Copy))

        S.barrier()
        mes.close()
        _, rT = views(S, "rT", [128, KC, TM], F32, KC, lambda t, i: t[:, i, :])
        _, aT = views(S, "aT", [128, KC, TM], BF16, KC, lambda t, i: t[:, i, :])
        _, gT = views(S, "gT", [128, 2 * GF, TM], BF16, 2 * GF, lambda t, i: t[:, i, :])
        for (tok0, T, halo) in blocks:
            HL = c.get("HALO", 128)
            TOKC_ = c.get("TOKC", 2048)
            for k in range(KC):
                if "oAG_get" in dr:
                    for cq in range(4):
                        if halo:
                            cb = max(cq * TOKC_ - HL, 0)
                        else:
                            cb = cq * TOKC_ + (tok0 - HL)
                        cd = cand.next()
                        _ores, _oap = dr["oAG_get"](k, cb, T)
                        S.dma("sp", [_ores], [cd], out=cd.t[:, 0:T], in_=_oap)
                        if cq == 0:
                            acc = selacc.next()
                            S.op("dve", [cd, fsel], [acc], lambda e: e.tensor_scalar(out=acc.t[:, 0:T], in0=cd.t[:, 0:T], scalar1=fsel.t[:, 0:1], scalar2=None, op0=ALU.mult))
                        else:
                            dst = aT[k] if cq == 3 else selacc.next()
                            S.op("dve", [cd, fsel, acc], [dst], lambda e: e.scalar_tensor_tensor(out=dst.t[:, 0:T], in0=cd.t[:, 0:T], scalar=fsel.t[:, cq:cq + 1], in1=acc.t[:, 0:T],
                                                                                              op0=ALU.mult, op1=ALU.add))
                            acc = dst
                else:
                    S.dma("sp", [dr["oT"]], [aT[k]], out=aT[k].t[:, 0:T], in_=dr["oT"].t[k * 128:(k + 1) * 128, tok0:tok0 + T])
                if halo and "hlAG_get" in dr:
                    for cq in range(4):
                        cd = candf.next()
                        _lres, _lap = dr["hlAG_get"](k, cq)
                        S.dma("sp", [_lres], [cd], out=cd.t[:, 0:T], in_=_lap)
                        if cq == 0:
                            S.op("dve", [cd, fhal], [rT[k]], lambda e: e.tensor_scalar(out=rT[k].t[:, 0:T], in0=cd.t[:, 0:T], scalar1=fhal.t[:, 0:1], scalar2=None, op0=ALU.mult))
                        else:
                            S.op("dve", [cd, fhal, rT[k]], [rT[k]], lambda e: e.scalar_tensor_tensor(out=rT[k].t[:, 0:T], in0=cd.t[:, 0:T], scalar=fhal.t[:, cq:cq + 1], in1=rT[k].t[:, 0:T],
                                                                                                    op0=ALU.mult, op1=ALU.add))
                else:
                    S.dma("sp", [dr["hT_in"]], [rT[k]], out=rT[k].t[:, 0:T], in_=dr["hT_in"].t[k * 128:(k + 1) * 128, tok0:tok0 + T])
            linear(dr["w_out"], 0, aT, T, [i * 128 for i in range(KC)], resid_evac(T), pring)
            layer_norm(0, T)
            qscale = 128.0 ** -0.5

            def q_evac(i, ps):
                S.op("act", [ps], [qT[i]], lambda e, i=i, ps=ps: e.activation(out=qT[i].t[:, 0:T], in_=ps.t[:, 0:T], func=AF.Copy, scale=qscale))
            linear(dr["w_q"], 0, aT, T, [h * 128 for h in range(XH)], q_evac, pring)
            for h in range(XH):
                pts = []
                for m in range(NM // 128):
                    ps = pring.next()
                    S.op("pe", [KT, qT[h]], [ps], lambda e, h=h, m=m, ps=ps: e.matmul(ps.t[:, 0:T], lhsT=KT.t[:, h, m * 128:(m + 1) * 128], rhs=qT[h].t[:, 0:T],
                                                                                      start=True, stop=True))
                    pt = pT.next()
                    S.op("act", [ps], [pt], lambda e, ps=ps, pt=pt: e.activation(out=pt.t[:, 0:T], in_=ps.t[:, 0:T], func=AF.Exp))
                    pts.append(pt)
                nm = len(pts)
                for m, pt in enumerate(pts):
                    S.op("pe", [ones_b, pt], [pA], lambda e, m=m, pt=pt: e.matmul(pA.t[:, 0:T], lhsT=ones_b.t[:], rhs=pt.t[:, 0:T], start=(m == 0), stop=(m == nm - 1)))
                for m, pt in enumerate(pts):
                    S.op("pe", [Vt[m], pt], [pB2], lambda e, m=m, h=h, pt=pt: e.matmul(pB2.t[:, 0:T], lhsT=Vt[m].t[:, h * 128:(h + 1) * 128], rhs=pt.t[:, 0:T],
                                                                                     start=(m == 0), stop=(m == nm - 1)))
                rd = tmpf.next()
                S.op("dve", [pA], [rd], lambda e, rd=rd: e.reciprocal(out=rd.t[:, 0:T], in_=pA.t[:, 0:T]))
                S.op("dve", [pB2, rd], [xoT[h]], lambda e, h=h, rd=rd: e.tensor_tensor(out=xoT[h].t[:, 0:T], in0=pB2.t[:, 0:T], in1=rd.t[:, 0:T], op=ALU.mult))
            linear(dr["w_o"], 0, xoT, T, [i * 128 for i in range(KC)], resid_evac(T), pring)
            layer_norm(1, T)
            groups = [list(range(g0, min(g0 + GF, FC))) for g0 in range(0, FC, GF)]

            def glu_a(f, ps):
                ab = tmpf.next()
                S.op("act", [carry[f]], [ab], lambda e: e.activation(out=ab.t[:, 0:2], in_=carry[f].t[:, 0:2], func=AF.Copy))
                S.op("act", [ps], [ab], lambda e: e.activation(out=ab.t[:, 2:2 + T], in_=ps.t[:, 0:T], func=AF.Copy))
                if halo:
                    S.op("dve", [ab, flag], [carry[f]], lambda e: e.tensor_scalar(out=carry[f].t[:, 0:2], in0=ab.t[:, T:T + 2], scalar1=flag.t[:, 0:1], scalar2=None, op0=ALU.mult))
                    return None
                S.op("dve", [ab], [carry[f]], lambda e: e.tensor_copy(out=carry[f].t[:, 0:2], in_=ab.t[:, T:T + 2]))
                c1 = tmpf.next()
                S.op("act", [ab, cw], [c1], lambda e: e.activation(out=c1.t[:, 0:T], in_=ab.t[:, 2:2 + T], func=AF.Copy, scale=cw.t[:, 2, f:f + 1]))
                c2 = tmpf.next()
                S.op("dve", [ab, cw, c1], [c2], lambda e: e.scalar_tensor_tensor(out=c2.t[:, 0:T], in0=ab.t[:, 1:1 + T], scalar=cw.t[:, 1, f:f + 1], in1=c1.t[:, 0:T], op0=ALU.mult, op1=ALU.add))
                S.op("dve", [ab, cw, c2], [c1], lambda e: e.scalar_tensor_tensor(out=c1.t[:, 0:T], in0=ab.t[:, 0:T], scalar=cw.t[:, 0, f:f + 1], in1=c2.t[:, 0:T], op0=ALU.mult, op1=ALU.add))
                gl = glr.next()
                S.op("act", [c1], [gl], lambda e: e.activation(out=gl.t[:, 0:T], in_=c1.t[:, 0:T], func=AF.Gelu))
                return gl

            def up_group(gi, fl):
                gb = (gi % 2) * GF
                for s0 in range(0, len(fl), 4):
                    sub = fl[s0:s0 + 4]
                    gls = {}

                    def ev_a(i, ps):
                        gls[sub[i]] = glu_a(sub[i], ps)
                    linear_g(S, wring, NW, dr["w_up"], 0, aT, T, [(sub[0] * 128, len(sub))], ev_a, pring)
                    if halo:
                        continue

                    def ev_u(i, ps):
                        f = sub[i]
                        gdst = gT[gb + (f - fl[0])]
                        S.op("dve", [gls[f], ps], [gdst], lambda e: e.tensor_tensor(out=gdst.t[:, 0:T], in0=gls[f].t[:, 0:T], in1=ps.t[:, 0:T], op=ALU.mult))
                    linear_g(S, wring, NW, dr["w_up"], 0, aT, T, [(DFF + sub[0] * 128, len(sub))], ev_u, pringu)

            def down_group(gi, fl):
                gb = (gi % 2) * GF
                nk = len(fl)
                acts = [gT[gb + j] for j in range(nk)]

                def ev(i, ps):
                    if gi == 0:
                        resid_evac(T)(i, ps)
                    else:
                        S.op("dve", [rT[i], ps], [rT[i]], lambda e, i=i, ps=ps: e.tensor_tensor(out=rT[i].t[:, 0:T], in0=rT[i].t[:, 0:T], in1=ps.t[:, 0:T], op=ALU.add))
                linear(dr["w_down"], fl[0] * 128, acts, T, [i * 128 for i in range(KC)], ev, pring)

            for gi, fl in enumerate(groups):
                up_group(gi, fl)
                if halo:
                    continue
                if gi > 0:
                    down_group(gi - 1, groups[gi - 1])
            if halo:
                continue
            down_group(len(groups) - 1, groups[-1])
            layer_norm(2, T)
            for k in range(KC):
                oo = c.get("out_off", 0)
                S.dma("sp", [rT[k]], [dr["hT_out"]], out=dr["hT_out"].t[k * 128:(k + 1) * 128, tok0 - oo:tok0 - oo + T], in_=rT[k].t[:, 0:T])
                if "hTb_put" in dr:
                    _bres, _bap = dr["hTb_put"](k, tok0 - c.get("outb_off", 0), T)
                    S.dma("sp", [aT[k]], [_bres], out=_bap, in_=aT[k].t[:, 0:T])
                elif "hTb_out" in dr:
                    bo = c.get("outb_off", 0)
                    S.dma("sp", [aT[k]], [dr["hTb_out"]], out=dr["hTb_out"].t[k * 128:(k + 1) * 128, tok0 - bo:tok0 - bo + T], in_=aT[k].t[:, 0:T])
                if "hl_put" in dr and (tok0, T, halo) == blocks[-1]:
                    _lres, _lap = dr["hl_put"](k)
                    S.dma("sp", [rT[k]], [_lres], out=_lap, in_=rT[k].t[:, T - 128:T])
        S.barrier()
    S.es = oes


NEG = -30000.0


def _oput(dr, r0, t0, T):
    if "oT_put" in dr:
        return dr["oT_put"](r0, t0, T)
    return dr["oT"], dr["oT"].t[r0:r0 + 128, t0:t0 + T]


def _linear(S, wring, NW, W, acts, T, cols, evac, ps_ring):
    groups = []
    i = 0
    while i < len(cols):
        n = 1
        while n < 4 and i + n < len(cols) and cols[i + n] == cols[i] + n * 128:
            n += 1
        groups.append((cols[i], n))
        i += n
    linear_g(S, wring, NW, W, 0, acts, T, groups, evac, ps_ring)


def emit_sb(S, c, dr, heads, col_q, col_k, col_v, orow0):
    D, SEQ = c["D"], c["SEQ"]
    KC = D // 128
    T = 512
    NH = len(heads)
    NW = c.get("NW", 4)
    NT = SEQ // 128
    oes = S.es
    S.new_phase()
    with ExitStack() as pes:
        S.es = pes
        _, aT = views(S, "aT", [128, KC, T], BF16, KC, lambda t, i: t[:, i, :])
        wring = make_wring(S, "wsl", NW)
        kT = [S.sbuf(f"kT{h}", [128, SEQ], BF16) for h in range(NH)]
        _, Vall = views(S, "Vall", [128, NT, NH * 128], BF16, NT, lambda t, i: t[:, i, :])
        qT = [S.sbuf(f"qT{h}", [128, T], BF16) for h in range(NH)]
        vf = Ring([S.sbuf(f"vf{i}", [128, T], F32) for i in range(2)])
        ident = S.sbuf("ident", [128, 128], F32)
        identb = S.sbuf("identb", [128, 128], BF16)
        negU = S.sbuf("negU", [128, 128], BF16)
        negO = S.sbuf("negO", [128, 128], BF16)
        mask = S.sbuf("mask", [128, 4, T], BF16)
        eT = [Ring([S.sbuf(f"eT{h}_{i}", [128, T], F32) for i in range(2)]) for h in range(NH)]
        spT = [Ring([S.sbuf(f"spT{h}_{i}", [128, T], BF16) for i in range(2)]) for h in range(NH)]
        tmpT = [Ring([S.sbuf(f"tmpT{h}_{i}", [128, T], F32) for i in range(2)]) for h in range(NH)]
        xT_ = [Ring([S.sbuf(f"xT{h}_{i}", [128, T], F32) for i in range(2)]) for h in range(NH)]
        wT = [Ring([S.sbuf(f"wT{h}_{i}", [128, T], BF16) for i in range(2)]) for h in range(NH)]
        carry = [S.sbuf(f"carry{h}", [128, T], F32) for h in range(NH)]
        oS = Ring([S.sbuf(f"oS{i}", [128, T], BF16) for i in range(2)])
        pb = [S.psum(f"pb{i}", [128, 512], F32) for i in range(8)]
        pz = [pb[0], pb[1]]
        pR = [pb[2], pb[3]]
        pC = [pb[4], pb[5]]
        pO = [pb[6], pb[7]]
        plin = Ring([pb[2], pb[3], pb[4], pb[5]])
        ptr = Ring([pb[6], pb[7]])

        S.dma("sp", [dr["ident"]], [ident], out=ident.t[:], in_=dr["ident"].t)
        S.dma("sp", [dr["identb"]], [identb], out=identb.t[:], in_=dr["identb"].t)
        S.dma("sp", [dr["negU"]], [negU], out=negU.t[:], in_=dr["negU"].t)
        S.dma("sp", [dr["sbmask"]], [mask], out=mask.t[:], in_=dr["sbmask"].t)
        S.op("dve", [], [negO], lambda e: e.memset(negO.t[:], -1.0))
        scale = 128.0 ** -0.5

        for tb in range(SEQ // T):
            t0 = tb * T
            for k in range(KC):
                S.dma("pool", [dr["xT"]], [aT[k]], out=aT[k].t[:, :], in_=dr["xT"].t[k * 128:(k + 1) * 128, t0:t0 + T])
            cols = [col_q + h * 128 for h in heads] + [col_k + h * 128 for h in heads] + [col_v + h * 128 for h in heads]

            def evac(i, ps):
                kind, hi = divmod(i, NH)
                if kind == 0:
                    S.op("act", [ps], [qT[hi]], lambda e: e.activation(out=qT[hi].t[:, :], in_=ps.t[:, :], func=AF.Copy, scale=scale))
                elif kind == 1:
                    S.op("act", [ps], [kT[hi]], lambda e: e.activation(out=kT[hi].t[:, t0:t0 + T], in_=ps.t[:, :], func=AF.Copy))
                else:
                    v = vf.next()
                    S.op("act", [ps], [v], lambda e: e.activation(out=v.t[:, :], in_=ps.t[:, :], func=AF.Copy))
                    for j in range(T // 128):
                        pt = ptr.next()
                        S.op("pe", [v, ident], [pt], lambda e, j=j, pt=pt: e.transpose(out=pt.t[:, 0:128], in_=v.t[:, j * 128:(j + 1) * 128], identity=ident.t[:]))
                        vt = Vall[t0 // 128 + j]
                        S.op("dve", [pt], [vt], lambda e, pt=pt, vt=vt: e.tensor_copy(out=vt.t[:, hi * 128:(hi + 1) * 128], in_=pt.t[:, 0:128]))
            _linear(S, wring, NW, dr["wA"], aT, T, cols, evac, plin)

            kts = list(range(t0 // 128 + 3, -1, -1))

            def emit_z(kt):
                for h in range(NH):
                    kd = kt - t0 // 128
                    S.op("pe", [kT[h], qT[h]], [pz[h]], lambda e, h=h: e.matmul(pz[h].t[:, :], lhsT=kT[h].t[:, kt * 128:(kt + 1) * 128], rhs=qT[h].t[:, :],
                                                                              start=True, stop=(kd < 0)))
                    if kd >= 0:
                        S.op("pe", [identb, mask], [pz[h]], lambda e, h=h, kd=kd: e.matmul(pz[h].t[:, :], lhsT=identb.t[:], rhs=mask.t[:, kd, :], start=False, stop=True))
            emit_z(kts[0])
            for n, kt in enumerate(kts):
                first = (n == 0)
                last = (n == len(kts) - 1)
                cur = {}
                for h in range(NH):
                    e_ = eT[h].next()
                    sp = spT[h].next()
                    S.op("act", [pz[h]], [e_], lambda e, h=h, e_=e_: e.activation(out=e_.t[:, :], in_=pz[h].t[:, :], func=AF.Exp))
                    S.op("act", [e_], [sp], lambda e, e_=e_, sp=sp: e.activation(out=sp.t[:, :], in_=e_.t[:, :], func=AF.Ln, bias=1.0))
                    cur[h] = (e_, sp)
                for h in range(NH):
                    e_, sp = cur[h]
                    S.op("pe", [negU, sp], [pR[h]], lambda e, h=h, sp=sp: e.matmul(pR[h].t[:, :], lhsT=negU.t[:], rhs=sp.t[:, :], start=True, stop=True))
                    if not last:
                        S.op("pe", [negO, sp], [pC[h]], lambda e, h=h, sp=sp: e.matmul(pC[h].t[:, :], lhsT=negO.t[:], rhs=sp.t[:, :], start=True, stop=True))
                if not last:
                    emit_z(kts[n + 1])
                for h in range(NH):
                    e_, sp = cur[h]
                    x_ = xT_[h].next()
                    if first:
                        S.op("act", [pR[h]], [x_], lambda e, h=h, x_=x_: e.activation(out=x_.t[:, :], in_=pR[h].t[:, :], func=AF.Exp))
                        if not last:
                            S.op("dve", [pC[h]], [carry[h]], lambda e, h=h: e.tensor_copy(out=carry[h].t[:, :], in_=pC[h].t[:, :]))
                    else:
                        tm = tmpT[h].next()
                        S.op("dve", [pR[h], carry[h]], [tm], lambda e, h=h, tm=tm: e.tensor_tensor(out=tm.t[:, :], in0=pR[h].t[:, :], in1=carry[h].t[:, :], op=ALU.add))
                        S.op("act", [tm], [x_], lambda e, tm=tm, x_=x_: e.activation(out=x_.t[:, :], in_=tm.t[:, :], func=AF.Exp))
                        if not last:
                            S.op("dve", [pC[h], carry[h]], [carry[h]], lambda e, h=h: e.tensor_tensor(out=carry[h].t[:, :], in0=pC[h].t[:, :], in1=carry[h].t[:, :], op=ALU.add))
                    w_ = wT[h].next()
                    S.op("pool", [e_, x_], [w_], lambda e, e_=e_, x_=x_, w_=w_: e.tensor_tensor(out=w_.t[:, :], in0=e_.t[:, :], in1=x_.t[:, :], op=ALU.mult))
                    vt = Vall[kt]
                    S.op("pe", [vt, w_], [pO[h]], lambda e, h=h, vt=vt, w_=w_: e.matmul(pO[h].t[:, :], lhsT=vt.t[:, h * 128:(h + 1) * 128], rhs=w_.t[:, :],
                                                                                       start=first, stop=last))
            for h in range(NH):
                o_ = oS.next()
                S.op("act", [pO[h]], [o_], lambda e, h=h, o_=o_: e.activation(out=o_.t[:, :], in_=pO[h].t[:, :], func=AF.Copy))
                r0 = orow0 + heads[h] * 128
                _ores, _oap = _oput(dr, r0, t0, T)
                S.dma("sp", [o_], [_ores], out=_oap, in_=o_.t[:, :])
        S.barrier()
    S.es = oes


def emit_hgrn(S, c, dr, heads, col_q, col_f, col_i, col_g, orow0):
    D, SEQ = c["D"], c["SEQ"]
    KC = D // 128
    T = 512
    C = 64
    NH = len(heads)
    NW = c.get("NW", 4)
    oes = S.es
    S.new_phase()
    with ExitStack() as pes:
        S.es = pes
        _, aT = views(S, "aT", [128, KC, T], BF16, KC, lambda t, i: t[:, i, :])
        wring = make_wring(S, "wsl", NW)
        ident = S.sbuf("ident", [128, 128], F32)
        identb = S.sbuf("identb", [128, 128], BF16)
        maskU = S.sbuf("maskU", [64, 64], F32)
        rmask = S.sbuf("rmask", [128, T], F32)
        NWt = S.sbuf("NWt", [128, 128], F32)
        epsR = S.sbuf("epsR", [128, 1], F32)
        lbr = S.sbuf("lbr", [128, 2, NH], F32)
        lb = S.sbuf("lb", [128, NH], F32)
        oml = S.sbuf("oml", [128, NH], F32)
        noml = S.sbuf("noml", [128, NH], F32)
        qd = [S.sbuf(f"qd{h}", [128, T], BF16) for h in range(NH)]
        kd = [S.sbuf(f"kd{h}", [128, T], BF16) for h in range(NH)]
        iTb = [S.sbuf(f"iTb{h}", [128, T], BF16) for h in range(NH)]
        gf = [S.sbuf(f"gf{h}", [128, T], F32) for h in range(NH)]
        eb = [S.sbuf(f"eb{h}", [128, T], F32) for h in range(NH)]
        qf = [S.sbuf(f"qf{h}", [128, T], F32) for h in range(NH)]
        sg = [S.sbuf(f"sg{h}", [128, T], F32) for h in range(NH)]
        oS = [S.sbuf(f"oS{h}", [128, T], BF16) for h in range(NH)]
        state = [S.sbuf(f"state{h}", [128, 128], F32) for h in range(NH)]
        stateb = [S.sbuf(f"stateb{h}", [128, 128], BF16) for h in range(NH)]
        tmp = Ring([S.sbuf(f"tmp{i}", [128, T], F32) for i in range(6)])
        kvt = Ring([S.sbuf(f"kvt{i}", [64, 256], BF16) for i in range(3)])
        gate = Ring([S.sbuf(f"gate{i}", [64, 128], F32) for i in range(3)])
        scb = Ring([S.sbuf(f"scb{i}", [64, 64], BF16) for i in range(3)])
        junk = Ring([S.sbuf(f"junk{i}", [64, 128], F32) for i in range(2)])
        ss = Ring([S.sbuf(f"ss{i}", [64, 4], F32) for i in range(4)])
        t1r = Ring([S.sbuf(f"t1r{i}", [64, 128], F32) for i in range(3)])
        t2r = Ring([S.sbuf(f"t2r{i}", [64, 128], BF16) for i in range(3)])
        st1 = Ring([S.sbuf(f"st1{i}", [128, 128], F32) for i in range(2)])
        pb = [S.psum(f"pb{i}", [128, 512], F32) for i in range(8)]
        plin = Ring(pb[0:4])
        ptr = Ring(pb[2:4])
        pwk = Ring(pb[4:8])

        S.dma("sp", [dr["ident"]], [ident], out=ident.t[:], in_=dr["ident"].t)
        S.dma("sp", [dr["identb"]], [identb], out=identb.t[:], in_=dr["identb"].t)
        S.dma("sp", [dr["maskU"]], [maskU], out=maskU.t[:], in_=dr["maskU"].t)
        S.dma("sp", [dr["rmask"]], [rmask], out=rmask.t[:], in_=dr["rmask"].t)
        S.dma("sp", [dr["hg_nw"]], [NWt], out=NWt.t[:], in_=dr["hg_nw"].t)
        S.dma("sp", [dr["hg_lb"]], [lbr], out=lbr.t[:], in_=dr["hg_lb"].t)
        S.op("dve", [], [epsR], lambda e: e.memset(epsR.t[:], 1e-6))
        S.op("dve", [lbr], [lb], lambda e: e.tensor_tensor(out=lb.t[:], in0=lbr.t[:, 0, :], in1=lbr.t[:, 1, :], op=ALU.subtract))
        S.op("act", [lb], [lb], lambda e: e.activation(out=lb.t[:], in_=lb.t[:], func=AF.Sigmoid))
        S.op("dve", [lb], [oml], lambda e: e.tensor_scalar(out=oml.t[:], in0=lb.t[:], scalar1=-1.0, scalar2=1.0, op0=ALU.mult, op1=ALU.add))
        S.op("dve", [oml], [noml], lambda e: e.tensor_scalar(out=noml.t[:], in0=oml.t[:], scalar1=-1.0, scalar2=None, op0=ALU.mult))
        for h in range(NH):
            S.op("dve", [], [state[h]], lambda e, h=h: e.memset(state[h].t[:], 0.0))
            S.op("dve", [], [stateb[h]], lambda e, h=h: e.memset(stateb[h].t[:], 0.0))

        for tb in range(SEQ // T):
            t0 = tb * T
            for k in range(KC):
                S.dma("pool", [dr["xT"]], [aT[k]], out=aT[k].t[:, :], in_=dr["xT"].t[k * 128:(k + 1) * 128, t0:t0 + T])
            cols = []
            hs_ = c.get("hstride", 128)
            for h in heads:
                cols += [col_q + h * hs_, col_f + h * hs_, col_i + h * hs_, col_g + h * hs_]

            def evac(i, ps):
                hi, kind = divmod(i, 4)
                if kind == 0:
                    S.op("act", [ps], [qf[hi]], lambda e: e.activation(out=qf[hi].t[:, :], in_=ps.t[:, :], func=AF.Copy))
                elif kind == 1:
                    S.op("act", [ps], [sg[hi]], lambda e: e.activation(out=sg[hi].t[:, :], in_=ps.t[:, :], func=AF.Sigmoid))
                elif kind == 2:
                    S.op("act", [ps], [iTb[hi]], lambda e: e.activation(out=iTb[hi].t[:, :], in_=ps.t[:, :], func=AF.Copy))
                else:
                    S.op("dve", [ps], [gf[hi]], lambda e: e.tensor_copy(out=gf[hi].t[:, :], in_=ps.t[:, :]))
            _linear(S, wring, NW, dr["wA"], aT, T, cols, evac, plin)

            for h in range(NH):
                f_ = tmp.next()
                S.op("dve", [sg[h], oml, lb], [f_], lambda e: e.tensor_scalar(out=f_.t[:, :], in0=sg[h].t[:, :], scalar1=oml.t[:, h:h + 1], scalar2=lb.t[:, h:h + 1],
                                                                               op0=ALU.mult, op1=ALU.add))
                lf = tmp.next()
                S.op("act", [f_], [lf], lambda e: e.activation(out=lf.t[:, :], in_=f_.t[:, :], func=AF.Ln))
                k_ = tmp.next()
                S.op("dve", [sg[h], noml, oml], [k_], lambda e: e.tensor_scalar(out=k_.t[:, :], in0=sg[h].t[:, :], scalar1=noml.t[:, h:h + 1], scalar2=oml.t[:, h:h + 1],
                                                                                 op0=ALU.mult, op1=ALU.add))
                b_ = tmp.next()
                S.op("dve", [rmask, lf], [b_], lambda e: e.tensor_tensor_scan(out=b_.t[:, :], data0=rmask.t[:, :], data1=lf.t[:, :], initial=0.0, op0=ALU.mult, op1=ALU.add))
                S.op("act", [b_], [eb[h]], lambda e: e.activation(out=eb[h].t[:, :], in_=b_.t[:, :], func=AF.Exp))
                enb = tmp.next()
                S.op("act", [b_], [enb], lambda e: e.activation(out=enb.t[:, :], in_=b_.t[:, :], func=AF.Exp, scale=-1.0))
                S.op("dve", [qf[h], eb[h]], [qd[h]], lambda e: e.tensor_tensor(out=qd[h].t[:, :], in0=qf[h].t[:, :], in1=eb[h].t[:, :], op=ALU.mult))
                S.op("pool", [k_, enb], [kd[h]], lambda e: e.tensor_tensor(out=kd[h].t[:, :], in0=k_.t[:, :], in1=enb.t[:, :], op=ALU.mult))

            for cc in range(T // C):
                cs = slice(cc * C, (cc + 1) * C)
                for h in range(NH):
                    TR = ptr.next()
                    P = pwk.next()
                    S.op("pe", [kd[h], identb], [TR], lambda e: e.matmul(TR.t[0:64, 0:128], lhsT=kd[h].t[:, cs], rhs=identb.t[:], start=True, stop=True))
                    S.op("pe", [iTb[h], identb], [TR], lambda e: e.matmul(TR.t[0:64, 128:256], lhsT=iTb[h].t[:, cs], rhs=identb.t[:], start=True, stop=True))
                    S.op("pe", [gf[h], ident], [TR], lambda e: e.transpose(out=TR.t[0:64, 256:384], in_=gf[h].t[:, cs], identity=ident.t[:]))
                    kv_ = kvt.next()
                    S.op("dve", [TR], [kv_], lambda e: e.tensor_copy(out=kv_.t[:, :], in_=TR.t[0:64, 0:256]))
                    g_ = gate.next()
                    S.op("act", [TR], [g_], lambda e: e.activation(out=g_.t[:, :], in_=TR.t[0:64, 256:384], func=AF.Silu))
                    S.op("pe", [kd[h], qd[h]], [P], lambda e: e.matmul(P.t[0:64, 0:64], lhsT=kd[h].t[:, cs], rhs=qd[h].t[:, cs], start=True, stop=True))
                    sc_ = scb.next()
                    S.op("dve", [P, maskU], [sc_], lambda e: e.tensor_tensor(out=sc_.t[:, :], in0=P.t[0:64, 0:64], in1=maskU.t[:, :], op=ALU.mult))
                    S.op("pe", [sc_, kv_], [P], lambda e: e.matmul(P.t[0:64, 64:192], lhsT=sc_.t[:, :], rhs=kv_.t[:, 128:256], start=True, stop=False))
                    S.op("pe", [qd[h], stateb[h]], [P], lambda e: e.matmul(P.t[0:64, 64:192], lhsT=qd[h].t[:, cs], rhs=stateb[h].t[:, :], start=False, stop=True))
                    S.op("pe", [kv_], [P], lambda e: e.matmul(P.t[:, 256:384], lhsT=kv_.t[:, 0:128], rhs=kv_.t[:, 128:256], start=True, stop=True))
                    s1 = st1.next()
                    S.op("dve", [state[h], P], [s1], lambda e: e.tensor_tensor(out=s1.t[:, :], in0=state[h].t[:, :], in1=P.t[:, 256:384], op=ALU.add))
                    S.op("dve", [s1, eb[h]], [state[h]], lambda e: e.tensor_scalar(out=state[h].t[:, :], in0=s1.t[:, :], scalar1=eb[h].t[:, cc * C + C - 1:cc * C + C], scalar2=None,
                                                                                  op0=ALU.mult))
                    S.op("act", [state[h]], [stateb[h]], lambda e: e.activation(out=stateb[h].t[:, :], in_=state[h].t[:, :], func=AF.Copy))
                    jk = junk.next()
                    s_ = ss.next()
                    S.op("act", [P], [jk, s_], lambda e: e.activation(out=jk.t[:, :], in_=P.t[0:64, 64:192], func=AF.Square, accum_out=s_.t[:, 0:1]))
                    S.op("act", [s_, epsR], [s_], lambda e: e.activation(out=s_.t[:, 1:2], in_=s_.t[:, 0:1], func=AF.Sqrt, scale=1.0 / 128.0, bias=epsR.t[0:64, 0:1]))
                    S.op("dve", [s_], [s_], lambda e: e.reciprocal(out=s_.t[:, 2:3], in_=s_.t[:, 1:2]))
                    t1 = t1r.next()
                    S.op("dve", [P, s_, NWt], [t1], lambda e: e.scalar_tensor_tensor(out=t1.t[:, :], in0=P.t[0:64, 64:192], scalar=s_.t[:, 2:3], in1=NWt.t[0:64, :],
                                                                                   op0=ALU.mult, op1=ALU.mult))
                    t2 = t2r.next()
                    S.op("pool", [t1, g_], [t2], lambda e: e.tensor_tensor(out=t2.t[:, :], in0=t1.t[:, :], in1=g_.t[:, :], op=ALU.mult))
                    S.op("pe", [t2, identb], [P], lambda e: e.matmul(P.t[:, 384:448], lhsT=t2.t[:, :], rhs=identb.t[0:64, 0:64], start=True, stop=True))
                    S.op("act", [P], [oS[h]], lambda e: e.activation(out=oS[h].t[:, cs], in_=P.t[:, 384:448], func=AF.Copy))
            for h in range(NH):
                r0 = orow0 + h * 128
                _ores, _oap = _oput(dr, r0, t0, T)
                S.dma("sp", [oS[h]], [_ores], out=_oap, in_=oS[h].t[:, :])
        S.barrier()
    S.es = oes


NEGB = 30000.0


def emit_nsa(S, c, dr):
    D, SEQ = c["D"], c["SEQ"]
    KC = D // 128
    T = 512
    R = 8
    NW = c.get("NW", 3)
    NT = SEQ // 128
    NB = SEQ // T
    NMT = (SEQ // 16 + 127) // 128
    NSLOT = NMT * 128
    scale = 128.0 ** -0.5
    oes = S.es
    S.new_phase()
    with ExitStack() as pes:
        S.es = pes
        _, aT = views(S, "aT", [128, KC, T], BF16, KC, lambda t, i: t[:, i, :])
        wring = make_wring(S, "wsl", NW)
        ident = S.sbuf("ident", [128, 128], F32)
        identb = S.sbuf("identb", [128, 128], BF16)
        ones_b = S.sbuf("ones_b", [128, 128], BF16)
        rotm = S.sbuf("rotm", [128, 128], BF16)
        ksT = S.sbuf("ksT", [128, SEQ], BF16)
        _, vsT = views(S, "vsT", [128, NT, 128], BF16, NT, lambda t, i: t[:, i, :])
        kwT = S.sbuf("kwT", [128, 1024], BF16)
        _, vwT = views(S, "vwT", [128, 8, 128], BF16, 8, lambda t, i: t[:, i, :])
        kcmpT = S.sbuf("kcmpT", [128, NSLOT], BF16)
        vcmpT = S.sbuf("vcmpT", [128, NSLOT], F32)
        _, vcmp = views(S, "vcmp", [128, NMT, 128], BF16, NMT, lambda t, i: t[:, i, :])
        kcb = S.sbuf("kcb", [128, 16 + T], BF16)
        vcb = S.sbuf("vcb", [128, 16 + T], BF16)
        w2 = S.sbuf("w2", [128, 2, 128], BF16)
        posT = S.sbuf("posT", [128, 2, 32], BF16)
        posb = S.sbuf("posb", [128, 2], F32)
        ov = S.sbuf("ov", [128, NMT, 128], BF16)
        ex32 = S.sbuf("ex32", [32, 16, 128], BF16)
        wmask = S.sbuf("wmask", [128, 8, T], BF16)
        cmask = S.sbuf("cmask", [128, 5, T], BF16)
        ABt = S.sbuf("ABt", [128, 256], F32)
        FBt = S.sbuf("FBt", [128, 256], F32)
        qn = [S.sbuf(f"qn{r}", [128, T], BF16) for r in range(R)]
        qr = [S.sbuf(f"qr{r}", [128, T], BF16) for r in range(R)]
        cosb = S.sbuf("cosb", [128, T], F32)
        sinb = S.sbuf("sinb", [128, T], F32)
        gT = S.sbuf("gT", [32, T], F32)
        gtok = S.sbuf("gtok", [128, 4, 24], F32)
        _, otok = views(S, "otok", [128, 4, R * 128], F32, 4, lambda t, i: t[:, i, :])
        selT = [[S.sbuf(f"selT{g}_{i}", [32, T], BF16) for g in range(4)] for i in range(1)][0]
        tf = Ring([S.sbuf(f"tf{i}", [128, T], F32) for i in range(4)])
        tb16 = Ring([S.sbuf(f"tb16{i}", [128, T], BF16) for i in range(3)])
        Er = Ring([S.sbuf(f"Er{i}", [128, T], BF16) for i in range(6)])
        Pr = Ring([S.sbuf(f"Pr{i}", [128, T], BF16) for i in range(3)])
        hid = Ring([S.sbuf(f"hid{i}", [128, 32], BF16) for i in range(2)])
        sm = Ring([S.sbuf(f"sm{i}", [128, 128], F32) for i in range(4)])
        smb = Ring([S.sbuf(f"smb{i}", [128, 128], BF16) for i in range(2)])
        m8 = Ring([S.sbuf(f"m8{i}", [128, 16], F32) for i in range(2)])
        fac = Ring([S.sbuf(f"fac{i}", [128, 8], F32) for i in range(4)])
        oS = Ring([S.sbuf(f"oS{i}", [128, T], BF16) for i in range(2)])
        pb = [S.psum(f"pb{i}", [128, 512], F32) for i in range(8)]
        plin = Ring([pb[0], pb[1], pb[2], pb[3]])
        pS = Ring(pb[2:4])
        pO = Ring([pb[4], pb[7]])
        pD = pb[5]
        pI = pb[6]
        pmisc = Ring([pb[4], pb[7], pb[5]])

        def ld(q, name, dst, src=None):
            S.dma(q, [dr[name]], [dst], out=dst.t[:], in_=(dr[name].t if src is None else src))
        ld("sp", "ident", ident); ld("sp", "identb", identb); ld("sp", "rotm", rotm)
        ld("pool", "cmp_w2", w2); ld("pool", "cmp_posT", posT)
        ld("sp", "ov", ov); ld("sp", "ex32", ex32); ld("sp", "wmask", wmask); ld("sp", "cmask", cmask)
        ld("sp", "ABt", ABt); ld("sp", "FBt", FBt)
        S.op("dve", [], [ones_b], lambda e: e.memset(ones_b.t[:], 1.0))
        S.op("dve", [], [kcb], lambda e: e.memset(kcb.t[:], 0.0))
        S.op("dve", [], [vcb], lambda e: e.memset(vcb.t[:], 0.0))
        S.op("dve", [], [vcmpT], lambda e: e.memset(vcmpT.t[:], 0.0))
        S.op("dve", [], [kcmpT], lambda e: e.memset(kcmpT.t[:], 0.0))

        def load_w1(j):
            slot = wring.next()
            v = slot.t[:, :].rearrange("p (k c) -> p k c", c=128)
            S.dma("pool", [dr["cmp_w1"]], [slot], out=v, in_=dr["cmp_w1"].t[j].rearrange("l d e -> d l e"))
            return slot, v
        for j in range(2):
            slot, sv = load_w1(j)
            ps = pmisc.next()
            for l in range(32):
                S.op("pe", [slot, posT], [ps], lambda e, l=l: e.matmul(ps.t[:, 0:1], lhsT=sv[:, l, :], rhs=posT.t[:, j, l:l + 1], start=(l == 0), stop=(l == 31)))
            S.op("act", [ps], [posb], lambda e: e.activation(out=posb.t[:, j:j + 1], in_=ps.t[:, 0:1], func=AF.Copy))

        CQ, CKC, CVC, CKS, CVS, CKW, CVW, CG = 0, 1024, 1152, 1280, 1408, 1536, 1664, 1792

        def rope(src_f32, dst_bf, sc):
            sb = tb16.next()
            S.op("act", [src_f32], [sb], lambda e: e.activation(out=sb.t[:, :], in_=src_f32.t[:, :], func=AF.Copy))
            ps = pmisc.next()
            S.op("pe", [rotm, sb], [ps], lambda e: e.matmul(ps.t[:, :], lhsT=rotm.t[:], rhs=sb.t[:, :], start=True, stop=True))
            t1 = tf.next()
            S.op("dve", [src_f32, cosb], [t1], lambda e: e.tensor_tensor(out=t1.t[:, :], in0=src_f32.t[:, :], in1=cosb.t[:, :], op=ALU.mult))
            t2 = tf.next()
            S.op("dve", [ps, sinb], [t2], lambda e: e.tensor_tensor(out=t2.t[:, :], in0=ps.t[:, :], in1=sinb.t[:, :], op=ALU.mult))
            S.op("pool", [t1, t2], [dst_bf[0]], lambda e: e.tensor_tensor(out=dst_bf[1], in0=t1.t[:, :], in1=t2.t[:, :], op=ALU.add))

        def to_tok(src_f32, dst_views, base):
            for j in range(T // 128):
                pt = pmisc.next()
                S.op("pe", [src_f32, ident], [pt], lambda e, j=j, pt=pt: e.transpose(out=pt.t[:, 0:128], in_=src_f32.t[:, j * 128:(j + 1) * 128], identity=ident.t[:]))
                dv = dst_views[base + j]
                S.op("dve", [pt], [dv], lambda e, pt=pt, dv=dv: e.tensor_copy(out=dv.t[:, :], in_=pt.t[:, 0:128]))

        for tb in range(NB):
            t0 = tb * T
            gt0 = t0 // 128
            for k in range(KC):
                if "hAG_get" in dr:
                    _hres, _hap = dr["hAG_get"](k, t0, T)
                    S.dma("sp", [_hres], [aT[k]], out=aT[k].t[:, :], in_=_hap)
                else:
                    S.dma("sp", [dr["hT"]], [aT[k]], out=aT[k].t[:, :], in_=dr["hT"].t[k * 128:(k + 1) * 128, t0:t0 + T])
            S.dma("sp", [dr["cosT"]], [cosb], out=cosb.t[:, :], in_=dr["cosT"].t[:, t0:t0 + T])
            S.dma("sp", [dr["sinT"]], [sinb], out=sinb.t[:, :], in_=dr["sinT"].t[:, t0:t0 + T])
            if tb > 0:
                S.op("dve", [kcb], [kcb], lambda e: e.tensor_copy(out=kcb.t[:, 0:16], in_=kcb.t[:, T:T + 16]))
                S.op("dve", [vcb], [vcb], lambda e: e.tensor_copy(out=vcb.t[:, 0:16], in_=vcb.t[:, T:T + 16]))
            cols = [CQ + r * 128 for r in range(R)] + [CKC, CVC, CKS, CVS, CKW, CVW]

            def evac(i, ps):
                if i < R:
                    qf = tf.next()
                    S.op("act", [ps], [qf], lambda e: e.activation(out=qf.t[:, :], in_=ps.t[:, :], func=AF.Copy, scale=scale))
                    S.op("dve", [qf], [qn[i]], lambda e: e.tensor_copy(out=qn[i].t[:, :], in_=qf.t[:, :]))
                    rope(qf, (qr[i], qr[i].t[:, :]), 1.0)
                elif i == R:
                    S.op("act", [ps], [kcb], lambda e: e.activation(out=kcb.t[:, 16:16 + T], in_=ps.t[:, :], func=AF.Copy))
                elif i == R + 1:
                    S.op("act", [ps], [vcb], lambda e: e.activation(out=vcb.t[:, 16:16 + T], in_=ps.t[:, :], func=AF.Copy))
                elif i == R + 2:
                    kf = tf.next()
                    S.op("act", [ps], [kf], lambda e: e.activation(out=kf.t[:, :], in_=ps.t[:, :], func=AF.Copy))
                    rope(kf, (ksT, ksT.t[:, t0:t0 + T]), 1.0)
                elif i == R + 3:
                    vf = tf.next()
                    S.op("act", [ps], [vf], lambda e: e.activation(out=vf.t[:, :], in_=ps.t[:, :], func=AF.Copy))
                    to_tok(vf, vsT, gt0)
                elif i == R + 4:
                    kf = tf.next()
                    S.op("act", [ps], [kf], lambda e: e.activation(out=kf.t[:, :], in_=ps.t[:, :], func=AF.Copy))
                    c0 = (tb % 2) * T
                    rope(kf, (kwT, kwT.t[:, c0:c0 + T]), 1.0)
                else:
                    vf = tf.next()
                    S.op("act", [ps], [vf], lambda e: e.activation(out=vf.t[:, :], in_=ps.t[:, :], func=AF.Copy))
                    to_tok(vf, vwT, (gt0 % 8))
            _linear(S, wring, NW, dr["wC"], aT, T, cols, evac, plin)
            slot = wring.next()
            gv = slot.t[:, :].rearrange("p (k c) -> p k c", c=128)[:, 0:KC, 0:24]
            S.dma("pool", [dr["wC"]], [slot], out=gv, in_=dr["wC"].t[0:KC * 128, CG:CG + 24].rearrange("(kc p) c -> p kc c", p=128))
            ps = plin.next()
            for k in range(KC):
                S.op("pe", [slot, aT[k]], [ps], lambda e, k=k: e.matmul(ps.t[0:24, :], lhsT=gv[:, k, 0:24], rhs=aT[k].t[:, :], start=(k == 0), stop=(k == KC - 1)))
            S.op("act", [ps], [gT], lambda e: e.activation(out=gT.t[0:24, :], in_=ps.t[0:24, :], func=AF.Sigmoid))
            for j in range(4):
                pt = pmisc.next()
                S.op("pe", [gT, ident], [pt], lambda e, j=j, pt=pt: e.transpose(out=pt.t[:, 0:24], in_=gT.t[0:24, j * 128:(j + 1) * 128], identity=ident.t[0:24, 0:24]))
                S.op("dve", [pt], [gtok], lambda e, j=j, pt=pt: e.tensor_copy(out=gtok.t[:, j, :], in_=pt.t[:, 0:24]))

            for j, (buf, dstT) in enumerate(((kcb, kcmpT), (vcb, vcmpT))):
                slot, sv = load_w1(j)
                ps = pmisc.next()
                for l in range(32):
                    S.op("pe", [slot, buf], [ps], lambda e, l=l: e.matmul(ps.t[:, 0:32], lhsT=sv[:, l, :], rhs=buf.t[:, l:l + 16 * 31 + 1:16], start=(l == 0), stop=(l == 31)))
                hd = hid.next()
                S.op("act", [ps, posb], [hd], lambda e: e.activation(out=hd.t[:, :], in_=ps.t[:, 0:32], func=AF.Gelu, bias=posb.t[:, j:j + 1]))
                ps2 = pmisc.next()
                S.op("pe", [w2, hd], [ps2], lambda e: e.matmul(ps2.t[:, 0:32], lhsT=w2.t[:, j, :], rhs=hd.t[:, :], start=True, stop=True))
                S.op("act", [ps2], [dstT], lambda e: e.activation(out=dstT.t[:, 32 * tb:32 * tb + 32], in_=ps2.t[:, 0:32], func=AF.Copy))
            mtc = (32 * tb) // 128
            pt = pmisc.next()
            S.op("pe", [vcmpT, ident], [pt], lambda e: e.transpose(out=pt.t[:, 0:128], in_=vcmpT.t[:, mtc * 128:(mtc + 1) * 128], identity=ident.t[:]))
            S.op("dve", [pt], [vcmp[mtc]], lambda e: e.tensor_copy(out=vcmp[mtc].t[:, :], in_=pt.t[:, 0:128]))

            nmt = mtc + 1
            for r in range(R):
                Es = []
                for mt in range(nmt):
                    v = tb - 4 * mt
                    ps = pS.next()
                    mks = ([v] if v < 4 else []) + ([4] if mt == 0 else [])
                    S.op("pe", [kcmpT, qn[r]], [ps], lambda e: e.matmul(ps.t[:, :], lhsT=kcmpT.t[:, mt * 128:(mt + 1) * 128], rhs=qn[r].t[:, :], start=True, stop=not mks))
                    for mi, mv in enumerate(mks):
                        S.op("pe", [identb, cmask], [ps], lambda e: e.matmul(ps.t[:, :], lhsT=identb.t[:], rhs=cmask.t[:, mv, :], start=False, stop=(mi == len(mks) - 1)))
                    E = Er.next()
                    S.op("act", [ps], [E], lambda e: e.activation(out=E.t[:, :], in_=ps.t[:, :], func=AF.Exp))
                    Es.append(E)
                pden = pS.next()
                for mt, E in enumerate(Es):
                    S.op("pe", [ones_b, E], [pden], lambda e, mt=mt, E=E: e.matmul(pden.t[:, :], lhsT=ones_b.t[:], rhs=E.t[:, :], start=(mt == 0), stop=(mt == nmt - 1)))
                rd = tf.next()
                S.op("dve", [pden], [rd], lambda e: e.tensor_scalar(out=rd.t[:, :], in0=pden.t[:, :], scalar1=1e-30, scalar2=None, op0=ALU.max))
                S.op("dve", [rd], [rd], lambda e: e.reciprocal(out=rd.t[:, :], in_=rd.t[:, :]))
                po = pO.next()
                Ps = []
                for mt, E in enumerate(Es):
                    P = Pr.next()
                    S.op("dve", [E, rd], [P], lambda e, E=E, P=P: e.tensor_tensor(out=P.t[:, :], in0=E.t[:, :], in1=rd.t[:, :], op=ALU.mult))
                    for tt in range(4):
                        S.op("pe", [P, vcmp[mt]], [po], lambda e, tt=tt, mt=mt, P=P: e.matmul(po.t[:, tt * 128:(tt + 1) * 128], lhsT=P.t[:, tt * 128:(tt + 1) * 128], rhs=vcmp[mt].t[:, :],
                                                                                           start=(mt == 0 and tt == 0), stop=(mt == nmt - 1), skip_group_check=True))
                        S.op("pe", [P, ov], [pI], lambda e, tt=tt, mt=mt, P=P: e.matmul(pI.t[:, tt * 128:(tt + 1) * 128], lhsT=P.t[:, tt * 128:(tt + 1) * 128], rhs=ov.t[:, mt, :],
                                                                                     start=(r == 0 and mt == 0 and tt == 0), stop=(r == R - 1 and mt == nmt - 1), skip_group_check=True))
                for tt in range(4):
                    S.op("dve", [po, gtok], [otok[tt]], lambda e, tt=tt: e.tensor_scalar(out=otok[tt].t[:, r * 128:(r + 1) * 128], in0=po.t[:, tt * 128:(tt + 1) * 128],
                                                                                      scalar1=gtok.t[:, tt, 3 * r:3 * r + 1], scalar2=None, op0=ALU.mult))

            for tt in range(4):
                gt = gt0 + tt
                x0 = 128 - 2 * gt
                s1 = sm.next()
                S.op("dve", [pI, ABt], [s1], lambda e: e.scalar_tensor_tensor(out=s1.t[:, :], in0=pI.t[:, tt * 128:(tt + 1) * 128], scalar=1.0, in1=ABt.t[:, x0:x0 + 128],
                                                                             op0=ALU.add, op1=ALU.mult))
                s2 = sm.next()
                S.op("dve", [s1, FBt], [s2], lambda e: e.scalar_tensor_tensor(out=s2.t[:, :], in0=s1.t[:, :], scalar=-1.0, in1=FBt.t[:, x0:x0 + 128], op0=ALU.add, op1=ALU.max))
                S.op("dve", [s2], [s2], lambda e: e.memset(s2.t[:, 0:1], 1e9))
                mm = m8.next()
                S.op("dve", [s2], [mm], lambda e: e.max(out=mm.t[:, 0:8], in_=s2.t[:, :]))
                s3 = sm.next()
                S.op("dve", [s2, mm], [s3], lambda e: e.match_replace(out=s3.t[:, :], in_to_replace=mm.t[:, 0:8], in_values=s2.t[:, :], imm_value=-1e30))
                S.op("dve", [s3], [mm], lambda e: e.max(out=mm.t[:, 8:16], in_=s3.t[:, :]))
                sb_ = smb.next()
                S.op("dve", [s2, mm], [sb_], lambda e: e.tensor_scalar(out=sb_.t[:, :], in0=s2.t[:, :], scalar1=mm.t[:, 15:16], scalar2=1.0, op0=ALU.is_ge, op1=ALU.subtract))
                for g in range(4):
                    pt = pmisc.next()
                    S.op("pe", [sb_, identb], [pt], lambda e, g=g, pt=pt: e.matmul(pt.t[0:32, 0:128], lhsT=sb_.t[:, 32 * g:32 * g + 32], rhs=identb.t[:], start=True, stop=True))
                    S.op("act", [pt], [selT[g]], lambda e, g=g, pt=pt: e.activation(out=selT[g].t[:, tt * 128:(tt + 1) * 128], in_=pt.t[0:32, 0:128], func=AF.Copy))

            def attend(r, kTsrc, kcol, vviews, vidx, kts, use_sel, gidx):
                po = pO.next()
                n = len(kts)
                for ni, kt in enumerate(kts):
                    rel = kt - gt0
                    ps = pS.next()
                    has_w = (not use_sel and True) or (rel >= 0)
                    S.op("pe", [kTsrc, qr[r]], [ps], lambda e: e.matmul(ps.t[:, :], lhsT=kTsrc.t[:, kcol(kt):kcol(kt) + 128], rhs=qr[r].t[:, :], start=True,
                                                                       stop=not (use_sel or has_w)))
                    if use_sel:
                        g = kt // 16
                        S.op("pe", [ex32, selT[g]], [ps], lambda e: e.matmul(ps.t[:, :], lhsT=ex32.t[:, kt % 16, :], rhs=selT[g].t[:, :], start=False, stop=not has_w))
                    if has_w:
                        S.op("pe", [identb, wmask], [ps], lambda e: e.matmul(ps.t[:, :], lhsT=identb.t[:], rhs=wmask.t[:, rel + 4, :], start=False, stop=True))
                    E = Er.next()
                    S.op("act", [ps], [E], lambda e: e.activation(out=E.t[:, :], in_=ps.t[:, :], func=AF.Exp))
                    vv = vviews[vidx(kt)]
                    for tt in range(4):
                        S.op("pe", [E, vv], [po], lambda e, tt=tt: e.matmul(po.t[:, tt * 128:(tt + 1) * 128], lhsT=E.t[:, tt * 128:(tt + 1) * 128], rhs=vv.t[:, :],
                                                                          start=(ni == 0 and tt == 0), stop=(ni == n - 1), skip_group_check=True))
                        S.op("pe", [E, ones_b], [pD], lambda e, tt=tt: e.matmul(pD.t[:, (r * 4 + tt) * 2:(r * 4 + tt) * 2 + 1], lhsT=E.t[:, tt * 128:(tt + 1) * 128], rhs=ones_b.t[:, 0:1],
                                                                              start=(ni == 0 and tt == 0), stop=(ni == n - 1), skip_group_check=True))
                fc = fac.next()
                S.op("dve", [pD], [fc], lambda e: e.reciprocal(out=fc.t[:, 0:4], in_=pD.t[:, r * 8:r * 8 + 8:2]))
                for tt in range(4):
                    S.op("dve", [fc, gtok], [fc], lambda e, tt=tt: e.tensor_tensor(out=fc.t[:, 4 + tt:5 + tt], in0=fc.t[:, tt:tt + 1], in1=gtok.t[:, tt, 3 * r + gidx:3 * r + gidx + 1], op=ALU.mult))
                    S.op("dve", [po, fc, otok[tt]], [otok[tt]], lambda e, tt=tt: e.scalar_tensor_tensor(
                        out=otok[tt].t[:, r * 128:(r + 1) * 128], in0=po.t[:, tt * 128:(tt + 1) * 128], scalar=fc.t[:, 4 + tt:5 + tt],
                        in1=otok[tt].t[:, r * 128:(r + 1) * 128], op0=ALU.mult, op1=ALU.add))

            for r in range(R):
                attend(r, ksT, lambda kt: kt * 128, vsT, lambda kt: kt, list(range(0, gt0 + 4)), True, 1)
                attend(r, kwT, lambda kt: (kt * 128) % 1024, vwT, lambda kt: kt % 8, list(range(max(0, gt0 - 4), gt0 + 4)), False, 2)

            for r in range(R):
                pt = pmisc.next()
                for tt in range(4):
                    S.op("pe", [otok[tt], ident], [pt], lambda e, tt=tt: e.transpose(out=pt.t[:, tt * 128:(tt + 1) * 128], in_=otok[tt].t[:, r * 128:(r + 1) * 128], identity=ident.t[:]))
                o_ = oS.next()
                S.op("act", [pt], [o_], lambda e: e.activation(out=o_.t[:, :], in_=pt.t[:, :], func=AF.Copy))
                _ores, _oap = _oput(dr, r * 128, t0, T)
                S.dma("sp", [o_], [_ores], out=_oap, in_=o_.t[:, :])
        S.barrier()
    S.es = oes


BF = ml_dtypes.bfloat16
def make_consts():
    c = {}
    c['ident'] = np.eye(128, dtype=np.float32)
    c['identb'] = np.eye(128).astype(BF)
    j = np.arange(128)[:, None]; s = np.arange(128)[None, :]
    c['negU'] = np.where(j >= s, -1.0, 0.0).astype(BF)
    sl = np.arange(128)[:, None, None]; kd = np.arange(4)[None, :, None]; t = np.arange(512)[None, None, :]
    c['sbmask'] = np.where(128 * kd + sl < t, 0.0, -30000.0).astype(BF)
    return c
def make_consts_hg():
    c = {}
    s = np.arange(64)[:, None]; t = np.arange(64)[None, :]
    c['maskU'] = (s <= t).astype(np.float32)
    c['rmask'] = np.tile((np.arange(512) % 64 != 0).astype(np.float32)[None, :], (128, 1))
    return c
def make_consts_nsa(S):
    c = {}
    rot = np.zeros((128, 128), np.float32)
    for dp in range(64):
        rot[dp + 64, dp] = -1.0
    for dp in range(64, 128):
        rot[dp - 64, dp] = 1.0
    c['rotm'] = rot.astype(BF)
    NMT = (S // 16 + 127) // 128
    m = np.arange(NMT * 128); n = m - 1
    j = np.arange(128)
    ovm = (n[:, None] >= 0) & (16 * n[:, None] < 64 * j[None, :] + 64) & (16 * n[:, None] + 32 > 64 * j[None, :]) & (j[None, :] < S // 64) & (n[:, None] <= (S - 32) // 16)
    c['ov'] = np.ascontiguousarray(ovm.reshape(NMT, 128, 128).transpose(1, 0, 2)).astype(BF)
    i = np.arange(32)[:, None, None]; ktl = np.arange(16)[None, :, None]; s = np.arange(128)[None, None, :]
    c['ex32'] = np.where(i == 2 * ktl + s // 64, 30000.0, 0.0).astype(BF)
    sl = np.arange(128)[:, None, None]; rel = (np.arange(8) - 4)[None, :, None]; tl = np.arange(512)[None, None, :]
    c['wmask'] = np.where((128 * rel + sl <= tl) & (128 * rel + sl > tl - 512), 0.0, -30000.0).astype(BF)
    v = np.arange(4)[None, :, None]
    cm = np.where(16 * sl + 15 <= 512 * v + tl, 0.0, -30000.0)
    cm4 = np.where(sl == 0, -30000.0, 0.0) + 0 * tl
    c['cmask'] = np.concatenate([cm, cm4], axis=1).astype(BF)
    p = np.arange(128)[:, None]; x = np.arange(256)[None, :]
    cc = (p >= 64).astype(np.int64)
    c['ABt'] = ((x - 128) <= cc).astype(np.float32)
    c['FBt'] = np.where(((x - 128) == cc) | ((x - 128) == cc - 1), 1e9, -2.0).astype(np.float32)
    half = 64
    inv = 10000.0 ** (-np.arange(half, dtype=np.float32) / half)
    ang = np.arange(S, dtype=np.float32)[None, :] * np.concatenate([inv, inv])[:, None]
    c['cosT'] = np.cos(ang).astype(np.float32)
    c['sinT'] = np.sin(ang).astype(np.float32)
    return c


from concourse.bass_utils import run_bass_kernel_spmd

D_MODEL = 4096
SEQ = 8192
BATCH = 2
DFF = 11008
NMEM = 256
ALPHA = 4.0 ** 0.25
NCORE = 8
TOKC = 2048
HALO = 128
GROUPS = [[0, 1, 2, 3], [4, 5, 6, 7]]


def _consts_all():
    c = make_consts()
    c.update(make_consts_hg())
    c.update(make_consts_nsa(SEQ))
    return c


def _build_fused():
    nc = bass.Bass("TRN2", target_bir_lowering=False)
    NT = HALO + TOKC
    KC, FC = D_MODEL // 128, DFF // 128
    with ExitStack() as es:
        S = Sched(nc, es)
        dr = {}

        def din(name, shape, dt=F32):
            dr[name] = S.dram(name, shape, dt, kind="ExternalInput")
        din("xT", [D_MODEL, SEQ]); din("xTc", [D_MODEL, NT]); din("memT", [D_MODEL, NMEM])
        din("wA", [D_MODEL, 3584]); din("wC", [D_MODEL, 1816])
        din("ident", [128, 128]); din("identb", [128, 128], BF16); din("negU", [128, 128], BF16); din("sbmask", [128, 4, 512], BF16)
        din("maskU", [64, 64]); din("rmask", [128, 512]); din("hg_nw", [128, 128]); din("hg_lb", [128, 2, 4])
        din("cmp_w1", [2, 32, 128, 128]); din("cmp_w2", [128, 2, 128]); din("cmp_posT", [128, 2, 32])
        din("rotm", [128, 128], BF16)
        din("ov", [128, 4, 128], BF16); din("ex32", [32, 16, 128], BF16); din("wmask", [128, 8, 512], BF16); din("cmask", [128, 5, 512], BF16)
        din("ABt", [128, 256]); din("FBt", [128, 256]); din("cosT", [128, SEQ]); din("sinT", [128, SEQ])
        din("flag", [128, 1]); din("fsel", [128, 4]); din("fhal", [128, 4])
        for l in range(2):
            din(f"w_out{l}", [D_MODEL, D_MODEL]); din(f"w_q{l}", [D_MODEL, 512]); din(f"w_kv{l}", [D_MODEL, 1024]); din(f"w_o{l}", [512, D_MODEL])
            din(f"w_up{l}", [D_MODEL, 2 * DFF]); din(f"w_down{l}", [DFF, D_MODEL])
            din(f"ln_g{l}", [128, 3, KC]); din(f"ln_b{l}", [128, 3, KC]); din(f"conv{l}", [128, 3, FC])
        outT = S.dram("outT", [D_MODEL, TOKC], F32, kind="ExternalOutput")
        NB = SEQ // 512
        oA = [S.dram_cc(f"oA{i}", [1024, 512], BF16) for i in range(NB)]
        oAG0 = [S.dram_cc(f"oAG0_{i}", [4096, 512], BF16) for i in range(NB)]
        oC = [S.dram_cc(f"oC{i}", [1024, 512], BF16) for i in range(NB)]
        oAG1 = [S.dram_cc(f"oAG1_{i}", [4096, 512], BF16) for i in range(NB)]
        h1loc = S.dram_cc("h1loc", [D_MODEL, NT], F32)
        hb = [S.dram_cc(f"hb{i}", [256, TOKC], BF16) for i in range(16)]
        hAG = [S.dram_cc(f"hAG{i}", [4 * 256, TOKC], BF16) for i in range(16)]
        hl = [S.dram_cc(f"hl{i}", [2048, 128], F32) for i in range(2)]
        hlAG = [S.dram_cc(f"hlAG{i}", [4 * 2048, 128], F32) for i in range(2)]

        def put_fn(lst):
            def f(r0, t0, T):
                return lst[t0 // 512], lst[t0 // 512].t[r0:r0 + 128, 0:T]
            return f

        def oget_fn(lst):
            def f(k, cb, T):
                i, off = divmod(cb, 512)
                return lst[i], lst[i].t[k * 128:(k + 1) * 128, off:off + T]
            return f

        def hb_put(k, c0, T):
            return hb[k // 2], hb[k // 2].t[(k % 2) * 128:(k % 2) * 128 + 128, c0:c0 + T]

        def hAG_get(k, t0, T):
            rk, tl = divmod(t0, TOKC)
            r0 = rk * 256 + (k % 2) * 128
            return hAG[k // 2], hAG[k // 2].t[r0:r0 + 128, tl:tl + T]

        def hl_put(k):
            return hl[k // 16], hl[k // 16].t[(k % 16) * 128:(k % 16) * 128 + 128, 0:128]

        def hlAG_get(k, cq):
            r0 = cq * 2048 + (k % 16) * 128
            return hlAG[k // 16], hlAG[k // 16].t[r0:r0 + 128, 0:128]

        cfgA = dict(D=D_MODEL, SEQ=SEQ, NW=4)
        drA = dict(dr); drA["oT_put"] = put_fn(oA)
        emit_sb(S, cfgA, drA, [0, 1], 0, 256, 512, 512)
        emit_sb(S, cfgA, drA, [2, 3], 512, 768, 1024, 512)
        emit_hgrn(S, dict(cfgA, hstride=512), drA, [0, 1, 2, 3], 1536, 1664, 1792, 1920, 0)
        for i in range(NB):
            S.collective("AllGather", GROUPS, oA[i], oAG0[i])
        blocks = [(0, HALO, True)] + [(HALO + i * 512, 512, False) for i in range(TOKC // 512)]
        cfgP = dict(D=D_MODEL, DFF=DFF, XH=4, NM=NMEM, T=512, alpha=ALPHA, eps=1e-5, GF=8, NW=3, HALO=HALO, TOKC=TOKC)

        def post_dr(l):
            d = dict(flag=dr["flag"], fsel=dr["fsel"], fhal=dr["fhal"], memT=dr["memT"])
            for k in ("w_out", "w_q", "w_kv", "w_o", "w_up", "w_down", "ln_g", "ln_b", "conv"):
                d[k] = dr[f"{k}{l}"]
            return d
        drB = post_dr(0)
        drB.update(oAG_get=oget_fn(oAG0), hT_in=dr["xTc"], hT_out=h1loc, hTb_put=hb_put, hl_put=hl_put)
        emit_post(S, dict(cfgP, out_off=0, outb_off=HALO), drB, blocks)
        for i in range(16):
            S.collective("AllGather", GROUPS, hb[i], hAG[i])
        for i in range(2):
            S.collective("AllGather", GROUPS, hl[i], hlAG[i])
        drC = dict(dr); drC["hAG_get"] = hAG_get; drC["oT_put"] = put_fn(oC)
        emit_nsa(S, dict(D=D_MODEL, SEQ=SEQ, NW=3, TOKC=TOKC), drC)
        for i in range(NB):
            S.collective("AllGather", GROUPS, oC[i], oAG1[i])
        drD = post_dr(1)
        drD.update(oAG_get=oget_fn(oAG1), hT_in=h1loc, hlAG_get=hlAG_get, hT_out=outT)
        emit_post(S, dict(cfgP, out_off=HALO), drD, blocks)
        S.wait_all("sp", [outT])
        S.wait_all("pool", [outT])
    return nc


def _arr3(v, n):
    return np.ascontiguousarray(np.asarray(v, np.float32).reshape(3, n, 128).transpose(2, 0, 1))


def kernel(**inp):
    inp = {k: np.asarray(v) for k, v in inp.items()}
    cs = _consts_all()
    cores = list(range(NCORE))
    KC, FC = D_MODEL // 128, DFF // 128
    NT = HALO + TOKC
    xT = [np.ascontiguousarray(inp["x"][b].T) for b in range(BATCH)]
    memT = [np.ascontiguousarray(inp["mem"][b].T) for b in range(BATCH)]
    w_in = inp["ab_w_in"][0]
    AW = 2048
    wo = inp["ab_w_out"][0]
    wo_perm = np.ascontiguousarray(np.concatenate([np.concatenate([wo[512 * j:512 * j + 512], wo[2048 + 512 * j:2048 + 512 * j + 512]], axis=0) for j in range(4)], axis=0))
    wn = inp["nsa_w_in"][0]
    shared = dict(ident=cs["ident"], identb=cs["identb"], negU=cs["negU"], sbmask=cs["sbmask"], maskU=cs["maskU"], rmask=cs["rmask"],
                  hg_nw=np.ascontiguousarray(np.tile(inp["hgrn_norm_w"][0][None, :], (128, 1))),
                  cmp_w1=np.ascontiguousarray(inp["nsa_cmp_w1"][0]), cmp_w2=np.ascontiguousarray(inp["nsa_cmp_w2"][0].transpose(1, 0, 2)),
                  cmp_posT=np.ascontiguousarray(inp["nsa_cmp_pos"][0].transpose(2, 0, 1)),
                  rotm=cs["rotm"], ov=cs["ov"], ex32=cs["ex32"], wmask=cs["wmask"], cmask=cs["cmask"], ABt=cs["ABt"], FBt=cs["FBt"],
                  cosT=cs["cosT"], sinT=cs["sinT"])
    wouts = [wo_perm, np.ascontiguousarray(inp["nsa_w_out"][0])]
    for l in range(2):
        shared[f"w_out{l}"] = wouts[l]
        shared[f"w_q{l}"] = np.ascontiguousarray(inp["xa_w_q"][l]); shared[f"w_kv{l}"] = np.ascontiguousarray(inp["xa_w_kv"][l])
        shared[f"w_o{l}"] = np.ascontiguousarray(inp["xa_w_o"][l]); shared[f"w_up{l}"] = np.ascontiguousarray(inp["ffn_w_up"][l])
        shared[f"w_down{l}"] = np.ascontiguousarray(inp["ffn_w_down"][l])
        shared[f"ln_g{l}"] = _arr3(inp["ln_g"][l], KC); shared[f"ln_b{l}"] = _arr3(inp["ln_b"][l], KC); shared[f"conv{l}"] = _arr3(inp["ffn_conv"][l], FC)
    maps = []
    for c in cores:
        b, j = divmod(c, 4)
        hs = slice(512 * j, 512 * j + 512)
        def hc(seg, h0, n):
            c0 = seg * AW + 512 * j + 128 * h0
            return w_in[:, c0:c0 + 128 * n]
        segs = [hc(4, 0, 2), hc(5, 0, 2), hc(6, 0, 2), hc(4, 2, 2), hc(5, 2, 2), hc(6, 2, 2)]
        for h in range(4):
            segs += [hc(0, h, 1), hc(1, h, 1), hc(2, h, 1), hc(3, h, 1)]
        wA = np.ascontiguousarray(np.concatenate(segs, axis=1))
        lb = np.ascontiguousarray(inp["hgrn_lb"][:, hs].reshape(2, 4, 128).transpose(2, 0, 1))
        g = j
        segc = [wn[:, 1024 * g:1024 * g + 1024]] + [wn[:, 4096 + 512 * i + 128 * g:4096 + 512 * i + 128 * g + 128] for i in range(6)] + \
               [wn[:, 4096 + 3072 + 24 * g:4096 + 3072 + 24 * g + 24]]
        wC = np.ascontiguousarray(np.concatenate(segc, axis=1))
        t0 = j * TOKC
        xTc = np.zeros((D_MODEL, NT), np.float32)
        xTc[:, HALO:] = xT[b][:, t0:t0 + TOKC]
        if j > 0:
            xTc[:, :HALO] = xT[b][:, t0 - HALO:t0]
        fsel = np.zeros((128, 4), np.float32); fsel[:, j] = 1.0
        fhal = np.zeros((128, 4), np.float32)
        if j > 0:
            fhal[:, j - 1] = 1.0
        m = dict(shared)
        m.update(xT=xT[b], xTc=xTc, memT=memT[b], wA=wA, wC=wC, hg_lb=lb, flag=np.full((128, 1), 1.0 if j > 0 else 0.0, np.float32), fsel=fsel, fhal=fhal)
        maps.append(m)
    res = run_bass_kernel_spmd(_build_fused(), maps, core_ids=cores)
    out = np.empty((BATCH, SEQ, D_MODEL), np.float32)
    for c in cores:
        b, j = divmod(c, 4)
        out[b, j * TOKC:(j + 1) * TOKC, :] = res.results[c]["outT"].T
    return out
```

```python
import numpy as np
import ml_dtypes
from contextlib import ExitStack
import concourse.bass as bass
import concourse.mybir as mybir


F32 = mybir.dt.float32
BF16 = mybir.dt.bfloat16
AF = mybir.ActivationFunctionType
ALU = mybir.AluOpType


class Res:
    __slots__ = ("name", "w", "r", "t", "multi", "mw")

    def __init__(self, name, t=None, multi=False):
        self.name = name
        self.w = None
        self.r = {}
        self.t = t
        self.multi = multi
        self.mw = {}

    def __getitem__(self, idx):
        return self.t[idx]


class Sched:
    def __init__(self, nc, es: ExitStack, n_dma_slots=8):
        self.nc = nc
        self.es = es
        self.es0 = es
        self.E = {"pe": nc.tensor, "act": nc.scalar, "dve": nc.vector, "pool": nc.gpsimd, "sp": nc.sync}
        self.sem = {k: es.enter_context(nc.semaphore("prog_" + k)) for k in self.E}
        self.cnt = {k: 0 for k in self.E}
        self.known = {k: {} for k in self.E}
        self.slots = {}
        for q in ("sp", "pool", "act"):
            self.slots[q] = [[es.enter_context(nc.semaphore(f"dq_{q}_{i}")), 0] for i in range(n_dma_slots)]
        self.slot_i = {q: 0 for q in self.slots}
        self.n_ins = 0
        self.n_wait = 0
        self.prefix = ""
        self.nphase = 0

    def new_phase(self):
        self.prefix = f"p{self.nphase}_"
        self.nphase += 1

    def sbuf(self, name, shape, dt):
        t = self.es.enter_context(self.nc.sbuf_tensor("sb_" + self.prefix + name, list(shape), dt))
        return Res(name, t)

    def psum(self, name, shape, dt=F32):
        t = self.es.enter_context(self.nc.psum_tensor("ps_" + self.prefix + name, list(shape), dt))
        return Res(name, t)

    def dram(self, name, shape, dt, kind="Internal", multi=True):
        t = self.nc.dram_tensor(name, list(shape), dt, kind=kind)
        return Res(name, t.ap(), multi=multi)

    def dram_cc(self, name, shape, dt):
        t = self.nc.dram_tensor(name, list(shape), dt)
        return Res(name, t.ap(), multi=True)

    def collective(self, kind, groups, src, dst):
        if not hasattr(self, "cc_sem"):
            self.cc_sem = self.es0.enter_context(self.nc.semaphore("cc_sem"))
            self.cc_cnt = 0
        self._deps("pool", [src], [])
        for ev in list(dst.mw.values()) + list(dst.r.values()):
            self._wait("pool", ev)
        ins = self.nc.gpsimd.collective_compute(kind, mybir.AluOpType.bypass, replica_groups=groups, ins=[src.t.opt()], outs=[dst.t.opt()])
        self.cc_cnt += 1
        ins.then_inc(self.cc_sem, 1)
        ev = (self.cc_sem, self.cc_cnt, "cc")
        src.r[id(self.cc_sem)] = ev
        dst.mw[id(self.cc_sem)] = ev
        self.n_ins += 1

    def _wait(self, eng, ev):
        sem, val, src = ev
        if src == eng and eng == "pe":
            return
        k = id(sem)
        if self.known[eng].get(k, 0) >= val:
            return
        self.E[eng].wait_ge(sem, val)
        self.known[eng][k] = val
        self.n_wait += 1

    def _deps(self, eng, reads, writes):
        for r in reads:
            if r.w is not None:
                self._wait(eng, r.w)
            for ev in r.mw.values():
                self._wait(eng, ev)
        for w in writes:
            if w.multi:
                continue
            if w.w is not None:
                self._wait(eng, w.w)
            for ev in w.r.values():
                self._wait(eng, ev)

    def _record(self, ev, reads, writes):
        sem = ev[0]
        for r in reads:
            r.r[id(sem)] = ev
        for w in writes:
            if w.multi:
                w.mw[id(sem)] = ev
                continue
            w.w = ev
            w.r = {}

    def op(self, eng, reads, writes, fn):
        self._deps(eng, reads, writes)
        ins = fn(self.E[eng])
        self.cnt[eng] += 1
        ins.then_inc(self.sem[eng], 1)
        ev = (self.sem[eng], self.cnt[eng], eng)
        self.known[eng][id(self.sem[eng])] = max(self.known[eng].get(id(self.sem[eng]), 0), 0)
        self._record(ev, reads, writes)
        self.n_ins += 1
        return ins

    def dma(self, q, reads, writes, out, in_, **kw):
        slots = self.slots[q]
        i = self.slot_i[q]
        self.slot_i[q] = (i + 1) % len(slots)
        s = slots[i]
        if s[1] > 0:
            self._wait(q, (s[0], s[1], "dma"))
        self._deps(q, reads, writes)
        ins = self.E[q].dma_start(out=out, in_=in_, **kw)
        s[1] += 16
        ins.then_inc(s[0], 16)
        ev = (s[0], s[1], "dma")
        self._record(ev, reads, writes)
        self.n_ins += 1
        return ins

    def wait_all(self, eng, resources):
        for r in resources:
            if r.w is not None:
                self._wait(eng, r.w)
            for ev in r.mw.values():
                self._wait(eng, ev)

    def barrier(self):
        evs = [(self.sem[k], self.cnt[k], k) for k in self.E if self.cnt[k] > 0]
        for q, sl in self.slots.items():
            for s in sl:
                if s[1] > 0:
                    evs.append((s[0], s[1], "dma"))
        for eng in self.E:
            for ev in evs:
                sem, val, src = ev
                if self.known[eng].get(id(sem), 0) >= val:
                    continue
                self.E[eng].wait_ge(sem, val)
                self.known[eng][id(sem)] = val


class Ring:
    def __init__(self, items):
        self.items = items
        self.i = 0

    def next(self):
        r = self.items[self.i]
        self.i = (self.i + 1) % len(self.items)
        return r


def views(S, name, shape, dt, n, idx_fn):
    t = S.es.enter_context(S.nc.sbuf_tensor("sb_" + S.prefix + name, list(shape), dt))
    return t, [Res(f"{name}{i}", idx_fn(t, i)) for i in range(n)]


def make_wring(S, name, nslots):
    _, sl = views(S, name, [128, nslots, 4096], BF16, nslots, lambda t, i: t[:, i, :])
    return Ring(sl)


def linear_g(S, wring, NW, W, r0, acts, T, groups, evac, banks, halo_skip=None):
    nk = len(acts)
    loads = []
    for gi, (c0, n) in enumerate(groups):
        per = 4096 // (n * 128)
        for k0 in range(0, nk, per):
            loads.append((gi, c0, n, k0, min(per, nk - k0)))
    issued = []

    def issue(j):
        gi, c0, n, k0, kn = loads[j]
        slot = wring.next()
        v = slot.t[:, 0:kn * n * 128].rearrange("p (k c) -> p k c", c=n * 128)
        S.dma("pool", [W], [slot], out=v, in_=W.t[r0 + k0 * 128:r0 + (k0 + kn) * 128, c0:c0 + n * 128].rearrange("(kc p) c -> p kc c", p=128))
        issued.append((slot, v))
    npre = NW - 1
    nxt = 0
    while nxt < min(npre, len(loads)):
        issue(nxt); nxt += 1
    ci = 0
    j = 0
    for gi, (c0, n) in enumerate(groups):
        ps = [banks.next() for _ in range(n)]
        while j < len(loads) and loads[j][0] == gi:
            if nxt < len(loads):
                issue(nxt); nxt += 1
            _, _, _, k0, kn = loads[j]
            slot, v = issued[j]
            for oc in range(n):
                for kk in range(kn):
                    k = k0 + kk
                    S.op("pe", [slot, acts[k]], [ps[oc]],
                         lambda e, oc=oc, kk=kk, k=k: e.matmul(ps[oc].t[:, 0:T], lhsT=v[:, kk, oc * 128:(oc + 1) * 128], rhs=acts[k].t[:, 0:T],
                                                              start=(k == 0), stop=(k == nk - 1)))
            j += 1
        for oc in range(n):
            evac(ci, ps[oc])
            ci += 1


def col_groups(c0, nchunks):
    g = []
    i = 0
    while i < nchunks:
        n = min(4, nchunks - i)
        g.append((c0 + i * 128, n))
        i += n
    return g


class Ring:
    def __init__(self, items):
        self.items = items
        self.i = 0

    def next(self):
        r = self.items[self.i]
        self.i = (self.i + 1) % len(self.items)
        return r


def views(S, name, shape, dt, n, idx_fn):
    t = S.es.enter_context(S.nc.sbuf_tensor("sb_" + S.prefix + name, list(shape), dt))
    return t, [Res(f"{name}{i}", idx_fn(t, i)) for i in range(n)]


def emit_post(S, c, dr, blocks):
    nc = S.nc
    D, DFF, XH, NM = c["D"], c["DFF"], c["XH"], c["NM"]
    KC, FC = D // 128, DFF // 128
    XW = XH * 128
    TM = c["T"]
    alpha, eps = c["alpha"], c["eps"]
    GF = c.get("GF", 8)
    NW = c.get("NW", 4)
    KCW = max(KC, GF, XH)
    oes = S.es
    S.new_phase()
    with ExitStack() as pes:
        S.es = pes
        wring = make_wring(S, "wsl", NW)
        ones_f = S.sbuf("ones_f", [128, 128], F32)
        ones_b = S.sbuf("ones_b", [128, 128], BF16)
        epsT = S.sbuf("epsT", [128, 1], F32)
        lng = S.sbuf("lng", [128, 3, KC], F32)
        lnb = S.sbuf("lnb", [128, 3, KC], F32)
        cw = S.sbuf("cw", [128, 3, FC], F32)
        flag = S.sbuf("flag", [128, 1], F32)
        if "oAG_get" in dr:
            cand = Ring([S.sbuf(f"cand{i}", [128, TM], BF16) for i in range(6)])
            selacc = Ring([S.sbuf(f"selacc{i}", [128, TM], BF16) for i in range(4)])
            fsel = S.sbuf("fsel", [128, 4], F32)
            S.dma("sp", [dr["fsel"]], [fsel], out=fsel.t[:], in_=dr["fsel"].t)
        if "hlAG_get" in dr:
            candf = Ring([S.sbuf(f"candf{i}", [128, 128], F32) for i in range(4)])
            fhal = S.sbuf("fhal", [128, 4], F32)
            S.dma("sp", [dr["fhal"]], [fhal], out=fhal.t[:], in_=dr["fhal"].t)
        S.dma("sp", [dr["flag"]], [flag], out=flag.t[:], in_=dr["flag"].t)
        carry = [S.sbuf(f"carry{f}", [128, 2], F32) for f in range(FC)]
        KT = S.sbuf("KT", [128, XH, NM], BF16)
        Vt = [S.sbuf(f"Vt{m}", [128, XW], BF16) for m in range(NM // 128)]
        qT = [S.sbuf(f"qT{h}", [128, TM], BF16) for h in range(XH)]
        xoT = [S.sbuf(f"xoT{h}", [128, TM], BF16) for h in range(XH)]
        pT = Ring([S.sbuf(f"pT{i}", [128, TM], BF16) for i in range(4)])
        tmpf = Ring([S.sbuf(f"tmpf{i}", [128, TM + 2], F32) for i in range(6)])
        st_mean = S.sbuf("st_mean", [128, TM], F32)
        st_rstd = S.sbuf("st_rstd", [128, TM], F32)
        st_nmr = S.sbuf("st_nmr", [128, TM], F32)
        pb = [S.psum(f"pb{i}", [128, 512], F32) for i in range(8)]
        pring = Ring(pb[0:4])
        pA, pB2 = pb[4], pb[5]
        pring2 = Ring(pb[6:8])
        pringu = Ring(pb[4:8])
        glr = Ring([S.sbuf(f"glr{i}", [128, TM], F32) for i in range(5)])

        S.op("dve", [], [ones_f], lambda e: e.memset(ones_f.t[:], 1.0))
        S.op("dve", [], [ones_b], lambda e: e.memset(ones_b.t[:], 1.0))
        S.op("dve", [], [epsT], lambda e: e.memset(epsT.t[:], eps))
        S.dma("sp", [dr["ln_g"]], [lng], out=lng.t[:], in_=dr["ln_g"].t)
        S.dma("sp", [dr["ln_b"]], [lnb], out=lnb.t[:], in_=dr["ln_b"].t)
        S.dma("sp", [dr["conv"]], [cw], out=cw.t[:], in_=dr["conv"].t)
        for f in range(FC):
            S.op("dve", [], [carry[f]], lambda e, f=f: e.memset(carry[f].t[:], 0.0))

        def load_w(W, r0, nk, c0, ncol=128):
            slot = wring.next()
            v = slot.t[:, 0:nk * ncol].rearrange("p (k c) -> p k c", c=ncol)
            S.dma("pool", [W], [slot], out=v, in_=W.t[r0:r0 + nk * 128, c0:c0 + ncol].rearrange("(kc p) c -> p kc c", p=128))
            return slot, v

        def linear(W, r0, acts, T, cols, evac, ps_ring):
            assert all(cols[i + 1] == cols[i] + 128 for i in range(len(cols) - 1))
            linear_g(S, wring, NW, W, r0, acts, T, col_groups(cols[0], len(cols)), evac, ps_ring)

        def layer_norm(li, T):
            for k in range(KC):
                S.op("pe", [ones_f, rT[k]], [pA],
                     lambda e, k=k: e.matmul(pA.t[:, 0:T], lhsT=ones_f.t[:], rhs=rT[k].t[:, 0:T], start=(k == 0), stop=(k == KC - 1)))
            for k in range(KC):
                sq = tmpf.next()
                S.op("act", [rT[k]], [sq], lambda e, k=k, sq=sq: e.activation(out=sq.t[:, 0:T], in_=rT[k].t[:, 0:T], func=AF.Square))
                S.op("pe", [ones_f, sq], [pB2],
                     lambda e, k=k, sq=sq: e.matmul(pB2.t[:, 0:T], lhsT=ones_f.t[:], rhs=sq.t[:, 0:T], start=(k == 0), stop=(k == KC - 1)))
            inv = 1.0 / D
            S.op("act", [pA], [st_mean], lambda e: e.activation(out=st_mean.t[:, 0:T], in_=pA.t[:, 0:T], func=AF.Copy, scale=inv))
            m2 = tmpf.next()
            S.op("dve", [st_mean], [m2], lambda e: e.tensor_tensor(out=m2.t[:, 0:T], in0=st_mean.t[:, 0:T], in1=st_mean.t[:, 0:T], op=ALU.mult))
            var = tmpf.next()
            S.op("dve", [pB2, m2], [var], lambda e: e.scalar_tensor_tensor(out=var.t[:, 0:T], in0=pB2.t[:, 0:T], scalar=inv, in1=m2.t[:, 0:T],
                                                                            op0=ALU.mult, op1=ALU.subtract))
            sd = tmpf.next()
            S.op("act", [var, epsT], [sd], lambda e: e.activation(out=sd.t[:, 0:T], in_=var.t[:, 0:T], func=AF.Sqrt, bias=epsT.t[:, 0:1]))
            S.op("dve", [sd], [st_rstd], lambda e: e.reciprocal(out=st_rstd.t[:, 0:T], in_=sd.t[:, 0:T]))
            S.op("dve", [st_mean, st_rstd], [st_nmr],
                 lambda e: e.scalar_tensor_tensor(out=st_nmr.t[:, 0:T], in0=st_mean.t[:, 0:T], scalar=-1.0, in1=st_rstd.t[:, 0:T], op0=ALU.mult, op1=ALU.mult))
            for k in range(KC):
                t1 = tmpf.next()
                S.op("dve", [rT[k], st_rstd], [t1], lambda e, k=k, t1=t1: e.tensor_tensor(out=t1.t[:, 0:T], in0=rT[k].t[:, 0:T], in1=st_rstd.t[:, 0:T], op=ALU.mult))
                t2 = tmpf.next()
                S.op("pool", [t1, st_nmr], [t2], lambda e, t1=t1, t2=t2: e.tensor_tensor(out=t2.t[:, 0:T], in0=t1.t[:, 0:T], in1=st_nmr.t[:, 0:T], op=ALU.add))
                S.op("act", [t2, lng, lnb], [rT[k]],
                     lambda e, k=k, t2=t2: e.activation(out=rT[k].t[:, 0:T], in_=t2.t[:, 0:T], func=AF.Identity,
                                                        scale=lng.t[:, li, k:k + 1], bias=lnb.t[:, li, k:k + 1]))
                S.op("act", [t2, lng, lnb], [aT[k]],
                     lambda e, k=k, t2=t2: e.activation(out=aT[k].t[:, 0:T], in_=t2.t[:, 0:T], func=AF.Identity,
                                                        scale=lng.t[:, li, k:k + 1], bias=lnb.t[:, li, k:k + 1]))

        def resid_evac(T):
            def ev(i, ps):
                S.op("dve", [rT[i], ps], [rT[i]],
                     lambda e, i=i, ps=ps: e.scalar_tensor_tensor(out=rT[i].t[:, 0:T], in0=rT[i].t[:, 0:T], scalar=alpha, in1=ps.t[:, 0:T],
                                                                  op0=ALU.mult, op1=ALU.add))
            return ev

        mes = ExitStack()
        S.es = mes
        memT = S.sbuf("memT", [128, KC, NM], BF16)
        S.es = pes
        S.dma("pool", [dr["memT"]], [memT], out=memT.t[:], in_=dr["memT"].t.rearrange("(kc p) m -> p kc m", p=128))
        memk = [Res(f"memk{k}", memT.t[:, k, :]) for k in range(KC)]
        for mk in memk:
            mk.w = memT.w

        def kt_evac(i, ps):
            S.op("act", [ps], [KT], lambda e, i=i, ps=ps: e.activation(out=KT.t[:, i, :], in_=ps.t[:, 0:NM], func=AF.Copy))
        linear(dr["w_kv"], 0, memk, NM, [h * 128 for h in range(XH)], kt_evac, pring)
        for h in range(XH):
            slot, sv = load_w(dr["w_kv"], 0, KC, XW + h * 128)
            for m in range(NM // 128):
                ps = pring.next()
                for k in range(KC):
                    S.op("pe", [slot, memT], [ps],
                         lambda e, k=k, m=m, slot=slot, ps=ps: e.matmul(ps.t[:, 0:128], lhsT=memT.t[:, k, m * 128:(m + 1) * 128], rhs=sv[:, k, :],
                                                                        start=(k == 0), stop=(k == KC - 1)))
                S.op("act", [ps], [Vt[m]], lambda e, m=m, h=h, ps=ps: e.activation(out=Vt[m].t[:, h * 128:(h + 1) * 128], in_=ps.t[:, 0:128], func=AF.Copy))

        S.barrier()
        mes.close()
        _, rT = views(S, "rT", [128, KC, TM], F32, KC, lambda t, i: t[:, i, :])
        _, aT = views(S, "aT", [128, KC, TM], BF16, KC, lambda t, i: t[:, i, :])
        _, gT = views(S, "gT", [128, 2 * GF, TM], BF16, 2 * GF, lambda t, i: t[:, i, :])
        for (tok0, T, halo) in blocks:
            HL = c.get("HALO", 128)
            TOKC_ = c.get("TOKC", 2048)
            for k in range(KC):
                if "oAG_get" in dr:
                    for cq in range(4):
                        if halo:
                            cb = max(cq * TOKC_ - HL, 0)
                        else:
                            cb = cq * TOKC_ + (tok0 - HL)
                        cd = cand.next()
                        _ores, _oap = dr["oAG_get"](k, cb, T)
                        S.dma("sp", [_ores], [cd], out=cd.t[:, 0:T], in_=_oap)
                        if cq == 0:
                            acc = selacc.next()
                            S.op("dve", [cd, fsel], [acc], lambda e: e.tensor_scalar(out=acc.t[:, 0:T], in0=cd.t[:, 0:T], scalar1=fsel.t[:, 0:1], scalar2=None, op0=ALU.mult))
                        else:
                            dst = aT[k] if cq == 3 else selacc.next()
                            S.op("dve", [cd, fsel, acc], [dst], lambda e: e.scalar_tensor_tensor(out=dst.t[:, 0:T], in0=cd.t[:, 0:T], scalar=fsel.t[:, cq:cq + 1], in1=acc.t[:, 0:T],
                                                                                              op0=ALU.mult, op1=ALU.add))
                            acc = dst
                else:
                    S.dma("sp", [dr["oT"]], [aT[k]], out=aT[k].t[:, 0:T], in_=dr["oT"].t[k * 128:(k + 1) * 128, tok0:tok0 + T])
                if halo and "hlAG_get" in dr:
                    for cq in range(4):
                        cd = candf.next()
                        _lres, _lap = dr["hlAG_get"](k, cq)
                        S.dma("sp", [_lres], [cd], out=cd.t[:, 0:T], in_=_lap)
                        if cq == 0:
                            S.op("dve", [cd, fhal], [rT[k]], lambda e: e.tensor_scalar(out=rT[k].t[:, 0:T], in0=cd.t[:, 0:T], scalar1=fhal.t[:, 0:1], scalar2=None, op0=ALU.mult))
                        else:
                            S.op("dve", [cd, fhal, rT[k]], [rT[k]], lambda e: e.scalar_tensor_tensor(out=rT[k].t[:, 0:T], in0=cd.t[:, 0:T], scalar=fhal.t[:, cq:cq + 1], in1=rT[k].t[:, 0:T],
                                                                                                    op0=ALU.mult, op1=ALU.add))
                else:
                    S.dma("sp", [dr["hT_in"]], [rT[k]], out=rT[k].t[:, 0:T], in_=dr["hT_in"].t[k * 128:(k + 1) * 128, tok0:tok0 + T])
            linear(dr["w_out"], 0, aT, T, [i * 128 for i in range(KC)], resid_evac(T), pring)
            layer_norm(0, T)
            qscale = 128.0 ** -0.5

            def q_evac(i, ps):
                S.op("act", [ps], [qT[i]], lambda e, i=i, ps=ps: e.activation(out=qT[i].t[:, 0:T], in_=ps.t[:, 0:T], func=AF.Copy, scale=qscale))
            linear(dr["w_q"], 0, aT, T, [h * 128 for h in range(XH)], q_evac, pring)
            for h in range(XH):
                pts = []
                for m in range(NM // 128):
                    ps = pring.next()
                    S.op("pe", [KT, qT[h]], [ps], lambda e, h=h, m=m, ps=ps: e.matmul(ps.t[:, 0:T], lhsT=KT.t[:, h, m * 128:(m + 1) * 128], rhs=qT[h].t[:, 0:T],
                                                                                      start=True, stop=True))
                    pt = pT.next()
                    S.op("act", [ps], [pt], lambda e, ps=ps, pt=pt: e.activation(out=pt.t[:, 0:T], in_=ps.t[:, 0:T], func=AF.Exp))
                    pts.append(pt)
                nm = len(pts)
                for m, pt in enumerate(pts):
                    S.op("pe", [ones_b, pt], [pA], lambda e, m=m, pt=pt: e.matmul(pA.t[:, 0:T], lhsT=ones_b.t[:], rhs=pt.t[:, 0:T], start=(m == 0), stop=(m == nm - 1)))
                for m, pt in enumerate(pts):
                    S.op("pe", [Vt[m], pt], [pB2], lambda e, m=m, h=h, pt=pt: e.matmul(pB2.t[:, 0:T], lhsT=Vt[m].t[:, h * 128:(h + 1) * 128], rhs=pt.t[:, 0:T],
                                                                                     start=(m == 0), stop=(m == nm - 1)))
                rd = tmpf.next()
                S.op("dve", [pA], [rd], lambda e, rd=rd: e.reciprocal(out=rd.t[:, 0:T], in_=pA.t[:, 0:T]))
                S.op("dve", [pB2, rd], [xoT[h]], lambda e, h=h, rd=rd: e.tensor_tensor(out=xoT[h].t[:, 0:T], in0=pB2.t[:, 0:T], in1=rd.t[:, 0:T], op=ALU.mult))
            linear(dr["w_o"], 0, xoT, T, [i * 128 for i in range(KC)], resid_evac(T), pring)
            layer_norm(1, T)
            groups = [list(range(g0, min(g0 + GF, FC))) for g0 in range(0, FC, GF)]

            def glu_a(f, ps):
                ab = tmpf.next()
                S.op("act", [carry[f]], [ab], lambda e: e.activation(out=ab.t[:, 0:2], in_=carry[f].t[:, 0:2], func=AF.Copy))
                S.op("act", [ps], [ab], lambda e: e.activation(out=ab.t[:, 2:2 + T], in_=ps.t[:, 0:T], func=AF.Copy))
                if halo:
                    S.op("dve", [ab, flag], [carry[f]], lambda e: e.tensor_scalar(out=carry[f].t[:, 0:2], in0=ab.t[:, T:T + 2], scalar1=flag.t[:, 0:1], scalar2=None, op0=ALU.mult))
                    return None
                S.op("dve", [ab], [carry[f]], lambda e: e.tensor_copy(out=carry[f].t[:, 0:2], in_=ab.t[:, T:T + 2]))
                c1 = tmpf.next()
                S.op("act", [ab, cw], [c1], lambda e: e.activation(out=c1.t[:, 0:T], in_=ab.t[:, 2:2 + T], func=AF.Copy, scale=cw.t[:, 2, f:f + 1]))
                c2 = tmpf.next()
                S.op("dve", [ab, cw, c1], [c2], lambda e: e.scalar_tensor_tensor(out=c2.t[:, 0:T], in0=ab.t[:, 1:1 + T], scalar=cw.t[:, 1, f:f + 1], in1=c1.t[:, 0:T], op0=ALU.mult, op1=ALU.add))
                S.op("dve", [ab, cw, c2], [c1], lambda e: e.scalar_tensor_tensor(out=c1.t[:, 0:T], in0=ab.t[:, 0:T], scalar=cw.t[:, 0, f:f + 1], in1=c2.t[:, 0:T], op0=ALU.mult, op1=ALU.add))
                gl = glr.next()
                S.op("act", [c1], [gl], lambda e: e.activation(out=gl.t[:, 0:T], in_=c1.t[:, 0:T], func=AF.Gelu))
                return gl

            def up_group(gi, fl):
                gb = (gi % 2) * GF
                for s0 in range(0, len(fl), 4):
                    sub = fl[s0:s0 + 4]
                    gls = {}

                    def ev_a(i, ps):
                        gls[sub[i]] = glu_a(sub[i], ps)
                    linear_g(S, wring, NW, dr["w_up"], 0, aT, T, [(sub[0] * 128, len(sub))], ev_a, pring)
                    if halo:
                        continue

                    def ev_u(i, ps):
                        f = sub[i]
                        gdst = gT[gb + (f - fl[0])]
                        S.op("dve", [gls[f], ps], [gdst], lambda e: e.tensor_tensor(out=gdst.t[:, 0:T], in0=gls[f].t[:, 0:T], in1=ps.t[:, 0:T], op=ALU.mult))
                    linear_g(S, wring, NW, dr["w_up"], 0, aT, T, [(DFF + sub[0] * 128, len(sub))], ev_u, pringu)

            def down_group(gi, fl):
                gb = (gi % 2) * GF
                nk = len(fl)
                acts = [gT[gb + j] for j in range(nk)]

                def ev(i, ps):
                    if gi == 0:
                        resid_evac(T)(i, ps)
                    else:
                        S.op("dve", [rT[i], ps], [rT[i]], lambda e, i=i, ps=ps: e.tensor_tensor(out=rT[i].t[:, 0:T], in0=rT[i].t[:, 0:T], in1=ps.t[:, 0:T], op=ALU.add))
                linear(dr["w_down"], fl[0] * 128, acts, T, [i * 128 for i in range(KC)], ev, pring)

            for gi, fl in enumerate(groups):
                up_group(gi, fl)
                if halo:
                    continue
                if gi > 0:
                    down_group(gi - 1, groups[gi - 1])
            if halo:
                continue
            down_group(len(groups) - 1, groups[-1])
            layer_norm(2, T)
            for k in range(KC):
                oo = c.get("out_off", 0)
                S.dma("sp", [rT[k]], [dr["hT_out"]], out=dr["hT_out"].t[k * 128:(k + 1) * 128, tok0 - oo:tok0 - oo + T], in_=rT[k].t[:, 0:T])
                if "hTb_put" in dr:
                    _bres, _bap = dr["hTb_put"](k, tok0 - c.get("outb_off", 0), T)
                    S.dma("sp", [aT[k]], [_bres], out=_bap, in_=aT[k].t[:, 0:T])
                elif "hTb_out" in dr:
                    bo = c.get("outb_off", 0)
                    S.dma("sp", [aT[k]], [dr["hTb_out"]], out=dr["hTb_out"].t[k * 128:(k + 1) * 128, tok0 - bo:tok0 - bo + T], in_=aT[k].t[:, 0:T])
                if "hl_put" in dr and (tok0, T, halo) == blocks[-1]:
                    _lres, _lap = dr["hl_put"](k)
                    S.dma("sp", [rT[k]], [_lres], out=_lap, in_=rT[k].t[:, T - 128:T])
        S.barrier()
    S.es = oes


NEG = -30000.0


def _oput(dr, r0, t0, T):
    if "oT_put" in dr:
        return dr["oT_put"](r0, t0, T)
    return dr["oT"], dr["oT"].t[r0:r0 + 128, t0:t0 + T]


def _linear(S, wring, NW, W, acts, T, cols, evac, ps_ring):
    groups = []
    i = 0
    while i < len(cols):
        n = 1
        while n < 4 and i + n < len(cols) and cols[i + n] == cols[i] + n * 128:
            n += 1
        groups.append((cols[i], n))
        i += n
    linear_g(S, wring, NW, W, 0, acts, T, groups, evac, ps_ring)


def emit_sb(S, c, dr, heads, col_q, col_k, col_v, orow0):
    D, SEQ = c["D"], c["SEQ"]
    KC = D // 128
    T = 512
    NH = len(heads)
    NW = c.get("NW", 4)
    NT = SEQ // 128
    oes = S.es
    S.new_phase()
    with ExitStack() as pes:
        S.es = pes
        _, aT = views(S, "aT", [128, KC, T], BF16, KC, lambda t, i: t[:, i, :])
        wring = make_wring(S, "wsl", NW)
        kT = [S.sbuf(f"kT{h}", [128, SEQ], BF16) for h in range(NH)]
        _, Vall = views(S, "Vall", [128, NT, NH * 128], BF16, NT, lambda t, i: t[:, i, :])
        qT = [S.sbuf(f"qT{h}", [128, T], BF16) for h in range(NH)]
        vf = Ring([S.sbuf(f"vf{i}", [128, T], F32) for i in range(2)])
        ident = S.sbuf("ident", [128, 128], F32)
        identb = S.sbuf("identb", [128, 128], BF16)
        negU = S.sbuf("negU", [128, 128], BF16)
        negO = S.sbuf("negO", [128, 128], BF16)
        mask = S.sbuf("mask", [128, 4, T], BF16)
        eT = [Ring([S.sbuf(f"eT{h}_{i}", [128, T], F32) for i in range(2)]) for h in range(NH)]
        spT = [Ring([S.sbuf(f"spT{h}_{i}", [128, T], BF16) for i in range(2)]) for h in range(NH)]
        tmpT = [Ring([S.sbuf(f"tmpT{h}_{i}", [128, T], F32) for i in range(2)]) for h in range(NH)]
        xT_ = [Ring([S.sbuf(f"xT{h}_{i}", [128, T], F32) for i in range(2)]) for h in range(NH)]
        wT = [Ring([S.sbuf(f"wT{h}_{i}", [128, T], BF16) for i in range(2)]) for h in range(NH)]
        carry = [S.sbuf(f"carry{h}", [128, T], F32) for h in range(NH)]
        oS = Ring([S.sbuf(f"oS{i}", [128, T], BF16) for i in range(2)])
        pb = [S.psum(f"pb{i}", [128, 512], F32) for i in range(8)]
        pz = [pb[0], pb[1]]
        pR = [pb[2], pb[3]]
        pC = [pb[4], pb[5]]
        pO = [pb[6], pb[7]]
        plin = Ring([pb[2], pb[3], pb[4], pb[5]])
        ptr = Ring([pb[6], pb[7]])

        S.dma("sp", [dr["ident"]], [ident], out=ident.t[:], in_=dr["ident"].t)
        S.dma("sp", [dr["identb"]], [identb], out=identb.t[:], in_=dr["identb"].t)
        S.dma("sp", [dr["negU"]], [negU], out=negU.t[:], in_=dr["negU"].t)
        S.dma("sp", [dr["sbmask"]], [mask], out=mask.t[:], in_=dr["sbmask"].t)
        S.op("dve", [], [negO], lambda e: e.memset(negO.t[:], -1.0))
        scale = 128.0 ** -0.5

        for tb in range(SEQ // T):
            t0 = tb * T
            for k in range(KC):
                S.dma("pool", [dr["xT"]], [aT[k]], out=aT[k].t[:, :], in_=dr["xT"].t[k * 128:(k + 1) * 128, t0:t0 + T])
            cols = [col_q + h * 128 for h in heads] + [col_k + h * 128 for h in heads] + [col_v + h * 128 for h in heads]

            def evac(i, ps):
                kind, hi = divmod(i, NH)
                if kind == 0:
                    S.op("act", [ps], [qT[hi]], lambda e: e.activation(out=qT[hi].t[:, :], in_=ps.t[:, :], func=AF.Copy, scale=scale))
                elif kind == 1:
                    S.op("act", [ps], [kT[hi]], lambda e: e.activation(out=kT[hi].t[:, t0:t0 + T], in_=ps.t[:, :], func=AF.Copy))
                else:
                    v = vf.next()
                    S.op("act", [ps], [v], lambda e: e.activation(out=v.t[:, :], in_=ps.t[:, :], func=AF.Copy))
                    for j in range(T // 128):
                        pt = ptr.next()
                        S.op("pe", [v, ident], [pt], lambda e, j=j, pt=pt: e.transpose(out=pt.t[:, 0:128], in_=v.t[:, j * 128:(j + 1) * 128], identity=ident.t[:]))
                        vt = Vall[t0 // 128 + j]
                        S.op("dve", [pt], [vt], lambda e, pt=pt, vt=vt: e.tensor_copy(out=vt.t[:, hi * 128:(hi + 1) * 128], in_=pt.t[:, 0:128]))
            _linear(S, wring, NW, dr["wA"], aT, T, cols, evac, plin)

            kts = list(range(t0 // 128 + 3, -1, -1))

            def emit_z(kt):
                for h in range(NH):
                    kd = kt - t0 // 128
                    S.op("pe", [kT[h], qT[h]], [pz[h]], lambda e, h=h: e.matmul(pz[h].t[:, :], lhsT=kT[h].t[:, kt * 128:(kt + 1) * 128], rhs=qT[h].t[:, :],
                                                                              start=True, stop=(kd < 0)))
                    if kd >= 0:
                        S.op("pe", [identb, mask], [pz[h]], lambda e, h=h, kd=kd: e.matmul(pz[h].t[:, :], lhsT=identb.t[:], rhs=mask.t[:, kd, :], start=False, stop=True))
            emit_z(kts[0])
            for n, kt in enumerate(kts):
                first = (n == 0)
                last = (n == len(kts) - 1)
                cur = {}
                for h in range(NH):
                    e_ = eT[h].next()
                    sp = spT[h].next()
                    S.op("act", [pz[h]], [e_], lambda e, h=h, e_=e_: e.activation(out=e_.t[:, :], in_=pz[h].t[:, :], func=AF.Exp))
                    S.op("act", [e_], [sp], lambda e, e_=e_, sp=sp: e.activation(out=sp.t[:, :], in_=e_.t[:, :], func=AF.Ln, bias=1.0))
                    cur[h] = (e_, sp)
                for h in range(NH):
                    e_, sp = cur[h]
                    S.op("pe", [negU, sp], [pR[h]], lambda e, h=h, sp=sp: e.matmul(pR[h].t[:, :], lhsT=negU.t[:], rhs=sp.t[:, :], start=True, stop=True))
                    if not last:
                        S.op("pe", [negO, sp], [pC[h]], lambda e, h=h, sp=sp: e.matmul(pC[h].t[:, :], lhsT=negO.t[:], rhs=sp.t[:, :], start=True, stop=True))
                if not last:
                    emit_z(kts[n + 1])
                for h in range(NH):
                    e_, sp = cur[h]
                    x_ = xT_[h].next()
                    if first:
                        S.op("act", [pR[h]], [x_], lambda e, h=h, x_=x_: e.activation(out=x_.t[:, :], in_=pR[h].t[:, :], func=AF.Exp))
                        if not last:
                            S.op("dve", [pC[h]], [carry[h]], lambda e, h=h: e.tensor_copy(out=carry[h].t[:, :], in_=pC[h].t[:, :]))
                    else:
                        tm = tmpT[h].next()
                        S.op("dve", [pR[h], carry[h]], [tm], lambda e, h=h, tm=tm: e.tensor_tensor(out=tm.t[:, :], in0=pR[h].t[:, :], in1=carry[h].t[:, :], op=ALU.add))
                        S.op("act", [tm], [x_], lambda e, tm=tm, x_=x_: e.activation(out=x_.t[:, :], in_=tm.t[:, :], func=AF.Exp))
                        if not last:
                            S.op("dve", [pC[h], carry[h]], [carry[h]], lambda e, h=h: e.tensor_tensor(out=carry[h].t[:, :], in0=pC[h].t[:, :], in1=carry[h].t[:, :], op=ALU.add))
                    w_ = wT[h].next()
                    S.op("pool", [e_, x_], [w_], lambda e, e_=e_, x_=x_, w_=w_: e.tensor_tensor(out=w_.t[:, :], in0=e_.t[:, :], in1=x_.t[:, :], op=ALU.mult))
                    vt = Vall[kt]
                    S.op("pe", [vt, w_], [pO[h]], lambda e, h=h, vt=vt, w_=w_: e.matmul(pO[h].t[:, :], lhsT=vt.t[:, h * 128:(h + 1) * 128], rhs=w_.t[:, :],
                                                                                       start=first, stop=last))
            for h in range(NH):
                o_ = oS.next()
                S.op("act", [pO[h]], [o_], lambda e, h=h, o_=o_: e.activation(out=o_.t[:, :], in_=pO[h].t[:, :], func=AF.Copy))
                r0 = orow0 + heads[h] * 128
                _ores, _oap = _oput(dr, r0, t0, T)
                S.dma("sp", [o_], [_ores], out=_oap, in_=o_.t[:, :])
        S.barrier()
    S.es = oes


def emit_hgrn(S, c, dr, heads, col_q, col_f, col_i, col_g, orow0):
    D, SEQ = c["D"], c["SEQ"]
    KC = D // 128
    T = 512
    C = 64
    NH = len(heads)
    NW = c.get("NW", 4)
    oes = S.es
    S.new_phase()
    with ExitStack() as pes:
        S.es = pes
        _, aT = views(S, "aT", [128, KC, T], BF16, KC, lambda t, i: t[:, i, :])
        wring = make_wring(S, "wsl", NW)
        ident = S.sbuf("ident", [128, 128], F32)
        identb = S.sbuf("identb", [128, 128], BF16)
        maskU = S.sbuf("maskU", [64, 64], F32)
        rmask = S.sbuf("rmask", [128, T], F32)
        NWt = S.sbuf("NWt", [128, 128], F32)
        epsR = S.sbuf("epsR", [128, 1], F32)
        lbr = S.sbuf("lbr", [128, 2, NH], F32)
        lb = S.sbuf("lb", [128, NH], F32)
        oml = S.sbuf("oml", [128, NH], F32)
        noml = S.sbuf("noml", [128, NH], F32)
        qd = [S.sbuf(f"qd{h}", [128, T], BF16) for h in range(NH)]
        kd = [S.sbuf(f"kd{h}", [128, T], BF16) for h in range(NH)]
        iTb = [S.sbuf(f"iTb{h}", [128, T], BF16) for h in range(NH)]
        gf = [S.sbuf(f"gf{h}", [128, T], F32) for h in range(NH)]
        eb = [S.sbuf(f"eb{h}", [128, T], F32) for h in range(NH)]
        qf = [S.sbuf(f"qf{h}", [128, T], F32) for h in range(NH)]
        sg = [S.sbuf(f"sg{h}", [128, T], F32) for h in range(NH)]
        oS = [S.sbuf(f"oS{h}", [128, T], BF16) for h in range(NH)]
        state = [S.sbuf(f"state{h}", [128, 128], F32) for h in range(NH)]
        stateb = [S.sbuf(f"stateb{h}", [128, 128], BF16) for h in range(NH)]
        tmp = Ring([S.sbuf(f"tmp{i}", [128, T], F32) for i in range(6)])
        kvt = Ring([S.sbuf(f"kvt{i}", [64, 256], BF16) for i in range(3)])
        gate = Ring([S.sbuf(f"gate{i}", [64, 128], F32) for i in range(3)])
        scb = Ring([S.sbuf(f"scb{i}", [64, 64], BF16) for i in range(3)])
        junk = Ring([S.sbuf(f"junk{i}", [64, 128], F32) for i in range(2)])
        ss = Ring([S.sbuf(f"ss{i}", [64, 4], F32) for i in range(4)])
        t1r = Ring([S.sbuf(f"t1r{i}", [64, 128], F32) for i in range(3)])
        t2r = Ring([S.sbuf(f"t2r{i}", [64, 128], BF16) for i in range(3)])
        st1 = Ring([S.sbuf(f"st1{i}", [128, 128], F32) for i in range(2)])
        pb = [S.psum(f"pb{i}", [128, 512], F32) for i in range(8)]
        plin = Ring(pb[0:4])
        ptr = Ring(pb[2:4])
        pwk = Ring(pb[4:8])

        S.dma("sp", [dr["ident"]], [ident], out=ident.t[:], in_=dr["ident"].t)
        S.dma("sp", [dr["identb"]], [identb], out=identb.t[:], in_=dr["identb"].t)
        S.dma("sp", [dr["maskU"]], [maskU], out=maskU.t[:], in_=dr["maskU"].t)
        S.dma("sp", [dr["rmask"]], [rmask], out=rmask.t[:], in_=dr["rmask"].t)
        S.dma("sp", [dr["hg_nw"]], [NWt], out=NWt.t[:], in_=dr["hg_nw"].t)
        S.dma("sp", [dr["hg_lb"]], [lbr], out=lbr.t[:], in_=dr["hg_lb"].t)
        S.op("dve", [], [epsR], lambda e: e.memset(epsR.t[:], 1e-6))
        S.op("dve", [lbr], [lb], lambda e: e.tensor_tensor(out=lb.t[:], in0=lbr.t[:, 0, :], in1=lbr.t[:, 1, :], op=ALU.subtract))
        S.op("act", [lb], [lb], lambda e: e.activation(out=lb.t[:], in_=lb.t[:], func=AF.Sigmoid))
        S.op("dve", [lb], [oml], lambda e: e.tensor_scalar(out=oml.t[:], in0=lb.t[:], scalar1=-1.0, scalar2=1.0, op0=ALU.mult, op1=ALU.add))
        S.op("dve", [oml], [noml], lambda e: e.tensor_scalar(out=noml.t[:], in0=oml.t[:], scalar1=-1.0, scalar2=None, op0=ALU.mult))
        for h in range(NH):
            S.op("dve", [], [state[h]], lambda e, h=h: e.memset(state[h].t[:], 0.0))
            S.op("dve", [], [stateb[h]], lambda e, h=h: e.memset(stateb[h].t[:], 0.0))

        for tb in range(SEQ // T):
            t0 = tb * T
            for k in range(KC):
                S.dma("pool", [dr["xT"]], [aT[k]], out=aT[k].t[:, :], in_=dr["xT"].t[k * 128:(k + 1) * 128, t0:t0 + T])
            cols = []
            hs_ = c.get("hstride", 128)
            for h in heads:
                cols += [col_q + h * hs_, col_f + h * hs_, col_i + h * hs_, col_g + h * hs_]

            def evac(i, ps):
                hi, kind = divmod(i, 4)
                if kind == 0:
                    S.op("act", [ps], [qf[hi]], lambda e: e.activation(out=qf[hi].t[:, :], in_=ps.t[:, :], func=AF.Copy))
                elif kind == 1:
                    S.op("act", [ps], [sg[hi]], lambda e: e.activation(out=sg[hi].t[:, :], in_=ps.t[:, :], func=AF.Sigmoid))
                elif kind == 2:
                    S.op("act", [ps], [iTb[hi]], lambda e: e.activation(out=iTb[hi].t[:, :], in_=ps.t[:, :], func=AF.Copy))
                else:
                    S.op("dve", [ps], [gf[hi]], lambda e: e.tensor_copy(out=gf[hi].t[:, :], in_=ps.t[:, :]))
            _linear(S, wring, NW, dr["wA"], aT, T, cols, evac, plin)

            for h in range(NH):
                f_ = tmp.next()
                S.op("dve", [sg[h], oml, lb], [f_], lambda e: e.tensor_scalar(out=f_.t[:, :], in0=sg[h].t[:, :], scalar1=oml.t[:, h:h + 1], scalar2=lb.t[:, h:h + 1],
                                                                               op0=ALU.mult, op1=ALU.add))
                lf = tmp.next()
                S.op("act", [f_], [lf], lambda e: e.activation(out=lf.t[:, :], in_=f_.t[:, :], func=AF.Ln))
                k_ = tmp.next()
                S.op("dve", [sg[h], noml, oml], [k_], lambda e: e.tensor_scalar(out=k_.t[:, :], in0=sg[h].t[:, :], scalar1=noml.t[:, h:h + 1], scalar2=oml.t[:, h:h + 1],
                                                                                 op0=ALU.mult, op1=ALU.add))
                b_ = tmp.next()
                S.op("dve", [rmask, lf], [b_], lambda e: e.tensor_tensor_scan(out=b_.t[:, :], data0=rmask.t[:, :], data1=lf.t[:, :], initial=0.0, op0=ALU.mult, op1=ALU.add))
                S.op("act", [b_], [eb[h]], lambda e: e.activation(out=eb[h].t[:, :], in_=b_.t[:, :], func=AF.Exp))
                enb = tmp.next()
                S.op("act", [b_], [enb], lambda e: e.activation(out=enb.t[:, :], in_=b_.t[:, :], func=AF.Exp, scale=-1.0))
                S.op("dve", [qf[h], eb[h]], [qd[h]], lambda e: e.tensor_tensor(out=qd[h].t[:, :], in0=qf[h].t[:, :], in1=eb[h].t[:, :], op=ALU.mult))
                S.op("pool", [k_, enb], [kd[h]], lambda e: e.tensor_tensor(out=kd[h].t[:, :], in0=k_.t[:, :], in1=enb.t[:, :], op=ALU.mult))

            for cc in range(T // C):
                cs = slice(cc * C, (cc + 1) * C)
                for h in range(NH):
                    TR = ptr.next()
                    P = pwk.next()
                    S.op("pe", [kd[h], identb], [TR], lambda e: e.matmul(TR.t[0:64, 0:128], lhsT=kd[h].t[:, cs], rhs=identb.t[:], start=True, stop=True))
                    S.op("pe", [iTb[h], identb], [TR], lambda e: e.matmul(TR.t[0:64, 128:256], lhsT=iTb[h].t[:, cs], rhs=identb.t[:], start=True, stop=True))
                    S.op("pe", [gf[h], ident], [TR], lambda e: e.transpose(out=TR.t[0:64, 256:384], in_=gf[h].t[:, cs], identity=ident.t[:]))
                    kv_ = kvt.next()
                    S.op("dve", [TR], [kv_], lambda e: e.tensor_copy(out=kv_.t[:, :], in_=TR.t[0:64, 0:256]))
                    g_ = gate.next()
                    S.op("act", [TR], [g_], lambda e: e.activation(out=g_.t[:, :], in_=TR.t[0:64, 256:384], func=AF.Silu))
                    S.op("pe", [kd[h], qd[h]], [P], lambda e: e.matmul(P.t[0:64, 0:64], lhsT=kd[h].t[:, cs], rhs=qd[h].t[:, cs], start=True, stop=True))
                    sc_ = scb.next()
                    S.op("dve", [P, maskU], [sc_], lambda e: e.tensor_tensor(out=sc_.t[:, :], in0=P.t[0:64, 0:64], in1=maskU.t[:, :], op=ALU.mult))
                    S.op("pe", [sc_, kv_], [P], lambda e: e.matmul(P.t[0:64, 64:192], lhsT=sc_.t[:, :], rhs=kv_.t[:, 128:256], start=True, stop=False))
                    S.op("pe", [qd[h], stateb[h]], [P], lambda e: e.matmul(P.t[0:64, 64:192], lhsT=qd[h].t[:, cs], rhs=stateb[h].t[:, :], start=False, stop=True))
                    S.op("pe", [kv_], [P], lambda e: e.matmul(P.t[:, 256:384], lhsT=kv_.t[:, 0:128], rhs=kv_.t[:, 128:256], start=True, stop=True))
                    s1 = st1.next()
                    S.op("dve", [state[h], P], [s1], lambda e: e.tensor_tensor(out=s1.t[:, :], in0=state[h].t[:, :], in1=P.t[:, 256:384], op=ALU.add))
                    S.op("dve", [s1, eb[h]], [state[h]], lambda e: e.tensor_scalar(out=state[h].t[:, :], in0=s1.t[:, :], scalar1=eb[h].t[:, cc * C + C - 1:cc * C + C], scalar2=None,
                                                                                  op0=ALU.mult))
                    S.op("act", [state[h]], [stateb[h]], lambda e: e.activation(out=stateb[h].t[:, :], in_=state[h].t[:, :], func=AF.Copy))
                    jk = junk.next()
                    s_ = ss.next()
                    S.op("act", [P], [jk, s_], lambda e: e.activation(out=jk.t[:, :], in_=P.t[0:64, 64:192], func=AF.Square, accum_out=s_.t[:, 0:1]))
                    S.op("act", [s_, epsR], [s_], lambda e: e.activation(out=s_.t[:, 1:2], in_=s_.t[:, 0:1], func=AF.Sqrt, scale=1.0 / 128.0, bias=epsR.t[0:64, 0:1]))
                    S.op("dve", [s_], [s_], lambda e: e.reciprocal(out=s_.t[:, 2:3], in_=s_.t[:, 1:2]))
                    t1 = t1r.next()
                    S.op("dve", [P, s_, NWt], [t1], lambda e: e.scalar_tensor_tensor(out=t1.t[:, :], in0=P.t[0:64, 64:192], scalar=s_.t[:, 2:3], in1=NWt.t[0:64, :],
                                                                                   op0=ALU.mult, op1=ALU.mult))
                    t2 = t2r.next()
                    S.op("pool", [t1, g_], [t2], lambda e: e.tensor_tensor(out=t2.t[:, :], in0=t1.t[:, :], in1=g_.t[:, :], op=ALU.mult))
                    S.op("pe", [t2, identb], [P], lambda e: e.matmul(P.t[:, 384:448], lhsT=t2.t[:, :], rhs=identb.t[0:64, 0:64], start=True, stop=True))
                    S.op("act", [P], [oS[h]], lambda e: e.activation(out=oS[h].t[:, cs], in_=P.t[:, 384:448], func=AF.Copy))
            for h in range(NH):
                r0 = orow0 + h * 128
                _ores, _oap = _oput(dr, r0, t0, T)
                S.dma("sp", [oS[h]], [_ores], out=_oap, in_=oS[h].t[:, :])
        S.barrier()
    S.es = oes


NEGB = 30000.0


def emit_nsa(S, c, dr):
    D, SEQ = c["D"], c["SEQ"]
    KC = D // 128
    T = 512
    R = 8
    NW = c.get("NW", 3)
    NT = SEQ // 128
    NB = SEQ // T
    NMT = (SEQ // 16 + 127) // 128
    NSLOT = NMT * 128
    scale = 128.0 ** -0.5
    oes = S.es
    S.new_phase()
    with ExitStack() as pes:
        S.es = pes
        _, aT = views(S, "aT", [128, KC, T], BF16, KC, lambda t, i: t[:, i, :])
        wring = make_wring(S, "wsl", NW)
        ident = S.sbuf("ident", [128, 128], F32)
        identb = S.sbuf("identb", [128, 128], BF16)
        ones_b = S.sbuf("ones_b", [128, 128], BF16)
        rotm = S.sbuf("rotm", [128, 128], BF16)
        ksT = S.sbuf("ksT", [128, SEQ], BF16)
        _, vsT = views(S, "vsT", [128, NT, 128], BF16, NT, lambda t, i: t[:, i, :])
        kwT = S.sbuf("kwT", [128, 1024], BF16)
        _, vwT = views(S, "vwT", [128, 8, 128], BF16, 8, lambda t, i: t[:, i, :])
        kcmpT = S.sbuf("kcmpT", [128, NSLOT], BF16)
        vcmpT = S.sbuf("vcmpT", [128, NSLOT], F32)
        _, vcmp = views(S, "vcmp", [128, NMT, 128], BF16, NMT, lambda t, i: t[:, i, :])
        kcb = S.sbuf("kcb", [128, 16 + T], BF16)
        vcb = S.sbuf("vcb", [128, 16 + T], BF16)
        w2 = S.sbuf("w2", [128, 2, 128], BF16)
        posT = S.sbuf("posT", [128, 2, 32], BF16)
        posb = S.sbuf("posb", [128, 2], F32)
        ov = S.sbuf("ov", [128, NMT, 128], BF16)
        ex32 = S.sbuf("ex32", [32, 16, 128], BF16)
        wmask = S.sbuf("wmask", [128, 8, T], BF16)
        cmask = S.sbuf("cmask", [128, 5, T], BF16)
        ABt = S.sbuf("ABt", [128, 256], F32)
        FBt = S.sbuf("FBt", [128, 256], F32)
        qn = [S.sbuf(f"qn{r}", [128, T], BF16) for r in range(R)]
        qr = [S.sbuf(f"qr{r}", [128, T], BF16) for r in range(R)]
        cosb = S.sbuf("cosb", [128, T], F32)
        sinb = S.sbuf("sinb", [128, T], F32)
        gT = S.sbuf("gT", [32, T], F32)
        gtok = S.sbuf("gtok", [128, 4, 24], F32)
        _, otok = views(S, "otok", [128, 4, R * 128], F32, 4, lambda t, i: t[:, i, :])
        selT = [[S.sbuf(f"selT{g}_{i}", [32, T], BF16) for g in range(4)] for i in range(1)][0]
        tf = Ring([S.sbuf(f"tf{i}", [128, T], F32) for i in range(4)])
        tb16 = Ring([S.sbuf(f"tb16{i}", [128, T], BF16) for i in range(3)])
        Er = Ring([S.sbuf(f"Er{i}", [128, T], BF16) for i in range(6)])
        Pr = Ring([S.sbuf(f"Pr{i}", [128, T], BF16) for i in range(3)])
        hid = Ring([S.sbuf(f"hid{i}", [128, 32], BF16) for i in range(2)])
        sm = Ring([S.sbuf(f"sm{i}", [128, 128], F32) for i in range(4)])
        smb = Ring([S.sbuf(f"smb{i}", [128, 128], BF16) for i in range(2)])
        m8 = Ring([S.sbuf(f"m8{i}", [128, 16], F32) for i in range(2)])
        fac = Ring([S.sbuf(f"fac{i}", [128, 8], F32) for i in range(4)])
        oS = Ring([S.sbuf(f"oS{i}", [128, T], BF16) for i in range(2)])
        pb = [S.psum(f"pb{i}", [128, 512], F32) for i in range(8)]
        plin = Ring([pb[0], pb[1], pb[2], pb[3]])
        pS = Ring(pb[2:4])
        pO = Ring([pb[4], pb[7]])
        pD = pb[5]
        pI = pb[6]
        pmisc = Ring([pb[4], pb[7], pb[5]])

        def ld(q, name, dst, src=None):
            S.dma(q, [dr[name]], [dst], out=dst.t[:], in_=(dr[name].t if src is None else src))
        ld("sp", "ident", ident); ld("sp", "identb", identb); ld("sp", "rotm", rotm)
        ld("pool", "cmp_w2", w2); ld("pool", "cmp_posT", posT)
        ld("sp", "ov", ov); ld("sp", "ex32", ex32); ld("sp", "wmask", wmask); ld("sp", "cmask", cmask)
        ld("sp", "ABt", ABt); ld("sp", "FBt", FBt)
        S.op("dve", [], [ones_b], lambda e: e.memset(ones_b.t[:], 1.0))
        S.op("dve", [], [kcb], lambda e: e.memset(kcb.t[:], 0.0))
        S.op("dve", [], [vcb], lambda e: e.memset(vcb.t[:], 0.0))
        S.op("dve", [], [vcmpT], lambda e: e.memset(vcmpT.t[:], 0.0))
        S.op("dve", [], [kcmpT], lambda e: e.memset(kcmpT.t[:], 0.0))

        def load_w1(j):
            slot = wring.next()
            v = slot.t[:, :].rearrange("p (k c) -> p k c", c=128)
            S.dma("pool", [dr["cmp_w1"]], [slot], out=v, in_=dr["cmp_w1"].t[j].rearrange("l d e -> d l e"))
            return slot, v
        for j in range(2):
            slot, sv = load_w1(j)
            ps = pmisc.next()
            for l in range(32):
                S.op("pe", [slot, posT], [ps], lambda e, l=l: e.matmul(ps.t[:, 0:1], lhsT=sv[:, l, :], rhs=posT.t[:, j, l:l + 1], start=(l == 0), stop=(l == 31)))
            S.op("act", [ps], [posb], lambda e: e.activation(out=posb.t[:, j:j + 1], in_=ps.t[:, 0:1], func=AF.Copy))

        CQ, CKC, CVC, CKS, CVS, CKW, CVW, CG = 0, 1024, 1152, 1280, 1408, 1536, 1664, 1792

        def rope(src_f32, dst_bf, sc):
            sb = tb16.next()
            S.op("act", [src_f32], [sb], lambda e: e.activation(out=sb.t[:, :], in_=src_f32.t[:, :], func=AF.Copy))
            ps = pmisc.next()
            S.op("pe", [rotm, sb], [ps], lambda e: e.matmul(ps.t[:, :], lhsT=rotm.t[:], rhs=sb.t[:, :], start=True, stop=True))
            t1 = tf.next()
            S.op("dve", [src_f32, cosb], [t1], lambda e: e.tensor_tensor(out=t1.t[:, :], in0=src_f32.t[:, :], in1=cosb.t[:, :], op=ALU.mult))
            t2 = tf.next()
            S.op("dve", [ps, sinb], [t2], lambda e: e.tensor_tensor(out=t2.t[:, :], in0=ps.t[:, :], in1=sinb.t[:, :], op=ALU.mult))
            S.op("pool", [t1, t2], [dst_bf[0]], lambda e: e.tensor_tensor(out=dst_bf[1], in0=t1.t[:, :], in1=t2.t[:, :], op=ALU.add))

        def to_tok(src_f32, dst_views, base):
            for j in range(T // 128):
                pt = pmisc.next()
                S.op("pe", [src_f32, ident], [pt], lambda e, j=j, pt=pt: e.transpose(out=pt.t[:, 0:128], in_=src_f32.t[:, j * 128:(j + 1) * 128], identity=ident.t[:]))
                dv = dst_views[base + j]
                S.op("dve", [pt], [dv], lambda e, pt=pt, dv=dv: e.tensor_copy(out=dv.t[:, :], in_=pt.t[:, 0:128]))

        for tb in range(NB):
            t0 = tb * T
            gt0 = t0 // 128
            for k in range(KC):
                if "hAG_get" in dr:
                    _hres, _hap = dr["hAG_get"](k, t0, T)
                    S.dma("sp", [_hres], [aT[k]], out=aT[k].t[:, :], in_=_hap)
                else:
                    S.dma("sp", [dr["hT"]], [aT[k]], out=aT[k].t[:, :], in_=dr["hT"].t[k * 128:(k + 1) * 128, t0:t0 + T])
            S.dma("sp", [dr["cosT"]], [cosb], out=cosb.t[:, :], in_=dr["cosT"].t[:, t0:t0 + T])
            S.dma("sp", [dr["sinT"]], [sinb], out=sinb.t[:, :], in_=dr["sinT"].t[:, t0:t0 + T])
            if tb > 0:
                S.op("dve", [kcb], [kcb], lambda e: e.tensor_copy(out=kcb.t[:, 0:16], in_=kcb.t[:, T:T + 16]))
                S.op("dve", [vcb], [vcb], lambda e: e.tensor_copy(out=vcb.t[:, 0:16], in_=vcb.t[:, T:T + 16]))
            cols = [CQ + r * 128 for r in range(R)] + [CKC, CVC, CKS, CVS, CKW, CVW]

            def evac(i, ps):
                if i < R:
                    qf = tf.next()
                    S.op("act", [ps], [qf], lambda e: e.activation(out=qf.t[:, :], in_=ps.t[:, :], func=AF.Copy, scale=scale))
                    S.op("dve", [qf], [qn[i]], lambda e: e.tensor_copy(out=qn[i].t[:, :], in_=qf.t[:, :]))
                    rope(qf, (qr[i], qr[i].t[:, :]), 1.0)
                elif i == R:
                    S.op("act", [ps], [kcb], lambda e: e.activation(out=kcb.t[:, 16:16 + T], in_=ps.t[:, :], func=AF.Copy))
                elif i == R + 1:
                    S.op("act", [ps], [vcb], lambda e: e.activation(out=vcb.t[:, 16:16 + T], in_=ps.t[:, :], func=AF.Copy))
                elif i == R + 2:
                    kf = tf.next()
                    S.op("act", [ps], [kf], lambda e: e.activation(out=kf.t[:, :], in_=ps.t[:, :], func=AF.Copy))
                    rope(kf, (ksT, ksT.t[:, t0:t0 + T]), 1.0)
                elif i == R + 3:
                    vf = tf.next()
                    S.op("act", [ps], [vf], lambda e: e.activation(out=vf.t[:, :], in_=ps.t[:, :], func=AF.Copy))
                    to_tok(vf, vsT, gt0)
                elif i == R + 4:
                    kf = tf.next()
                    S.op("act", [ps], [kf], lambda e: e.activation(out=kf.t[:, :], in_=ps.t[:, :], func=AF.Copy))
                    c0 = (tb % 2) * T
                    rope(kf, (kwT, kwT.t[:, c0:c0 + T]), 1.0)
                else:
                    vf = tf.next()
                    S.op("act", [ps], [vf], lambda e: e.activation(out=vf.t[:, :], in_=ps.t[:, :], func=AF.Copy))
                    to_tok(vf, vwT, (gt0 % 8))
            _linear(S, wring, NW, dr["wC"], aT, T, cols, evac, plin)
            slot = wring.next()
            gv = slot.t[:, :].rearrange("p (k c) -> p k c", c=128)[:, 0:KC, 0:24]
            S.dma("pool", [dr["wC"]], [slot], out=gv, in_=dr["wC"].t[0:KC * 128, CG:CG + 24].rearrange("(kc p) c -> p kc c", p=128))
            ps = plin.next()
            for k in range(KC):
                S.op("pe", [slot, aT[k]], [ps], lambda e, k=k: e.matmul(ps.t[0:24, :], lhsT=gv[:, k, 0:24], rhs=aT[k].t[:, :], start=(k == 0), stop=(k == KC - 1)))
            S.op("act", [ps], [gT], lambda e: e.activation(out=gT.t[0:24, :], in_=ps.t[0:24, :], func=AF.Sigmoid))
            for j in range(4):
                pt = pmisc.next()
                S.op("pe", [gT, ident], [pt], lambda e, j=j, pt=pt: e.transpose(out=pt.t[:, 0:24], in_=gT.t[0:24, j * 128:(j + 1) * 128], identity=ident.t[0:24, 0:24]))
                S.op("dve", [pt], [gtok], lambda e, j=j, pt=pt: e.tensor_copy(out=gtok.t[:, j, :], in_=pt.t[:, 0:24]))

            for j, (buf, dstT) in enumerate(((kcb, kcmpT), (vcb, vcmpT))):
                slot, sv = load_w1(j)
                ps = pmisc.next()
                for l in range(32):
                    S.op("pe", [slot, buf], [ps], lambda e, l=l: e.matmul(ps.t[:, 0:32], lhsT=sv[:, l, :], rhs=buf.t[:, l:l + 16 * 31 + 1:16], start=(l == 0), stop=(l == 31)))
                hd = hid.next()
                S.op("act", [ps, posb], [hd], lambda e: e.activation(out=hd.t[:, :], in_=ps.t[:, 0:32], func=AF.Gelu, bias=posb.t[:, j:j + 1]))
                ps2 = pmisc.next()
                S.op("pe", [w2, hd], [ps2], lambda e: e.matmul(ps2.t[:, 0:32], lhsT=w2.t[:, j, :], rhs=hd.t[:, :], start=True, stop=True))
                S.op("act", [ps2], [dstT], lambda e: e.activation(out=dstT.t[:, 32 * tb:32 * tb + 32], in_=ps2.t[:, 0:32], func=AF.Copy))
            mtc = (32 * tb) // 128
            pt = pmisc.next()
            S.op("pe", [vcmpT, ident], [pt], lambda e: e.transpose(out=pt.t[:, 0:128], in_=vcmpT.t[:, mtc * 128:(mtc + 1) * 128], identity=ident.t[:]))
            S.op("dve", [pt], [vcmp[mtc]], lambda e: e.tensor_copy(out=vcmp[mtc].t[:, :], in_=pt.t[:, 0:128]))

            nmt = mtc + 1
            for r in range(R):
                Es = []
                for mt in range(nmt):
                    v = tb - 4 * mt
                    ps = pS.next()
                    mks = ([v] if v < 4 else []) + ([4] if mt == 0 else [])
                    S.op("pe", [kcmpT, qn[r]], [ps], lambda e: e.matmul(ps.t[:, :], lhsT=kcmpT.t[:, mt * 128:(mt + 1) * 128], rhs=qn[r].t[:, :], start=True, stop=not mks))
                    for mi, mv in enumerate(mks):
                        S.op("pe", [identb, cmask], [ps], lambda e: e.matmul(ps.t[:, :], lhsT=identb.t[:], rhs=cmask.t[:, mv, :], start=False, stop=(mi == len(mks) - 1)))
                    E = Er.next()
                    S.op("act", [ps], [E], lambda e: e.activation(out=E.t[:, :], in_=ps.t[:, :], func=AF.Exp))
                    Es.append(E)
                pden = pS.next()
                for mt, E in enumerate(Es):
                    S.op("pe", [ones_b, E], [pden], lambda e, mt=mt, E=E: e.matmul(pden.t[:, :], lhsT=ones_b.t[:], rhs=E.t[:, :], start=(mt == 0), stop=(mt == nmt - 1)))
                rd = tf.next()
                S.op("dve", [pden], [rd], lambda e: e.tensor_scalar(out=rd.t[:, :], in0=pden.t[:, :], scalar1=1e-30, scalar2=None, op0=ALU.max))
                S.op("dve", [rd], [rd], lambda e: e.reciprocal(out=rd.t[:, :], in_=rd.t[:, :]))
                po = pO.next()
                Ps = []
                for mt, E in enumerate(Es):
                    P = Pr.next()
                    S.op("dve", [E, rd], [P], lambda e, E=E, P=P: e.tensor_tensor(out=P.t[:, :], in0=E.t[:, :], in1=rd.t[:, :], op=ALU.mult))
                    for tt in range(4):
                        S.op("pe", [P, vcmp[mt]], [po], lambda e, tt=tt, mt=mt, P=P: e.matmul(po.t[:, tt * 128:(tt + 1) * 128], lhsT=P.t[:, tt * 128:(tt + 1) * 128], rhs=vcmp[mt].t[:, :],
                                                                                           start=(mt == 0 and tt == 0), stop=(mt == nmt - 1), skip_group_check=True))
                        S.op("pe", [P, ov], [pI], lambda e, tt=tt, mt=mt, P=P: e.matmul(pI.t[:, tt * 128:(tt + 1) * 128], lhsT=P.t[:, tt * 128:(tt + 1) * 128], rhs=ov.t[:, mt, :],
                                                                                     start=(r == 0 and mt == 0 and tt == 0), stop=(r == R - 1 and mt == nmt - 1), skip_group_check=True))
                for tt in range(4):
                    S.op("dve", [po, gtok], [otok[tt]], lambda e, tt=tt: e.tensor_scalar(out=otok[tt].t[:, r * 128:(r + 1) * 128], in0=po.t[:, tt * 128:(tt + 1) * 128],
                                                                                      scalar1=gtok.t[:, tt, 3 * r:3 * r + 1], scalar2=None, op0=ALU.mult))

            for tt in range(4):
                gt = gt0 + tt
                x0 = 128 - 2 * gt
                s1 = sm.next()
                S.op("dve", [pI, ABt], [s1], lambda e: e.scalar_tensor_tensor(out=s1.t[:, :], in0=pI.t[:, tt * 128:(tt + 1) * 128], scalar=1.0, in1=ABt.t[:, x0:x0 + 128],
                                                                             op0=ALU.add, op1=ALU.mult))
                s2 = sm.next()
                S.op("dve", [s1, FBt], [s2], lambda e: e.scalar_tensor_tensor(out=s2.t[:, :], in0=s1.t[:, :], scalar=-1.0, in1=FBt.t[:, x0:x0 + 128], op0=ALU.add, op1=ALU.max))
                S.op("dve", [s2], [s2], lambda e: e.memset(s2.t[:, 0:1], 1e9))
                mm = m8.next()
                S.op("dve", [s2], [mm], lambda e: e.max(out=mm.t[:, 0:8], in_=s2.t[:, :]))
                s3 = sm.next()
                S.op("dve", [s2, mm], [s3], lambda e: e.match_replace(out=s3.t[:, :], in_to_replace=mm.t[:, 0:8], in_values=s2.t[:, :], imm_value=-1e30))
                S.op("dve", [s3], [mm], lambda e: e.max(out=mm.t[:, 8:16], in_=s3.t[:, :]))
                sb_ = smb.next()
                S.op("dve", [s2, mm], [sb_], lambda e: e.tensor_scalar(out=sb_.t[:, :], in0=s2.t[:, :], scalar1=mm.t[:, 15:16], scalar2=1.0, op0=ALU.is_ge, op1=ALU.subtract))
                for g in range(4):
                    pt = pmisc.next()
                    S.op("pe", [sb_, identb], [pt], lambda e, g=g, pt=pt: e.matmul(pt.t[0:32, 0:128], lhsT=sb_.t[:, 32 * g:32 * g + 32], rhs=identb.t[:], start=True, stop=True))
                    S.op("act", [pt], [selT[g]], lambda e, g=g, pt=pt: e.activation(out=selT[g].t[:, tt * 128:(tt + 1) * 128], in_=pt.t[0:32, 0:128], func=AF.Copy))

            def attend(r, kTsrc, kcol, vviews, vidx, kts, use_sel, gidx):
                po = pO.next()
                n = len(kts)
                pss = {}

                def emit_s(ni):
                    kt = kts[ni]
                    rel = kt - gt0
                    ps = pS.next()
                    pss[ni] = ps
                    has_w = (not use_sel) or (rel >= 0)
                    S.op("pe", [kTsrc, qr[r]], [ps], lambda e: e.matmul(ps.t[:, :], lhsT=kTsrc.t[:, kcol(kt):kcol(kt) + 128], rhs=qr[r].t[:, :], start=True,
                                                                       stop=not (use_sel or has_w)))
                    if use_sel:
                        g = kt // 16
                        S.op("pe", [ex32, selT[g]], [ps], lambda e: e.matmul(ps.t[:, :], lhsT=ex32.t[:, kt % 16, :], rhs=selT[g].t[:, :], start=False, stop=not has_w))
                    if has_w:
                        S.op("pe", [identb, wmask], [ps], lambda e: e.matmul(ps.t[:, :], lhsT=identb.t[:], rhs=wmask.t[:, rel + 4, :], start=False, stop=True))
                emit_s(0)
                for ni, kt in enumerate(kts):
                    if ni + 1 < n:
                        emit_s(ni + 1)
                    ps = pss.pop(ni)
                    E = Er.next()
                    S.op("act", [ps], [E], lambda e: e.activation(out=E.t[:, :], in_=ps.t[:, :], func=AF.Exp))
                    vv = vviews[vidx(kt)]
                    for tt in range(4):
                        S.op("pe", [E, vv], [po], lambda e, tt=tt: e.matmul(po.t[:, tt * 128:(tt + 1) * 128], lhsT=E.t[:, tt * 128:(tt + 1) * 128], rhs=vv.t[:, :],
                                                                          start=(ni == 0 and tt == 0), stop=(ni == n - 1), skip_group_check=True))
                        S.op("pe", [E, ones_b], [pD], lambda e, tt=tt: e.matmul(pD.t[:, (r * 4 + tt) * 2:(r * 4 + tt) * 2 + 1], lhsT=E.t[:, tt * 128:(tt + 1) * 128], rhs=ones_b.t[:, 0:1],
                                                                              start=(ni == 0 and tt == 0), stop=(ni == n - 1), skip_group_check=True))
                fc = fac.next()
                S.op("dve", [pD], [fc], lambda e: e.reciprocal(out=fc.t[:, 0:4], in_=pD.t[:, r * 8:r * 8 + 8:2]))
                for tt in range(4):
                    S.op("dve", [fc, gtok], [fc], lambda e, tt=tt: e.tensor_tensor(out=fc.t[:, 4 + tt:5 + tt], in0=fc.t[:, tt:tt + 1], in1=gtok.t[:, tt, 3 * r + gidx:3 * r + gidx + 1], op=ALU.mult))
                    S.op("dve", [po, fc, otok[tt]], [otok[tt]], lambda e, tt=tt: e.scalar_tensor_tensor(
                        out=otok[tt].t[:, r * 128:(r + 1) * 128], in0=po.t[:, tt * 128:(tt + 1) * 128], scalar=fc.t[:, 4 + tt:5 + tt],
                        in1=otok[tt].t[:, r * 128:(r + 1) * 128], op0=ALU.mult, op1=ALU.add))

            for r in range(R):
                attend(r, ksT, lambda kt: kt * 128, vsT, lambda kt: kt, list(range(0, gt0 + 4)), True, 1)
                attend(r, kwT, lambda kt: (kt * 128) % 1024, vwT, lambda kt: kt % 8, list(range(max(0, gt0 - 4), gt0 + 4)), False, 2)

            for r in range(R):
                pt = pmisc.next()
                for tt in range(4):
                    S.op("pe", [otok[tt], ident], [pt], lambda e, tt=tt: e.transpose(out=pt.t[:, tt * 128:(tt + 1) * 128], in_=otok[tt].t[:, r * 128:(r + 1) * 128], identity=ident.t[:]))
                o_ = oS.next()
                S.op("act", [pt], [o_], lambda e: e.activation(out=o_.t[:, :], in_=pt.t[:, :], func=AF.Copy))
                _ores, _oap = _oput(dr, r * 128, t0, T)
                S.dma("sp", [o_], [_ores], out=_oap, in_=o_.t[:, :])
        S.barrier()
    S.es = oes


BF = ml_dtypes.bfloat16
def make_consts():
    c = {}
    c['ident'] = np.eye(128, dtype=np.float32)
    c['identb'] = np.eye(128).astype(BF)
    j = np.arange(128)[:, None]; s = np.arange(128)[None, :]
    c['negU'] = np.where(j >= s, -1.0, 0.0).astype(BF)
    sl = np.arange(128)[:, None, None]; kd = np.arange(4)[None, :, None]; t = np.arange(512)[None, None, :]
    c['sbmask'] = np.where(128 * kd + sl < t, 0.0, -30000.0).astype(BF)
    return c
def make_consts_hg():
    c = {}
    s = np.arange(64)[:, None]; t = np.arange(64)[None, :]
    c['maskU'] = (s <= t).astype(np.float32)
    c['rmask'] = np.tile((np.arange(512) % 64 != 0).astype(np.float32)[None, :], (128, 1))
    return c
def make_consts_nsa(S):
    c = {}
    rot = np.zeros((128, 128), np.float32)
    for dp in range(64):
        rot[dp + 64, dp] = -1.0
    for dp in range(64, 128):
        rot[dp - 64, dp] = 1.0
    c['rotm'] = rot.astype(BF)
    NMT = (S // 16 + 127) // 128
    m = np.arange(NMT * 128); n = m - 1
    j = np.arange(128)
    ovm = (n[:, None] >= 0) & (16 * n[:, None] < 64 * j[None, :] + 64) & (16 * n[:, None] + 32 > 64 * j[None, :]) & (j[None, :] < S // 64) & (n[:, None] <= (S - 32) // 16)
    c['ov'] = np.ascontiguousarray(ovm.reshape(NMT, 128, 128).transpose(1, 0, 2)).astype(BF)
    i = np.arange(32)[:, None, None]; ktl = np.arange(16)[None, :, None]; s = np.arange(128)[None, None, :]
    c['ex32'] = np.where(i == 2 * ktl + s // 64, 30000.0, 0.0).astype(BF)
    sl = np.arange(128)[:, None, None]; rel = (np.arange(8) - 4)[None, :, None]; tl = np.arange(512)[None, None, :]
    c['wmask'] = np.where((128 * rel + sl <= tl) & (128 * rel + sl > tl - 512), 0.0, -30000.0).astype(BF)
    v = np.arange(4)[None, :, None]
    cm = np.where(16 * sl + 15 <= 512 * v + tl, 0.0, -30000.0)
    cm4 = np.where(sl == 0, -30000.0, 0.0) + 0 * tl
    c['cmask'] = np.concatenate([cm, cm4], axis=1).astype(BF)
    p = np.arange(128)[:, None]; x = np.arange(256)[None, :]
    cc = (p >= 64).astype(np.int64)
    c['ABt'] = ((x - 128) <= cc).astype(np.float32)
    c['FBt'] = np.where(((x - 128) == cc) | ((x - 128) == cc - 1), 1e9, -2.0).astype(np.float32)
    half = 64
    inv = 10000.0 ** (-np.arange(half, dtype=np.float32) / half)
    ang = np.arange(S, dtype=np.float32)[None, :] * np.concatenate([inv, inv])[:, None]
    c['cosT'] = np.cos(ang).astype(np.float32)
    c['sinT'] = np.sin(ang).astype(np.float32)
    return c


from concourse.bass_utils import run_bass_kernel_spmd

D_MODEL = 4096
SEQ = 8192
BATCH = 2
DFF = 11008
NMEM = 256
ALPHA = 4.0 ** 0.25
NCORE = 8
TOKC = 2048
HALO = 128
GROUPS = [[0, 1, 2, 3], [4, 5, 6, 7]]


def _consts_all():
    c = make_consts()
    c.update(make_consts_hg())
    c.update(make_consts_nsa(SEQ))
    return c


def _build_fused():
    nc = bass.Bass("TRN2", target_bir_lowering=False)
    NT = HALO + TOKC
    KC, FC = D_MODEL // 128, DFF // 128
    with ExitStack() as es:
        S = Sched(nc, es)
        dr = {}

        def din(name, shape, dt=F32):
            dr[name] = S.dram(name, shape, dt, kind="ExternalInput")
        din("xT", [D_MODEL, SEQ]); din("xTc", [D_MODEL, NT]); din("memT", [D_MODEL, NMEM])
        din("wA", [D_MODEL, 3584]); din("wC", [D_MODEL, 1816])
        din("ident", [128, 128]); din("identb", [128, 128], BF16); din("negU", [128, 128], BF16); din("sbmask", [128, 4, 512], BF16)
        din("maskU", [64, 64]); din("rmask", [128, 512]); din("hg_nw", [128, 128]); din("hg_lb", [128, 2, 4])
        din("cmp_w1", [2, 32, 128, 128]); din("cmp_w2", [128, 2, 128]); din("cmp_posT", [128, 2, 32])
        din("rotm", [128, 128], BF16)
        din("ov", [128, 4, 128], BF16); din("ex32", [32, 16, 128], BF16); din("wmask", [128, 8, 512], BF16); din("cmask", [128, 5, 512], BF16)
        din("ABt", [128, 256]); din("FBt", [128, 256]); din("cosT", [128, SEQ]); din("sinT", [128, SEQ])
        din("flag", [128, 1]); din("fsel", [128, 4]); din("fhal", [128, 4])
        for l in range(2):
            din(f"w_out{l}", [D_MODEL, D_MODEL]); din(f"w_q{l}", [D_MODEL, 512]); din(f"w_kv{l}", [D_MODEL, 1024]); din(f"w_o{l}", [512, D_MODEL])
            din(f"w_up{l}", [D_MODEL, 2 * DFF]); din(f"w_down{l}", [DFF, D_MODEL])
            din(f"ln_g{l}", [128, 3, KC]); din(f"ln_b{l}", [128, 3, KC]); din(f"conv{l}", [128, 3, FC])
        outT = S.dram("outT", [D_MODEL, TOKC], F32, kind="ExternalOutput")
        NB = SEQ // 512
        oA = [S.dram_cc(f"oA{i}", [1024, 512], BF16) for i in range(NB)]
        oAG0 = [S.dram_cc(f"oAG0_{i}", [4096, 512], BF16) for i in range(NB)]
        oC = [S.dram_cc(f"oC{i}", [1024, 512], BF16) for i in range(NB)]
        oAG1 = [S.dram_cc(f"oAG1_{i}", [4096, 512], BF16) for i in range(NB)]
        h1loc = S.dram_cc("h1loc", [D_MODEL, NT], F32)
        hb = [S.dram_cc(f"hb{i}", [256, TOKC], BF16) for i in range(16)]
        hAG = [S.dram_cc(f"hAG{i}", [4 * 256, TOKC], BF16) for i in range(16)]
        hl = [S.dram_cc(f"hl{i}", [2048, 128], F32) for i in range(2)]
        hlAG = [S.dram_cc(f"hlAG{i}", [4 * 2048, 128], F32) for i in range(2)]

        def put_fn(lst):
            def f(r0, t0, T):
                return lst[t0 // 512], lst[t0 // 512].t[r0:r0 + 128, 0:T]
            return f

        def oget_fn(lst):
            def f(k, cb, T):
                i, off = divmod(cb, 512)
                return lst[i], lst[i].t[k * 128:(k + 1) * 128, off:off + T]
            return f

        def hb_put(k, c0, T):
            return hb[k // 2], hb[k // 2].t[(k % 2) * 128:(k % 2) * 128 + 128, c0:c0 + T]

        def hAG_get(k, t0, T):
            rk, tl = divmod(t0, TOKC)
            r0 = rk * 256 + (k % 2) * 128
            return hAG[k // 2], hAG[k // 2].t[r0:r0 + 128, tl:tl + T]

        def hl_put(k):
            return hl[k // 16], hl[k // 16].t[(k % 16) * 128:(k % 16) * 128 + 128, 0:128]

        def hlAG_get(k, cq):
            r0 = cq * 2048 + (k % 16) * 128
            return hlAG[k // 16], hlAG[k // 16].t[r0:r0 + 128, 0:128]

        cfgA = dict(D=D_MODEL, SEQ=SEQ, NW=4)
        drA = dict(dr); drA["oT_put"] = put_fn(oA)
        emit_sb(S, cfgA, drA, [0, 1], 0, 256, 512, 512)
        emit_sb(S, cfgA, drA, [2, 3], 512, 768, 1024, 512)
        emit_hgrn(S, dict(cfgA, hstride=512), drA, [0, 1, 2, 3], 1536, 1664, 1792, 1920, 0)
        for i in range(NB):
            S.collective("AllGather", GROUPS, oA[i], oAG0[i])
        blocks = [(0, HALO, True)] + [(HALO + i * 512, 512, False) for i in range(TOKC // 512)]
        cfgP = dict(D=D_MODEL, DFF=DFF, XH=4, NM=NMEM, T=512, alpha=ALPHA, eps=1e-5, GF=8, NW=3, HALO=HALO, TOKC=TOKC)

        def post_dr(l):
            d = dict(flag=dr["flag"], fsel=dr["fsel"], fhal=dr["fhal"], memT=dr["memT"])
            for k in ("w_out", "w_q", "w_kv", "w_o", "w_up", "w_down", "ln_g", "ln_b", "conv"):
                d[k] = dr[f"{k}{l}"]
            return d
        drB = post_dr(0)
        drB.update(oAG_get=oget_fn(oAG0), hT_in=dr["xTc"], hT_out=h1loc, hTb_put=hb_put, hl_put=hl_put)
        emit_post(S, dict(cfgP, out_off=0, outb_off=HALO), drB, blocks)
        for i in range(16):
            S.collective("AllGather", GROUPS, hb[i], hAG[i])
        for i in range(2):
            S.collective("AllGather", GROUPS, hl[i], hlAG[i])
        drC = dict(dr); drC["hAG_get"] = hAG_get; drC["oT_put"] = put_fn(oC)
        emit_nsa(S, dict(D=D_MODEL, SEQ=SEQ, NW=3, TOKC=TOKC), drC)
        for i in range(NB):
            S.collective("AllGather", GROUPS, oC[i], oAG1[i])
        drD = post_dr(1)
        drD.update(oAG_get=oget_fn(oAG1), hT_in=h1loc, hlAG_get=hlAG_get, hT_out=outT)
        emit_post(S, dict(cfgP, out_off=HALO), drD, blocks)
        S.wait_all("sp", [outT])
        S.wait_all("pool", [outT])
    return nc


def _arr3(v, n):
    return np.ascontiguousarray(np.asarray(v, np.float32).reshape(3, n, 128).transpose(2, 0, 1))


def kernel(**inp):
    inp = {k: np.asarray(v) for k, v in inp.items()}
    cs = _consts_all()
    cores = list(range(NCORE))
    KC, FC = D_MODEL // 128, DFF // 128
    NT = HALO + TOKC
    xT = [np.ascontiguousarray(inp["x"][b].T) for b in range(BATCH)]
    memT = [np.ascontiguousarray(inp["mem"][b].T) for b in range(BATCH)]
    w_in = inp["ab_w_in"][0]
    AW = 2048
    wo = inp["ab_w_out"][0]
    wo_perm = np.ascontiguousarray(np.concatenate([np.concatenate([wo[512 * j:512 * j + 512], wo[2048 + 512 * j:2048 + 512 * j + 512]], axis=0) for j in range(4)], axis=0))
    wn = inp["nsa_w_in"][0]
    shared = dict(ident=cs["ident"], identb=cs["identb"], negU=cs["negU"], sbmask=cs["sbmask"], maskU=cs["maskU"], rmask=cs["rmask"],
                  hg_nw=np.ascontiguousarray(np.tile(inp["hgrn_norm_w"][0][None, :], (128, 1))),
                  cmp_w1=np.ascontiguousarray(inp["nsa_cmp_w1"][0]), cmp_w2=np.ascontiguousarray(inp["nsa_cmp_w2"][0].transpose(1, 0, 2)),
                  cmp_posT=np.ascontiguousarray(inp["nsa_cmp_pos"][0].transpose(2, 0, 1)),
                  rotm=cs["rotm"], ov=cs["ov"], ex32=cs["ex32"], wmask=cs["wmask"], cmask=cs["cmask"], ABt=cs["ABt"], FBt=cs["FBt"],
                  cosT=cs["cosT"], sinT=cs["sinT"])
    wouts = [wo_perm, np.ascontiguousarray(inp["nsa_w_out"][0])]
    for l in range(2):
        shared[f"w_out{l}"] = wouts[l]
        shared[f"w_q{l}"] = np.ascontiguousarray(inp["xa_w_q"][l]); shared[f"w_kv{l}"] = np.ascontiguousarray(inp["xa_w_kv"][l])
        shared[f"w_o{l}"] = np.ascontiguousarray(inp["xa_w_o"][l]); shared[f"w_up{l}"] = np.ascontiguousarray(inp["ffn_w_up"][l])
        shared[f"w_down{l}"] = np.ascontiguousarray(inp["ffn_w_down"][l])
        shared[f"ln_g{l}"] = _arr3(inp["ln_g"][l], KC); shared[f"ln_b{l}"] = _arr3(inp["ln_b"][l], KC); shared[f"conv{l}"] = _arr3(inp["ffn_conv"][l], FC)
    maps = []
    for c in cores:
        b, j = divmod(c, 4)
        hs = slice(512 * j, 512 * j + 512)
        def hc(seg, h0, n):
            c0 = seg * AW + 512 * j + 128 * h0
            return w_in[:, c0:c0 + 128 * n]
        segs = [hc(4, 0, 2), hc(5, 0, 2), hc(6, 0, 2), hc(4, 2, 2), hc(5, 2, 2), hc(6, 2, 2)]
        for h in range(4):
            segs += [hc(0, h, 1), hc(1, h, 1), hc(2, h, 1), hc(3, h, 1)]
        wA = np.ascontiguousarray(np.concatenate(segs, axis=1))
        lb = np.ascontiguousarray(inp["hgrn_lb"][:, hs].reshape(2, 4, 128).transpose(2, 0, 1))
        g = j
        segc = [wn[:, 1024 * g:1024 * g + 1024]] + [wn[:, 4096 + 512 * i + 128 * g:4096 + 512 * i + 128 * g + 128] for i in range(6)] + \
               [wn[:, 4096 + 3072 + 24 * g:4096 + 3072 + 24 * g + 24]]
        wC = np.ascontiguousarray(np.concatenate(segc, axis=1))
        t0 = j * TOKC
        xTc = np.zeros((D_MODEL, NT), np.float32)
        xTc[:, HALO:] = xT[b][:, t0:t0 + TOKC]
        if j > 0:
            xTc[:, :HALO] = xT[b][:, t0 - HALO:t0]
        fsel = np.zeros((128, 4), np.float32); fsel[:, j] = 1.0
        fhal = np.zeros((128, 4), np.float32)
        if j > 0:
            fhal[:, j - 1] = 1.0
        m = dict(shared)
        m.update(xT=xT[b], xTc=xTc, memT=memT[b], wA=wA, wC=wC, hg_lb=lb, flag=np.full((128, 1), 1.0 if j > 0 else 0.0, np.float32), fsel=fsel, fhal=fhal)
        maps.append(m)
    res = run_bass_kernel_spmd(_build_fused(), maps, core_ids=cores)
    out = np.empty((BATCH, SEQ, D_MODEL), np.float32)
    for c in cores:
        b, j = divmod(c, 4)
        out[b, j * TOKC:(j + 1) * TOKC, :] = res.results[c]["outT"].T
    return out
```

```python
import numpy as np
import ml_dtypes
from contextlib import ExitStack
import concourse.bass as bass
import concourse.mybir as mybir


F32 = mybir.dt.float32
BF16 = mybir.dt.bfloat16
AF = mybir.ActivationFunctionType
ALU = mybir.AluOpType


class Res:
    __slots__ = ("name", "w", "r", "t", "multi", "mw")

    def __init__(self, name, t=None, multi=False):
        self.name = name
        self.w = None
        self.r = {}
        self.t = t
        self.multi = multi
        self.mw = {}

    def __getitem__(self, idx):
        return self.t[idx]


class Sched:
    def __init__(self, nc, es: ExitStack, n_dma_slots=8):
        self.nc = nc
        self.es = es
        self.es0 = es
        self.E = {"pe": nc.tensor, "act": nc.scalar, "dve": nc.vector, "pool": nc.gpsimd, "sp": nc.sync}
        self.sem = {k: es.enter_context(nc.semaphore("prog_" + k)) for k in self.E}
        self.cnt = {k: 0 for k in self.E}
        self.known = {k: {} for k in self.E}
        self.slots = {}
        for q in ("sp", "pool", "act"):
            self.slots[q] = [[es.enter_context(nc.semaphore(f"dq_{q}_{i}")), 0] for i in range(n_dma_slots)]
        self.slot_i = {q: 0 for q in self.slots}
        self.n_ins = 0
        self.n_wait = 0
        self.prefix = ""
        self.nphase = 0

    def new_phase(self):
        self.prefix = f"p{self.nphase}_"
        self.nphase += 1

    def sbuf(self, name, shape, dt):
        t = self.es.enter_context(self.nc.sbuf_tensor("sb_" + self.prefix + name, list(shape), dt))
        return Res(name, t)

    def psum(self, name, shape, dt=F32):
        t = self.es.enter_context(self.nc.psum_tensor("ps_" + self.prefix + name, list(shape), dt))
        return Res(name, t)

    def dram(self, name, shape, dt, kind="Internal", multi=True):
        t = self.nc.dram_tensor(name, list(shape), dt, kind=kind)
        return Res(name, t.ap(), multi=multi)

    def dram_cc(self, name, shape, dt):
        t = self.nc.dram_tensor(name, list(shape), dt)
        return Res(name, t.ap(), multi=True)

    def collective(self, kind, groups, src, dst):
        if not hasattr(self, "cc_sem"):
            self.cc_sem = self.es0.enter_context(self.nc.semaphore("cc_sem"))
            self.cc_cnt = 0
        self._deps("pool", [src], [])
        for ev in list(dst.mw.values()) + list(dst.r.values()):
            self._wait("pool", ev)
        ins = self.nc.gpsimd.collective_compute(kind, mybir.AluOpType.bypass, replica_groups=groups, ins=[src.t.opt()], outs=[dst.t.opt()])
        self.cc_cnt += 1
        ins.then_inc(self.cc_sem, 1)
        ev = (self.cc_sem, self.cc_cnt, "cc")
        src.r[id(self.cc_sem)] = ev
        dst.mw[id(self.cc_sem)] = ev
        self.n_ins += 1

    def _wait(self, eng, ev):
        sem, val, src = ev
        if src == eng and eng == "pe":
            return
        k = id(sem)
        if self.known[eng].get(k, 0) >= val:
            return
        self.E[eng].wait_ge(sem, val)
        self.known[eng][k] = val
        self.n_wait += 1

    def _deps(self, eng, reads, writes):
        for r in reads:
            if r.w is not None:
                self._wait(eng, r.w)
            for ev in r.mw.values():
                self._wait(eng, ev)
        for w in writes:
            if w.multi:
                continue
            if w.w is not None:
                self._wait(eng, w.w)
            for ev in w.r.values():
                self._wait(eng, ev)

    def _record(self, ev, reads, writes):
        sem = ev[0]
        for r in reads:
            r.r[id(sem)] = ev
        for w in writes:
            if w.multi:
                w.mw[id(sem)] = ev
                continue
            w.w = ev
            w.r = {}

    def op(self, eng, reads, writes, fn):
        self._deps(eng, reads, writes)
        ins = fn(self.E[eng])
        self.cnt[eng] += 1
        ins.then_inc(self.sem[eng], 1)
        ev = (self.sem[eng], self.cnt[eng], eng)
        self.known[eng][id(self.sem[eng])] = max(self.known[eng].get(id(self.sem[eng]), 0), 0)
        self._record(ev, reads, writes)
        self.n_ins += 1
        return ins

    def dma(self, q, reads, writes, out, in_, **kw):
        slots = self.slots[q]
        i = self.slot_i[q]
        self.slot_i[q] = (i + 1) % len(slots)
        s = slots[i]
        if s[1] > 0:
            self._wait(q, (s[0], s[1], "dma"))
        self._deps(q, reads, writes)
        ins = self.E[q].dma_start(out=out, in_=in_, **kw)
        s[1] += 16
        ins.then_inc(s[0], 16)
        ev = (s[0], s[1], "dma")
        self._record(ev, reads, writes)
        self.n_ins += 1
        return ins

    def wait_all(self, eng, resources):
        for r in resources:
            if r.w is not None:
                self._wait(eng, r.w)
            for ev in r.mw.values():
                self._wait(eng, ev)

    def barrier(self):
        evs = [(self.sem[k], self.cnt[k], k) for k in self.E if self.cnt[k] > 0]
        for q, sl in self.slots.items():
            for s in sl:
                if s[1] > 0:
                    evs.append((s[0], s[1], "dma"))
        for eng in self.E:
            for ev in evs:
                sem, val, src = ev
                if self.known[eng].get(id(sem), 0) >= val:
                    continue
                self.E[eng].wait_ge(sem, val)
                self.known[eng][id(sem)] = val


class Ring:
    def __init__(self, items):
        self.items = items
        self.i = 0

    def next(self):
        r = self.items[self.i]
        self.i = (self.i + 1) % len(self.items)
        return r


def views(S, name, shape, dt, n, idx_fn):
    t = S.es.enter_context(S.nc.sbuf_tensor("sb_" + S.prefix + name, list(shape), dt))
    return t, [Res(f"{name}{i}", idx_fn(t, i)) for i in range(n)]


def make_wring(S, name, nslots):
    _, sl = views(S, name, [128, nslots, 4096], BF16, nslots, lambda t, i: t[:, i, :])
    return Ring(sl)


def linear_g(S, wring, NW, W, r0, acts, T, groups, evac, banks, halo_skip=None):
    nk = len(acts)
    loads = []
    for gi, (c0, n) in enumerate(groups):
        per = 4096 // (n * 128)
        for k0 in range(0, nk, per):
            loads.append((gi, c0, n, k0, min(per, nk - k0)))
    issued = []

    def issue(j):
        gi, c0, n, k0, kn = loads[j]
        slot = wring.next()
        v = slot.t[:, 0:kn * n * 128].rearrange("p (k c) -> p k c", c=n * 128)
        S.dma("pool", [W], [slot], out=v, in_=W.t[r0 + k0 * 128:r0 + (k0 + kn) * 128, c0:c0 + n * 128].rearrange("(kc p) c -> p kc c", p=128))
        issued.append((slot, v))
    npre = NW - 1
    nxt = 0
    while nxt < min(npre, len(loads)):
        issue(nxt); nxt += 1
    ci = 0
    j = 0
    for gi, (c0, n) in enumerate(groups):
        ps = [banks.next() for _ in range(n)]
        while j < len(loads) and loads[j][0] == gi:
            if nxt < len(loads):
                issue(nxt); nxt += 1
            _, _, _, k0, kn = loads[j]
            slot, v = issued[j]
            for oc in range(n):
                for kk in range(kn):
                    k = k0 + kk
                    S.op("pe", [slot, acts[k]], [ps[oc]],
                         lambda e, oc=oc, kk=kk, k=k: e.matmul(ps[oc].t[:, 0:T], lhsT=v[:, kk, oc * 128:(oc + 1) * 128], rhs=acts[k].t[:, 0:T],
                                                              start=(k == 0), stop=(k == nk - 1)))
            j += 1
        for oc in range(n):
            evac(ci, ps[oc])
            ci += 1


def col_groups(c0, nchunks):
    g = []
    i = 0
    while i < nchunks:
        n = min(4, nchunks - i)
        g.append((c0 + i * 128, n))
        i += n
    return g


class Ring:
    def __init__(self, items):
        self.items = items
        self.i = 0

    def next(self):
        r = self.items[self.i]
        self.i = (self.i + 1) % len(self.items)
        return r


def views(S, name, shape, dt, n, idx_fn):
    t = S.es.enter_context(S.nc.sbuf_tensor("sb_" + S.prefix + name, list(shape), dt))
    return t, [Res(f"{name}{i}", idx_fn(t, i)) for i in range(n)]


def emit_post(S, c, dr, blocks):
    nc = S.nc
    D, DFF, XH, NM = c["D"], c["DFF"], c["XH"], c["NM"]
    KC, FC = D // 128, DFF // 128
    XW = XH * 128
    TM = c["T"]
    alpha, eps = c["alpha"], c["eps"]
    GF = c.get("GF", 8)
    NW = c.get("NW", 4)
    KCW = max(KC, GF, XH)
    oes = S.es
    S.new_phase()
    with ExitStack() as pes:
        S.es = pes
        wring = make_wring(S, "wsl", NW)
        ones_f = S.sbuf("ones_f", [128, 128], F32)
        ones_b = S.sbuf("ones_b", [128, 128], BF16)
        epsT = S.sbuf("epsT", [128, 1], F32)
        lng = S.sbuf("lng", [128, 3, KC], F32)
        lnb = S.sbuf("lnb", [128, 3, KC], F32)
        cw = S.sbuf("cw", [128, 3, FC], F32)
        flag = S.sbuf("flag", [128, 1], F32)
        if "oAG_get" in dr:
            cand = Ring([S.sbuf(f"cand{i}", [128, TM], BF16) for i in range(6)])
            selacc = Ring([S.sbuf(f"selacc{i}", [128, TM], BF16) for i in range(4)])
            fsel = S.sbuf("fsel", [128, 4], F32)
            S.dma("sp", [dr["fsel"]], [fsel], out=fsel.t[:], in_=dr["fsel"].t)
        if "hlAG_get" in dr:
            candf = Ring([S.sbuf(f"candf{i}", [128, 128], F32) for i in range(4)])
            fhal = S.sbuf("fhal", [128, 4], F32)
            S.dma("sp", [dr["fhal"]], [fhal], out=fhal.t[:], in_=dr["fhal"].t)
        S.dma("sp", [dr["flag"]], [flag], out=flag.t[:], in_=dr["flag"].t)
        carry = [S.sbuf(f"carry{f}", [128, 2], F32) for f in range(FC)]
        KT = S.sbuf("KT", [128, XH, NM], BF16)
        Vt = [S.sbuf(f"Vt{m}", [128, XW], BF16) for m in range(NM // 128)]
        qT = [S.sbuf(f"qT{h}", [128, TM], BF16) for h in range(XH)]
        xoT = [S.sbuf(f"xoT{h}", [128, TM], BF16) for h in range(XH)]
        pT = Ring([S.sbuf(f"pT{i}", [128, TM], BF16) for i in range(4)])
        tmpf = Ring([S.sbuf(f"tmpf{i}", [128, TM + 2], F32) for i in range(6)])
        st_mean = S.sbuf("st_mean", [128, TM], F32)
        st_rstd = S.sbuf("st_rstd", [128, TM], F32)
        st_nmr = S.sbuf("st_nmr", [128, TM], F32)
        pb = [S.psum(f"pb{i}", [128, 512], F32) for i in range(8)]
        pring = Ring(pb[0:4])
        pA, pB2 = pb[4], pb[5]
        pring2 = Ring(pb[6:8])
        pringu = Ring(pb[4:8])
        glr = Ring([S.sbuf(f"glr{i}", [128, TM], F32) for i in range(5)])

        S.op("dve", [], [ones_f], lambda e: e.memset(ones_f.t[:], 1.0))
        S.op("dve", [], [ones_b], lambda e: e.memset(ones_b.t[:], 1.0))
        S.op("dve", [], [epsT], lambda e: e.memset(epsT.t[:], eps))
        S.dma("sp", [dr["ln_g"]], [lng], out=lng.t[:], in_=dr["ln_g"].t)
        S.dma("sp", [dr["ln_b"]], [lnb], out=lnb.t[:], in_=dr["ln_b"].t)
        S.dma("sp", [dr["conv"]], [cw], out=cw.t[:], in_=dr["conv"].t)
        for f in range(FC):
            S.op("dve", [], [carry[f]], lambda e, f=f: e.memset(carry[f].t[:], 0.0))

        def load_w(W, r0, nk, c0, ncol=128):
            slot = wring.next()
            v = slot.t[:, 0:nk * ncol].rearrange("p (k c) -> p k c", c=ncol)
            S.dma("pool", [W], [slot], out=v, in_=W.t[r0:r0 + nk * 128, c0:c0 + ncol].rearrange("(kc p) c -> p kc c", p=128))
            return slot, v

        def linear(W, r0, acts, T, cols, evac, ps_ring):
            assert all(cols[i + 1] == cols[i] + 128 for i in range(len(cols) - 1))
            linear_g(S, wring, NW, W, r0, acts, T, col_groups(cols[0], len(cols)), evac, ps_ring)

        def layer_norm(li, T):
            for k in range(KC):
                S.op("pe", [ones_f, rT[k]], [pA],
                     lambda e, k=k: e.matmul(pA.t[:, 0:T], lhsT=ones_f.t[:], rhs=rT[k].t[:, 0:T], start=(k == 0), stop=(k == KC - 1)))
            for k in range(KC):
                sq = tmpf.next()
                S.op("act", [rT[k]], [sq], lambda e, k=k, sq=sq: e.activation(out=sq.t[:, 0:T], in_=rT[k].t[:, 0:T], func=AF.Square))
                S.op("pe", [ones_f, sq], [pB2],
                     lambda e, k=k, sq=sq: e.matmul(pB2.t[:, 0:T], lhsT=ones_f.t[:], rhs=sq.t[:, 0:T], start=(k == 0), stop=(k == KC - 1)))
            inv = 1.0 / D
            S.op("act", [pA], [st_mean], lambda e: e.activation(out=st_mean.t[:, 0:T], in_=pA.t[:, 0:T], func=AF.Copy, scale=inv))
            m2 = tmpf.next()
            S.op("dve", [st_mean], [m2], lambda e: e.tensor_tensor(out=m2.t[:, 0:T], in0=st_mean.t[:, 0:T], in1=st_mean.t[:, 0:T], op=ALU.mult))
            var = tmpf.next()
            S.op("dve", [pB2, m2], [var], lambda e: e.scalar_tensor_tensor(out=var.t[:, 0:T], in0=pB2.t[:, 0:T], scalar=inv, in1=m2.t[:, 0:T],
                                                                            op0=ALU.mult, op1=ALU.subtract))
            sd = tmpf.next()
            S.op("act", [var, epsT], [sd], lambda e: e.activation(out=sd.t[:, 0:T], in_=var.t[:, 0:T], func=AF.Sqrt, bias=epsT.t[:, 0:1]))
            S.op("dve", [sd], [st_rstd], lambda e: e.reciprocal(out=st_rstd.t[:, 0:T], in_=sd.t[:, 0:T]))
            S.op("dve", [st_mean, st_rstd], [st_nmr],
                 lambda e: e.scalar_tensor_tensor(out=st_nmr.t[:, 0:T], in0=st_mean.t[:, 0:T], scalar=-1.0, in1=st_rstd.t[:, 0:T], op0=ALU.mult, op1=ALU.mult))
            for k in range(KC):
                t1 = tmpf.next()
                S.op("dve", [rT[k], st_rstd], [t1], lambda e, k=k, t1=t1: e.tensor_tensor(out=t1.t[:, 0:T], in0=rT[k].t[:, 0:T], in1=st_rstd.t[:, 0:T], op=ALU.mult))
                t2 = tmpf.next()
                S.op("pool", [t1, st_nmr], [t2], lambda e, t1=t1, t2=t2: e.tensor_tensor(out=t2.t[:, 0:T], in0=t1.t[:, 0:T], in1=st_nmr.t[:, 0:T], op=ALU.add))
                S.op("act", [t2, lng, lnb], [rT[k]],
                     lambda e, k=k, t2=t2: e.activation(out=rT[k].t[:, 0:T], in_=t2.t[:, 0:T], func=AF.Identity,
                                                        scale=lng.t[:, li, k:k + 1], bias=lnb.t[:, li, k:k + 1]))
                S.op("act", [t2, lng, lnb], [aT[k]],
                     lambda e, k=k, t2=t2: e.activation(out=aT[k].t[:, 0:T], in_=t2.t[:, 0:T], func=AF.Identity,
                                                        scale=lng.t[:, li, k:k + 1], bias=lnb.t[:, li, k:k + 1]))

        def resid_evac(T):
            def ev(i, ps):
                S.op("dve", [rT[i], ps], [rT[i]],
                     lambda e, i=i, ps=ps: e.scalar_tensor_tensor(out=rT[i].t[:, 0:T], in0=rT[i].t[:, 0:T], scalar=alpha, in1=ps.t[:, 0:T],
                                                                  op0=ALU.mult, op1=ALU.add))
            return ev

        mes = ExitStack()
        S.es = mes
        memT = S.sbuf("memT", [128, KC, NM], BF16)
        S.es = pes
        S.dma("pool", [dr["memT"]], [memT], out=memT.t[:], in_=dr["memT"].t.rearrange("(kc p) m -> p kc m", p=128))
        memk = [Res(f"memk{k}", memT.t[:, k, :]) for k in range(KC)]
        for mk in memk:
            mk.w = memT.w

        def kt_evac(i, ps):
            S.op("act", [ps], [KT], lambda e, i=i, ps=ps: e.activation(out=KT.t[:, i, :], in_=ps.t[:, 0:NM], func=AF.Copy))
        linear(dr["w_kv"], 0, memk, NM, [h * 128 for h in range(XH)], kt_evac, pring)
        for h in range(XH):
            slot, sv = load_w(dr["w_kv"], 0, KC, XW + h * 128)
            for m in range(NM // 128):
                ps = pring.next()
                for k in range(KC):
                    S.op("pe", [slot, memT], [ps],
                         lambda e, k=k, m=m, slot=slot, ps=ps: e.matmul(ps.t[:, 0:128], lhsT=memT.t[:, k, m * 128:(m + 1) * 128], rhs=sv[:, k, :],
                                                                        start=(k == 0), stop=(k == KC - 1)))
                S.op("act", [ps], [Vt[m]], lambda e, m=m, h=h, ps=ps: e.activation(out=Vt[m].t[:, h * 128:(h + 1) * 128], in_=ps.t[:, 0:128], func=AF.Copy))

        S.barrier()
        mes.close()
        _, rT = views(S, "rT", [128, KC, TM], F32, KC, lambda t, i: t[:, i, :])
        _, aT = views(S, "aT", [128, KC, TM], BF16, KC, lambda t, i: t[:, i, :])
        _, gT = views(S, "gT", [128, 2 * GF, TM], BF16, 2 * GF, lambda t, i: t[:, i, :])
        for (tok0, T, halo) in blocks:
            HL = c.get("HALO", 128)
            TOKC_ = c.get("TOKC", 2048)
            for k in range(KC):
                if "oAG_get" in dr:
                    for cq in range(4):
                        if halo:
                            cb = max(cq * TOKC_ - HL, 0)
                        else:
                            cb = cq * TOKC_ + (tok0 - HL)
                        cd = cand.next()
                        _ores, _oap = dr["oAG_get"](k, cb, T)
                        S.dma("sp", [_ores], [cd], out=cd.t[:, 0:T], in_=_oap)
                        if cq == 0:
                            acc = selacc.next()
                            S.op("dve", [cd, fsel], [acc], lambda e: e.tensor_scalar(out=acc.t[:, 0:T], in0=cd.t[:, 0:T], scalar1=fsel.t[:, 0:1], scalar2=None, op0=ALU.mult))
                        else:
                            dst = aT[k] if cq == 3 else selacc.next()
                            S.op("dve", [cd, fsel, acc], [dst], lambda e: e.scalar_tensor_tensor(out=dst.t[:, 0:T], in0=cd.t[:, 0:T], scalar=fsel.t[:, cq:cq + 1], in1=acc.t[:, 0:T],
                                                                                              op0=ALU.mult, op1=ALU.add))
                            acc = dst
                else:
                    S.dma("sp", [dr["oT"]], [aT[k]], out=aT[k].t[:, 0:T], in_=dr["oT"].t[k * 128:(k + 1) * 128, tok0:tok0 + T])
                if halo and "hlAG_get" in dr:
                    for cq in range(4):
                        cd = candf.next()
                        _lres, _lap = dr["hlAG_get"](k, cq)
                        S.dma("sp", [_lres], [cd], out=cd.t[:, 0:T], in_=_lap)
                        if cq == 0:
                            S.op("dve", [cd, fhal], [rT[k]], lambda e: e.tensor_scalar(out=rT[k].t[:, 0:T], in0=cd.t[:, 0:T], scalar1=fhal.t[:, 0:1], scalar2=None, op0=ALU.mult))
                        else:
                            S.op("dve", [cd, fhal, rT[k]], [rT[k]], lambda e: e.scalar_tensor_tensor(out=rT[k].t[:, 0:T], in0=cd.t[:, 0:T], scalar=fhal.t[:, cq:cq + 1], in1=rT[k].t[:, 0:T],
                                                                                                    op0=ALU.mult, op1=ALU.add))
                else:
                    S.dma("sp", [dr["hT_in"]], [rT[k]], out=rT[k].t[:, 0:T], in_=dr["hT_in"].t[k * 128:(k + 1) * 128, tok0:tok0 + T])
            linear(dr["w_out"], 0, aT, T, [i * 128 for i in range(KC)], resid_evac(T), pring)
            layer_norm(0, T)
            qscale = 128.0 ** -0.5

            def q_evac(i, ps):
                S.op("act", [ps], [qT[i]], lambda e, i=i, ps=ps: e.activation(out=qT[i].t[:, 0:T], in_=ps.t[:, 0:T], func=AF.Copy, scale=qscale))
            linear(dr["w_q"], 0, aT, T, [h * 128 for h in range(XH)], q_evac, pring)
            for h in range(XH):
                pts = []
                for m in range(NM // 128):
                    ps = pring.next()
                    S.op("pe", [KT, qT[h]], [ps], lambda e, h=h, m=m, ps=ps: e.matmul(ps.t[:, 0:T], lhsT=KT.t[:, h, m * 128:(m + 1) * 128], rhs=qT[h].t[:, 0:T],
                                                                                      start=True, stop=True))
                    pt = pT.next()
                    S.op("act", [ps], [pt], lambda e, ps=ps, pt=pt: e.activation(out=pt.t[:, 0:T], in_=ps.t[:, 0:T], func=AF.Exp))
                    pts.append(pt)
                nm = len(pts)
                for m, pt in enumerate(pts):
                    S.op("pe", [ones_b, pt], [pA], lambda e, m=m, pt=pt: e.matmul(pA.t[:, 0:T], lhsT=ones_b.t[:], rhs=pt.t[:, 0:T], start=(m == 0), stop=(m == nm - 1)))
                for m, pt in enumerate(pts):
                    S.op("pe", [Vt[m], pt], [pB2], lambda e, m=m, h=h, pt=pt: e.matmul(pB2.t[:, 0:T], lhsT=Vt[m].t[:, h * 128:(h + 1) * 128], rhs=pt.t[:, 0:T],
                                                                                     start=(m == 0), stop=(m == nm - 1)))
                rd = tmpf.next()
                S.op("dve", [pA], [rd], lambda e, rd=rd: e.reciprocal(out=rd.t[:, 0:T], in_=pA.t[:, 0:T]))
                S.op("dve", [pB2, rd], [xoT[h]], lambda e, h=h, rd=rd: e.tensor_tensor(out=xoT[h].t[:, 0:T], in0=pB2.t[:, 0:T], in1=rd.t[:, 0:T], op=ALU.mult))
            linear(dr["w_o"], 0, xoT, T, [i * 128 for i in range(KC)], resid_evac(T), pring)
            layer_norm(1, T)
            groups = [list(range(g0, min(g0 + GF, FC))) for g0 in range(0, FC, GF)]

            def glu_a(f, ps):
                ab = tmpf.next()
                S.op("act", [carry[f]], [ab], lambda e: e.activation(out=ab.t[:, 0:2], in_=carry[f].t[:, 0:2], func=AF.Copy))
                S.op("act", [ps], [ab], lambda e: e.activation(out=ab.t[:, 2:2 + T], in_=ps.t[:, 0:T], func=AF.Copy))
                if halo:
                    S.op("dve", [ab, flag], [carry[f]], lambda e: e.tensor_scalar(out=carry[f].t[:, 0:2], in0=ab.t[:, T:T + 2], scalar1=flag.t[:, 0:1], scalar2=None, op0=ALU.mult))
                    return None
                S.op("dve", [ab], [carry[f]], lambda e: e.tensor_copy(out=carry[f].t[:, 0:2], in_=ab.t[:, T:T + 2]))
                c1 = tmpf.next()
                S.op("act", [ab, cw], [c1], lambda e: e.activation(out=c1.t[:, 0:T], in_=ab.t[:, 2:2 + T], func=AF.Copy, scale=cw.t[:, 2, f:f + 1]))
                c2 = tmpf.next()
                S.op("dve", [ab, cw, c1], [c2], lambda e: e.scalar_tensor_tensor(out=c2.t[:, 0:T], in0=ab.t[:, 1:1 + T], scalar=cw.t[:, 1, f:f + 1], in1=c1.t[:, 0:T], op0=ALU.mult, op1=ALU.add))
                S.op("dve", [ab, cw, c2], [c1], lambda e: e.scalar_tensor_tensor(out=c1.t[:, 0:T], in0=ab.t[:, 0:T], scalar=cw.t[:, 0, f:f + 1], in1=c2.t[:, 0:T], op0=ALU.mult, op1=ALU.add))
                gl = glr.next()
                S.op("act", [c1], [gl], lambda e: e.activation(out=gl.t[:, 0:T], in_=c1.t[:, 0:T], func=AF.Gelu))
                return gl

            def up_group(gi, fl):
                gb = (gi % 2) * GF
                for s0 in range(0, len(fl), 4):
                    sub = fl[s0:s0 + 4]
                    gls = {}

                    def ev_a(i, ps):
                        gls[sub[i]] = glu_a(sub[i], ps)
                    linear_g(S, wring, NW, dr["w_up"], 0, aT, T, [(sub[0] * 128, len(sub))], ev_a, pring)
                    if halo:
                        continue

                    def ev_u(i, ps):
                        f = sub[i]
                        gdst = gT[gb + (f - fl[0])]
                        S.op("dve", [gls[f], ps], [gdst], lambda e: e.tensor_tensor(out=gdst.t[:, 0:T], in0=gls[f].t[:, 0:T], in1=ps.t[:, 0:T], op=ALU.mult))
                    linear_g(S, wring, NW, dr["w_up"], 0, aT, T, [(DFF + sub[0] * 128, len(sub))], ev_u, pringu)

            def down_group(gi, fl):
                gb = (gi % 2) * GF
                nk = len(fl)
                acts = [gT[gb + j] for j in range(nk)]

                def ev(i, ps):
                    if gi == 0:
                        resid_evac(T)(i, ps)
                    else:
                        S.op("dve", [rT[i], ps], [rT[i]], lambda e, i=i, ps=ps: e.tensor_tensor(out=rT[i].t[:, 0:T], in0=rT[i].t[:, 0:T], in1=ps.t[:, 0:T], op=ALU.add))
                linear(dr["w_down"], fl[0] * 128, acts, T, [i * 128 for i in range(KC)], ev, pring)

            for gi, fl in enumerate(groups):
                up_group(gi, fl)
                if halo:
                    continue
                if gi > 0:
                    down_group(gi - 1, groups[gi - 1])
            if halo:
                continue
            down_group(len(groups) - 1, groups[-1])
            layer_norm(2, T)
            for k in range(KC):
                oo = c.get("out_off", 0)
                S.dma("sp", [rT[k]], [dr["hT_out"]], out=dr["hT_out"].t[k * 128:(k + 1) * 128, tok0 - oo:tok0 - oo + T], in_=rT[k].t[:, 0:T])
                if "hTb_put" in dr:
                    _bres, _bap = dr["hTb_put"](k, tok0 - c.get("outb_off", 0), T)
                    S.dma("sp", [aT[k]], [_bres], out=_bap, in_=aT[k].t[:, 0:T])
                elif "hTb_out" in dr:
                    bo = c.get("outb_off", 0)
                    S.dma("sp", [aT[k]], [dr["hTb_out"]], out=dr["hTb_out"].t[k * 128:(k + 1) * 128, tok0 - bo:tok0 - bo + T], in_=aT[k].t[:, 0:T])
                if "hl_put" in dr and (tok0, T, halo) == blocks[-1]:
                    _lres, _lap = dr["hl_put"](k)
                    S.dma("sp", [rT[k]], [_lres], out=_lap, in_=rT[k].t[:, T - 128:T])
        S.barrier()
    S.es = oes


NEG = -30000.0


def _oput(dr, r0, t0, T):
    if "oT_put" in dr:
        return dr["oT_put"](r0, t0, T)
    return dr["oT"], dr["oT"].t[r0:r0 + 128, t0:t0 + T]


def _linear(S, wring, NW, W, acts, T, cols, evac, ps_ring):
    groups = []
    i = 0
    while i < len(cols):
        n = 1
        while n < 4 and i + n < len(cols) and cols[i + n] == cols[i] + n * 128:
            n += 1
        groups.append((cols[i], n))
        i += n
    linear_g(S, wring, NW, W, 0, acts, T, groups, evac, ps_ring)


def emit_sb(S, c, dr, heads, col_q, col_k, col_v, orow0):
    D, SEQ = c["D"], c["SEQ"]
    KC = D // 128
    T = 512
    NH = len(heads)
    NW = c.get("NW", 4)
    NT = SEQ // 128
    oes = S.es
    S.new_phase()
    with ExitStack() as pes:
        S.es = pes
        aTt, aT = views(S, "aT", [128, KC, T], BF16, KC, lambda t, i: t[:, i, :])
        wring = make_wring(S, "wsl", NW)
        kT = [S.sbuf(f"kT{h}", [128, SEQ], BF16) for h in range(NH)]
        _, Vall = views(S, "Vall", [128, NT, NH * 128], BF16, NT, lambda t, i: t[:, i, :])
        qT = [S.sbuf(f"qT{h}", [128, T], BF16) for h in range(NH)]
        vf = Ring([S.sbuf(f"vf{i}", [128, T], F32) for i in range(2)])
        ident = S.sbuf("ident", [128, 128], F32)
        identb = S.sbuf("identb", [128, 128], BF16)
        negU = S.sbuf("negU", [128, 128], BF16)
        negO = S.sbuf("negO", [128, 128], BF16)
        mask = S.sbuf("mask", [128, 4, T], BF16)
        eT = [Ring([S.sbuf(f"eT{h}_{i}", [128, T], F32) for i in range(2)]) for h in range(NH)]
        spT = [Ring([S.sbuf(f"spT{h}_{i}", [128, T], BF16) for i in range(2)]) for h in range(NH)]
        tmpT = [Ring([S.sbuf(f"tmpT{h}_{i}", [128, T], F32) for i in range(2)]) for h in range(NH)]
        xT_ = [Ring([S.sbuf(f"xT{h}_{i}", [128, T], F32) for i in range(2)]) for h in range(NH)]
        wT = [Ring([S.sbuf(f"wT{h}_{i}", [128, T], BF16) for i in range(2)]) for h in range(NH)]
        carry = [S.sbuf(f"carry{h}", [128, T], F32) for h in range(NH)]
        oS = Ring([S.sbuf(f"oS{i}", [128, T], BF16) for i in range(2)])
        pb = [S.psum(f"pb{i}", [128, 512], F32) for i in range(8)]
        pz = [pb[0], pb[1]]
        pR = [pb[2], pb[3]]
        pC = [pb[4], pb[5]]
        pO = [pb[6], pb[7]]
        plin = Ring([pb[2], pb[3], pb[4], pb[5]])
        ptr = Ring([pb[6], pb[7]])

        S.dma("sp", [dr["ident"]], [ident], out=ident.t[:], in_=dr["ident"].t)
        S.dma("sp", [dr["identb"]], [identb], out=identb.t[:], in_=dr["identb"].t)
        S.dma("sp", [dr["negU"]], [negU], out=negU.t[:], in_=dr["negU"].t)
        S.dma("sp", [dr["sbmask"]], [mask], out=mask.t[:], in_=dr["sbmask"].t)
        S.op("dve", [], [negO], lambda e: e.memset(negO.t[:], -1.0))
        scale = 128.0 ** -0.5

        for tb in range(SEQ // T):
            t0 = tb * T
            hk = max(KC // 2, 1)
            for k0 in range(0, KC, hk):
                S.dma("pool", [dr["xT"]], aT[k0:k0 + hk], out=aTt[:, k0:k0 + hk, :],
                      in_=dr["xT"].t[k0 * 128:(k0 + hk) * 128, t0:t0 + T].rearrange("(kc p) t -> p kc t", p=128))
            cols = [col_q + h * 128 for h in heads] + [col_k + h * 128 for h in heads] + [col_v + h * 128 for h in heads]

            def evac(i, ps):
                kind, hi = divmod(i, NH)
                if kind == 0:
                    S.op("act", [ps], [qT[hi]], lambda e: e.activation(out=qT[hi].t[:, :], in_=ps.t[:, :], func=AF.Copy, scale=scale))
                elif kind == 1:
                    S.op("act", [ps], [kT[hi]], lambda e: e.activation(out=kT[hi].t[:, t0:t0 + T], in_=ps.t[:, :], func=AF.Copy))
                else:
                    v = vf.next()
                    S.op("act", [ps], [v], lambda e: e.activation(out=v.t[:, :], in_=ps.t[:, :], func=AF.Copy))
                    for j in range(T // 128):
                        pt = ptr.next()
                        S.op("pe", [v, ident], [pt], lambda e, j=j, pt=pt: e.transpose(out=pt.t[:, 0:128], in_=v.t[:, j * 128:(j + 1) * 128], identity=ident.t[:]))
                        vt = Vall[t0 // 128 + j]
                        S.op("dve", [pt], [vt], lambda e, pt=pt, vt=vt: e.tensor_copy(out=vt.t[:, hi * 128:(hi + 1) * 128], in_=pt.t[:, 0:128]))
            _linear(S, wring, NW, dr["wA"], aT, T, cols, evac, plin)

            kts = list(range(t0 // 128 + 3, -1, -1))

            def emit_z(kt):
                for h in range(NH):
                    kd = kt - t0 // 128
                    S.op("pe", [kT[h], qT[h]], [pz[h]], lambda e, h=h: e.matmul(pz[h].t[:, :], lhsT=kT[h].t[:, kt * 128:(kt + 1) * 128], rhs=qT[h].t[:, :],
                                                                              start=True, stop=(kd < 0)))
                    if kd >= 0:
                        S.op("pe", [identb, mask], [pz[h]], lambda e, h=h, kd=kd: e.matmul(pz[h].t[:, :], lhsT=identb.t[:], rhs=mask.t[:, kd, :], start=False, stop=True))
            emit_z(kts[0])
            for n, kt in enumerate(kts):
                first = (n == 0)
                last = (n == len(kts) - 1)
                cur = {}
                for h in range(NH):
                    e_ = eT[h].next()
                    sp = spT[h].next()
                    S.op("act", [pz[h]], [e_], lambda e, h=h, e_=e_: e.activation(out=e_.t[:, :], in_=pz[h].t[:, :], func=AF.Exp))
                    S.op("act", [e_], [sp], lambda e, e_=e_, sp=sp: e.activation(out=sp.t[:, :], in_=e_.t[:, :], func=AF.Ln, bias=1.0))
                    cur[h] = (e_, sp)
                for h in range(NH):
                    e_, sp = cur[h]
                    S.op("pe", [negU, sp], [pR[h]], lambda e, h=h, sp=sp: e.matmul(pR[h].t[:, :], lhsT=negU.t[:], rhs=sp.t[:, :], start=True, stop=True))
                    if not last:
                        S.op("pe", [negO, sp], [pC[h]], lambda e, h=h, sp=sp: e.matmul(pC[h].t[:, :], lhsT=negO.t[:], rhs=sp.t[:, :], start=True, stop=True))
                if not last:
                    emit_z(kts[n + 1])
                for h in range(NH):
                    e_, sp = cur[h]
                    x_ = xT_[h].next()
                    if first:
                        S.op("act", [pR[h]], [x_], lambda e, h=h, x_=x_: e.activation(out=x_.t[:, :], in_=pR[h].t[:, :], func=AF.Exp))
                        if not last:
                            S.op("dve", [pC[h]], [carry[h]], lambda e, h=h: e.tensor_copy(out=carry[h].t[:, :], in_=pC[h].t[:, :]))
                    else:
                        tm = tmpT[h].next()
                        S.op("dve", [pR[h], carry[h]], [tm], lambda e, h=h, tm=tm: e.tensor_tensor(out=tm.t[:, :], in0=pR[h].t[:, :], in1=carry[h].t[:, :], op=ALU.add))
                        S.op("act", [tm], [x_], lambda e, tm=tm, x_=x_: e.activation(out=x_.t[:, :], in_=tm.t[:, :], func=AF.Exp))
                        if not last:
                            S.op("dve", [pC[h], carry[h]], [carry[h]], lambda e, h=h: e.tensor_tensor(out=carry[h].t[:, :], in0=pC[h].t[:, :], in1=carry[h].t[:, :], op=ALU.add))
                    w_ = wT[h].next()
                    S.op("dve", [e_, x_], [w_], lambda e, e_=e_, x_=x_, w_=w_: e.tensor_tensor(out=w_.t[:, :], in0=e_.t[:, :], in1=x_.t[:, :], op=ALU.mult))
                    vt = Vall[kt]
                    S.op("pe", [vt, w_], [pO[h]], lambda e, h=h, vt=vt, w_=w_: e.matmul(pO[h].t[:, :], lhsT=vt.t[:, h * 128:(h + 1) * 128], rhs=w_.t[:, :],
                                                                                       start=first, stop=last))
            for h in range(NH):
                o_ = oS.next()
                S.op("act", [pO[h]], [o_], lambda e, h=h, o_=o_: e.activation(out=o_.t[:, :], in_=pO[h].t[:, :], func=AF.Copy))
                r0 = orow0 + heads[h] * 128
                _ores, _oap = _oput(dr, r0, t0, T)
                S.dma("sp", [o_], [_ores], out=_oap, in_=o_.t[:, :])
        S.barrier()
    S.es = oes


def emit_hgrn(S, c, dr, heads, col_q, col_f, col_i, col_g, orow0):
    D, SEQ = c["D"], c["SEQ"]
    KC = D // 128
    T = 512
    C = 64
    NH = len(heads)
    NW = c.get("NW", 4)
    oes = S.es
    S.new_phase()
    with ExitStack() as pes:
        S.es = pes
        aTt, aT = views(S, "aT", [128, KC, T], BF16, KC, lambda t, i: t[:, i, :])
        wring = make_wring(S, "wsl", NW)
        ident = S.sbuf("ident", [128, 128], F32)
        identb = S.sbuf("identb", [128, 128], BF16)
        maskU = S.sbuf("maskU", [64, 64], F32)
        rmask = S.sbuf("rmask", [128, T], F32)
        NWt = S.sbuf("NWt", [128, 128], F32)
        epsR = S.sbuf("epsR", [128, 1], F32)
        lbr = S.sbuf("lbr", [128, 2, NH], F32)
        lb = S.sbuf("lb", [128, NH], F32)
        oml = S.sbuf("oml", [128, NH], F32)
        noml = S.sbuf("noml", [128, NH], F32)
        qd = [S.sbuf(f"qd{h}", [128, T], BF16) for h in range(NH)]
        kd = [S.sbuf(f"kd{h}", [128, T], BF16) for h in range(NH)]
        iTb = [S.sbuf(f"iTb{h}", [128, T], BF16) for h in range(NH)]
        gf = [S.sbuf(f"gf{h}", [128, T], F32) for h in range(NH)]
        eb = [S.sbuf(f"eb{h}", [128, T], F32) for h in range(NH)]
        qf = [S.sbuf(f"qf{h}", [128, T], F32) for h in range(NH)]
        sg = [S.sbuf(f"sg{h}", [128, T], F32) for h in range(NH)]
        oS = [S.sbuf(f"oS{h}", [128, T], BF16) for h in range(NH)]
        state = [S.sbuf(f"state{h}", [128, 128], F32) for h in range(NH)]
        stateb = [S.sbuf(f"stateb{h}", [128, 128], BF16) for h in range(NH)]
        tmp = Ring([S.sbuf(f"tmp{i}", [128, T], F32) for i in range(6)])
        kvt = Ring([S.sbuf(f"kvt{i}", [64, 256], BF16) for i in range(3)])
        gate = Ring([S.sbuf(f"gate{i}", [64, 128], F32) for i in range(3)])
        scb = Ring([S.sbuf(f"scb{i}", [64, 64], BF16) for i in range(3)])
        junk = Ring([S.sbuf(f"junk{i}", [64, 128], F32) for i in range(2)])
        ss = Ring([S.sbuf(f"ss{i}", [64, 4], F32) for i in range(4)])
        t1r = Ring([S.sbuf(f"t1r{i}", [64, 128], F32) for i in range(3)])
        t2r = Ring([S.sbuf(f"t2r{i}", [64, 128], BF16) for i in range(3)])
        st1 = Ring([S.sbuf(f"st1{i}", [128, 128], F32) for i in range(2)])
        pb = [S.psum(f"pb{i}", [128, 512], F32) for i in range(8)]
        plin = Ring(pb[0:4])
        ptr = Ring(pb[2:4])
        pwk = Ring(pb[4:8])

        S.dma("sp", [dr["ident"]], [ident], out=ident.t[:], in_=dr["ident"].t)
        S.dma("sp", [dr["identb"]], [identb], out=identb.t[:], in_=dr["identb"].t)
        S.dma("sp", [dr["maskU"]], [maskU], out=maskU.t[:], in_=dr["maskU"].t)
        S.dma("sp", [dr["rmask"]], [rmask], out=rmask.t[:], in_=dr["rmask"].t)
        S.dma("sp", [dr["hg_nw"]], [NWt], out=NWt.t[:], in_=dr["hg_nw"].t)
        S.dma("sp", [dr["hg_lb"]], [lbr], out=lbr.t[:], in_=dr["hg_lb"].t)
        S.op("dve", [], [epsR], lambda e: e.memset(epsR.t[:], 1e-6))
        S.op("dve", [lbr], [lb], lambda e: e.tensor_tensor(out=lb.t[:], in0=lbr.t[:, 0, :], in1=lbr.t[:, 1, :], op=ALU.subtract))
        S.op("act", [lb], [lb], lambda e: e.activation(out=lb.t[:], in_=lb.t[:], func=AF.Sigmoid))
        S.op("dve", [lb], [oml], lambda e: e.tensor_scalar(out=oml.t[:], in0=lb.t[:], scalar1=-1.0, scalar2=1.0, op0=ALU.mult, op1=ALU.add))
        S.op("dve", [oml], [noml], lambda e: e.tensor_scalar(out=noml.t[:], in0=oml.t[:], scalar1=-1.0, scalar2=None, op0=ALU.mult))
        for h in range(NH):
            S.op("dve", [], [state[h]], lambda e, h=h: e.memset(state[h].t[:], 0.0))
            S.op("dve", [], [stateb[h]], lambda e, h=h: e.memset(stateb[h].t[:], 0.0))

        for tb in range(SEQ // T):
            t0 = tb * T
            hk = max(KC // 2, 1)
            for k0 in range(0, KC, hk):
                S.dma("pool", [dr["xT"]], aT[k0:k0 + hk], out=aTt[:, k0:k0 + hk, :],
                      in_=dr["xT"].t[k0 * 128:(k0 + hk) * 128, t0:t0 + T].rearrange("(kc p) t -> p kc t", p=128))
            cols = []
            hs_ = c.get("hstride", 128)
            for h in heads:
                cols += [col_q + h * hs_, col_f + h * hs_, col_i + h * hs_, col_g + h * hs_]

            def evac(i, ps):
                hi, kind = divmod(i, 4)
                if kind == 0:
                    S.op("act", [ps], [qf[hi]], lambda e: e.activation(out=qf[hi].t[:, :], in_=ps.t[:, :], func=AF.Copy))
                elif kind == 1:
                    S.op("act", [ps], [sg[hi]], lambda e: e.activation(out=sg[hi].t[:, :], in_=ps.t[:, :], func=AF.Sigmoid))
                elif kind == 2:
                    S.op("act", [ps], [iTb[hi]], lambda e: e.activation(out=iTb[hi].t[:, :], in_=ps.t[:, :], func=AF.Copy))
                else:
                    S.op("dve", [ps], [gf[hi]], lambda e: e.tensor_copy(out=gf[hi].t[:, :], in_=ps.t[:, :]))
            _linear(S, wring, NW, dr["wA"], aT, T, cols, evac, plin)

            for h in range(NH):
                f_ = tmp.next()
                S.op("dve", [sg[h], oml, lb], [f_], lambda e: e.tensor_scalar(out=f_.t[:, :], in0=sg[h].t[:, :], scalar1=oml.t[:, h:h + 1], scalar2=lb.t[:, h:h + 1],
                                                                               op0=ALU.mult, op1=ALU.add))
                lf = tmp.next()
                S.op("act", [f_], [lf], lambda e: e.activation(out=lf.t[:, :], in_=f_.t[:, :], func=AF.Ln))
                k_ = tmp.next()
                S.op("dve", [sg[h], noml, oml], [k_], lambda e: e.tensor_scalar(out=k_.t[:, :], in0=sg[h].t[:, :], scalar1=noml.t[:, h:h + 1], scalar2=oml.t[:, h:h + 1],
                                                                                 op0=ALU.mult, op1=ALU.add))
                b_ = tmp.next()
                S.op("dve", [rmask, lf], [b_], lambda e: e.tensor_tensor_scan(out=b_.t[:, :], data0=rmask.t[:, :], data1=lf.t[:, :], initial=0.0, op0=ALU.mult, op1=ALU.add))
                S.op("act", [b_], [eb[h]], lambda e: e.activation(out=eb[h].t[:, :], in_=b_.t[:, :], func=AF.Exp))
                enb = tmp.next()
                S.op("act", [b_], [enb], lambda e: e.activation(out=enb.t[:, :], in_=b_.t[:, :], func=AF.Exp, scale=-1.0))
                S.op("dve", [qf[h], eb[h]], [qd[h]], lambda e: e.tensor_tensor(out=qd[h].t[:, :], in0=qf[h].t[:, :], in1=eb[h].t[:, :], op=ALU.mult))
                S.op("pool", [k_, enb], [kd[h]], lambda e: e.tensor_tensor(out=kd[h].t[:, :], in0=k_.t[:, :], in1=enb.t[:, :], op=ALU.mult))

            for cc in range(T // C):
                cs = slice(cc * C, (cc + 1) * C)
                for h in range(NH):
                    TR = ptr.next()
                    P = pwk.next()
                    S.op("pe", [kd[h], identb], [TR], lambda e: e.matmul(TR.t[0:64, 0:128], lhsT=kd[h].t[:, cs], rhs=identb.t[:], start=True, stop=True))
                    S.op("pe", [iTb[h], identb], [TR], lambda e: e.matmul(TR.t[0:64, 128:256], lhsT=iTb[h].t[:, cs], rhs=identb.t[:], start=True, stop=True))
                    S.op("pe", [gf[h], ident], [TR], lambda e: e.transpose(out=TR.t[0:64, 256:384], in_=gf[h].t[:, cs], identity=ident.t[:]))
                    kv_ = kvt.next()
                    S.op("dve", [TR], [kv_], lambda e: e.tensor_copy(out=kv_.t[:, :], in_=TR.t[0:64, 0:256]))
                    g_ = gate.next()
                    S.op("act", [TR], [g_], lambda e: e.activation(out=g_.t[:, :], in_=TR.t[0:64, 256:384], func=AF.Silu))
                    S.op("pe", [kd[h], qd[h]], [P], lambda e: e.matmul(P.t[0:64, 0:64], lhsT=kd[h].t[:, cs], rhs=qd[h].t[:, cs], start=True, stop=True))
                    sc_ = scb.next()
                    S.op("dve", [P, maskU], [sc_], lambda e: e.tensor_tensor(out=sc_.t[:, :], in0=P.t[0:64, 0:64], in1=maskU.t[:, :], op=ALU.mult))
                    S.op("pe", [sc_, kv_], [P], lambda e: e.matmul(P.t[0:64, 64:192], lhsT=sc_.t[:, :], rhs=kv_.t[:, 128:256], start=True, stop=False))
                    S.op("pe", [qd[h], stateb[h]], [P], lambda e: e.matmul(P.t[0:64, 64:192], lhsT=qd[h].t[:, cs], rhs=stateb[h].t[:, :], start=False, stop=True))
                    S.op("pe", [kv_], [P], lambda e: e.matmul(P.t[:, 256:384], lhsT=kv_.t[:, 0:128], rhs=kv_.t[:, 128:256], start=True, stop=True))
                    s1 = st1.next()
                    S.op("dve", [state[h], P], [s1], lambda e: e.tensor_tensor(out=s1.t[:, :], in0=state[h].t[:, :], in1=P.t[:, 256:384], op=ALU.add))
                    S.op("dve", [s1, eb[h]], [state[h]], lambda e: e.tensor_scalar(out=state[h].t[:, :], in0=s1.t[:, :], scalar1=eb[h].t[:, cc * C + C - 1:cc * C + C], scalar2=None,
                                                                                  op0=ALU.mult))
                    S.op("act", [state[h]], [stateb[h]], lambda e: e.activation(out=stateb[h].t[:, :], in_=state[h].t[:, :], func=AF.Copy))
                    jk = junk.next()
                    s_ = ss.next()
                    S.op("act", [P], [jk, s_], lambda e: e.activation(out=jk.t[:, :], in_=P.t[0:64, 64:192], func=AF.Square, accum_out=s_.t[:, 0:1]))
                    S.op("act", [s_, epsR], [s_], lambda e: e.activation(out=s_.t[:, 1:2], in_=s_.t[:, 0:1], func=AF.Sqrt, scale=1.0 / 128.0, bias=epsR.t[0:64, 0:1]))
                    S.op("dve", [s_], [s_], lambda e: e.reciprocal(out=s_.t[:, 2:3], in_=s_.t[:, 1:2]))
                    t1 = t1r.next()
                    S.op("dve", [P, s_, NWt], [t1], lambda e: e.scalar_tensor_tensor(out=t1.t[:, :], in0=P.t[0:64, 64:192], scalar=s_.t[:, 2:3], in1=NWt.t[0:64, :],
                                                                                   op0=ALU.mult, op1=ALU.mult))
                    t2 = t2r.next()
                    S.op("pool", [t1, g_], [t2], lambda e: e.tensor_tensor(out=t2.t[:, :], in0=t1.t[:, :], in1=g_.t[:, :], op=ALU.mult))
                    S.op("pe", [t2, identb], [P], lambda e: e.matmul(P.t[:, 384:448], lhsT=t2.t[:, :], rhs=identb.t[0:64, 0:64], start=True, stop=True))
                    S.op("act", [P], [oS[h]], lambda e: e.activation(out=oS[h].t[:, cs], in_=P.t[:, 384:448], func=AF.Copy))
            for h in range(NH):
                r0 = orow0 + h * 128
                _ores, _oap = _oput(dr, r0, t0, T)
                S.dma("sp", [oS[h]], [_ores], out=_oap, in_=oS[h].t[:, :])
        S.barrier()
    S.es = oes


NEGB = 30000.0


def emit_nsa(S, c, dr):
    D, SEQ = c["D"], c["SEQ"]
    KC = D // 128
    T = 512
    R = 8
    NW = c.get("NW", 3)
    NT = SEQ // 128
    NB = SEQ // T
    NMT = (SEQ // 16 + 127) // 128
    NSLOT = NMT * 128
    scale = 128.0 ** -0.5
    oes = S.es
    S.new_phase()
    with ExitStack() as pes:
        S.es = pes
        _, aT = views(S, "aT", [128, KC, T], BF16, KC, lambda t, i: t[:, i, :])
        wring = make_wring(S, "wsl", NW)
        ident = S.sbuf("ident", [128, 128], F32)
        identb = S.sbuf("identb", [128, 128], BF16)
        ones_b = S.sbuf("ones_b", [128, 128], BF16)
        rotm = S.sbuf("rotm", [128, 128], BF16)
        ksT = S.sbuf("ksT", [128, SEQ], BF16)
        _, vsT = views(S, "vsT", [128, NT, 128], BF16, NT, lambda t, i: t[:, i, :])
        kwT = S.sbuf("kwT", [128, 1024], BF16)
        _, vwT = views(S, "vwT", [128, 8, 128], BF16, 8, lambda t, i: t[:, i, :])
        kcmpT = S.sbuf("kcmpT", [128, NSLOT], BF16)
        vcmpT = S.sbuf("vcmpT", [128, NSLOT], F32)
        _, vcmp = views(S, "vcmp", [128, NMT, 128], BF16, NMT, lambda t, i: t[:, i, :])
        kcb = S.sbuf("kcb", [128, 16 + T], BF16)
        vcb = S.sbuf("vcb", [128, 16 + T], BF16)
        w2 = S.sbuf("w2", [128, 2, 128], BF16)
        posT = S.sbuf("posT", [128, 2, 32], BF16)
        posb = S.sbuf("posb", [128, 2], F32)
        ov = S.sbuf("ov", [128, NMT, 128], BF16)
        ex32 = S.sbuf("ex32", [32, 16, 128], BF16)
        wmask = S.sbuf("wmask", [128, 8, T], BF16)
        cmask = S.sbuf("cmask", [128, 5, T], BF16)
        ABt = S.sbuf("ABt", [128, 256], F32)
        FBt = S.sbuf("FBt", [128, 256], F32)
        qn = [S.sbuf(f"qn{r}", [128, T], BF16) for r in range(R)]
        qr = [S.sbuf(f"qr{r}", [128, T], BF16) for r in range(R)]
        cosb = S.sbuf("cosb", [128, T], F32)
        sinb = S.sbuf("sinb", [128, T], F32)
        gT = S.sbuf("gT", [32, T], F32)
        gtok = S.sbuf("gtok", [128, 4, 24], F32)
        _, otok = views(S, "otok", [128, 4, R * 128], F32, 4, lambda t, i: t[:, i, :])
        selT = [[S.sbuf(f"selT{g}_{i}", [32, T], BF16) for g in range(4)] for i in range(1)][0]
        tf = Ring([S.sbuf(f"tf{i}", [128, T], F32) for i in range(4)])
        tb16 = Ring([S.sbuf(f"tb16{i}", [128, T], BF16) for i in range(3)])
        Er = Ring([S.sbuf(f"Er{i}", [128, T], BF16) for i in range(6)])
        Pr = Ring([S.sbuf(f"Pr{i}", [128, T], BF16) for i in range(3)])
        hid = Ring([S.sbuf(f"hid{i}", [128, 32], BF16) for i in range(2)])
        sm = Ring([S.sbuf(f"sm{i}", [128, 128], F32) for i in range(4)])
        smb = Ring([S.sbuf(f"smb{i}", [128, 128], BF16) for i in range(2)])
        m8 = Ring([S.sbuf(f"m8{i}", [128, 16], F32) for i in range(2)])
        fac = Ring([S.sbuf(f"fac{i}", [128, 8], F32) for i in range(4)])
        oS = Ring([S.sbuf(f"oS{i}", [128, T], BF16) for i in range(2)])
        pb = [S.psum(f"pb{i}", [128, 512], F32) for i in range(8)]
        plin = Ring([pb[0], pb[1], pb[2], pb[3]])
        pS = Ring(pb[2:4])
        pO = Ring([pb[4], pb[7]])
        pD = pb[5]
        pI = pb[6]
        pmisc = Ring([pb[4], pb[7], pb[5]])

        def ld(q, name, dst, src=None):
            S.dma(q, [dr[name]], [dst], out=dst.t[:], in_=(dr[name].t if src is None else src))
        ld("sp", "ident", ident); ld("sp", "identb", identb); ld("sp", "rotm", rotm)
        ld("pool", "cmp_w2", w2); ld("pool", "cmp_posT", posT)
        ld("sp", "ov", ov); ld("sp", "ex32", ex32); ld("sp", "wmask", wmask); ld("sp", "cmask", cmask)
        ld("sp", "ABt", ABt); ld("sp", "FBt", FBt)
        S.op("dve", [], [ones_b], lambda e: e.memset(ones_b.t[:], 1.0))
        S.op("dve", [], [kcb], lambda e: e.memset(kcb.t[:], 0.0))
        S.op("dve", [], [vcb], lambda e: e.memset(vcb.t[:], 0.0))
        S.op("dve", [], [vcmpT], lambda e: e.memset(vcmpT.t[:], 0.0))
        S.op("dve", [], [kcmpT], lambda e: e.memset(kcmpT.t[:], 0.0))

        def load_w1(j):
            slot = wring.next()
            v = slot.t[:, :].rearrange("p (k c) -> p k c", c=128)
            S.dma("pool", [dr["cmp_w1"]], [slot], out=v, in_=dr["cmp_w1"].t[j].rearrange("l d e -> d l e"))
            return slot, v
        for j in range(2):
            slot, sv = load_w1(j)
            ps = pmisc.next()
            for l in range(32):
                S.op("pe", [slot, posT], [ps], lambda e, l=l: e.matmul(ps.t[:, 0:1], lhsT=sv[:, l, :], rhs=posT.t[:, j, l:l + 1], start=(l == 0), stop=(l == 31)))
            S.op("act", [ps], [posb], lambda e: e.activation(out=posb.t[:, j:j + 1], in_=ps.t[:, 0:1], func=AF.Copy))

        CQ, CKC, CVC, CKS, CVS, CKW, CVW, CG = 0, 1024, 1152, 1280, 1408, 1536, 1664, 1792

        def rope(src_f32, dst_bf, sc):
            sb = tb16.next()
            S.op("act", [src_f32], [sb], lambda e: e.activation(out=sb.t[:, :], in_=src_f32.t[:, :], func=AF.Copy))
            ps = pmisc.next()
            S.op("pe", [rotm, sb], [ps], lambda e: e.matmul(ps.t[:, :], lhsT=rotm.t[:], rhs=sb.t[:, :], start=True, stop=True))
            t1 = tf.next()
            S.op("dve", [src_f32, cosb], [t1], lambda e: e.tensor_tensor(out=t1.t[:, :], in0=src_f32.t[:, :], in1=cosb.t[:, :], op=ALU.mult))
            t2 = tf.next()
            S.op("dve", [ps, sinb], [t2], lambda e: e.tensor_tensor(out=t2.t[:, :], in0=ps.t[:, :], in1=sinb.t[:, :], op=ALU.mult))
            S.op("pool", [t1, t2], [dst_bf[0]], lambda e: e.tensor_tensor(out=dst_bf[1], in0=t1.t[:, :], in1=t2.t[:, :], op=ALU.add))

        def to_tok(src_f32, dst_views, base):
            for j in range(T // 128):
                pt = pmisc.next()
                S.op("pe", [src_f32, ident], [pt], lambda e, j=j, pt=pt: e.transpose(out=pt.t[:, 0:128], in_=src_f32.t[:, j * 128:(j + 1) * 128], identity=ident.t[:]))
                dv = dst_views[base + j]
                S.op("dve", [pt], [dv], lambda e, pt=pt, dv=dv: e.tensor_copy(out=dv.t[:, :], in_=pt.t[:, 0:128]))

        for tb in range(NB):
            t0 = tb * T
            gt0 = t0 // 128
            for k in range(KC):
                if "hAG_get" in dr:
                    _hres, _hap = dr["hAG_get"](k, t0, T)
                    S.dma("sp", [_hres], [aT[k]], out=aT[k].t[:, :], in_=_hap)
                else:
                    S.dma("sp", [dr["hT"]], [aT[k]], out=aT[k].t[:, :], in_=dr["hT"].t[k * 128:(k + 1) * 128, t0:t0 + T])
            S.dma("sp", [dr["cosT"]], [cosb], out=cosb.t[:, :], in_=dr["cosT"].t[:, t0:t0 + T])
            S.dma("sp", [dr["sinT"]], [sinb], out=sinb.t[:, :], in_=dr["sinT"].t[:, t0:t0 + T])
            if tb > 0:
                S.op("dve", [kcb], [kcb], lambda e: e.tensor_copy(out=kcb.t[:, 0:16], in_=kcb.t[:, T:T + 16]))
                S.op("dve", [vcb], [vcb], lambda e: e.tensor_copy(out=vcb.t[:, 0:16], in_=vcb.t[:, T:T + 16]))
            cols = [CQ + r * 128 for r in range(R)] + [CKC, CVC, CKS, CVS, CKW, CVW]

            def evac(i, ps):
                if i < R:
                    qf = tf.next()
                    S.op("act", [ps], [qf], lambda e: e.activation(out=qf.t[:, :], in_=ps.t[:, :], func=AF.Copy, scale=scale))
                    S.op("dve", [qf], [qn[i]], lambda e: e.tensor_copy(out=qn[i].t[:, :], in_=qf.t[:, :]))
                    rope(qf, (qr[i], qr[i].t[:, :]), 1.0)
                elif i == R:
                    S.op("act", [ps], [kcb], lambda e: e.activation(out=kcb.t[:, 16:16 + T], in_=ps.t[:, :], func=AF.Copy))
                elif i == R + 1:
                    S.op("act", [ps], [vcb], lambda e: e.activation(out=vcb.t[:, 16:16 + T], in_=ps.t[:, :], func=AF.Copy))
                elif i == R + 2:
                    kf = tf.next()
                    S.op("act", [ps], [kf], lambda e: e.activation(out=kf.t[:, :], in_=ps.t[:, :], func=AF.Copy))
                    rope(kf, (ksT, ksT.t[:, t0:t0 + T]), 1.0)
                elif i == R + 3:
                    vf = tf.next()
                    S.op("act", [ps], [vf], lambda e: e.activation(out=vf.t[:, :], in_=ps.t[:, :], func=AF.Copy))
                    to_tok(vf, vsT, gt0)
                elif i == R + 4:
                    kf = tf.next()
                    S.op("act", [ps], [kf], lambda e: e.activation(out=kf.t[:, :], in_=ps.t[:, :], func=AF.Copy))
                    c0 = (tb % 2) * T
                    rope(kf, (kwT, kwT.t[:, c0:c0 + T]), 1.0)
                else:
                    vf = tf.next()
                    S.op("act", [ps], [vf], lambda e: e.activation(out=vf.t[:, :], in_=ps.t[:, :], func=AF.Copy))
                    to_tok(vf, vwT, (gt0 % 8))
            _linear(S, wring, NW, dr["wC"], aT, T, cols, evac, plin)
            slot = wring.next()
            gv = slot.t[:, :].rearrange("p (k c) -> p k c", c=128)[:, 0:KC, 0:24]
            S.dma("pool", [dr["wC"]], [slot], out=gv, in_=dr["wC"].t[0:KC * 128, CG:CG + 24].rearrange("(kc p) c -> p kc c", p=128))
            ps = plin.next()
            for k in range(KC):
                S.op("pe", [slot, aT[k]], [ps], lambda e, k=k: e.matmul(ps.t[0:24, :], lhsT=gv[:, k, 0:24], rhs=aT[k].t[:, :], start=(k == 0), stop=(k == KC - 1)))
            S.op("act", [ps], [gT], lambda e: e.activation(out=gT.t[0:24, :], in_=ps.t[0:24, :], func=AF.Sigmoid))
            for j in range(4):
                pt = pmisc.next()
                S.op("pe", [gT, ident], [pt], lambda e, j=j, pt=pt: e.transpose(out=pt.t[:, 0:24], in_=gT.t[0:24, j * 128:(j + 1) * 128], identity=ident.t[0:24, 0:24]))
                S.op("dve", [pt], [gtok], lambda e, j=j, pt=pt: e.tensor_copy(out=gtok.t[:, j, :], in_=pt.t[:, 0:24]))

            for j, (buf, dstT) in enumerate(((kcb, kcmpT), (vcb, vcmpT))):
                slot, sv = load_w1(j)
                ps = pmisc.next()
                for l in range(32):
                    S.op("pe", [slot, buf], [ps], lambda e, l=l: e.matmul(ps.t[:, 0:32], lhsT=sv[:, l, :], rhs=buf.t[:, l:l + 16 * 31 + 1:16], start=(l == 0), stop=(l == 31)))
                hd = hid.next()
                S.op("act", [ps, posb], [hd], lambda e: e.activation(out=hd.t[:, :], in_=ps.t[:, 0:32], func=AF.Gelu, bias=posb.t[:, j:j + 1]))
                ps2 = pmisc.next()
                S.op("pe", [w2, hd], [ps2], lambda e: e.matmul(ps2.t[:, 0:32], lhsT=w2.t[:, j, :], rhs=hd.t[:, :], start=True, stop=True))
                S.op("act", [ps2], [dstT], lambda e: e.activation(out=dstT.t[:, 32 * tb:32 * tb + 32], in_=ps2.t[:, 0:32], func=AF.Copy))
            mtc = (32 * tb) // 128
            pt = pmisc.next()
            S.op("pe", [vcmpT, ident], [pt], lambda e: e.transpose(out=pt.t[:, 0:128], in_=vcmpT.t[:, mtc * 128:(mtc + 1) * 128], identity=ident.t[:]))
            S.op("dve", [pt], [vcmp[mtc]], lambda e: e.tensor_copy(out=vcmp[mtc].t[:, :], in_=pt.t[:, 0:128]))

            nmt = mtc + 1
            for r in range(R):
                Es = []
                for mt in range(nmt):
                    v = tb - 4 * mt
                    ps = pS.next()
                    mks = ([v] if v < 4 else []) + ([4] if mt == 0 else [])
                    S.op("pe", [kcmpT, qn[r]], [ps], lambda e: e.matmul(ps.t[:, :], lhsT=kcmpT.t[:, mt * 128:(mt + 1) * 128], rhs=qn[r].t[:, :], start=True, stop=not mks))
                    for mi, mv in enumerate(mks):
                        S.op("pe", [identb, cmask], [ps], lambda e: e.matmul(ps.t[:, :], lhsT=identb.t[:], rhs=cmask.t[:, mv, :], start=False, stop=(mi == len(mks) - 1)))
                    E = Er.next()
                    S.op("act", [ps], [E], lambda e: e.activation(out=E.t[:, :], in_=ps.t[:, :], func=AF.Exp))
                    Es.append(E)
                pden = pS.next()
                for mt, E in enumerate(Es):
                    S.op("pe", [ones_b, E], [pden], lambda e, mt=mt, E=E: e.matmul(pden.t[:, :], lhsT=ones_b.t[:], rhs=E.t[:, :], start=(mt == 0), stop=(mt == nmt - 1)))
                rd = tf.next()
                S.op("dve", [pden], [rd], lambda e: e.tensor_scalar(out=rd.t[:, :], in0=pden.t[:, :], scalar1=1e-30, scalar2=None, op0=ALU.max))
                S.op("dve", [rd], [rd], lambda e: e.reciprocal(out=rd.t[:, :], in_=rd.t[:, :]))
                po = pO.next()
                Ps = []
                for mt, E in enumerate(Es):
                    P = Pr.next()
                    S.op("dve", [E, rd], [P], lambda e, E=E, P=P: e.tensor_tensor(out=P.t[:, :], in0=E.t[:, :], in1=rd.t[:, :], op=ALU.mult))
                    for tt in range(4):
                        S.op("pe", [P, vcmp[mt]], [po], lambda e, tt=tt, mt=mt, P=P: e.matmul(po.t[:, tt * 128:(tt + 1) * 128], lhsT=P.t[:, tt * 128:(tt + 1) * 128], rhs=vcmp[mt].t[:, :],
                                                                                           start=(mt == 0 and tt == 0), stop=(mt == nmt - 1), skip_group_check=True))
                        S.op("pe", [P, ov], [pI], lambda e, tt=tt, mt=mt, P=P: e.matmul(pI.t[:, tt * 128:(tt + 1) * 128], lhsT=P.t[:, tt * 128:(tt + 1) * 128], rhs=ov.t[:, mt, :],
                                                                                     start=(r == 0 and mt == 0 and tt == 0), stop=(r == R - 1 and mt == nmt - 1), skip_group_check=True))
                for tt in range(4):
                    S.op("dve", [po, gtok], [otok[tt]], lambda e, tt=tt: e.tensor_scalar(out=otok[tt].t[:, r * 128:(r + 1) * 128], in0=po.t[:, tt * 128:(tt + 1) * 128],
                                                                                      scalar1=gtok.t[:, tt, 3 * r:3 * r + 1], scalar2=None, op0=ALU.mult))

            for tt in range(4):
                gt = gt0 + tt
                x0 = 128 - 2 * gt
                s1 = sm.next()
                S.op("dve", [pI, ABt], [s1], lambda e: e.scalar_tensor_tensor(out=s1.t[:, :], in0=pI.t[:, tt * 128:(tt + 1) * 128], scalar=1.0, in1=ABt.t[:, x0:x0 + 128],
                                                                             op0=ALU.add, op1=ALU.mult))
                s2 = sm.next()
                S.op("dve", [s1, FBt], [s2], lambda e: e.scalar_tensor_tensor(out=s2.t[:, :], in0=s1.t[:, :], scalar=-1.0, in1=FBt.t[:, x0:x0 + 128], op0=ALU.add, op1=ALU.max))
                S.op("dve", [s2], [s2], lambda e: e.memset(s2.t[:, 0:1], 1e9))
                mm = m8.next()
                S.op("dve", [s2], [mm], lambda e: e.max(out=mm.t[:, 0:8], in_=s2.t[:, :]))
                s3 = sm.next()
                S.op("dve", [s2, mm], [s3], lambda e: e.match_replace(out=s3.t[:, :], in_to_replace=mm.t[:, 0:8], in_values=s2.t[:, :], imm_value=-1e30))
                S.op("dve", [s3], [mm], lambda e: e.max(out=mm.t[:, 8:16], in_=s3.t[:, :]))
                sb_ = smb.next()
                S.op("dve", [s2, mm], [sb_], lambda e: e.tensor_scalar(out=sb_.t[:, :], in0=s2.t[:, :], scalar1=mm.t[:, 15:16], scalar2=1.0, op0=ALU.is_ge, op1=ALU.subtract))
                for g in range(4):
                    pt = pmisc.next()
                    S.op("pe", [sb_, identb], [pt], lambda e, g=g, pt=pt: e.matmul(pt.t[0:32, 0:128], lhsT=sb_.t[:, 32 * g:32 * g + 32], rhs=identb.t[:], start=True, stop=True))
                    S.op("act", [pt], [selT[g]], lambda e, g=g, pt=pt: e.activation(out=selT[g].t[:, tt * 128:(tt + 1) * 128], in_=pt.t[0:32, 0:128], func=AF.Copy))

            def attend(r, kTsrc, kcol, vviews, vidx, kts, use_sel, gidx):
                po = pO.next()
                n = len(kts)
                pss = {}

                def emit_s(ni):
                    kt = kts[ni]
                    rel = kt - gt0
                    ps = pS.next()
                    pss[ni] = ps
                    has_w = (not use_sel) or (rel >= 0)
                    S.op("pe", [kTsrc, qr[r]], [ps], lambda e: e.matmul(ps.t[:, :], lhsT=kTsrc.t[:, kcol(kt):kcol(kt) + 128], rhs=qr[r].t[:, :], start=True,
                                                                       stop=not (use_sel or has_w)))
                    if use_sel:
                        g = kt // 16
                        S.op("pe", [ex32, selT[g]], [ps], lambda e: e.matmul(ps.t[:, :], lhsT=ex32.t[:, kt % 16, :], rhs=selT[g].t[:, :], start=False, stop=not has_w))
                    if has_w:
                        S.op("pe", [identb, wmask], [ps], lambda e: e.matmul(ps.t[:, :], lhsT=identb.t[:], rhs=wmask.t[:, rel + 4, :], start=False, stop=True))
                emit_s(0)
                for ni, kt in enumerate(kts):
                    if ni + 1 < n:
                        emit_s(ni + 1)
                    ps = pss.pop(ni)
                    E = Er.next()
                    S.op("act", [ps], [E], lambda e: e.activation(out=E.t[:, :], in_=ps.t[:, :], func=AF.Exp))
                    vv = vviews[vidx(kt)]
                    for tt in range(4):
                        S.op("pe", [E, vv], [po], lambda e, tt=tt: e.matmul(po.t[:, tt * 128:(tt + 1) * 128], lhsT=E.t[:, tt * 128:(tt + 1) * 128], rhs=vv.t[:, :],
                                                                          start=(ni == 0 and tt == 0), stop=(ni == n - 1), skip_group_check=True))
                        S.op("pe", [E, ones_b], [pD], lambda e, tt=tt: e.matmul(pD.t[:, (r * 4 + tt) * 2:(r * 4 + tt) * 2 + 1], lhsT=E.t[:, tt * 128:(tt + 1) * 128], rhs=ones_b.t[:, 0:1],
                                                                              start=(ni == 0 and tt == 0), stop=(ni == n - 1), skip_group_check=True))
                fc = fac.next()
                S.op("dve", [pD], [fc], lambda e: e.reciprocal(out=fc.t[:, 0:4], in_=pD.t[:, r * 8:r * 8 + 8:2]))
                for tt in range(4):
                    S.op("dve", [fc, gtok], [fc], lambda e, tt=tt: e.tensor_tensor(out=fc.t[:, 4 + tt:5 + tt], in0=fc.t[:, tt:tt + 1], in1=gtok.t[:, tt, 3 * r + gidx:3 * r + gidx + 1], op=ALU.mult))
                    S.op("dve", [po, fc, otok[tt]], [otok[tt]], lambda e, tt=tt: e.scalar_tensor_tensor(
                        out=otok[tt].t[:, r * 128:(r + 1) * 128], in0=po.t[:, tt * 128:(tt + 1) * 128], scalar=fc.t[:, 4 + tt:5 + tt],
                        in1=otok[tt].t[:, r * 128:(r + 1) * 128], op0=ALU.mult, op1=ALU.add))

            for r in range(R):
                attend(r, ksT, lambda kt: kt * 128, vsT, lambda kt: kt, list(range(0, gt0 + 4)), True, 1)
                attend(r, kwT, lambda kt: (kt * 128) % 1024, vwT, lambda kt: kt % 8, list(range(max(0, gt0 - 4), gt0 + 4)), False, 2)

            for r in range(R):
                pt = pmisc.next()
                for tt in range(4):
                    S.op("pe", [otok[tt], ident], [pt], lambda e, tt=tt: e.transpose(out=pt.t[:, tt * 128:(tt + 1) * 128], in_=otok[tt].t[:, r * 128:(r + 1) * 128], identity=ident.t[:]))
                o_ = oS.next()
                S.op("act", [pt], [o_], lambda e: e.activation(out=o_.t[:, :], in_=pt.t[:, :], func=AF.Copy))
                _ores, _oap = _oput(dr, r * 128, t0, T)
                S.dma("sp", [o_], [_ores], out=_oap, in_=o_.t[:, :])
        S.barrier()
    S.es = oes


BF = ml_dtypes.bfloat16
def make_consts():
    c = {}
    c['ident'] = np.eye(128, dtype=np.float32)
    c['identb'] = np.eye(128).astype(BF)
    j = np.arange(128)[:, None]; s = np.arange(128)[None, :]
    c['negU'] = np.where(j >= s, -1.0, 0.0).astype(BF)
    sl = np.arange(128)[:, None, None]; kd = np.arange(4)[None, :, None]; t = np.arange(512)[None, None, :]
    c['sbmask'] = np.where(128 * kd + sl < t, 0.0, -30000.0).astype(BF)
    return c
def make_consts_hg():
    c = {}
    s = np.arange(64)[:, None]; t = np.arange(64)[None, :]
    c['maskU'] = (s <= t).astype(np.float32)
    c['rmask'] = np.tile((np.arange(512) % 64 != 0).astype(np.float32)[None, :], (128, 1))
    return c
def make_consts_nsa(S):
    c = {}
    rot = np.zeros((128, 128), np.float32)
    for dp in range(64):
        rot[dp + 64, dp] = -1.0
    for dp in range(64, 128):
        rot[dp - 64, dp] = 1.0
    c['rotm'] = rot.astype(BF)
    NMT = (S // 16 + 127) // 128
    m = np.arange(NMT * 128); n = m - 1
    j = np.arange(128)
    ovm = (n[:, None] >= 0) & (16 * n[:, None] < 64 * j[None, :] + 64) & (16 * n[:, None] + 32 > 64 * j[None, :]) & (j[None, :] < S // 64) & (n[:, None] <= (S - 32) // 16)
    c['ov'] = np.ascontiguousarray(ovm.reshape(NMT, 128, 128).transpose(1, 0, 2)).astype(BF)
    i = np.arange(32)[:, None, None]; ktl = np.arange(16)[None, :, None]; s = np.arange(128)[None, None, :]
    c['ex32'] = np.where(i == 2 * ktl + s // 64, 30000.0, 0.0).astype(BF)
    sl = np.arange(128)[:, None, None]; rel = (np.arange(8) - 4)[None, :, None]; tl = np.arange(512)[None, None, :]
    c['wmask'] = np.where((128 * rel + sl <= tl) & (128 * rel + sl > tl - 512), 0.0, -30000.0).astype(BF)
    v = np.arange(4)[None, :, None]
    cm = np.where(16 * sl + 15 <= 512 * v + tl, 0.0, -30000.0)
    cm4 = np.where(sl == 0, -30000.0, 0.0) + 0 * tl
    c['cmask'] = np.concatenate([cm, cm4], axis=1).astype(BF)
    p = np.arange(128)[:, None]; x = np.arange(256)[None, :]
    cc = (p >= 64).astype(np.int64)
    c['ABt'] = ((x - 128) <= cc).astype(np.float32)
    c['FBt'] = np.where(((x - 128) == cc) | ((x - 128) == cc - 1), 1e9, -2.0).astype(np.float32)
    half = 64
    inv = 10000.0 ** (-np.arange(half, dtype=np.float32) / half)
    ang = np.arange(S, dtype=np.float32)[None, :] * np.concatenate([inv, inv])[:, None]
    c['cosT'] = np.cos(ang).astype(np.float32)
    c['sinT'] = np.sin(ang).astype(np.float32)
    return c


from concourse.bass_utils import run_bass_kernel_spmd

D_MODEL = 4096
SEQ = 8192
BATCH = 2
DFF = 11008
NMEM = 256
ALPHA = 4.0 ** 0.25
NCORE = 8
TOKC = 2048
HALO = 128
GROUPS = [[0, 1, 2, 3], [4, 5, 6, 7]]


def _consts_all():
    c = make_consts()
    c.update(make_consts_hg())
    c.update(make_consts_nsa(SEQ))
    return c


def _build_fused():
    nc = bass.Bass("TRN2", target_bir_lowering=False)
    NT = HALO + TOKC
    KC, FC = D_MODEL // 128, DFF // 128
    with ExitStack() as es:
        S = Sched(nc, es)
        dr = {}

        def din(name, shape, dt=F32):
            dr[name] = S.dram(name, shape, dt, kind="ExternalInput")
        din("xT", [D_MODEL, SEQ]); din("xTc", [D_MODEL, NT]); din("memT", [D_MODEL, NMEM])
        din("wA", [D_MODEL, 3584]); din("wC", [D_MODEL, 1816])
        din("ident", [128, 128]); din("identb", [128, 128], BF16); din("negU", [128, 128], BF16); din("sbmask", [128, 4, 512], BF16)
        din("maskU", [64, 64]); din("rmask", [128, 512]); din("hg_nw", [128, 128]); din("hg_lb", [128, 2, 4])
        din("cmp_w1", [2, 32, 128, 128]); din("cmp_w2", [128, 2, 128]); din("cmp_posT", [128, 2, 32])
        din("rotm", [128, 128], BF16)
        din("ov", [128, 4, 128], BF16); din("ex32", [32, 16, 128], BF16); din("wmask", [128, 8, 512], BF16); din("cmask", [128, 5, 512], BF16)
        din("ABt", [128, 256]); din("FBt", [128, 256]); din("cosT", [128, SEQ]); din("sinT", [128, SEQ])
        din("flag", [128, 1]); din("fsel", [128, 4]); din("fhal", [128, 4])
        for l in range(2):
            din(f"w_out{l}", [D_MODEL, D_MODEL]); din(f"w_q{l}", [D_MODEL, 512]); din(f"w_kv{l}", [D_MODEL, 1024]); din(f"w_o{l}", [512, D_MODEL])
            din(f"w_up{l}", [D_MODEL, 2 * DFF]); din(f"w_down{l}", [DFF, D_MODEL])
            din(f"ln_g{l}", [128, 3, KC]); din(f"ln_b{l}", [128, 3, KC]); din(f"conv{l}", [128, 3, FC])
        outT = S.dram("outT", [D_MODEL, TOKC], F32, kind="ExternalOutput")
        NB = SEQ // 512
        oA = [S.dram_cc(f"oA{i}", [1024, 512], BF16) for i in range(NB)]
        oAG0 = [S.dram_cc(f"oAG0_{i}", [4096, 512], BF16) for i in range(NB)]
        oC = [S.dram_cc(f"oC{i}", [1024, 512], BF16) for i in range(NB)]
        oAG1 = [S.dram_cc(f"oAG1_{i}", [4096, 512], BF16) for i in range(NB)]
        h1loc = S.dram_cc("h1loc", [D_MODEL, NT], F32)
        hb = [S.dram_cc(f"hb{i}", [256, TOKC], BF16) for i in range(16)]
        hAG = [S.dram_cc(f"hAG{i}", [4 * 256, TOKC], BF16) for i in range(16)]
        hl = [S.dram_cc(f"hl{i}", [2048, 128], F32) for i in range(2)]
        hlAG = [S.dram_cc(f"hlAG{i}", [4 * 2048, 128], F32) for i in range(2)]

        def put_fn(lst):
            def f(r0, t0, T):
                return lst[t0 // 512], lst[t0 // 512].t[r0:r0 + 128, 0:T]
            return f

        def oget_fn(lst):
            def f(k, cb, T):
                i, off = divmod(cb, 512)
                return lst[i], lst[i].t[k * 128:(k + 1) * 128, off:off + T]
            return f

        def hb_put(k, c0, T):
            return hb[k // 2], hb[k // 2].t[(k % 2) * 128:(k % 2) * 128 + 128, c0:c0 + T]

        def hAG_get(k, t0, T):
            rk, tl = divmod(t0, TOKC)
            r0 = rk * 256 + (k % 2) * 128
            return hAG[k // 2], hAG[k // 2].t[r0:r0 + 128, tl:tl + T]

        def hl_put(k):
            return hl[k // 16], hl[k // 16].t[(k % 16) * 128:(k % 16) * 128 + 128, 0:128]

        def hlAG_get(k, cq):
            r0 = cq * 2048 + (k % 16) * 128
            return hlAG[k // 16], hlAG[k // 16].t[r0:r0 + 128, 0:128]

        cfgA = dict(D=D_MODEL, SEQ=SEQ, NW=4)
        drA = dict(dr); drA["oT_put"] = put_fn(oA)
        emit_sb(S, cfgA, drA, [0, 1], 0, 256, 512, 512)
        emit_sb(S, cfgA, drA, [2, 3], 512, 768, 1024, 512)
        emit_hgrn(S, dict(cfgA, hstride=512), drA, [0, 1, 2, 3], 1536, 1664, 1792, 1920, 0)
        for i in range(NB):
            S.collective("AllGather", GROUPS, oA[i], oAG0[i])
        blocks = [(0, HALO, True)] + [(HALO + i * 512, 512, False) for i in range(TOKC // 512)]
        cfgP = dict(D=D_MODEL, DFF=DFF, XH=4, NM=NMEM, T=512, alpha=ALPHA, eps=1e-5, GF=8, NW=3, HALO=HALO, TOKC=TOKC)

        def post_dr(l):
            d = dict(flag=dr["flag"], fsel=dr["fsel"], fhal=dr["fhal"], memT=dr["memT"])
            for k in ("w_out", "w_q", "w_kv", "w_o", "w_up", "w_down", "ln_g", "ln_b", "conv"):
                d[k] = dr[f"{k}{l}"]
            return d
        drB = post_dr(0)
        drB.update(oAG_get=oget_fn(oAG0), hT_in=dr["xTc"], hT_out=h1loc, hTb_put=hb_put, hl_put=hl_put)
        emit_post(S, dict(cfgP, out_off=0, outb_off=HALO), drB, blocks)
        for i in range(16):
            S.collective("AllGather", GROUPS, hb[i], hAG[i])
        for i in range(2):
            S.collective("AllGather", GROUPS, hl[i], hlAG[i])
        drC = dict(dr); drC["hAG_get"] = hAG_get; drC["oT_put"] = put_fn(oC)
        emit_nsa(S, dict(D=D_MODEL, SEQ=SEQ, NW=3, TOKC=TOKC), drC)
        for i in range(NB):
            S.collective("AllGather", GROUPS, oC[i], oAG1[i])
        drD = post_dr(1)
        drD.update(oAG_get=oget_fn(oAG1), hT_in=h1loc, hlAG_get=hlAG_get, hT_out=outT)
        emit_post(S, dict(cfgP, out_off=HALO), drD, blocks)
        S.wait_all("sp", [outT])
        S.wait_all("pool", [outT])
    return nc


def _arr3(v, n):
    return np.ascontiguousarray(np.asarray(v, np.float32).reshape(3, n, 128).transpose(2, 0, 1))


def kernel(**inp):
    inp = {k: np.asarray(v) for k, v in inp.items()}
    cs = _consts_all()
    cores = list(range(NCORE))
    KC, FC = D_MODEL // 128, DFF // 128
    NT = HALO + TOKC
    xT = [np.ascontiguousarray(inp["x"][b].T) for b in range(BATCH)]
    memT = [np.ascontiguousarray(inp["mem"][b].T) for b in range(BATCH)]
    w_in = inp["ab_w_in"][0]
    AW = 2048
    wo = inp["ab_w_out"][0]
    wo_perm = np.ascontiguousarray(np.concatenate([np.concatenate([wo[512 * j:512 * j + 512], wo[2048 + 512 * j:2048 + 512 * j + 512]], axis=0) for j in range(4)], axis=0))
    wn = inp["nsa_w_in"][0]
    shared = dict(ident=cs["ident"], identb=cs["identb"], negU=cs["negU"], sbmask=cs["sbmask"], maskU=cs["maskU"], rmask=cs["rmask"],
                  hg_nw=np.ascontiguousarray(np.tile(inp["hgrn_norm_w"][0][None, :], (128, 1))),
                  cmp_w1=np.ascontiguousarray(inp["nsa_cmp_w1"][0]), cmp_w2=np.ascontiguousarray(inp["nsa_cmp_w2"][0].transpose(1, 0, 2)),
                  cmp_posT=np.ascontiguousarray(inp["nsa_cmp_pos"][0].transpose(2, 0, 1)),
                  rotm=cs["rotm"], ov=cs["ov"], ex32=cs["ex32"], wmask=cs["wmask"], cmask=cs["cmask"], ABt=cs["ABt"], FBt=cs["FBt"],
                  cosT=cs["cosT"], sinT=cs["sinT"])
    wouts = [wo_perm, np.ascontiguousarray(inp["nsa_w_out"][0])]
    for l in range(2):
        shared[f"w_out{l}"] = wouts[l]
        shared[f"w_q{l}"] = np.ascontiguousarray(inp["xa_w_q"][l]); shared[f"w_kv{l}"] = np.ascontiguousarray(inp["xa_w_kv"][l])
        shared[f"w_o{l}"] = np.ascontiguousarray(inp["xa_w_o"][l]); shared[f"w_up{l}"] = np.ascontiguousarray(inp["ffn_w_up"][l])
        shared[f"w_down{l}"] = np.ascontiguousarray(inp["ffn_w_down"][l])
        shared[f"ln_g{l}"] = _arr3(inp["ln_g"][l], KC); shared[f"ln_b{l}"] = _arr3(inp["ln_b"][l], KC); shared[f"conv{l}"] = _arr3(inp["ffn_conv"][l], FC)
    maps = []
    for c in cores:
        b, j = divmod(c, 4)
        hs = slice(512 * j, 512 * j + 512)
        def hc(seg, h0, n):
            c0 = seg * AW + 512 * j + 128 * h0
            return w_in[:, c0:c0 + 128 * n]
        segs = [hc(4, 0, 2), hc(5, 0, 2), hc(6, 0, 2), hc(4, 2, 2), hc(5, 2, 2), hc(6, 2, 2)]
        for h in range(4):
            segs += [hc(0, h, 1), hc(1, h, 1), hc(2, h, 1), hc(3, h, 1)]
        wA = np.ascontiguousarray(np.concatenate(segs, axis=1))
        lb = np.ascontiguousarray(inp["hgrn_lb"][:, hs].reshape(2, 4, 128).transpose(2, 0, 1))
        g = j
        segc = [wn[:, 1024 * g:1024 * g + 1024]] + [wn[:, 4096 + 512 * i + 128 * g:4096 + 512 * i + 128 * g + 128] for i in range(6)] + \
               [wn[:, 4096 + 3072 + 24 * g:4096 + 3072 + 24 * g + 24]]
        wC = np.ascontiguousarray(np.concatenate(segc, axis=1))
        t0 = j * TOKC
        xTc = np.zeros((D_MODEL, NT), np.float32)
        xTc[:, HALO:] = xT[b][:, t0:t0 + TOKC]
        if j > 0:
            xTc[:, :HALO] = xT[b][:, t0 - HALO:t0]
        fsel = np.zeros((128, 4), np.float32); fsel[:, j] = 1.0
        fhal = np.zeros((128, 4), np.float32)
        if j > 0:
            fhal[:, j - 1] = 1.0
        m = dict(shared)
        m.update(xT=xT[b], xTc=xTc, memT=memT[b], wA=wA, wC=wC, hg_lb=lb, flag=np.full((128, 1), 1.0 if j > 0 else 0.0, np.float32), fsel=fsel, fhal=fhal)
        maps.append(m)
    res = run_bass_kernel_spmd(_build_fused(), maps, core_ids=cores)
    out = np.empty((BATCH, SEQ, D_MODEL), np.float32)
    for c in cores:
        b, j = divmod(c, 4)
        out[b, j * TOKC:(j + 1) * TOKC, :] = res.results[c]["outT"].T
    return out
```

```python
import numpy as np
import ml_dtypes
from contextlib import ExitStack
import concourse.bass as bass
import concourse.mybir as mybir


F32 = mybir.dt.float32
BF16 = mybir.dt.bfloat16
AF = mybir.ActivationFunctionType
ALU = mybir.AluOpType


class Res:
    __slots__ = ("name", "w", "r", "t", "multi", "mw")

    def __init__(self, name, t=None, multi=False):
        self.name = name
        self.w = None
        self.r = {}
        self.t = t
        self.multi = multi
        self.mw = {}

    def __getitem__(self, idx):
        return self.t[idx]


class Sched:
    def __init__(self, nc, es: ExitStack, n_dma_slots=8):
        self.nc = nc
        self.es = es
        self.es0 = es
        self.E = {"pe": nc.tensor, "act": nc.scalar, "dve": nc.vector, "pool": nc.gpsimd, "sp": nc.sync}
        self.sem = {k: es.enter_context(nc.semaphore("prog_" + k)) for k in self.E}
        self.cnt = {k: 0 for k in self.E}
        self.known = {k: {} for k in self.E}
        self.slots = {}
        for q in ("sp", "pool", "act"):
            self.slots[q] = [[es.enter_context(nc.semaphore(f"dq_{q}_{i}")), 0] for i in range(n_dma_slots)]
        self.slot_i = {q: 0 for q in self.slots}
        self.n_ins = 0
        self.n_wait = 0
        self.prefix = ""
        self.nphase = 0

    def new_phase(self):
        self.prefix = f"p{self.nphase}_"
        self.nphase += 1

    def sbuf(self, name, shape, dt):
        t = self.es.enter_context(self.nc.sbuf_tensor("sb_" + self.prefix + name, list(shape), dt))
        return Res(name, t)

    def psum(self, name, shape, dt=F32):
        t = self.es.enter_context(self.nc.psum_tensor("ps_" + self.prefix + name, list(shape), dt))
        return Res(name, t)

    def dram(self, name, shape, dt, kind="Internal", multi=True):
        t = self.nc.dram_tensor(name, list(shape), dt, kind=kind)
        return Res(name, t.ap(), multi=multi)

    def dram_cc(self, name, shape, dt):
        t = self.nc.dram_tensor(name, list(shape), dt)
        return Res(name, t.ap(), multi=True)

    def collective(self, kind, groups, src, dst):
        if not hasattr(self, "cc_sem"):
            self.cc_sem = self.es0.enter_context(self.nc.semaphore("cc_sem"))
            self.cc_cnt = 0
        self._deps("pool", [src], [])
        for ev in list(dst.mw.values()) + list(dst.r.values()):
            self._wait("pool", ev)
        ins = self.nc.gpsimd.collective_compute(kind, mybir.AluOpType.bypass, replica_groups=groups, ins=[src.t.opt()], outs=[dst.t.opt()])
        self.cc_cnt += 1
        ins.then_inc(self.cc_sem, 1)
        ev = (self.cc_sem, self.cc_cnt, "cc")
        src.r[id(self.cc_sem)] = ev
        dst.mw[id(self.cc_sem)] = ev
        self.n_ins += 1

    def _wait(self, eng, ev):
        sem, val, src = ev
        if src == eng and eng == "pe":
            return
        k = id(sem)
        if self.known[eng].get(k, 0) >= val:
            return
        self.E[eng].wait_ge(sem, val)
        self.known[eng][k] = val
        self.n_wait += 1

    def _deps(self, eng, reads, writes):
        for r in reads:
            if r.w is not None:
                self._wait(eng, r.w)
            for ev in r.mw.values():
                self._wait(eng, ev)
        for w in writes:
            if w.multi:
                continue
            if w.w is not None:
                self._wait(eng, w.w)
            for ev in w.r.values():
                self._wait(eng, ev)

    def _record(self, ev, reads, writes):
        sem = ev[0]
        for r in reads:
            r.r[id(sem)] = ev
        for w in writes:
            if w.multi:
                w.mw[id(sem)] = ev
                continue
            w.w = ev
            w.r = {}

    def op(self, eng, reads, writes, fn):
        self._deps(eng, reads, writes)
        ins = fn(self.E[eng])
        self.cnt[eng] += 1
        ins.then_inc(self.sem[eng], 1)
        ev = (self.sem[eng], self.cnt[eng], eng)
        self.known[eng][id(self.sem[eng])] = max(self.known[eng].get(id(self.sem[eng]), 0), 0)
        self._record(ev, reads, writes)
        self.n_ins += 1
        return ins

    def dma(self, q, reads, writes, out, in_, **kw):
        slots = self.slots[q]
        i = self.slot_i[q]
        self.slot_i[q] = (i + 1) % len(slots)
        s = slots[i]
        if s[1] > 0:
            self._wait(q, (s[0], s[1], "dma"))
        self._deps(q, reads, writes)
        ins = self.E[q].dma_start(out=out, in_=in_, **kw)
        s[1] += 16
        ins.then_inc(s[0], 16)
        ev = (s[0], s[1], "dma")
        self._record(ev, reads, writes)
        self.n_ins += 1
        return ins

    def wait_all(self, eng, resources):
        for r in resources:
            if r.w is not None:
                self._wait(eng, r.w)
            for ev in r.mw.values():
                self._wait(eng, ev)

    def barrier(self):
        evs = [(self.sem[k], self.cnt[k], k) for k in self.E if self.cnt[k] > 0]
        for q, sl in self.slots.items():
            for s in sl:
                if s[1] > 0:
                    evs.append((s[0], s[1], "dma"))
        for eng in self.E:
            for ev in evs:
                sem, val, src = ev
                if self.known[eng].get(id(sem), 0) >= val:
                    continue
                self.E[eng].wait_ge(sem, val)
                self.known[eng][id(sem)] = val


class Ring:
    def __init__(self, items):
        self.items = items
        self.i = 0

    def next(self):
        r = self.items[self.i]
        self.i = (self.i + 1) % len(self.items)
        return r


def views(S, name, shape, dt, n, idx_fn):
    t = S.es.enter_context(S.nc.sbuf_tensor("sb_" + S.prefix + name, list(shape), dt))
    return t, [Res(f"{name}{i}", idx_fn(t, i)) for i in range(n)]


def make_wring(S, name, nslots):
    _, sl = views(S, name, [128, nslots, 4096], BF16, nslots, lambda t, i: t[:, i, :])
    return Ring(sl)


def linear_g(S, wring, NW, W, r0, acts, T, groups, evac, banks, halo_skip=None):
    nk = len(acts)
    loads = []
    for gi, (c0, n) in enumerate(groups):
        per = 4096 // (n * 128)
        for k0 in range(0, nk, per):
            loads.append((gi, c0, n, k0, min(per, nk - k0)))
    issued = []

    def issue(j):
        gi, c0, n, k0, kn = loads[j]
        slot = wring.next()
        v = slot.t[:, 0:kn * n * 128].rearrange("p (k c) -> p k c", c=n * 128)
        S.dma("pool", [W], [slot], out=v, in_=W.t[r0 + k0 * 128:r0 + (k0 + kn) * 128, c0:c0 + n * 128].rearrange("(kc p) c -> p kc c", p=128))
        issued.append((slot, v))
    npre = NW - 1
    nxt = 0
    while nxt < min(npre, len(loads)):
        issue(nxt); nxt += 1
    ci = 0
    j = 0
    for gi, (c0, n) in enumerate(groups):
        ps = [banks.next() for _ in range(n)]
        while j < len(loads) and loads[j][0] == gi:
            if nxt < len(loads):
                issue(nxt); nxt += 1
            _, _, _, k0, kn = loads[j]
            slot, v = issued[j]
            for oc in range(n):
                for kk in range(kn):
                    k = k0 + kk
                    S.op("pe", [slot, acts[k]], [ps[oc]],
                         lambda e, oc=oc, kk=kk, k=k: e.matmul(ps[oc].t[:, 0:T], lhsT=v[:, kk, oc * 128:(oc + 1) * 128], rhs=acts[k].t[:, 0:T],
                                                              start=(k == 0), stop=(k == nk - 1)))
            j += 1
        for oc in range(n):
            evac(ci, ps[oc])
            ci += 1


def col_groups(c0, nchunks):
    g = []
    i = 0
    while i < nchunks:
        n = min(4, nchunks - i)
        g.append((c0 + i * 128, n))
        i += n
    return g


class Ring:
    def __init__(self, items):
        self.items = items
        self.i = 0

    def next(self):
        r = self.items[self.i]
        self.i = (self.i + 1) % len(self.items)
        return r


def views(S, name, shape, dt, n, idx_fn):
    t = S.es.enter_context(S.nc.sbuf_tensor("sb_" + S.prefix + name, list(shape), dt))
    return t, [Res(f"{name}{i}", idx_fn(t, i)) for i in range(n)]


def emit_post(S, c, dr, blocks):
    nc = S.nc
    D, DFF, XH, NM = c["D"], c["DFF"], c["XH"], c["NM"]
    KC, FC = D // 128, DFF // 128
    XW = XH * 128
    TM = c["T"]
    alpha, eps = c["alpha"], c["eps"]
    GF = c.get("GF", 8)
    NW = c.get("NW", 4)
    KCW = max(KC, GF, XH)
    oes = S.es
    S.new_phase()
    with ExitStack() as pes:
        S.es = pes
        wring = make_wring(S, "wsl", NW)
        ones_f = S.sbuf("ones_f", [128, 128], F32)
        ones_b = S.sbuf("ones_b", [128, 128], BF16)
        epsT = S.sbuf("epsT", [128, 1], F32)
        lng = S.sbuf("lng", [128, 3, KC], F32)
        lnb = S.sbuf("lnb", [128, 3, KC], F32)
        cw = S.sbuf("cw", [128, 3, FC], F32)
        flag = S.sbuf("flag", [128, 1], F32)
        if "oAG_get" in dr:
            cand = Ring([S.sbuf(f"cand{i}", [128, TM], BF16) for i in range(6)])
            selacc = Ring([S.sbuf(f"selacc{i}", [128, TM], BF16) for i in range(4)])
            fsel = S.sbuf("fsel", [128, 4], F32)
            S.dma("sp", [dr["fsel"]], [fsel], out=fsel.t[:], in_=dr["fsel"].t)
        if "hlAG_get" in dr:
            candf = Ring([S.sbuf(f"candf{i}", [128, 128], F32) for i in range(4)])
            fhal = S.sbuf("fhal", [128, 4], F32)
            S.dma("sp", [dr["fhal"]], [fhal], out=fhal.t[:], in_=dr["fhal"].t)
        S.dma("sp", [dr["flag"]], [flag], out=flag.t[:], in_=dr["flag"].t)
        carry = [S.sbuf(f"carry{f}", [128, 2], F32) for f in range(FC)]
        KT = S.sbuf("KT", [128, XH, NM], BF16)
        Vt = [S.sbuf(f"Vt{m}", [128, XW], BF16) for m in range(NM // 128)]
        qT = [S.sbuf(f"qT{h}", [128, TM], BF16) for h in range(XH)]
        xoT = [S.sbuf(f"xoT{h}", [128, TM], BF16) for h in range(XH)]
        pT = Ring([S.sbuf(f"pT{i}", [128, TM], BF16) for i in range(4)])
        tmpf = Ring([S.sbuf(f"tmpf{i}", [128, TM + 2], F32) for i in range(6)])
        st_mean = S.sbuf("st_mean", [128, TM], F32)
        st_rstd = S.sbuf("st_rstd", [128, TM], F32)
        st_nmr = S.sbuf("st_nmr", [128, TM], F32)
        pb = [S.psum(f"pb{i}", [128, 512], F32) for i in range(8)]
        pring = Ring(pb[0:4])
        pA, pB2 = pb[4], pb[5]
        pring2 = Ring(pb[6:8])
        pringu = Ring(pb[4:8])
        glr = Ring([S.sbuf(f"glr{i}", [128, TM], F32) for i in range(5)])

        S.op("dve", [], [ones_f], lambda e: e.memset(ones_f.t[:], 1.0))
        S.op("dve", [], [ones_b], lambda e: e.memset(ones_b.t[:], 1.0))
        S.op("dve", [], [epsT], lambda e: e.memset(epsT.t[:], eps))
        S.dma("sp", [dr["ln_g"]], [lng], out=lng.t[:], in_=dr["ln_g"].t)
        S.dma("sp", [dr["ln_b"]], [lnb], out=lnb.t[:], in_=dr["ln_b"].t)
        S.dma("sp", [dr["conv"]], [cw], out=cw.t[:], in_=dr["conv"].t)
        for f in range(FC):
            S.op("dve", [], [carry[f]], lambda e, f=f: e.memset(carry[f].t[:], 0.0))

        def load_w(W, r0, nk, c0, ncol=128):
            slot = wring.next()
            v = slot.t[:, 0:nk * ncol].rearrange("p (k c) -> p k c", c=ncol)
            S.dma("pool", [W], [slot], out=v, in_=W.t[r0:r0 + nk * 128, c0:c0 + ncol].rearrange("(kc p) c -> p kc c", p=128))
            return slot, v

        def linear(W, r0, acts, T, cols, evac, ps_ring):
            assert all(cols[i + 1] == cols[i] + 128 for i in range(len(cols) - 1))
            linear_g(S, wring, NW, W, r0, acts, T, col_groups(cols[0], len(cols)), evac, ps_ring)

        def layer_norm(li, T):
            for k in range(KC):
                S.op("pe", [ones_f, rT[k]], [pA],
                     lambda e, k=k: e.matmul(pA.t[:, 0:T], lhsT=ones_f.t[:], rhs=rT[k].t[:, 0:T], start=(k == 0), stop=(k == KC - 1)))
            for k in range(KC):
                sq = tmpf.next()
                S.op("act", [rT[k]], [sq], lambda e, k=k, sq=sq: e.activation(out=sq.t[:, 0:T], in_=rT[k].t[:, 0:T], func=AF.Square))
                S.op("pe", [ones_f, sq], [pB2],
                     lambda e, k=k, sq=sq: e.matmul(pB2.t[:, 0:T], lhsT=ones_f.t[:], rhs=sq.t[:, 0:T], start=(k == 0), stop=(k == KC - 1)))
            inv = 1.0 / D
            S.op("act", [pA], [st_mean], lambda e: e.activation(out=st_mean.t[:, 0:T], in_=pA.t[:, 0:T], func=AF.Copy, scale=inv))
            m2 = tmpf.next()
            S.op("dve", [st_mean], [m2], lambda e: e.tensor_tensor(out=m2.t[:, 0:T], in0=st_mean.t[:, 0:T], in1=st_mean.t[:, 0:T], op=ALU.mult))
            var = tmpf.next()
            S.op("dve", [pB2, m2], [var], lambda e: e.scalar_tensor_tensor(out=var.t[:, 0:T], in0=pB2.t[:, 0:T], scalar=inv, in1=m2.t[:, 0:T],
                                                                            op0=ALU.mult, op1=ALU.subtract))
            sd = tmpf.next()
            S.op("act", [var, epsT], [sd], lambda e: e.activation(out=sd.t[:, 0:T], in_=var.t[:, 0:T], func=AF.Sqrt, bias=epsT.t[:, 0:1]))
            S.op("dve", [sd], [st_rstd], lambda e: e.reciprocal(out=st_rstd.t[:, 0:T], in_=sd.t[:, 0:T]))
            S.op("dve", [st_mean, st_rstd], [st_nmr],
                 lambda e: e.scalar_tensor_tensor(out=st_nmr.t[:, 0:T], in0=st_mean.t[:, 0:T], scalar=-1.0, in1=st_rstd.t[:, 0:T], op0=ALU.mult, op1=ALU.mult))
            for k in range(KC):
                t1 = tmpf.next()
                S.op("dve", [rT[k], st_rstd], [t1], lambda e, k=k, t1=t1: e.tensor_tensor(out=t1.t[:, 0:T], in0=rT[k].t[:, 0:T], in1=st_rstd.t[:, 0:T], op=ALU.mult))
                t2 = tmpf.next()
                S.op("pool", [t1, st_nmr], [t2], lambda e, t1=t1, t2=t2: e.tensor_tensor(out=t2.t[:, 0:T], in0=t1.t[:, 0:T], in1=st_nmr.t[:, 0:T], op=ALU.add))
                S.op("act", [t2, lng, lnb], [rT[k]],
                     lambda e, k=k, t2=t2: e.activation(out=rT[k].t[:, 0:T], in_=t2.t[:, 0:T], func=AF.Identity,
                                                        scale=lng.t[:, li, k:k + 1], bias=lnb.t[:, li, k:k + 1]))
                S.op("act", [t2, lng, lnb], [aT[k]],
                     lambda e, k=k, t2=t2: e.activation(out=aT[k].t[:, 0:T], in_=t2.t[:, 0:T], func=AF.Identity,
                                                        scale=lng.t[:, li, k:k + 1], bias=lnb.t[:, li, k:k + 1]))

        def resid_evac(T):
            def ev(i, ps):
                S.op("dve", [rT[i], ps], [rT[i]],
                     lambda e, i=i, ps=ps: e.scalar_tensor_tensor(out=rT[i].t[:, 0:T], in0=rT[i].t[:, 0:T], scalar=alpha, in1=ps.t[:, 0:T],
                                                                  op0=ALU.mult, op1=ALU.add))
            return ev

        mes = ExitStack()
        S.es = mes
        memT = S.sbuf("memT", [128, KC, NM], BF16)
        S.es = pes
        S.dma("pool", [dr["memT"]], [memT], out=memT.t[:], in_=dr["memT"].t.rearrange("(kc p) m -> p kc m", p=128))
        memk = [Res(f"memk{k}", memT.t[:, k, :]) for k in range(KC)]
        for mk in memk:
            mk.w = memT.w

        def kt_evac(i, ps):
            S.op("act", [ps], [KT], lambda e, i=i, ps=ps: e.activation(out=KT.t[:, i, :], in_=ps.t[:, 0:NM], func=AF.Copy))
        linear(dr["w_kv"], 0, memk, NM, [h * 128 for h in range(XH)], kt_evac, pring)
        for h in range(XH):
            slot, sv = load_w(dr["w_kv"], 0, KC, XW + h * 128)
            for m in range(NM // 128):
                ps = pring.next()
                for k in range(KC):
                    S.op("pe", [slot, memT], [ps],
                         lambda e, k=k, m=m, slot=slot, ps=ps: e.matmul(ps.t[:, 0:128], lhsT=memT.t[:, k, m * 128:(m + 1) * 128], rhs=sv[:, k, :],
                                                                        start=(k == 0), stop=(k == KC - 1)))
                S.op("act", [ps], [Vt[m]], lambda e, m=m, h=h, ps=ps: e.activation(out=Vt[m].t[:, h * 128:(h + 1) * 128], in_=ps.t[:, 0:128], func=AF.Copy))

        S.barrier()
        mes.close()
        _, rT = views(S, "rT", [128, KC, TM], F32, KC, lambda t, i: t[:, i, :])
        _, aT = views(S, "aT", [128, KC, TM], BF16, KC, lambda t, i: t[:, i, :])
        _, gT = views(S, "gT", [128, 2 * GF, TM], BF16, 2 * GF, lambda t, i: t[:, i, :])
        for (tok0, T, halo) in blocks:
            HL = c.get("HALO", 128)
            TOKC_ = c.get("TOKC", 2048)
            for k in range(KC):
                if "oAG_get" in dr:
                    for cq in range(4):
                        if halo:
                            cb = max(cq * TOKC_ - HL, 0)
                        else:
                            cb = cq * TOKC_ + (tok0 - HL)
                        cd = cand.next()
                        _ores, _oap = dr["oAG_get"](k, cb, T)
                        S.dma("sp", [_ores], [cd], out=cd.t[:, 0:T], in_=_oap)
                        if cq == 0:
                            acc = selacc.next()
                            S.op("dve", [cd, fsel], [acc], lambda e: e.tensor_scalar(out=acc.t[:, 0:T], in0=cd.t[:, 0:T], scalar1=fsel.t[:, 0:1], scalar2=None, op0=ALU.mult))
                        else:
                            dst = aT[k] if cq == 3 else selacc.next()
                            S.op("dve", [cd, fsel, acc], [dst], lambda e: e.scalar_tensor_tensor(out=dst.t[:, 0:T], in0=cd.t[:, 0:T], scalar=fsel.t[:, cq:cq + 1], in1=acc.t[:, 0:T],
                                                                                              op0=ALU.mult, op1=ALU.add))
                            acc = dst
                else:
                    S.dma("sp", [dr["oT"]], [aT[k]], out=aT[k].t[:, 0:T], in_=dr["oT"].t[k * 128:(k + 1) * 128, tok0:tok0 + T])
                if halo and "hlAG_get" in dr:
                    for cq in range(4):
                        cd = candf.next()
                        _lres, _lap = dr["hlAG_get"](k, cq)
                        S.dma("sp", [_lres], [cd], out=cd.t[:, 0:T], in_=_lap)
                        if cq == 0:
                            S.op("dve", [cd, fhal], [rT[k]], lambda e: e.tensor_scalar(out=rT[k].t[:, 0:T], in0=cd.t[:, 0:T], scalar1=fhal.t[:, 0:1], scalar2=None, op0=ALU.mult))
                        else:
                            S.op("dve", [cd, fhal, rT[k]], [rT[k]], lambda e: e.scalar_tensor_tensor(out=rT[k].t[:, 0:T], in0=cd.t[:, 0:T], scalar=fhal.t[:, cq:cq + 1], in1=rT[k].t[:, 0:T],
                                                                                                    op0=ALU.mult, op1=ALU.add))
                else:
                    S.dma("sp", [dr["hT_in"]], [rT[k]], out=rT[k].t[:, 0:T], in_=dr["hT_in"].t[k * 128:(k + 1) * 128, tok0:tok0 + T])
            linear(dr["w_out"], 0, aT, T, [i * 128 for i in range(KC)], resid_evac(T), pring)
            layer_norm(0, T)
            qscale = 128.0 ** -0.5

            def q_evac(i, ps):
                S.op("act", [ps], [qT[i]], lambda e, i=i, ps=ps: e.activation(out=qT[i].t[:, 0:T], in_=ps.t[:, 0:T], func=AF.Copy, scale=qscale))
            linear(dr["w_q"], 0, aT, T, [h * 128 for h in range(XH)], q_evac, pring)
            for h in range(XH):
                pts = []
                for m in range(NM // 128):
                    ps = pring.next()
                    S.op("pe", [KT, qT[h]], [ps], lambda e, h=h, m=m, ps=ps: e.matmul(ps.t[:, 0:T], lhsT=KT.t[:, h, m * 128:(m + 1) * 128], rhs=qT[h].t[:, 0:T],
                                                                                      start=True, stop=True))
                    pt = pT.next()
                    S.op("act", [ps], [pt], lambda e, ps=ps, pt=pt: e.activation(out=pt.t[:, 0:T], in_=ps.t[:, 0:T], func=AF.Exp))
                    pts.append(pt)
                nm = len(pts)
                for m, pt in enumerate(pts):
                    S.op("pe", [ones_b, pt], [pA], lambda e, m=m, pt=pt: e.matmul(pA.t[:, 0:T], lhsT=ones_b.t[:], rhs=pt.t[:, 0:T], start=(m == 0), stop=(m == nm - 1)))
                for m, pt in enumerate(pts):
                    S.op("pe", [Vt[m], pt], [pB2], lambda e, m=m, h=h, pt=pt: e.matmul(pB2.t[:, 0:T], lhsT=Vt[m].t[:, h * 128:(h + 1) * 128], rhs=pt.t[:, 0:T],
                                                                                     start=(m == 0), stop=(m == nm - 1)))
                rd = tmpf.next()
                S.op("dve", [pA], [rd], lambda e, rd=rd: e.reciprocal(out=rd.t[:, 0:T], in_=pA.t[:, 0:T]))
                S.op("dve", [pB2, rd], [xoT[h]], lambda e, h=h, rd=rd: e.tensor_tensor(out=xoT[h].t[:, 0:T], in0=pB2.t[:, 0:T], in1=rd.t[:, 0:T], op=ALU.mult))
            linear(dr["w_o"], 0, xoT, T, [i * 128 for i in range(KC)], resid_evac(T), pring)
            layer_norm(1, T)
            groups = [list(range(g0, min(g0 + GF, FC))) for g0 in range(0, FC, GF)]

            def glu_a(f, ps):
                ab = tmpf.next()
                S.op("act", [carry[f]], [ab], lambda e: e.activation(out=ab.t[:, 0:2], in_=carry[f].t[:, 0:2], func=AF.Copy))
                S.op("act", [ps], [ab], lambda e: e.activation(out=ab.t[:, 2:2 + T], in_=ps.t[:, 0:T], func=AF.Copy))
                if halo:
                    S.op("dve", [ab, flag], [carry[f]], lambda e: e.tensor_scalar(out=carry[f].t[:, 0:2], in0=ab.t[:, T:T + 2], scalar1=flag.t[:, 0:1], scalar2=None, op0=ALU.mult))
                    return None
                S.op("dve", [ab], [carry[f]], lambda e: e.tensor_copy(out=carry[f].t[:, 0:2], in_=ab.t[:, T:T + 2]))
                c1 = tmpf.next()
                S.op("act", [ab, cw], [c1], lambda e: e.activation(out=c1.t[:, 0:T], in_=ab.t[:, 2:2 + T], func=AF.Copy, scale=cw.t[:, 2, f:f + 1]))
                c2 = tmpf.next()
                S.op("dve", [ab, cw, c1], [c2], lambda e: e.scalar_tensor_tensor(out=c2.t[:, 0:T], in0=ab.t[:, 1:1 + T], scalar=cw.t[:, 1, f:f + 1], in1=c1.t[:, 0:T], op0=ALU.mult, op1=ALU.add))
                S.op("dve", [ab, cw, c2], [c1], lambda e: e.scalar_tensor_tensor(out=c1.t[:, 0:T], in0=ab.t[:, 0:T], scalar=cw.t[:, 0, f:f + 1], in1=c2.t[:, 0:T], op0=ALU.mult, op1=ALU.add))
                gl = glr.next()
                S.op("act", [c1], [gl], lambda e: e.activation(out=gl.t[:, 0:T], in_=c1.t[:, 0:T], func=AF.Gelu))
                return gl

            def up_group(gi, fl):
                gb = (gi % 2) * GF
                for s0 in range(0, len(fl), 4):
                    sub = fl[s0:s0 + 4]
                    gls = {}

                    def ev_a(i, ps):
                        gls[sub[i]] = glu_a(sub[i], ps)
                    linear_g(S, wring, NW, dr["w_up"], 0, aT, T, [(sub[0] * 128, len(sub))], ev_a, pring)
                    if halo:
                        continue

                    def ev_u(i, ps):
                        f = sub[i]
                        gdst = gT[gb + (f - fl[0])]
                        S.op("dve", [gls[f], ps], [gdst], lambda e: e.tensor_tensor(out=gdst.t[:, 0:T], in0=gls[f].t[:, 0:T], in1=ps.t[:, 0:T], op=ALU.mult))
                    linear_g(S, wring, NW, dr["w_up"], 0, aT, T, [(DFF + sub[0] * 128, len(sub))], ev_u, pringu)

            def down_group(gi, fl):
                gb = (gi % 2) * GF
                nk = len(fl)
                acts = [gT[gb + j] for j in range(nk)]

                def ev(i, ps):
                    if gi == 0:
                        resid_evac(T)(i, ps)
                    else:
                        S.op("dve", [rT[i], ps], [rT[i]], lambda e, i=i, ps=ps: e.tensor_tensor(out=rT[i].t[:, 0:T], in0=rT[i].t[:, 0:T], in1=ps.t[:, 0:T], op=ALU.add))
                linear(dr["w_down"], fl[0] * 128, acts, T, [i * 128 for i in range(KC)], ev, pring)

            for gi, fl in enumerate(groups):
                up_group(gi, fl)
                if halo:
                    continue
                if gi > 0:
                    down_group(gi - 1, groups[gi - 1])
            if halo:
                continue
            down_group(len(groups) - 1, groups[-1])
            layer_norm(2, T)
            for k in range(KC):
                oo = c.get("out_off", 0)
                S.dma("sp", [rT[k]], [dr["hT_out"]], out=dr["hT_out"].t[k * 128:(k + 1) * 128, tok0 - oo:tok0 - oo + T], in_=rT[k].t[:, 0:T])
                if "hTb_put" in dr:
                    _bres, _bap = dr["hTb_put"](k, tok0 - c.get("outb_off", 0), T)
                    S.dma("sp", [aT[k]], [_bres], out=_bap, in_=aT[k].t[:, 0:T])
                elif "hTb_out" in dr:
                    bo = c.get("outb_off", 0)
                    S.dma("sp", [aT[k]], [dr["hTb_out"]], out=dr["hTb_out"].t[k * 128:(k + 1) * 128, tok0 - bo:tok0 - bo + T], in_=aT[k].t[:, 0:T])
                if "hl_put" in dr and (tok0, T, halo) == blocks[-1]:
                    _lres, _lap = dr["hl_put"](k)
                    S.dma("sp", [rT[k]], [_lres], out=_lap, in_=rT[k].t[:, T - 128:T])
        S.barrier()
    S.es = oes


NEG = -30000.0


def _oput(dr, r0, t0, T):
    if "oT_put" in dr:
        return dr["oT_put"](r0, t0, T)
    return dr["oT"], dr["oT"].t[r0:r0 + 128, t0:t0 + T]


def _linear(S, wring, NW, W, acts, T, cols, evac, ps_ring):
    groups = []
    i = 0
    while i < len(cols):
        n = 1
        while n < 4 and i + n < len(cols) and cols[i + n] == cols[i] + n * 128:
            n += 1
        groups.append((cols[i], n))
        i += n
    linear_g(S, wring, NW, W, 0, acts, T, groups, evac, ps_ring)


def emit_sb(S, c, dr, heads, col_q, col_k, col_v, orow0):
    D, SEQ = c["D"], c["SEQ"]
    KC = D // 128
    T = 512
    NH = len(heads)
    NW = c.get("NW", 4)
    NT = SEQ // 128
    oes = S.es
    S.new_phase()
    with ExitStack() as pes:
        S.es = pes
        aTt, aT = views(S, "aT", [128, KC, T], BF16, KC, lambda t, i: t[:, i, :])
        wring = make_wring(S, "wsl", NW)
        kT = [S.sbuf(f"kT{h}", [128, SEQ], BF16) for h in range(NH)]
        _, Vall = views(S, "Vall", [128, NT, NH * 128], BF16, NT, lambda t, i: t[:, i, :])
        qT = [S.sbuf(f"qT{h}", [128, T], BF16) for h in range(NH)]
        vf = Ring([S.sbuf(f"vf{i}", [128, T], F32) for i in range(2)])
        ident = S.sbuf("ident", [128, 128], F32)
        identb = S.sbuf("identb", [128, 128], BF16)
        negU = S.sbuf("negU", [128, 128], BF16)
        negO = S.sbuf("negO", [128, 128], BF16)
        mask = S.sbuf("mask", [128, 4, T], BF16)
        eT = [Ring([S.sbuf(f"eT{h}_{i}", [128, T], F32) for i in range(2)]) for h in range(NH)]
        spT = [Ring([S.sbuf(f"spT{h}_{i}", [128, T], BF16) for i in range(2)]) for h in range(NH)]
        tmpT = [Ring([S.sbuf(f"tmpT{h}_{i}", [128, T], F32) for i in range(2)]) for h in range(NH)]
        xT_ = [Ring([S.sbuf(f"xT{h}_{i}", [128, T], F32) for i in range(2)]) for h in range(NH)]
        wT = [Ring([S.sbuf(f"wT{h}_{i}", [128, T], BF16) for i in range(2)]) for h in range(NH)]
        carry = [S.sbuf(f"carry{h}", [128, T], F32) for h in range(NH)]
        oS = Ring([S.sbuf(f"oS{i}", [128, T], BF16) for i in range(2)])
        pb = [S.psum(f"pb{i}", [128, 512], F32) for i in range(8)]
        pz = [pb[0], pb[1]]
        pR = [pb[2], pb[3]]
        pC = [pb[4], pb[5]]
        pO = [pb[6], pb[7]]
        plin = Ring([pb[2], pb[3], pb[4], pb[5]])
        ptr = Ring([pb[6], pb[7]])

        S.dma("sp", [dr["ident"]], [ident], out=ident.t[:], in_=dr["ident"].t)
        S.dma("sp", [dr["identb"]], [identb], out=identb.t[:], in_=dr["identb"].t)
        S.dma("sp", [dr["negU"]], [negU], out=negU.t[:], in_=dr["negU"].t)
        S.dma("sp", [dr["sbmask"]], [mask], out=mask.t[:], in_=dr["sbmask"].t)
        S.op("dve", [], [negO], lambda e: e.memset(negO.t[:], -1.0))
        scale = 128.0 ** -0.5

        for tb in range(SEQ // T):
            t0 = tb * T
            hk = max(KC // 2, 1)
            for k0 in range(0, KC, hk):
                S.dma("pool", [dr["xT"]], aT[k0:k0 + hk], out=aTt[:, k0:k0 + hk, :],
                      in_=dr["xT"].t[k0 * 128:(k0 + hk) * 128, t0:t0 + T].rearrange("(kc p) t -> p kc t", p=128))
            cols = [col_q + h * 128 for h in heads] + [col_k + h * 128 for h in heads] + [col_v + h * 128 for h in heads]

            def evac(i, ps):
                kind, hi = divmod(i, NH)
                if kind == 0:
                    S.op("act", [ps], [qT[hi]], lambda e: e.activation(out=qT[hi].t[:, :], in_=ps.t[:, :], func=AF.Copy, scale=scale))
                elif kind == 1:
                    S.op("act", [ps], [kT[hi]], lambda e: e.activation(out=kT[hi].t[:, t0:t0 + T], in_=ps.t[:, :], func=AF.Copy))
                else:
                    v = vf.next()
                    S.op("act", [ps], [v], lambda e: e.activation(out=v.t[:, :], in_=ps.t[:, :], func=AF.Copy))
                    for j in range(T // 128):
                        pt = ptr.next()
                        S.op("pe", [v, ident], [pt], lambda e, j=j, pt=pt: e.transpose(out=pt.t[:, 0:128], in_=v.t[:, j * 128:(j + 1) * 128], identity=ident.t[:]))
                        vt = Vall[t0 // 128 + j]
                        S.op("dve", [pt], [vt], lambda e, pt=pt, vt=vt: e.tensor_copy(out=vt.t[:, hi * 128:(hi + 1) * 128], in_=pt.t[:, 0:128]))
            _linear(S, wring, NW, dr["wA"], aT, T, cols, evac, plin)

            kts = list(range(t0 // 128 + 3, -1, -1))

            def emit_z(kt):
                for h in range(NH):
                    kd = kt - t0 // 128
                    S.op("pe", [kT[h], qT[h]], [pz[h]], lambda e, h=h: e.matmul(pz[h].t[:, :], lhsT=kT[h].t[:, kt * 128:(kt + 1) * 128], rhs=qT[h].t[:, :],
                                                                              start=True, stop=(kd < 0)))
                    if kd >= 0:
                        S.op("pe", [identb, mask], [pz[h]], lambda e, h=h, kd=kd: e.matmul(pz[h].t[:, :], lhsT=identb.t[:], rhs=mask.t[:, kd, :], start=False, stop=True))
            NK = len(kts)
            cur = {}

            def stage1(n):
                cur[n] = {}
                for h in range(NH):
                    e_ = eT[h].next()
                    sp = spT[h].next()
                    S.op("act", [pz[h]], [e_], lambda e, h=h, e_=e_: e.activation(out=e_.t[:, :], in_=pz[h].t[:, :], func=AF.Exp))
                    S.op("act", [e_], [sp], lambda e, e_=e_, sp=sp: e.activation(out=sp.t[:, :], in_=e_.t[:, :], func=AF.Ln, bias=1.0))
                    cur[n][h] = (e_, sp)

            def stage2(n):
                last = (n == NK - 1)
                for h in range(NH):
                    e_, sp = cur[n][h]
                    S.op("pe", [negU, sp], [pR[h]], lambda e, h=h, sp=sp: e.matmul(pR[h].t[:, :], lhsT=negU.t[:], rhs=sp.t[:, :], start=True, stop=True))
                    if not last:
                        S.op("pe", [negO, sp], [pC[h]], lambda e, h=h, sp=sp: e.matmul(pC[h].t[:, :], lhsT=negO.t[:], rhs=sp.t[:, :], start=True, stop=True))

            def stage3a(n):
                first = (n == 0)
                last = (n == NK - 1)
                xs = {}
                for h in range(NH):
                    x_ = xT_[h].next()
                    if first:
                        S.op("act", [pR[h]], [x_], lambda e, h=h, x_=x_: e.activation(out=x_.t[:, :], in_=pR[h].t[:, :], func=AF.Exp))
                        if not last:
                            S.op("dve", [pC[h]], [carry[h]], lambda e, h=h: e.tensor_copy(out=carry[h].t[:, :], in_=pC[h].t[:, :]))
                    else:
                        tm = tmpT[h].next()
                        S.op("dve", [pR[h], carry[h]], [tm], lambda e, h=h, tm=tm: e.tensor_tensor(out=tm.t[:, :], in0=pR[h].t[:, :], in1=carry[h].t[:, :], op=ALU.add))
                        S.op("act", [tm], [x_], lambda e, tm=tm, x_=x_: e.activation(out=x_.t[:, :], in_=tm.t[:, :], func=AF.Exp))
                        if not last:
                            S.op("dve", [pC[h], carry[h]], [carry[h]], lambda e, h=h: e.tensor_tensor(out=carry[h].t[:, :], in0=pC[h].t[:, :], in1=carry[h].t[:, :], op=ALU.add))
                    xs[h] = x_
                return xs

            def stage3b(n, xs):
                first = (n == 0)
                last = (n == NK - 1)
                kt = kts[n]
                for h in range(NH):
                    e_, sp = cur[n][h]
                    x_ = xs[h]
                    w_ = wT[h].next()
                    S.op("dve", [e_, x_], [w_], lambda e, e_=e_, x_=x_, w_=w_: e.tensor_tensor(out=w_.t[:, :], in0=e_.t[:, :], in1=x_.t[:, :], op=ALU.mult))
                    vt = Vall[kt]
                    S.op("pe", [vt, w_], [pO[h]], lambda e, h=h, vt=vt, w_=w_: e.matmul(pO[h].t[:, :], lhsT=vt.t[:, h * 128:(h + 1) * 128], rhs=w_.t[:, :],
                                                                                       start=first, stop=last))
                del cur[n]

            emit_z(kts[0])
            stage1(0)
            stage2(0)
            if NK > 1:
                emit_z(kts[1])
            for n in range(NK):
                if n + 1 < NK:
                    stage1(n + 1)
                xs = stage3a(n)
                if n + 1 < NK:
                    stage2(n + 1)
                    if n + 2 < NK:
                        emit_z(kts[n + 2])
                stage3b(n, xs)
            for h in range(NH):
                o_ = oS.next()
                S.op("act", [pO[h]], [o_], lambda e, h=h, o_=o_: e.activation(out=o_.t[:, :], in_=pO[h].t[:, :], func=AF.Copy))
                r0 = orow0 + heads[h] * 128
                _ores, _oap = _oput(dr, r0, t0, T)
                S.dma("sp", [o_], [_ores], out=_oap, in_=o_.t[:, :])
        S.barrier()
    S.es = oes


def emit_hgrn(S, c, dr, heads, col_q, col_f, col_i, col_g, orow0):
    D, SEQ = c["D"], c["SEQ"]
    KC = D // 128
    T = 512
    C = 64
    NH = len(heads)
    NW = c.get("NW", 4)
    oes = S.es
    S.new_phase()
    with ExitStack() as pes:
        S.es = pes
        aTt, aT = views(S, "aT", [128, KC, T], BF16, KC, lambda t, i: t[:, i, :])
        wring = make_wring(S, "wsl", NW)
        ident = S.sbuf("ident", [128, 128], F32)
        identb = S.sbuf("identb", [128, 128], BF16)
        maskU = S.sbuf("maskU", [64, 64], F32)
        rmask = S.sbuf("rmask", [128, T], F32)
        NWt = S.sbuf("NWt", [128, 128], F32)
        epsR = S.sbuf("epsR", [128, 1], F32)
        lbr = S.sbuf("lbr", [128, 2, NH], F32)
        lb = S.sbuf("lb", [128, NH], F32)
        oml = S.sbuf("oml", [128, NH], F32)
        noml = S.sbuf("noml", [128, NH], F32)
        qd = [S.sbuf(f"qd{h}", [128, T], BF16) for h in range(NH)]
        kd = [S.sbuf(f"kd{h}", [128, T], BF16) for h in range(NH)]
        iTb = [S.sbuf(f"iTb{h}", [128, T], BF16) for h in range(NH)]
        gf = [S.sbuf(f"gf{h}", [128, T], F32) for h in range(NH)]
        eb = [S.sbuf(f"eb{h}", [128, T], F32) for h in range(NH)]
        qf = [S.sbuf(f"qf{h}", [128, T], F32) for h in range(NH)]
        sg = [S.sbuf(f"sg{h}", [128, T], F32) for h in range(NH)]
        oS = [S.sbuf(f"oS{h}", [128, T], BF16) for h in range(NH)]
        state = [S.sbuf(f"state{h}", [128, 128], F32) for h in range(NH)]
        stateb = [S.sbuf(f"stateb{h}", [128, 128], BF16) for h in range(NH)]
        tmp = Ring([S.sbuf(f"tmp{i}", [128, T], F32) for i in range(6)])
        kvt = Ring([S.sbuf(f"kvt{i}", [64, 256], BF16) for i in range(3)])
        gate = Ring([S.sbuf(f"gate{i}", [64, 128], F32) for i in range(3)])
        scb = Ring([S.sbuf(f"scb{i}", [64, 64], BF16) for i in range(3)])
        junk = Ring([S.sbuf(f"junk{i}", [64, 128], F32) for i in range(2)])
        ss = Ring([S.sbuf(f"ss{i}", [64, 4], F32) for i in range(4)])
        t1r = Ring([S.sbuf(f"t1r{i}", [64, 128], F32) for i in range(3)])
        t2r = Ring([S.sbuf(f"t2r{i}", [64, 128], BF16) for i in range(3)])
        st1 = Ring([S.sbuf(f"st1{i}", [128, 128], F32) for i in range(2)])
        pb = [S.psum(f"pb{i}", [128, 512], F32) for i in range(8)]
        plin = Ring(pb[0:4])
        ptr = Ring(pb[2:4])
        pwk = Ring(pb[4:8])

        S.dma("sp", [dr["ident"]], [ident], out=ident.t[:], in_=dr["ident"].t)
        S.dma("sp", [dr["identb"]], [identb], out=identb.t[:], in_=dr["identb"].t)
        S.dma("sp", [dr["maskU"]], [maskU], out=maskU.t[:], in_=dr["maskU"].t)
        S.dma("sp", [dr["rmask"]], [rmask], out=rmask.t[:], in_=dr["rmask"].t)
        S.dma("sp", [dr["hg_nw"]], [NWt], out=NWt.t[:], in_=dr["hg_nw"].t)
        S.dma("sp", [dr["hg_lb"]], [lbr], out=lbr.t[:], in_=dr["hg_lb"].t)
        S.op("dve", [], [epsR], lambda e: e.memset(epsR.t[:], 1e-6))
        S.op("dve", [lbr], [lb], lambda e: e.tensor_tensor(out=lb.t[:], in0=lbr.t[:, 0, :], in1=lbr.t[:, 1, :], op=ALU.subtract))
        S.op("act", [lb], [lb], lambda e: e.activation(out=lb.t[:], in_=lb.t[:], func=AF.Sigmoid))
        S.op("dve", [lb], [oml], lambda e: e.tensor_scalar(out=oml.t[:], in0=lb.t[:], scalar1=-1.0, scalar2=1.0, op0=ALU.mult, op1=ALU.add))
        S.op("dve", [oml], [noml], lambda e: e.tensor_scalar(out=noml.t[:], in0=oml.t[:], scalar1=-1.0, scalar2=None, op0=ALU.mult))
        for h in range(NH):
            S.op("dve", [], [state[h]], lambda e, h=h: e.memset(state[h].t[:], 0.0))
            S.op("dve", [], [stateb[h]], lambda e, h=h: e.memset(stateb[h].t[:], 0.0))

        for tb in range(SEQ // T):
            t0 = tb * T
            hk = max(KC // 2, 1)
            for k0 in range(0, KC, hk):
                S.dma("pool", [dr["xT"]], aT[k0:k0 + hk], out=aTt[:, k0:k0 + hk, :],
                      in_=dr["xT"].t[k0 * 128:(k0 + hk) * 128, t0:t0 + T].rearrange("(kc p) t -> p kc t", p=128))
            cols = []
            hs_ = c.get("hstride", 128)
            for h in heads:
                cols += [col_q + h * hs_, col_f + h * hs_, col_i + h * hs_, col_g + h * hs_]

            def evac(i, ps):
                hi, kind = divmod(i, 4)
                if kind == 0:
                    S.op("act", [ps], [qf[hi]], lambda e: e.activation(out=qf[hi].t[:, :], in_=ps.t[:, :], func=AF.Copy))
                elif kind == 1:
                    S.op("act", [ps], [sg[hi]], lambda e: e.activation(out=sg[hi].t[:, :], in_=ps.t[:, :], func=AF.Sigmoid))
                elif kind == 2:
                    S.op("act", [ps], [iTb[hi]], lambda e: e.activation(out=iTb[hi].t[:, :], in_=ps.t[:, :], func=AF.Copy))
                else:
                    S.op("dve", [ps], [gf[hi]], lambda e: e.tensor_copy(out=gf[hi].t[:, :], in_=ps.t[:, :]))
            _linear(S, wring, NW, dr["wA"], aT, T, cols, evac, plin)

            for h in range(NH):
                f_ = tmp.next()
                S.op("dve", [sg[h], oml, lb], [f_], lambda e: e.tensor_scalar(out=f_.t[:, :], in0=sg[h].t[:, :], scalar1=oml.t[:, h:h + 1], scalar2=lb.t[:, h:h + 1],
                                                                               op0=ALU.mult, op1=ALU.add))
                lf = tmp.next()
                S.op("act", [f_], [lf], lambda e: e.activation(out=lf.t[:, :], in_=f_.t[:, :], func=AF.Ln))
                k_ = tmp.next()
                S.op("dve", [sg[h], noml, oml], [k_], lambda e: e.tensor_scalar(out=k_.t[:, :], in0=sg[h].t[:, :], scalar1=noml.t[:, h:h + 1], scalar2=oml.t[:, h:h + 1],
                                                                                 op0=ALU.mult, op1=ALU.add))
                b_ = tmp.next()
                S.op("dve", [rmask, lf], [b_], lambda e: e.tensor_tensor_scan(out=b_.t[:, :], data0=rmask.t[:, :], data1=lf.t[:, :], initial=0.0, op0=ALU.mult, op1=ALU.add))
                S.op("act", [b_], [eb[h]], lambda e: e.activation(out=eb[h].t[:, :], in_=b_.t[:, :], func=AF.Exp))
                enb = tmp.next()
                S.op("act", [b_], [enb], lambda e: e.activation(out=enb.t[:, :], in_=b_.t[:, :], func=AF.Exp, scale=-1.0))
                S.op("dve", [qf[h], eb[h]], [qd[h]], lambda e: e.tensor_tensor(out=qd[h].t[:, :], in0=qf[h].t[:, :], in1=eb[h].t[:, :], op=ALU.mult))
                S.op("pool", [k_, enb], [kd[h]], lambda e: e.tensor_tensor(out=kd[h].t[:, :], in0=k_.t[:, :], in1=enb.t[:, :], op=ALU.mult))

            for cc in range(T // C):
                cs = slice(cc * C, (cc + 1) * C)
                for h in range(NH):
                    TR = ptr.next()
                    P = pwk.next()
                    S.op("pe", [kd[h], identb], [TR], lambda e: e.matmul(TR.t[0:64, 0:128], lhsT=kd[h].t[:, cs], rhs=identb.t[:], start=True, stop=True))
                    S.op("pe", [iTb[h], identb], [TR], lambda e: e.matmul(TR.t[0:64, 128:256], lhsT=iTb[h].t[:, cs], rhs=identb.t[:], start=True, stop=True))
                    S.op("pe", [gf[h], ident], [TR], lambda e: e.transpose(out=TR.t[0:64, 256:384], in_=gf[h].t[:, cs], identity=ident.t[:]))
                    kv_ = kvt.next()
                    S.op("dve", [TR], [kv_], lambda e: e.tensor_copy(out=kv_.t[:, :], in_=TR.t[0:64, 0:256]))
                    g_ = gate.next()
                    S.op("act", [TR], [g_], lambda e: e.activation(out=g_.t[:, :], in_=TR.t[0:64, 256:384], func=AF.Silu))
                    S.op("pe", [kd[h], qd[h]], [P], lambda e: e.matmul(P.t[0:64, 0:64], lhsT=kd[h].t[:, cs], rhs=qd[h].t[:, cs], start=True, stop=True))
                    sc_ = scb.next()
                    S.op("dve", [P, maskU], [sc_], lambda e: e.tensor_tensor(out=sc_.t[:, :], in0=P.t[0:64, 0:64], in1=maskU.t[:, :], op=ALU.mult))
                    S.op("pe", [sc_, kv_], [P], lambda e: e.matmul(P.t[0:64, 64:192], lhsT=sc_.t[:, :], rhs=kv_.t[:, 128:256], start=True, stop=False))
                    S.op("pe", [qd[h], stateb[h]], [P], lambda e: e.matmul(P.t[0:64, 64:192], lhsT=qd[h].t[:, cs], rhs=stateb[h].t[:, :], start=False, stop=True))
                    S.op("pe", [kv_], [P], lambda e: e.matmul(P.t[:, 256:384], lhsT=kv_.t[:, 0:128], rhs=kv_.t[:, 128:256], start=True, stop=True))
                    s1 = st1.next()
                    S.op("dve", [state[h], P], [s1], lambda e: e.tensor_tensor(out=s1.t[:, :], in0=state[h].t[:, :], in1=P.t[:, 256:384], op=ALU.add))
                    S.op("dve", [s1, eb[h]], [state[h]], lambda e: e.tensor_scalar(out=state[h].t[:, :], in0=s1.t[:, :], scalar1=eb[h].t[:, cc * C + C - 1:cc * C + C], scalar2=None,
                                                                                  op0=ALU.mult))
                    S.op("act", [state[h]], [stateb[h]], lambda e: e.activation(out=stateb[h].t[:, :], in_=state[h].t[:, :], func=AF.Copy))
                    jk = junk.next()
                    s_ = ss.next()
                    S.op("act", [P], [jk, s_], lambda e: e.activation(out=jk.t[:, :], in_=P.t[0:64, 64:192], func=AF.Square, accum_out=s_.t[:, 0:1]))
                    S.op("act", [s_, epsR], [s_], lambda e: e.activation(out=s_.t[:, 1:2], in_=s_.t[:, 0:1], func=AF.Sqrt, scale=1.0 / 128.0, bias=epsR.t[0:64, 0:1]))
                    S.op("dve", [s_], [s_], lambda e: e.reciprocal(out=s_.t[:, 2:3], in_=s_.t[:, 1:2]))
                    t1 = t1r.next()
                    S.op("dve", [P, s_, NWt], [t1], lambda e: e.scalar_tensor_tensor(out=t1.t[:, :], in0=P.t[0:64, 64:192], scalar=s_.t[:, 2:3], in1=NWt.t[0:64, :],
                                                                                   op0=ALU.mult, op1=ALU.mult))
                    t2 = t2r.next()
                    S.op("pool", [t1, g_], [t2], lambda e: e.tensor_tensor(out=t2.t[:, :], in0=t1.t[:, :], in1=g_.t[:, :], op=ALU.mult))
                    S.op("pe", [t2, identb], [P], lambda e: e.matmul(P.t[:, 384:448], lhsT=t2.t[:, :], rhs=identb.t[0:64, 0:64], start=True, stop=True))
                    S.op("act", [P], [oS[h]], lambda e: e.activation(out=oS[h].t[:, cs], in_=P.t[:, 384:448], func=AF.Copy))
            for h in range(NH):
                r0 = orow0 + h * 128
                _ores, _oap = _oput(dr, r0, t0, T)
                S.dma("sp", [oS[h]], [_ores], out=_oap, in_=oS[h].t[:, :])
        S.barrier()
    S.es = oes


NEGB = 30000.0


def emit_nsa(S, c, dr):
    D, SEQ = c["D"], c["SEQ"]
    KC = D // 128
    T = 512
    R = 8
    NW = c.get("NW", 3)
    NT = SEQ // 128
    NB = SEQ // T
    NMT = (SEQ // 16 + 127) // 128
    NSLOT = NMT * 128
    scale = 128.0 ** -0.5
    oes = S.es
    S.new_phase()
    with ExitStack() as pes:
        S.es = pes
        _, aT = views(S, "aT", [128, KC, T], BF16, KC, lambda t, i: t[:, i, :])
        wring = make_wring(S, "wsl", NW)
        ident = S.sbuf("ident", [128, 128], F32)
        identb = S.sbuf("identb", [128, 128], BF16)
        ones_b = S.sbuf("ones_b", [128, 128], BF16)
        rotm = S.sbuf("rotm", [128, 128], BF16)
        ksT = S.sbuf("ksT", [128, SEQ], BF16)
        _, vsT = views(S, "vsT", [128, NT, 128], BF16, NT, lambda t, i: t[:, i, :])
        kwT = S.sbuf("kwT", [128, 1024], BF16)
        _, vwT = views(S, "vwT", [128, 8, 128], BF16, 8, lambda t, i: t[:, i, :])
        kcmpT = S.sbuf("kcmpT", [128, NSLOT], BF16)
        vcmpT = S.sbuf("vcmpT", [128, NSLOT], F32)
        _, vcmp = views(S, "vcmp", [128, NMT, 128], BF16, NMT, lambda t, i: t[:, i, :])
        kcb = S.sbuf("kcb", [128, 16 + T], BF16)
        vcb = S.sbuf("vcb", [128, 16 + T], BF16)
        w2 = S.sbuf("w2", [128, 2, 128], BF16)
        posT = S.sbuf("posT", [128, 2, 32], BF16)
        posb = S.sbuf("posb", [128, 2], F32)
        ov = S.sbuf("ov", [128, NMT, 128], BF16)
        ex32 = S.sbuf("ex32", [32, 16, 128], BF16)
        wmask = S.sbuf("wmask", [128, 8, T], BF16)
        cmask = S.sbuf("cmask", [128, 5, T], BF16)
        ABt = S.sbuf("ABt", [128, 256], F32)
        FBt = S.sbuf("FBt", [128, 256], F32)
        qn = [S.sbuf(f"qn{r}", [128, T], BF16) for r in range(R)]
        qr = [S.sbuf(f"qr{r}", [128, T], BF16) for r in range(R)]
        cosb = S.sbuf("cosb", [128, T], F32)
        sinb = S.sbuf("sinb", [128, T], F32)
        gT = S.sbuf("gT", [32, T], F32)
        gtok = S.sbuf("gtok", [128, 4, 24], F32)
        _, otok = views(S, "otok", [128, 4, R * 128], F32, 4, lambda t, i: t[:, i, :])
        selT = [[S.sbuf(f"selT{g}_{i}", [32, T], BF16) for g in range(4)] for i in range(1)][0]
        tf = Ring([S.sbuf(f"tf{i}", [128, T], F32) for i in range(4)])
        tb16 = Ring([S.sbuf(f"tb16{i}", [128, T], BF16) for i in range(3)])
        Er = Ring([S.sbuf(f"Er{i}", [128, T], BF16) for i in range(6)])
        Pr = Ring([S.sbuf(f"Pr{i}", [128, T], BF16) for i in range(3)])
        hid = Ring([S.sbuf(f"hid{i}", [128, 32], BF16) for i in range(2)])
        sm = Ring([S.sbuf(f"sm{i}", [128, 128], F32) for i in range(4)])
        smb = Ring([S.sbuf(f"smb{i}", [128, 128], BF16) for i in range(2)])
        m8 = Ring([S.sbuf(f"m8{i}", [128, 16], F32) for i in range(2)])
        fac = Ring([S.sbuf(f"fac{i}", [128, 8], F32) for i in range(4)])
        oS = Ring([S.sbuf(f"oS{i}", [128, T], BF16) for i in range(2)])
        pb = [S.psum(f"pb{i}", [128, 512], F32) for i in range(8)]
        plin = Ring([pb[0], pb[1], pb[2], pb[3]])
        pS = Ring(pb[2:4])
        pO = Ring([pb[4], pb[7]])
        pD = pb[5]
        pI = pb[6]
        pmisc = Ring([pb[4], pb[7], pb[5]])

        def ld(q, name, dst, src=None):
            S.dma(q, [dr[name]], [dst], out=dst.t[:], in_=(dr[name].t if src is None else src))
        ld("sp", "ident", ident); ld("sp", "identb", identb); ld("sp", "rotm", rotm)
        ld("pool", "cmp_w2", w2); ld("pool", "cmp_posT", posT)
        ld("sp", "ov", ov); ld("sp", "ex32", ex32); ld("sp", "wmask", wmask); ld("sp", "cmask", cmask)
        ld("sp", "ABt", ABt); ld("sp", "FBt", FBt)
        S.op("dve", [], [ones_b], lambda e: e.memset(ones_b.t[:], 1.0))
        S.op("dve", [], [kcb], lambda e: e.memset(kcb.t[:], 0.0))
        S.op("dve", [], [vcb], lambda e: e.memset(vcb.t[:], 0.0))
        S.op("dve", [], [vcmpT], lambda e: e.memset(vcmpT.t[:], 0.0))
        S.op("dve", [], [kcmpT], lambda e: e.memset(kcmpT.t[:], 0.0))

        def load_w1(j):
            slot = wring.next()
            v = slot.t[:, :].rearrange("p (k c) -> p k c", c=128)
            S.dma("pool", [dr["cmp_w1"]], [slot], out=v, in_=dr["cmp_w1"].t[j].rearrange("l d e -> d l e"))
            return slot, v
        for j in range(2):
            slot, sv = load_w1(j)
            ps = pmisc.next()
            for l in range(32):
                S.op("pe", [slot, posT], [ps], lambda e, l=l: e.matmul(ps.t[:, 0:1], lhsT=sv[:, l, :], rhs=posT.t[:, j, l:l + 1], start=(l == 0), stop=(l == 31)))
            S.op("act", [ps], [posb], lambda e: e.activation(out=posb.t[:, j:j + 1], in_=ps.t[:, 0:1], func=AF.Copy))

        CQ, CKC, CVC, CKS, CVS, CKW, CVW, CG = 0, 1024, 1152, 1280, 1408, 1536, 1664, 1792

        def rope(src_f32, dst_bf, sc):
            sb = tb16.next()
            S.op("act", [src_f32], [sb], lambda e: e.activation(out=sb.t[:, :], in_=src_f32.t[:, :], func=AF.Copy))
            ps = pmisc.next()
            S.op("pe", [rotm, sb], [ps], lambda e: e.matmul(ps.t[:, :], lhsT=rotm.t[:], rhs=sb.t[:, :], start=True, stop=True))
            t1 = tf.next()
            S.op("dve", [src_f32, cosb], [t1], lambda e: e.tensor_tensor(out=t1.t[:, :], in0=src_f32.t[:, :], in1=cosb.t[:, :], op=ALU.mult))
            t2 = tf.next()
            S.op("dve", [ps, sinb], [t2], lambda e: e.tensor_tensor(out=t2.t[:, :], in0=ps.t[:, :], in1=sinb.t[:, :], op=ALU.mult))
            S.op("pool", [t1, t2], [dst_bf[0]], lambda e: e.tensor_tensor(out=dst_bf[1], in0=t1.t[:, :], in1=t2.t[:, :], op=ALU.add))

        def to_tok(src_f32, dst_views, base):
            for j in range(T // 128):
                pt = pmisc.next()
                S.op("pe", [src_f32, ident], [pt], lambda e, j=j, pt=pt: e.transpose(out=pt.t[:, 0:128], in_=src_f32.t[:, j * 128:(j + 1) * 128], identity=ident.t[:]))
                dv = dst_views[base + j]
                S.op("dve", [pt], [dv], lambda e, pt=pt, dv=dv: e.tensor_copy(out=dv.t[:, :], in_=pt.t[:, 0:128]))

        for tb in range(NB):
            t0 = tb * T
            gt0 = t0 // 128
            for k in range(KC):
                if "hAG_get" in dr:
                    _hres, _hap = dr["hAG_get"](k, t0, T)
                    S.dma("sp", [_hres], [aT[k]], out=aT[k].t[:, :], in_=_hap)
                else:
                    S.dma("sp", [dr["hT"]], [aT[k]], out=aT[k].t[:, :], in_=dr["hT"].t[k * 128:(k + 1) * 128, t0:t0 + T])
            S.dma("sp", [dr["cosT"]], [cosb], out=cosb.t[:, :], in_=dr["cosT"].t[:, t0:t0 + T])
            S.dma("sp", [dr["sinT"]], [sinb], out=sinb.t[:, :], in_=dr["sinT"].t[:, t0:t0 + T])
            if tb > 0:
                S.op("dve", [kcb], [kcb], lambda e: e.tensor_copy(out=kcb.t[:, 0:16], in_=kcb.t[:, T:T + 16]))
                S.op("dve", [vcb], [vcb], lambda e: e.tensor_copy(out=vcb.t[:, 0:16], in_=vcb.t[:, T:T + 16]))
            cols = [CQ + r * 128 for r in range(R)] + [CKC, CVC, CKS, CVS, CKW, CVW]

            def evac(i, ps):
                if i < R:
                    qf = tf.next()
                    S.op("act", [ps], [qf], lambda e: e.activation(out=qf.t[:, :], in_=ps.t[:, :], func=AF.Copy, scale=scale))
                    S.op("dve", [qf], [qn[i]], lambda e: e.tensor_copy(out=qn[i].t[:, :], in_=qf.t[:, :]))
                    rope(qf, (qr[i], qr[i].t[:, :]), 1.0)
                elif i == R:
                    S.op("act", [ps], [kcb], lambda e: e.activation(out=kcb.t[:, 16:16 + T], in_=ps.t[:, :], func=AF.Copy))
                elif i == R + 1:
                    S.op("act", [ps], [vcb], lambda e: e.activation(out=vcb.t[:, 16:16 + T], in_=ps.t[:, :], func=AF.Copy))
                elif i == R + 2:
                    kf = tf.next()
                    S.op("act", [ps], [kf], lambda e: e.activation(out=kf.t[:, :], in_=ps.t[:, :], func=AF.Copy))
                    rope(kf, (ksT, ksT.t[:, t0:t0 + T]), 1.0)
                elif i == R + 3:
                    vf = tf.next()
                    S.op("act", [ps], [vf], lambda e: e.activation(out=vf.t[:, :], in_=ps.t[:, :], func=AF.Copy))
                    to_tok(vf, vsT, gt0)
                elif i == R + 4:
                    kf = tf.next()
                    S.op("act", [ps], [kf], lambda e: e.activation(out=kf.t[:, :], in_=ps.t[:, :], func=AF.Copy))
                    c0 = (tb % 2) * T
                    rope(kf, (kwT, kwT.t[:, c0:c0 + T]), 1.0)
                else:
                    vf = tf.next()
                    S.op("act", [ps], [vf], lambda e: e.activation(out=vf.t[:, :], in_=ps.t[:, :], func=AF.Copy))
                    to_tok(vf, vwT, (gt0 % 8))
            _linear(S, wring, NW, dr["wC"], aT, T, cols, evac, plin)
            slot = wring.next()
            gv = slot.t[:, :].rearrange("p (k c) -> p k c", c=128)[:, 0:KC, 0:24]
            S.dma("pool", [dr["wC"]], [slot], out=gv, in_=dr["wC"].t[0:KC * 128, CG:CG + 24].rearrange("(kc p) c -> p kc c", p=128))
            ps = plin.next()
            for k in range(KC):
                S.op("pe", [slot, aT[k]], [ps], lambda e, k=k: e.matmul(ps.t[0:24, :], lhsT=gv[:, k, 0:24], rhs=aT[k].t[:, :], start=(k == 0), stop=(k == KC - 1)))
            S.op("act", [ps], [gT], lambda e: e.activation(out=gT.t[0:24, :], in_=ps.t[0:24, :], func=AF.Sigmoid))
            for j in range(4):
                pt = pmisc.next()
                S.op("pe", [gT, ident], [pt], lambda e, j=j, pt=pt: e.transpose(out=pt.t[:, 0:24], in_=gT.t[0:24, j * 128:(j + 1) * 128], identity=ident.t[0:24, 0:24]))
                S.op("dve", [pt], [gtok], lambda e, j=j, pt=pt: e.tensor_copy(out=gtok.t[:, j, :], in_=pt.t[:, 0:24]))

            for j, (buf, dstT) in enumerate(((kcb, kcmpT), (vcb, vcmpT))):
                slot, sv = load_w1(j)
                ps = pmisc.next()
                for l in range(32):
                    S.op("pe", [slot, buf], [ps], lambda e, l=l: e.matmul(ps.t[:, 0:32], lhsT=sv[:, l, :], rhs=buf.t[:, l:l + 16 * 31 + 1:16], start=(l == 0), stop=(l == 31)))
                hd = hid.next()
                S.op("act", [ps, posb], [hd], lambda e: e.activation(out=hd.t[:, :], in_=ps.t[:, 0:32], func=AF.Gelu, bias=posb.t[:, j:j + 1]))
                ps2 = pmisc.next()
                S.op("pe", [w2, hd], [ps2], lambda e: e.matmul(ps2.t[:, 0:32], lhsT=w2.t[:, j, :], rhs=hd.t[:, :], start=True, stop=True))
                S.op("act", [ps2], [dstT], lambda e: e.activation(out=dstT.t[:, 32 * tb:32 * tb + 32], in_=ps2.t[:, 0:32], func=AF.Copy))
            mtc = (32 * tb) // 128
            pt = pmisc.next()
            S.op("pe", [vcmpT, ident], [pt], lambda e: e.transpose(out=pt.t[:, 0:128], in_=vcmpT.t[:, mtc * 128:(mtc + 1) * 128], identity=ident.t[:]))
            S.op("dve", [pt], [vcmp[mtc]], lambda e: e.tensor_copy(out=vcmp[mtc].t[:, :], in_=pt.t[:, 0:128]))

            nmt = mtc + 1
            for r in range(R):
                Es = []
                for mt in range(nmt):
                    v = tb - 4 * mt
                    ps = pS.next()
                    mks = ([v] if v < 4 else []) + ([4] if mt == 0 else [])
                    S.op("pe", [kcmpT, qn[r]], [ps], lambda e: e.matmul(ps.t[:, :], lhsT=kcmpT.t[:, mt * 128:(mt + 1) * 128], rhs=qn[r].t[:, :], start=True, stop=not mks))
                    for mi, mv in enumerate(mks):
                        S.op("pe", [identb, cmask], [ps], lambda e: e.matmul(ps.t[:, :], lhsT=identb.t[:], rhs=cmask.t[:, mv, :], start=False, stop=(mi == len(mks) - 1)))
                    E = Er.next()
                    S.op("act", [ps], [E], lambda e: e.activation(out=E.t[:, :], in_=ps.t[:, :], func=AF.Exp))
                    Es.append(E)
                pden = pS.next()
                for mt, E in enumerate(Es):
                    S.op("pe", [ones_b, E], [pden], lambda e, mt=mt, E=E: e.matmul(pden.t[:, :], lhsT=ones_b.t[:], rhs=E.t[:, :], start=(mt == 0), stop=(mt == nmt - 1)))
                rd = tf.next()
                S.op("dve", [pden], [rd], lambda e: e.tensor_scalar(out=rd.t[:, :], in0=pden.t[:, :], scalar1=1e-30, scalar2=None, op0=ALU.max))
                S.op("dve", [rd], [rd], lambda e: e.reciprocal(out=rd.t[:, :], in_=rd.t[:, :]))
                po = pO.next()
                Ps = []
                for mt, E in enumerate(Es):
                    P = Pr.next()
                    S.op("dve", [E, rd], [P], lambda e, E=E, P=P: e.tensor_tensor(out=P.t[:, :], in0=E.t[:, :], in1=rd.t[:, :], op=ALU.mult))
                    for tt in range(4):
                        S.op("pe", [P, vcmp[mt]], [po], lambda e, tt=tt, mt=mt, P=P: e.matmul(po.t[:, tt * 128:(tt + 1) * 128], lhsT=P.t[:, tt * 128:(tt + 1) * 128], rhs=vcmp[mt].t[:, :],
                                                                                           start=(mt == 0 and tt == 0), stop=(mt == nmt - 1), skip_group_check=True))
                        S.op("pe", [P, ov], [pI], lambda e, tt=tt, mt=mt, P=P: e.matmul(pI.t[:, tt * 128:(tt + 1) * 128], lhsT=P.t[:, tt * 128:(tt + 1) * 128], rhs=ov.t[:, mt, :],
                                                                                     start=(r == 0 and mt == 0 and tt == 0), stop=(r == R - 1 and mt == nmt - 1), skip_group_check=True))
                for tt in range(4):
                    S.op("dve", [po, gtok], [otok[tt]], lambda e, tt=tt: e.tensor_scalar(out=otok[tt].t[:, r * 128:(r + 1) * 128], in0=po.t[:, tt * 128:(tt + 1) * 128],
                                                                                      scalar1=gtok.t[:, tt, 3 * r:3 * r + 1], scalar2=None, op0=ALU.mult))

            for tt in range(4):
                gt = gt0 + tt
                x0 = 128 - 2 * gt
                s1 = sm.next()
                S.op("dve", [pI, ABt], [s1], lambda e: e.scalar_tensor_tensor(out=s1.t[:, :], in0=pI.t[:, tt * 128:(tt + 1) * 128], scalar=1.0, in1=ABt.t[:, x0:x0 + 128],
                                                                             op0=ALU.add, op1=ALU.mult))
                s2 = sm.next()
                S.op("dve", [s1, FBt], [s2], lambda e: e.scalar_tensor_tensor(out=s2.t[:, :], in0=s1.t[:, :], scalar=-1.0, in1=FBt.t[:, x0:x0 + 128], op0=ALU.add, op1=ALU.max))
                S.op("dve", [s2], [s2], lambda e: e.memset(s2.t[:, 0:1], 1e9))
                mm = m8.next()
                S.op("dve", [s2], [mm], lambda e: e.max(out=mm.t[:, 0:8], in_=s2.t[:, :]))
                s3 = sm.next()
                S.op("dve", [s2, mm], [s3], lambda e: e.match_replace(out=s3.t[:, :], in_to_replace=mm.t[:, 0:8], in_values=s2.t[:, :], imm_value=-1e30))
                S.op("dve", [s3], [mm], lambda e: e.max(out=mm.t[:, 8:16], in_=s3.t[:, :]))
                sb_ = smb.next()
                S.op("dve", [s2, mm], [sb_], lambda e: e.tensor_scalar(out=sb_.t[:, :], in0=s2.t[:, :], scalar1=mm.t[:, 15:16], scalar2=1.0, op0=ALU.is_ge, op1=ALU.subtract))
                for g in range(4):
                    pt = pmisc.next()
                    S.op("pe", [sb_, identb], [pt], lambda e, g=g, pt=pt: e.matmul(pt.t[0:32, 0:128], lhsT=sb_.t[:, 32 * g:32 * g + 32], rhs=identb.t[:], start=True, stop=True))
                    S.op("act", [pt], [selT[g]], lambda e, g=g, pt=pt: e.activation(out=selT[g].t[:, tt * 128:(tt + 1) * 128], in_=pt.t[0:32, 0:128], func=AF.Copy))

            def attend(r, kTsrc, kcol, vviews, vidx, kts, use_sel, gidx):
                po = pO.next()
                n = len(kts)
                pss = {}

                def emit_s(ni):
                    kt = kts[ni]
                    rel = kt - gt0
                    ps = pS.next()
                    pss[ni] = ps
                    has_w = (not use_sel) or (rel >= 0)
                    S.op("pe", [kTsrc, qr[r]], [ps], lambda e: e.matmul(ps.t[:, :], lhsT=kTsrc.t[:, kcol(kt):kcol(kt) + 128], rhs=qr[r].t[:, :], start=True,
                                                                       stop=not (use_sel or has_w)))
                    if use_sel:
                        g = kt // 16
                        S.op("pe", [ex32, selT[g]], [ps], lambda e: e.matmul(ps.t[:, :], lhsT=ex32.t[:, kt % 16, :], rhs=selT[g].t[:, :], start=False, stop=not has_w))
                    if has_w:
                        S.op("pe", [identb, wmask], [ps], lambda e: e.matmul(ps.t[:, :], lhsT=identb.t[:], rhs=wmask.t[:, rel + 4, :], start=False, stop=True))
                emit_s(0)
                for ni, kt in enumerate(kts):
                    if ni + 1 < n:
                        emit_s(ni + 1)
                    ps = pss.pop(ni)
                    E = Er.next()
                    S.op("act", [ps], [E], lambda e: e.activation(out=E.t[:, :], in_=ps.t[:, :], func=AF.Exp))
                    vv = vviews[vidx(kt)]
                    for tt in range(4):
                        S.op("pe", [E, vv], [po], lambda e, tt=tt: e.matmul(po.t[:, tt * 128:(tt + 1) * 128], lhsT=E.t[:, tt * 128:(tt + 1) * 128], rhs=vv.t[:, :],
                                                                          start=(ni == 0 and tt == 0), stop=(ni == n - 1), skip_group_check=True))
                        S.op("pe", [E, ones_b], [pD], lambda e, tt=tt: e.matmul(pD.t[:, (r * 4 + tt) * 2:(r * 4 + tt) * 2 + 1], lhsT=E.t[:, tt * 128:(tt + 1) * 128], rhs=ones_b.t[:, 0:1],
                                                                              start=(ni == 0 and tt == 0), stop=(ni == n - 1), skip_group_check=True))
                fc = fac.next()
                S.op("dve", [pD], [fc], lambda e: e.reciprocal(out=fc.t[:, 0:4], in_=pD.t[:, r * 8:r * 8 + 8:2]))
                for tt in range(4):
                    S.op("dve", [fc, gtok], [fc], lambda e, tt=tt: e.tensor_tensor(out=fc.t[:, 4 + tt:5 + tt], in0=fc.t[:, tt:tt + 1], in1=gtok.t[:, tt, 3 * r + gidx:3 * r + gidx + 1], op=ALU.mult))
                    S.op("dve", [po, fc, otok[tt]], [otok[tt]], lambda e, tt=tt: e.scalar_tensor_tensor(
                        out=otok[tt].t[:, r * 128:(r + 1) * 128], in0=po.t[:, tt * 128:(tt + 1) * 128], scalar=fc.t[:, 4 + tt:5 + tt],
                        in1=otok[tt].t[:, r * 128:(r + 1) * 128], op0=ALU.mult, op1=ALU.add))

            for r in range(R):
                attend(r, ksT, lambda kt: kt * 128, vsT, lambda kt: kt, list(range(0, gt0 + 4)), True, 1)
                attend(r, kwT, lambda kt: (kt * 128) % 1024, vwT, lambda kt: kt % 8, list(range(max(0, gt0 - 4), gt0 + 4)), False, 2)

            for r in range(R):
                pt = pmisc.next()
                for tt in range(4):
                    S.op("pe", [otok[tt], ident], [pt], lambda e, tt=tt: e.transpose(out=pt.t[:, tt * 128:(tt + 1) * 128], in_=otok[tt].t[:, r * 128:(r + 1) * 128], identity=ident.t[:]))
                o_ = oS.next()
                S.op("act", [pt], [o_], lambda e: e.activation(out=o_.t[:, :], in_=pt.t[:, :], func=AF.Copy))
                _ores, _oap = _oput(dr, r * 128, t0, T)
                S.dma("sp", [o_], [_ores], out=_oap, in_=o_.t[:, :])
        S.barrier()
    S.es = oes


BF = ml_dtypes.bfloat16
def make_consts():
    c = {}
    c['ident'] = np.eye(128, dtype=np.float32)
    c['identb'] = np.eye(128).astype(BF)
    j = np.arange(128)[:, None]; s = np.arange(128)[None, :]
    c['negU'] = np.where(j >= s, -1.0, 0.0).astype(BF)
    sl = np.arange(128)[:, None, None]; kd = np.arange(4)[None, :, None]; t = np.arange(512)[None, None, :]
    c['sbmask'] = np.where(128 * kd + sl < t, 0.0, -30000.0).astype(BF)
    return c
def make_consts_hg():
    c = {}
    s = np.arange(64)[:, None]; t = np.arange(64)[None, :]
    c['maskU'] = (s <= t).astype(np.float32)
    c['rmask'] = np.tile((np.arange(512) % 64 != 0).astype(np.float32)[None, :], (128, 1))
    return c
def make_consts_nsa(S):
    c = {}
    rot = np.zeros((128, 128), np.float32)
    for dp in range(64):
        rot[dp + 64, dp] = -1.0
    for dp in range(64, 128):
        rot[dp - 64, dp] = 1.0
    c['rotm'] = rot.astype(BF)
    NMT = (S // 16 + 127) // 128
    m = np.arange(NMT * 128); n = m - 1
    j = np.arange(128)
    ovm = (n[:, None] >= 0) & (16 * n[:, None] < 64 * j[None, :] + 64) & (16 * n[:, None] + 32 > 64 * j[None, :]) & (j[None, :] < S // 64) & (n[:, None] <= (S - 32) // 16)
    c['ov'] = np.ascontiguousarray(ovm.reshape(NMT, 128, 128).transpose(1, 0, 2)).astype(BF)
    i = np.arange(32)[:, None, None]; ktl = np.arange(16)[None, :, None]; s = np.arange(128)[None, None, :]
    c['ex32'] = np.where(i == 2 * ktl + s // 64, 30000.0, 0.0).astype(BF)
    sl = np.arange(128)[:, None, None]; rel = (np.arange(8) - 4)[None, :, None]; tl = np.arange(512)[None, None, :]
    c['wmask'] = np.where((128 * rel + sl <= tl) & (128 * rel + sl > tl - 512), 0.0, -30000.0).astype(BF)
    v = np.arange(4)[None, :, None]
    cm = np.where(16 * sl + 15 <= 512 * v + tl, 0.0, -30000.0)
    cm4 = np.where(sl == 0, -30000.0, 0.0) + 0 * tl
    c['cmask'] = np.concatenate([cm, cm4], axis=1).astype(BF)
    p = np.arange(128)[:, None]; x = np.arange(256)[None, :]
    cc = (p >= 64).astype(np.int64)
    c['ABt'] = ((x - 128) <= cc).astype(np.float32)
    c['FBt'] = np.where(((x - 128) == cc) | ((x - 128) == cc - 1), 1e9, -2.0).astype(np.float32)
    half = 64
    inv = 10000.0 ** (-np.arange(half, dtype=np.float32) / half)
    ang = np.arange(S, dtype=np.float32)[None, :] * np.concatenate([inv, inv])[:, None]
    c['cosT'] = np.cos(ang).astype(np.float32)
    c['sinT'] = np.sin(ang).astype(np.float32)
    return c


from concourse.bass_utils import run_bass_kernel_spmd

D_MODEL = 4096
SEQ = 8192
BATCH = 2
DFF = 11008
NMEM = 256
ALPHA = 4.0 ** 0.25
NCORE = 8
TOKC = 2048
HALO = 128
GROUPS = [[0, 1, 2, 3], [4, 5, 6, 7]]


def _consts_all():
    c = make_consts()
    c.update(make_consts_hg())
    c.update(make_consts_nsa(SEQ))
    return c


def _build_fused():
    nc = bass.Bass("TRN2", target_bir_lowering=False)
    NT = HALO + TOKC
    KC, FC = D_MODEL // 128, DFF // 128
    with ExitStack() as es:
        S = Sched(nc, es)
        dr = {}

        def din(name, shape, dt=F32):
            dr[name] = S.dram(name, shape, dt, kind="ExternalInput")
        din("xT", [D_MODEL, SEQ]); din("xTc", [D_MODEL, NT]); din("memT", [D_MODEL, NMEM])
        din("wA", [D_MODEL, 3584]); din("wC", [D_MODEL, 1816])
        din("ident", [128, 128]); din("identb", [128, 128], BF16); din("negU", [128, 128], BF16); din("sbmask", [128, 4, 512], BF16)
        din("maskU", [64, 64]); din("rmask", [128, 512]); din("hg_nw", [128, 128]); din("hg_lb", [128, 2, 4])
        din("cmp_w1", [2, 32, 128, 128]); din("cmp_w2", [128, 2, 128]); din("cmp_posT", [128, 2, 32])
        din("rotm", [128, 128], BF16)
        din("ov", [128, 4, 128], BF16); din("ex32", [32, 16, 128], BF16); din("wmask", [128, 8, 512], BF16); din("cmask", [128, 5, 512], BF16)
        din("ABt", [128, 256]); din("FBt", [128, 256]); din("cosT", [128, SEQ]); din("sinT", [128, SEQ])
        din("flag", [128, 1]); din("fsel", [128, 4]); din("fhal", [128, 4])
        for l in range(2):
            din(f"w_out{l}", [D_MODEL, D_MODEL]); din(f"w_q{l}", [D_MODEL, 512]); din(f"w_kv{l}", [D_MODEL, 1024]); din(f"w_o{l}", [512, D_MODEL])
            din(f"w_up{l}", [D_MODEL, 2 * DFF]); din(f"w_down{l}", [DFF, D_MODEL])
            din(f"ln_g{l}", [128, 3, KC]); din(f"ln_b{l}", [128, 3, KC]); din(f"conv{l}", [128, 3, FC])
        outT = S.dram("outT", [D_MODEL, TOKC], F32, kind="ExternalOutput")
        NB = SEQ // 512
        oA = [S.dram_cc(f"oA{i}", [1024, 512], BF16) for i in range(NB)]
        oAG0 = [S.dram_cc(f"oAG0_{i}", [4096, 512], BF16) for i in range(NB)]
        oC = [S.dram_cc(f"oC{i}", [1024, 512], BF16) for i in range(NB)]
        oAG1 = [S.dram_cc(f"oAG1_{i}", [4096, 512], BF16) for i in range(NB)]
        h1loc = S.dram_cc("h1loc", [D_MODEL, NT], F32)
        hb = [S.dram_cc(f"hb{i}", [256, TOKC], BF16) for i in range(16)]
        hAG = [S.dram_cc(f"hAG{i}", [4 * 256, TOKC], BF16) for i in range(16)]
        hl = [S.dram_cc(f"hl{i}", [2048, 128], F32) for i in range(2)]
        hlAG = [S.dram_cc(f"hlAG{i}", [4 * 2048, 128], F32) for i in range(2)]

        def put_fn(lst):
            def f(r0, t0, T):
                return lst[t0 // 512], lst[t0 // 512].t[r0:r0 + 128, 0:T]
            return f

        def oget_fn(lst):
            def f(k, cb, T):
                i, off = divmod(cb, 512)
                return lst[i], lst[i].t[k * 128:(k + 1) * 128, off:off + T]
            return f

        def hb_put(k, c0, T):
            return hb[k // 2], hb[k // 2].t[(k % 2) * 128:(k % 2) * 128 + 128, c0:c0 + T]

        def hAG_get(k, t0, T):
            rk, tl = divmod(t0, TOKC)
            r0 = rk * 256 + (k % 2) * 128
            return hAG[k // 2], hAG[k // 2].t[r0:r0 + 128, tl:tl + T]

        def hl_put(k):
            return hl[k // 16], hl[k // 16].t[(k % 16) * 128:(k % 16) * 128 + 128, 0:128]

        def hlAG_get(k, cq):
            r0 = cq * 2048 + (k % 16) * 128
            return hlAG[k // 16], hlAG[k // 16].t[r0:r0 + 128, 0:128]

        cfgA = dict(D=D_MODEL, SEQ=SEQ, NW=4)
        drA = dict(dr); drA["oT_put"] = put_fn(oA)
        emit_sb(S, cfgA, drA, [0, 1], 0, 256, 512, 512)
        emit_sb(S, cfgA, drA, [2, 3], 512, 768, 1024, 512)
        emit_hgrn(S, dict(cfgA, hstride=512), drA, [0, 1, 2, 3], 1536, 1664, 1792, 1920, 0)
        for i in range(NB):
            S.collective("AllGather", GROUPS, oA[i], oAG0[i])
        blocks = [(0, HALO, True)] + [(HALO + i * 512, 512, False) for i in range(TOKC // 512)]
        cfgP = dict(D=D_MODEL, DFF=DFF, XH=4, NM=NMEM, T=512, alpha=ALPHA, eps=1e-5, GF=8, NW=3, HALO=HALO, TOKC=TOKC)

        def post_dr(l):
            d = dict(flag=dr["flag"], fsel=dr["fsel"], fhal=dr["fhal"], memT=dr["memT"])
            for k in ("w_out", "w_q", "w_kv", "w_o", "w_up", "w_down", "ln_g", "ln_b", "conv"):
                d[k] = dr[f"{k}{l}"]
            return d
        drB = post_dr(0)
        drB.update(oAG_get=oget_fn(oAG0), hT_in=dr["xTc"], hT_out=h1loc, hTb_put=hb_put, hl_put=hl_put)
        emit_post(S, dict(cfgP, out_off=0, outb_off=HALO), drB, blocks)
        for i in range(16):
            S.collective("AllGather", GROUPS, hb[i], hAG[i])
        for i in range(2):
            S.collective("AllGather", GROUPS, hl[i], hlAG[i])
        drC = dict(dr); drC["hAG_get"] = hAG_get; drC["oT_put"] = put_fn(oC)
        emit_nsa(S, dict(D=D_MODEL, SEQ=SEQ, NW=3, TOKC=TOKC), drC)
        for i in range(NB):
            S.collective("AllGather", GROUPS, oC[i], oAG1[i])
        drD = post_dr(1)
        drD.update(oAG_get=oget_fn(oAG1), hT_in=h1loc, hlAG_get=hlAG_get, hT_out=outT)
        emit_post(S, dict(cfgP, out_off=HALO), drD, blocks)
        S.wait_all("sp", [outT])
        S.wait_all("pool", [outT])
    return nc


def _arr3(v, n):
    return np.ascontiguousarray(np.asarray(v, np.float32).reshape(3, n, 128).transpose(2, 0, 1))


def kernel(**inp):
    inp = {k: np.asarray(v) for k, v in inp.items()}
    cs = _consts_all()
    cores = list(range(NCORE))
    KC, FC = D_MODEL // 128, DFF // 128
    NT = HALO + TOKC
    xT = [np.ascontiguousarray(inp["x"][b].T) for b in range(BATCH)]
    memT = [np.ascontiguousarray(inp["mem"][b].T) for b in range(BATCH)]
    w_in = inp["ab_w_in"][0]
    AW = 2048
    wo = inp["ab_w_out"][0]
    wo_perm = np.ascontiguousarray(np.concatenate([np.concatenate([wo[512 * j:512 * j + 512], wo[2048 + 512 * j:2048 + 512 * j + 512]], axis=0) for j in range(4)], axis=0))
    wn = inp["nsa_w_in"][0]
    shared = dict(ident=cs["ident"], identb=cs["identb"], negU=cs["negU"], sbmask=cs["sbmask"], maskU=cs["maskU"], rmask=cs["rmask"],
                  hg_nw=np.ascontiguousarray(np.tile(inp["hgrn_norm_w"][0][None, :], (128, 1))),
                  cmp_w1=np.ascontiguousarray(inp["nsa_cmp_w1"][0]), cmp_w2=np.ascontiguousarray(inp["nsa_cmp_w2"][0].transpose(1, 0, 2)),
                  cmp_posT=np.ascontiguousarray(inp["nsa_cmp_pos"][0].transpose(2, 0, 1)),
                  rotm=cs["rotm"], ov=cs["ov"], ex32=cs["ex32"], wmask=cs["wmask"], cmask=cs["cmask"], ABt=cs["ABt"], FBt=cs["FBt"],
                  cosT=cs["cosT"], sinT=cs["sinT"])
    wouts = [wo_perm, np.ascontiguousarray(inp["nsa_w_out"][0])]
    for l in range(2):
        shared[f"w_out{l}"] = wouts[l]
        shared[f"w_q{l}"] = np.ascontiguousarray(inp["xa_w_q"][l]); shared[f"w_kv{l}"] = np.ascontiguousarray(inp["xa_w_kv"][l])
        shared[f"w_o{l}"] = np.ascontiguousarray(inp["xa_w_o"][l]); shared[f"w_up{l}"] = np.ascontiguousarray(inp["ffn_w_up"][l])
        shared[f"w_down{l}"] = np.ascontiguousarray(inp["ffn_w_down"][l])
        shared[f"ln_g{l}"] = _arr3(inp["ln_g"][l], KC); shared[f"ln_b{l}"] = _arr3(inp["ln_b"][l], KC); shared[f"conv{l}"] = _arr3(inp["ffn_conv"][l], FC)
    maps = []
    for c in cores:
        b, j = divmod(c, 4)
        hs = slice(512 * j, 512 * j + 512)
        def hc(seg, h0, n):
            c0 = seg * AW + 512 * j + 128 * h0
            return w_in[:, c0:c0 + 128 * n]
        segs = [hc(4, 0, 2), hc(5, 0, 2), hc(6, 0, 2), hc(4, 2, 2), hc(5, 2, 2), hc(6, 2, 2)]
        for h in range(4):
            segs += [hc(0, h, 1), hc(1, h, 1), hc(2, h, 1), hc(3, h, 1)]
        wA = np.ascontiguousarray(np.concatenate(segs, axis=1))
        lb = np.ascontiguousarray(inp["hgrn_lb"][:, hs].reshape(2, 4, 128).transpose(2, 0, 1))
        g = j
        segc = [wn[:, 1024 * g:1024 * g + 1024]] + [wn[:, 4096 + 512 * i + 128 * g:4096 + 512 * i + 128 * g + 128] for i in range(6)] + \
               [wn[:, 4096 + 3072 + 24 * g:4096 + 3072 + 24 * g + 24]]
        wC = np.ascontiguousarray(np.concatenate(segc, axis=1))
        t0 = j * TOKC
        xTc = np.zeros((D_MODEL, NT), np.float32)
        xTc[:, HALO:] = xT[b][:, t0:t0 + TOKC]
        if j > 0:
            xTc[:, :HALO] = xT[b][:, t0 - HALO:t0]
        fsel = np.zeros((128, 4), np.float32); fsel[:, j] = 1.0
        fhal = np.zeros((128, 4), np.float32)
        if j > 0:
            fhal[:, j - 1] = 1.0
        m = dict(shared)
        m.update(xT=xT[b], xTc=xTc, memT=memT[b], wA=wA, wC=wC, hg_lb=lb, flag=np.full((128, 1), 1.0 if j > 0 else 0.0, np.float32), fsel=fsel, fhal=fhal)
        maps.append(m)
    res = run_bass_kernel_spmd(_build_fused(), maps, core_ids=cores)
    out = np.empty((BATCH, SEQ, D_MODEL), np.float32)
    for c in cores:
        b, j = divmod(c, 4)
        out[b, j * TOKC:(j + 1) * TOKC, :] = res.results[c]["outT"].T
    return out
```
